# Optimizing a Trainium2 kernel written in Bass

```python
import math
import jax
import jax.numpy as jnp
from jax import lax
import numpy as np

D_MODEL = 2048
BATCH = 4
SEQ = 8192
DEPTH = 2

GRID_W = 64
CTX_LEN = 256
HEAD_DIM = 128
ROPE_THETA = 10000.0

WA_HEADS = 8
WA_KV_HEADS = 2
WINDOW = 128
WA_BLOCK = 128

DF_HEADS = 8
DF_QK_DIM = 64
DF_V_DIM = 2 * DF_QK_DIM
DF_Q_BLOCK = 128

DN_HEADS = 8
DN_DK = 128
DN_DV = 128
DN_CONV = 3
DN_CHUNK = 64

D_FF = 5632
FFN_CONV = 3

LN_EPS = 1e-6
RMS_EPS = 1e-6
DEEPNORM_ALPHA = (2 * DEPTH) ** 0.25
DEEPNORM_BETA = (8 * DEPTH) ** -0.25

WA_Q = WA_HEADS * HEAD_DIM
WA_KV = WA_KV_HEADS * HEAD_DIM
DF_QK = DF_HEADS * 2 * DF_QK_DIM
DF_V = DF_HEADS * DF_V_DIM
DN_QK = DN_HEADS * DN_DK
DN_V = DN_HEADS * DN_DV
DN_QKV = 2 * DN_QK + DN_V
IN_SPLITS = (WA_Q, WA_KV, WA_KV, DF_QK, DF_QK, DF_V, DN_QKV, DN_V, 2 * DN_HEADS, 2 * DN_HEADS, 3 * D_MODEL)
IN_WIDTH = sum(IN_SPLITS)

f32 = jnp.float32

kernel_name = 'hybrid_gated_branch_diffusion_trunk'


def layer_norm(x, g=None, b=None):
    xf = x.astype(f32)
    mu = jnp.mean(xf, axis=-1, keepdims=True)
    var = jnp.mean(jnp.square(xf - mu), axis=-1, keepdims=True)
    y = (xf - mu) * lax.rsqrt(var + LN_EPS)
    if g is not None:
        y = y * g.astype(f32) + b.astype(f32)
    return y.astype(x.dtype)


def rms_norm(x, w):
    xf = x.astype(f32)
    return xf * lax.rsqrt(jnp.mean(jnp.square(xf), axis=-1, keepdims=True) + RMS_EPS) * w.astype(f32)


def l2_normalize(x):
    return x * lax.rsqrt(jnp.sum(jnp.square(x), axis=-1, keepdims=True) + 1e-6)


def ada_ln(x, shift, scale):
    return layer_norm(x) * (1 + scale[:, None]) + shift[:, None]


def dwconv_centred(x, w, b=None):
    k = w.shape[0]
    pad = k // 2
    n = x.shape[1]
    xp = jnp.pad(x, ((0, 0), (pad, pad), (0, 0)))
    y = xp[:, 0:n] * w[0]
    for i in range(1, k):
        y = y + xp[:, i:i + n] * w[i]
    return y if b is None else y + b


def axial_rope_tables(n, dim):
    rows = n // GRID_W
    row = jnp.repeat(jnp.arange(rows, dtype=f32), GRID_W)
    col = jnp.tile(jnp.arange(GRID_W, dtype=f32), rows)
    axis_dim = dim // 2
    inv_freq = ROPE_THETA ** (-jnp.arange(0, axis_dim, 2, dtype=f32) / axis_dim)
    ang_r = row[:, None] * inv_freq[None]
    ang_c = col[:, None] * inv_freq[None]
    ang = jnp.concatenate([ang_r, ang_r, ang_c, ang_c], axis=-1)
    return jnp.cos(ang), jnp.sin(ang)


def apply_axial_rope(x, cos, sin):
    x1, x2, x3, x4 = jnp.split(x.astype(f32), 4, axis=-1)
    rot = jnp.concatenate([-x2, x1, -x4, x3], axis=-1)
    return (x.astype(f32) * cos[:, None, :] + rot * sin[:, None, :]).astype(x.dtype)


def _in_split(p):
    offs, acc = [], 0
    for w in IN_SPLITS[:-1]:
        acc += w
        offs.append(acc)
    return jnp.split(p, offs, axis=-1)


def window_sink_attention(q, k, v, ck, cv, sink):
    bsz, n, h, d = q.shape
    g = h // WA_KV_HEADS
    nb = n // WA_BLOCK
    scale = d ** -0.5
    qb = q.reshape(bsz, nb, WA_BLOCK, WA_KV_HEADS, g, d)

    def band(t):
        tp = jnp.pad(t, ((0, 0), (WA_BLOCK, WA_BLOCK), (0, 0), (0, 0)))
        tp = tp.reshape(bsz, nb + 2, WA_BLOCK, WA_KV_HEADS, d)
        return jnp.concatenate([tp[:, :-2], tp[:, 1:-1], tp[:, 2:]], axis=2)

    kb, vb = band(k), band(v)
    s_loc = jnp.einsum('bnqkgd,bnjkd->bnkgqj', qb, kb).astype(f32) * scale
    qpos = jnp.arange(nb)[:, None] * WA_BLOCK + jnp.arange(WA_BLOCK)[None]
    kpos = jnp.arange(nb)[:, None] * WA_BLOCK - WA_BLOCK + jnp.arange(3 * WA_BLOCK)[None]
    kp = kpos[:, None, :]
    valid = (kp >= 0) & (kp < n) & (jnp.abs(kp - qpos[:, :, None]) <= WINDOW)
    s_loc = jnp.where(valid[None, :, None, None], s_loc, -jnp.inf)
    s_ctx = jnp.einsum('bnqkgd,bjkd->bnkgqj', qb, ck).astype(f32) * scale
    sink_b = sink.astype(f32).reshape(WA_KV_HEADS, g)[None, None, :, :, None, None]
    m = jnp.maximum(jnp.maximum(s_loc.max(-1, keepdims=True), s_ctx.max(-1, keepdims=True)), sink_b)
    p_loc = jnp.exp(s_loc - m)
    p_ctx = jnp.exp(s_ctx - m)
    inv_den = 1.0 / (p_loc.sum(-1, keepdims=True) + p_ctx.sum(-1, keepdims=True) + jnp.exp(sink_b - m))
    o = (jnp.einsum('bnkgqj,bnjkd->bnqkgd', p_loc * inv_den, vb.astype(f32))
         + jnp.einsum('bnkgqj,bjkd->bnqkgd', p_ctx * inv_den, cv.astype(f32)))
    return o.reshape(bsz, n, h * d)


def ctx_sink_attention(q, k, v, sink):
    bsz, L, h, d = q.shape
    g = h // WA_KV_HEADS
    qg = q.reshape(bsz, L, WA_KV_HEADS, g, d)
    s = jnp.einsum('bqkgd,bjkd->bkgqj', qg, k).astype(f32) * d ** -0.5
    sink_b = sink.astype(f32).reshape(WA_KV_HEADS, g)[None, :, :, None, None]
    m = jnp.maximum(s.max(-1, keepdims=True), sink_b)
    p = jnp.exp(s - m)
    p = p / (p.sum(-1, keepdims=True) + jnp.exp(sink_b - m))
    return jnp.einsum('bkgqj,bjkd->bqkgd', p, v.astype(f32)).reshape(bsz, L, h * d)


def diff_attend(q1, q2, k1, k2, v, lam):
    scale = q1.shape[-1] ** -0.5
    s1 = jnp.einsum('bqhd,bkhd->bhqk', q1, k1).astype(f32) * scale
    s2 = jnp.einsum('bqhd,bkhd->bhqk', q2, k2).astype(f32) * scale
    w = jax.nn.softmax(s1, axis=-1) - lam * jax.nn.softmax(s2, axis=-1)
    return jnp.einsum('bhqk,bkhd->bqhd', w, v.astype(f32))


def diff_attention_blocks(q1, q2, k1, k2, v, lam):
    bsz, n, h, dq = q1.shape
    nb = n // DF_Q_BLOCK

    def blocks(t):
        return jnp.moveaxis(t.reshape(bsz, nb, DF_Q_BLOCK, h, dq), 1, 0)

    o = lax.map(lambda qq: diff_attend(qq[0], qq[1], k1, k2, v, lam), (blocks(q1), blocks(q2)))
    return jnp.moveaxis(o, 0, 1).reshape(bsz, n, h, v.shape[-1])


def diff_post(o, subln, lam_init):
    bsz, n = o.shape[:2]
    return (rms_norm(o, subln) * (1.0 - lam_init)).reshape(bsz, n, DF_V)


def gdn_prep(qkv, conv_w):
    qkv = jax.nn.silu(dwconv_centred(qkv, conv_w)).astype(f32)
    q, k, v = jnp.split(qkv, [DN_QK, 2 * DN_QK], axis=-1)
    bsz, n = q.shape[:2]
    q = l2_normalize(q.reshape(bsz, n, DN_HEADS, DN_DK)) * DN_DK ** -0.5
    k = l2_normalize(k.reshape(bsz, n, DN_HEADS, DN_DK))
    v = v.reshape(bsz, n, DN_HEADS, DN_DV)
    return q, k, v


def gdn_gates(a, b, a_log, dt_bias):
    bsz, n = a.shape[:2]
    a = a.astype(f32).reshape(bsz, n, 2, DN_HEADS)
    b = b.astype(f32).reshape(bsz, n, 2, DN_HEADS)
    g = -jnp.exp(a_log.astype(f32)) * jax.nn.softplus(a + dt_bias.astype(f32))
    return g, jax.nn.sigmoid(b)


def gated_delta_chunked(q, k, v, g, beta, s0, want_out):
    bsz, n, h, dk = k.shape
    dv = v.shape[-1]
    c = DN_CHUNK
    nc = n // c

    def chunks(t):
        return jnp.moveaxis(t.reshape((bsz, nc, c) + t.shape[2:]), 2, 3)

    qc, kc, vc, gc, bc = chunks(q), chunks(k), chunks(v), chunks(g), chunks(beta)
    gcum = jnp.cumsum(gc, axis=-1)
    tril = jnp.tril(jnp.ones((c, c), bool))
    tril_strict = jnp.tril(jnp.ones((c, c), bool), -1)
    decay = jnp.where(tril, jnp.exp(jnp.minimum(gcum[..., :, None] - gcum[..., None, :], 0.0)), 0.0)
    kb = kc * bc[..., None]
    m_low = jnp.where(tril_strict, jnp.einsum('bnhid,bnhjd->bnhij', kb, kc) * decay, 0.0)
    a_mat = m_low + jnp.eye(c, dtype=f32)
    rhs = jnp.concatenate([vc * bc[..., None], kb * jnp.exp(gcum)[..., None]], axis=-1)
    sol = lax.linalg.triangular_solve(a_mat, rhs, left_side=True, lower=True, unit_diagonal=True)
    u, w = sol[..., :dv], sol[..., dv:]
    kdec = kc * jnp.exp(gcum[..., -1:] - gcum)[..., None]
    glast = jnp.exp(gcum[..., -1])
    xs = (u, w, kdec, glast)
    if want_out:
        attn = jnp.einsum('bnhid,bnhjd->bnhij', qc, kc) * decay
        qg = qc * jnp.exp(gcum)[..., None]
        xs = xs + (attn, qg)
    xs = tuple(jnp.moveaxis(t, 1, 0) for t in xs)

    def step(s, inp):
        u_i, w_i, kdec_i, gl_i = inp[:4]
        v_new = u_i - jnp.einsum('bhcd,bhde->bhce', w_i, s)
        s_new = s * gl_i[..., None, None] + jnp.einsum('bhcd,bhce->bhde', kdec_i, v_new)
        o_i = None
        if want_out:
            attn_i, qg_i = inp[4:]
            o_i = jnp.einsum('bhcd,bhde->bhce', qg_i, s) + jnp.einsum('bhij,bhje->bhie', attn_i, v_new)
        return s_new, o_i

    s_final, o = lax.scan(step, s0, xs)
    if not want_out:
        return None, s_final
    o = jnp.moveaxis(jnp.moveaxis(o, 0, 1), 2, 3).reshape(bsz, n, h, dv)
    return o, s_final


def gdn_bidirectional(q, k, v, g, beta, s0, want_out):
    o_f, s_f = gated_delta_chunked(q, k, v, g[:, :, 0], beta[:, :, 0], s0[0], want_out)
    flip = lambda t: jnp.flip(t, axis=1)
    o_b, s_b = gated_delta_chunked(flip(q), flip(k), flip(v), flip(g[:, :, 1]), flip(beta[:, :, 1]), s0[1], want_out)
    o = o_f + flip(o_b) if want_out else None
    return o, (s_f, s_b)


def gdn_output(o, z, w):
    bsz, n = z.shape[:2]
    zf = z.astype(f32).reshape(bsz, n, DN_HEADS, DN_DV)
    return (rms_norm(o, w) * jax.nn.silu(zf)).reshape(bsz, n, DN_V)


def merge_branches(ya, yb, yc, gates, w_pa, w_pb, w_pc, w_o):
    dt = w_o.dtype
    ga, gb, gc = jnp.split(jax.nn.sigmoid(gates.astype(f32)), 3, axis=-1)
    m = ga * (ya.astype(dt) @ w_pa) + gb * (yb.astype(dt) @ w_pb) + gc * (yc.astype(dt) @ w_pc)
    return m.astype(dt) @ w_o


def mix_sublayer(hx, hc, rope_a, rope_d, layer, w_in, sink, lam_q1, lam_k1, lam_q2, lam_k2, df_subln,
                 dn_conv, dn_a_log, dn_dt_bias, dn_norm, w_pa, w_pb, w_pc, w_o, ctx_out):
    bsz, n, _ = hx.shape
    L = hc.shape[1]
    (qa, ka, va, qd, kd, vd, qkv_n, z_n, a_n, b_n, gates) = _in_split(hx @ w_in)
    (qa_c, ka_c, va_c, qd_c, kd_c, vd_c, qkv_nc, z_nc, a_nc, b_nc, gates_c) = _in_split(hc @ w_in)
    cos_a, sin_a = rope_a
    cos_d, sin_d = rope_d

    qa = apply_axial_rope(qa.reshape(bsz, n, WA_HEADS, HEAD_DIM), cos_a, sin_a)
    ka = apply_axial_rope(ka.reshape(bsz, n, WA_KV_HEADS, HEAD_DIM), cos_a, sin_a)
    va = va.reshape(bsz, n, WA_KV_HEADS, HEAD_DIM)
    ka_c = ka_c.reshape(bsz, L, WA_KV_HEADS, HEAD_DIM)
    va_c = va_c.reshape(bsz, L, WA_KV_HEADS, HEAD_DIM)
    ya = window_sink_attention(qa, ka, va, ka_c, va_c, sink)

    lam_init = 0.8 - 0.6 * math.exp(-0.3 * layer)
    lam = (jnp.exp(jnp.sum(lam_q1.astype(f32) * lam_k1.astype(f32)))
           - jnp.exp(jnp.sum(lam_q2.astype(f32) * lam_k2.astype(f32))) + lam_init)
    qd = qd.reshape(bsz, n, DF_HEADS, 2, DF_QK_DIM)
    kd = kd.reshape(bsz, n, DF_HEADS, 2, DF_QK_DIM)
    qd_c = qd_c.reshape(bsz, L, DF_HEADS, 2, DF_QK_DIM)
    kd_c = kd_c.reshape(bsz, L, DF_HEADS, 2, DF_QK_DIM)
    vd_c = vd_c.reshape(bsz, L, DF_HEADS, DF_V_DIM)
    q1 = apply_axial_rope(qd[:, :, :, 0], cos_d, sin_d)
    q2 = apply_axial_rope(qd[:, :, :, 1], cos_d, sin_d)
    k1 = jnp.concatenate([kd_c[:, :, :, 0], apply_axial_rope(kd[:, :, :, 0], cos_d, sin_d)], axis=1)
    k2 = jnp.concatenate([kd_c[:, :, :, 1], apply_axial_rope(kd[:, :, :, 1], cos_d, sin_d)], axis=1)
    v_all = jnp.concatenate([vd_c, vd.reshape(bsz, n, DF_HEADS, DF_V_DIM)], axis=1)
    yb = diff_post(diff_attention_blocks(q1, q2, k1, k2, v_all, lam), df_subln, lam_init)

    zero = jnp.zeros((bsz, DN_HEADS, DN_DK, DN_DV), f32)
    qn_c, kn_c, vn_c = gdn_prep(qkv_nc, dn_conv)
    g_c, beta_c = gdn_gates(a_nc, b_nc, dn_a_log, dn_dt_bias)
    on_c, s_ctx = gdn_bidirectional(qn_c, kn_c, vn_c, g_c, beta_c, (zero, zero), ctx_out)
    qn, kn, vn = gdn_prep(qkv_n, dn_conv)
    g_n, beta_n = gdn_gates(a_n, b_n, dn_a_log, dn_dt_bias)
    on, _ = gdn_bidirectional(qn, kn, vn, g_n, beta_n, s_ctx, True)
    yc = gdn_output(on, z_n, dn_norm)

    out_x = merge_branches(ya, yb, yc, gates, w_pa, w_pb, w_pc, w_o)
    if not ctx_out:
        return out_x, None
    ya_c = ctx_sink_attention(qa_c.reshape(bsz, L, WA_HEADS, HEAD_DIM), ka_c, va_c, sink)
    yb_c = diff_post(diff_attend(qd_c[:, :, :, 0], qd_c[:, :, :, 1], kd_c[:, :, :, 0], kd_c[:, :, :, 1], vd_c, lam),
                     df_subln, lam_init)
    yc_c = gdn_output(on_c, z_nc, dn_norm)
    out_c = merge_branches(ya_c, yb_c, yc_c, gates_c, w_pa, w_pb, w_pc, w_o)
    return out_x, out_c


def conv_ffn(h, w_up, conv_w, conv_b, w_down):
    u = dwconv_centred(h @ w_up, conv_w, conv_b)
    a, b = jnp.split(u, 2, axis=-1)
    return (jax.nn.silu(a) * b) @ w_down


def setup_inputs(seed: int = 0) -> dict:
    key = jax.random.key(seed)
    ks = jax.random.split(key, 32)
    cnt = [0]

    def nxt():
        cnt[0] += 1
        return ks[cnt[0] - 1]

    def nrm(shape, std):
        return jax.random.normal(nxt(), shape, f32) * std

    D = D_MODEL
    beta = DEEPNORM_BETA
    dt = jnp.exp(jax.random.uniform(nxt(), (DEPTH, 2, DN_HEADS), f32, math.log(1e-3), math.log(1e-1)))
    a_log = jnp.log(jax.random.uniform(nxt(), (DEPTH, 2, DN_HEADS), f32, 1.0, 16.0))
    return {
        'x': nrm((BATCH, SEQ, D), 1.0),
        'c': nrm((BATCH, D), 1.0),
        'ctx': nrm((BATCH, CTX_LEN, D), 1.0),
        'c_ctx': nrm((D,), 1.0),
        'w_mod': nrm((DEPTH, D, 6 * D), D ** -0.5),
        'b_mod': nrm((DEPTH, 6 * D), 0.01),
        'w_in': nrm((DEPTH, D, IN_WIDTH), D ** -0.5),
        'wa_sink': nrm((DEPTH, WA_HEADS), 1.0),
        'df_lam_q1': nrm((DEPTH, DF_QK_DIM), 0.1),
        'df_lam_k1': nrm((DEPTH, DF_QK_DIM), 0.1),
        'df_lam_q2': nrm((DEPTH, DF_QK_DIM), 0.1),
        'df_lam_k2': nrm((DEPTH, DF_QK_DIM), 0.1),
        'df_subln': 1.0 + nrm((DEPTH, DF_V_DIM), 0.02),
        'dn_conv': nrm((DEPTH, DN_CONV, DN_QKV), DN_CONV ** -0.5),
        'dn_a_log': a_log,
        'dn_dt_bias': dt + jnp.log(-jnp.expm1(-dt)),
        'dn_norm': 1.0 + nrm((DEPTH, DN_DV), 0.02),
        'w_branch_a': nrm((DEPTH, WA_Q, D), WA_Q ** -0.5 * beta),
        'w_branch_b': nrm((DEPTH, DF_V, D), DF_V ** -0.5 * beta),
        'w_branch_c': nrm((DEPTH, DN_V, D), DN_V ** -0.5 * beta),
        'w_o': nrm((DEPTH, D, D), D ** -0.5 * beta),
        'ln1_g': 1.0 + nrm((DEPTH, D), 0.02),
        'ln1_b': nrm((DEPTH, D), 0.01),
        'w_up': nrm((DEPTH, D, 2 * D_FF), D ** -0.5),
        'ffn_conv_w': nrm((DEPTH, FFN_CONV, 2 * D_FF), FFN_CONV ** -0.5),
        'ffn_conv_b': nrm((DEPTH, 2 * D_FF), 0.01),
        'w_down': nrm((DEPTH, D_FF, D), D_FF ** -0.5 * beta),
        'ln2_g': 1.0 + nrm((DEPTH, D), 0.02),
        'ln2_b': nrm((DEPTH, D), 0.01),
    }


def reference(x, c, ctx, c_ctx, w_mod, b_mod, w_in, wa_sink, df_lam_q1, df_lam_k1, df_lam_q2, df_lam_k2,
              df_subln, dn_conv, dn_a_log, dn_dt_bias, dn_norm, w_branch_a, w_branch_b, w_branch_c, w_o,
              ln1_g, ln1_b, w_up, ffn_conv_w, ffn_conv_b, w_down, ln2_g, ln2_b):
    n = x.shape[1]
    rope_a = axial_rope_tables(n, HEAD_DIM)
    rope_d = axial_rope_tables(n, DF_QK_DIM)
    cc = c_ctx[None]
    cx = ctx
    for l in range(DEPTH):
        ctx_out = l < DEPTH - 1
        mx = jnp.split(jax.nn.silu(c) @ w_mod[l] + b_mod[l], 6, axis=-1)
        mc = jnp.split(jax.nn.silu(cc) @ w_mod[l] + b_mod[l], 6, axis=-1)
        hx = ada_ln(x, mx[0], mx[1])
        hc = ada_ln(cx, mc[0], mc[1])
        ox, oc = mix_sublayer(hx, hc, rope_a, rope_d, l, w_in[l], wa_sink[l], df_lam_q1[l], df_lam_k1[l],
                              df_lam_q2[l], df_lam_k2[l], df_subln[l], dn_conv[l], dn_a_log[l], dn_dt_bias[l],
                              dn_norm[l], w_branch_a[l], w_branch_b[l], w_branch_c[l], w_o[l], ctx_out)
        x = layer_norm(DEEPNORM_ALPHA * x + mx[2][:, None] * ox, ln1_g[l], ln1_b[l])
        fx = conv_ffn(ada_ln(x, mx[3], mx[4]), w_up[l], ffn_conv_w[l], ffn_conv_b[l], w_down[l])
        x = layer_norm(DEEPNORM_ALPHA * x + mx[5][:, None] * fx, ln2_g[l], ln2_b[l])
        if ctx_out:
            cx = layer_norm(DEEPNORM_ALPHA * cx + mc[2][:, None] * oc, ln1_g[l], ln1_b[l])
            fc = conv_ffn(ada_ln(cx, mc[3], mc[4]), w_up[l], ffn_conv_w[l], ffn_conv_b[l], w_down[l])
            cx = layer_norm(DEEPNORM_ALPHA * cx + mc[5][:, None] * fc, ln2_g[l], ln2_b[l])
    return x
```

```python
import numpy as np
from contextlib import ExitStack
import concourse.bass as bass
import concourse.mybir as mybir
from concourse.bass_utils import run_bass_kernel_spmd

F32 = mybir.dt.float32
BF16 = mybir.dt.bfloat16
AF = mybir.ActivationFunctionType
ALU = mybir.AluOpType
AX = mybir.AxisListType

ENGS = ("pe", "act", "dve", "pool", "sp")
EPOCH = 30000
NDMASEM = 40


import types


def freeze(fn):
    if fn is None or fn.__closure__ is None:
        return fn
    cells = tuple(types.CellType(c.cell_contents) for c in fn.__closure__)
    return types.FunctionType(fn.__code__, fn.__globals__, fn.__name__, fn.__defaults__, cells)


class T:
    __slots__ = ("t", "name", "w", "r", "rd")

    def __init__(self, t, name=""):
        self.t = t
        self.name = name
        self.w = None
        self.r = {}
        self.rd = {}

    def sub(self, name=""):
        return T(self.t, name or self.name)

    def __getitem__(self, idx):
        return self.t[idx]


class Op:
    __slots__ = ("eng", "seq", "fn", "deps", "signal", "isdma", "sem", "val", "dsem", "dval", "dprev", "dslot")

    def __init__(self, eng, seq, fn, isdma):
        self.eng = eng
        self.seq = seq
        self.fn = fn
        self.deps = []
        self.signal = False
        self.isdma = isdma
        self.sem = None
        self.val = 0
        self.dsem = None
        self.dval = 0
        self.dprev = 0


class Prog:
    def __init__(self, nc, same_raw=True):
        self.nc = nc
        self.ops = {e: [] for e in ENGS}
        self.seen = {e: {} for e in ENGS}
        self.seen_dma = {e: set() for e in ENGS}
        self.pending = {e: [] for e in ENGS}
        self.ndma = 0
        self.same_raw = same_raw
        self.out_dmas = []
        self.all_dmas_unwaited = []
        self.dcount = {e: 0 for e in ENGS}
        self.lastd = {}

    def _need(self, op, tgt):
        if tgt is None or tgt is op:
            return
        e = op.eng
        if tgt.isdma:
            if id(tgt) in self.seen_dma[e]:
                return
            self.seen_dma[e].add(id(tgt))
            op.deps.append(tgt)
            return
        if tgt.eng == e:
            if e == "pe" or not self.same_raw:
                return
        if self.seen[e].get(tgt.eng, 0) >= tgt.seq:
            return
        self.seen[e][tgt.eng] = tgt.seq
        tgt.signal = True
        op.deps.append(tgt)

    def _record(self, eng, fn, reads, writes, isdma=False, raw_only_same=True):
        ops = self.ops[eng]
        op = Op(eng, len(ops) + 1, fn, isdma)
        if isdma:
            op.dslot = self.dcount[eng] % NDMASEM
            self.dcount[eng] += 1
            self.lastd[(eng, op.dslot)] = op
        for t in self.pending[eng]:
            self._need(op, t)
        self.pending[eng] = []
        for b in reads:
            self._need(op, b.w)
        for b in writes:
            self._need(op, b.w)
            for re_, r in b.r.items():
                if re_ == eng and not isdma:
                    continue
                self._need(op, r)
            for r in b.rd.values():
                self._need(op, r)
        for b in reads:
            if isdma:
                b.rd[(eng, op.dslot)] = op
            else:
                b.r[eng] = op
        for b in writes:
            b.w = op
            b.r = {}
            b.rd = {}
        ops.append(op)
        return op

    def op(self, eng, fn, reads=(), writes=()):
        return self._record(eng, freeze(fn), reads, writes)

    def dma(self, out_ap, in_ap, reads=(), writes=(), eng="sp", is_out=False, **kw):
        def fn(e, out_ap=out_ap, in_ap=in_ap, kw=kw):
            return e.dma_start(out=out_ap, in_=in_ap, **kw)
        op = self._record(eng, fn, reads, writes, isdma=True)
        op.signal = True
        self.ndma += 1
        if is_out:
            self.out_dmas.append(op)
        return op

    def barrier(self, label=None):
        import sys as _sys
        if not hasattr(self, "marks"):
            self.marks = []
        self.marks.append((label or _sys._getframe(1).f_code.co_name, {e: len(self.ops[e]) for e in ENGS}))
        lasts = []
        for e in ENGS:
            for o in reversed(self.ops[e]):
                if not o.isdma:
                    lasts.append(o)
                    break
        dmas = list(self.lastd.values())
        for e in ENGS:
            self.pending[e] = self.pending[e] + lasts + dmas

    def emit(self, stack):
        nc = self.nc
        self.barrier()
        fin = {}
        for e in ENGS:
            op = Op(e, len(self.ops[e]) + 1, None, False)
            for t in self.pending[e]:
                self._need(op, t)
            self.ops[e].append(op)
        for e in ENGS:
            cnt = 0
            sems = []
            for op in self.ops[e]:
                if op.isdma or not op.signal:
                    continue
                ep = cnt // EPOCH
                while len(sems) <= ep:
                    sems.append(stack.enter_context(nc.semaphore(f"c_{e}_{len(sems)}")))
                cnt += 1
                op.sem = sems[ep]
                op.val = cnt - ep * EPOCH
        dpool = {}
        duse = {}
        for e in ENGS:
            k = 0
            for op in self.ops[e]:
                if not op.isdma:
                    continue
                if e not in dpool:
                    dpool[e] = [stack.enter_context(nc.semaphore(f"d_{e}_{i}")) for i in range(NDMASEM)]
                    duse[e] = [0] * NDMASEM
                i = op.dslot
                k += 1
                op.dsem = dpool[e][i]
                op.dprev = 16 * duse[e][i]
                duse[e][i] += 1
                op.dval = 16 * duse[e][i]
        block = stack.enter_context(nc.Block())

        def run(eng, e):
            for op in self.ops[e]:
                for d in op.deps:
                    if d.isdma:
                        eng.wait_ge(d.dsem, d.dval)
                    else:
                        eng.wait_ge(d.sem, d.val)
                if op.fn is None:
                    continue
                if op.isdma:
                    if op.dprev:
                        eng.wait_ge(op.dsem, op.dprev)
                    op.fn(eng).then_inc(op.dsem, 16)
                else:
                    ins = op.fn(eng)
                    if op.signal:
                        ins.then_inc(op.sem, 1)

        block.tensor(lambda eng: run(eng, "pe"))
        block.scalar(lambda eng: run(eng, "act"))
        block.vector(lambda eng: run(eng, "dve"))
        block.gpsimd(lambda eng: run(eng, "pool"))
        block.sync(lambda eng: run(eng, "sp"))

    def stats(self):
        return {e: len(self.ops[e]) for e in ENGS}

import math

class Cfg:
    def __init__(self, D=2048, N=8192, L=256, DFF=5632, NL=2):
        self.D, self.N, self.L, self.DFF, self.NL = D, N, L, DFF, NL
        self.T = N + L
        self.KC = D // 128
        self.NT = self.T // 128
        self.IN_W = 1024 + 256 + 256 + 1024 + 1024 + 1024 + 3072 + 1024 + 16 + 16 + 3 * D

GRID_W = 64
ALPHA = None
LN_EPS = 1e-6

O_QA, O_KA, O_VA, O_QD, O_KD, O_VD, O_QKV, O_Z, O_A, O_B, O_G = (
    0, 1024, 1280, 1536, 2560, 3584, 4608, 7680, 8704, 8720, 8736)

M_ID, M_TRILS, M_TRIUI, M_TRIUS, M_TRILI, M_ONES, M_BD16, M_OFF32, M_OFF64, M_OFF128, M_NEG = range(11)
NMASK = 11


def rope_tables(n, dim):
    rows = n // GRID_W
    row = np.repeat(np.arange(rows, dtype=np.float32), GRID_W)
    col = np.tile(np.arange(GRID_W, dtype=np.float32), rows)
    axis_dim = dim // 2
    inv_freq = (10000.0 ** (-np.arange(0, axis_dim, 2, dtype=np.float32) / axis_dim)).astype(np.float32)
    ang_r = row[:, None] * inv_freq[None]
    ang_c = col[:, None] * inv_freq[None]
    ang = np.concatenate([ang_r, ang_r, ang_c, ang_c], axis=-1).astype(np.float32)
    q = dim // 4
    sgn = np.concatenate([-np.ones(q), np.ones(q), -np.ones(q), np.ones(q)]).astype(np.float32)
    return np.cos(ang).T.astype(np.float32), (np.sin(ang) * sgn[None]).T.astype(np.float32)


def make_consts(cfg):
    T, L = cfg.T, cfg.L
    def tab(dim, rep, scale):
        c, s = rope_tables(cfg.N, dim)
        out = np.zeros((4, 128, T), np.float32)
        c = np.tile(c, (rep, 1)); s = np.tile(s, (rep, 1))
        out[0, :, :L] = scale; out[0, :, L:] = c * scale
        out[1, :, L:] = s * scale
        out[2, :, :L] = 1.0; out[2, :, L:] = c
        out[3, :, L:] = s
        return out
    ropeA = tab(128, 1, 128 ** -0.5)
    ropeD = tab(64, 2, 64 ** -0.5)
    p = np.arange(128)[:, None]; f = np.arange(128)[None, :]
    m = np.zeros((NMASK, 128, 128), np.float32)
    m[M_ID] = (p == f); m[M_TRILS] = (p > f); m[M_TRIUI] = (p <= f); m[M_TRIUS] = (p < f); m[M_TRILI] = (p >= f)
    m[M_ONES] = 1.0
    m[M_NEG] = -1.0
    m[M_BD16] = (p // 16 == f // 16)
    m[M_OFF32] = (p // 32 == f // 32) & (p // 16 != f // 16)
    m[M_OFF64] = (p // 64 == f // 64) & (p // 32 != f // 32)
    m[M_OFF128] = (p // 64 != f // 64)
    return {"ropeA": ropeA, "ropeD": ropeD, "cmask": m}


def lockstep(gens, width):
    active = []
    it = iter(gens)
    done = False
    while True:
        while len(active) < width and not done:
            try:
                active.append(next(it))
            except StopIteration:
                done = True
        if not active:
            break
        for g in list(active):
            try:
                next(g)
            except StopIteration:
                active.remove(g)


class B:
    def __init__(self, cfg, dbg=()):
        self.cfg = cfg
        self.dbg = set(dbg)
        self.nc = bass.Bass("TRN2", target_bir_lowering=False)
        self.p = Prog(self.nc)
        self.dram = {}
        self.outs = []
        self.bank = 0

    def inp(self, name, shape, dt=F32):
        t = T(self.nc.dram_tensor(name, list(shape), dt, kind="ExternalInput").ap(), name)
        self.dram[name] = t
        return t

    def out(self, name, shape, dt=F32):
        t = T(self.nc.dram_tensor(name, list(shape), dt, kind="ExternalOutput").ap(), name)
        self.dram[name] = t
        self.outs.append(name)
        return t

    def scr(self, name, shape, dt):
        kind = "ExternalOutput" if name in self.dbg else "Internal"
        t = T(self.nc.dram_tensor(name, list(shape), dt, kind=kind).ap(), name)
        self.dram[name] = t
        if name in self.dbg:
            self.outs.append(name)
        return t

    def sb(self, st, name, shape, dt=F32):
        self.uid = getattr(self, "uid", 0) + 1
        name = f"{name}_{self.uid}"
        return T(st.enter_context(self.nc.sbuf_tensor(name, list(shape), dt)), name)

    def ps(self, st, name, shape, dt=F32):
        self.uid = getattr(self, "uid", 0) + 1
        name = f"{name}_{self.uid}"
        return T(st.enter_context(self.nc.psum_tensor(name, list(shape), dt)), name)

    def rsqrt(self, OUT, out_ap, IN, in_ap, eps, scale=1.0):
        p = self.p
        p.op("act", lambda e: e.activation(out=out_ap, in_=in_ap, func=AF.Sqrt, bias=self.epsc[eps][:out_ap.shape[0], :], scale=scale), reads=[IN, self.epst], writes=[OUT])
        p.op("dve", lambda e: e.reciprocal(out=out_ap, in_=out_ap), reads=[OUT], writes=[OUT])

    def declare(self):
        c = self.cfg
        D, T_, L, N, DFF, NL = c.D, c.T, c.L, c.N, c.DFF, c.NL
        i = self.inp
        i("x", [N, D]); i("c", [D]); i("ctx", [L, D]); i("c_ctx", [D])
        i("w_mod", [NL, D, 6 * D]); i("b_mod", [NL, 6 * D]); i("w_in", [NL, D, c.IN_W])
        i("wa_sink", [NL, 8])
        for n in ("df_lam_q1", "df_lam_k1", "df_lam_q2", "df_lam_k2"):
            i(n, [NL, 64])
        i("df_subln", [NL, 128]); i("dn_conv", [NL, 3, 3072]); i("dn_a_log", [NL, 16]); i("dn_dt_bias", [NL, 16])
        i("dn_norm", [NL, 128])
        i("w_branch_a", [NL, 1024, D]); i("w_branch_b", [NL, 1024, D]); i("w_branch_c", [NL, 1024, D])
        i("w_o", [NL, D, D]); i("ln1_g", [NL, D]); i("ln1_b", [NL, D])
        i("w_up", [NL, D, 2 * DFF]); i("ffn_conv_w", [NL, 3, 2 * DFF]); i("ffn_conv_b", [NL, 2 * DFF])
        i("w_down", [NL, DFF, D]); i("ln2_g", [NL, D]); i("ln2_b", [NL, D])
        i("ropeA", [4, 128, T_]); i("ropeD", [4, 128, T_]); i("cmask", [NMASK, 128, 128])
        self.out("y", [N, D])
        s = self.scr
        s("xres0", [T_, D], F32); s("xres1", [T_, D], F32); s("hT", [D, T_], BF16); s("modd", [NL, 2, 6 * D], F32)
        s("qaT", [1024, T_], BF16); s("kaT", [256, T_], BF16); s("va", [T_, 256], BF16)
        s("qdT", [1024, T_], BF16); s("kdT", [1024, T_], BF16); s("vd", [T_, 1024], BF16)
        s("gpre", [3072, T_], F32); s("zs", [T_, 1024], F32); s("ab", [T_, 32], F32); s("gtsT", [3 * D, T_], BF16)
        s("osub", [T_, D], F32)
        s("ya", [T_, 1024], BF16); s("yb", [T_, 1024], BF16); s("yc", [T_, 1024], BF16)
        s("qnT", [1024, T_], F32); s("knT", [1024, T_], F32); s("kn", [T_, 1024], F32); s("vn", [T_, 1024], F32)
        s("of", [T_, 1024], F32); s("mT", [D, T_], BF16); s("gT", [DFF, T_], BF16)
        s("yaT", [1024, T_], BF16); s("ybT", [1024, T_], BF16); s("ycT", [1024, T_], BF16)

    def consts(self, st):
        p = self.p
        cm = self.dram["cmask"]
        self.mask = self.sb(st, "mask", [128, NMASK, 128], F32)
        p.dma(self.mask[:], cm.t.rearrange("m p f -> p m f"), reads=[cm], writes=[self.mask])
        self.identb = self.sb(st, "identb", [128, 128], BF16)
        p.op("dve", lambda e: e.tensor_copy(out=self.identb[:], in_=self.mask[:, M_ID, :]), reads=[self.mask], writes=[self.identb])
        self.epst = self.sb(st, "epst", [128, 4], F32)
        self.epsc = {}
        for i_, v_ in enumerate((1e-6, 0.0, 1.0)):
            p.op("pool", lambda e, i_=i_, v_=v_: e.memset(self.epst[:, i_:i_ + 1], v_), writes=[self.epst])
            self.epsc[v_] = self.epst[:, i_:i_ + 1]
        c = self.cfg
        self.modc = self.sb(st, "modc", [128, c.NL, 2, 6 * c.KC], F32)

    def stage_mod(self):
        c, p, nc = self.cfg, self.p, self.nc
        KC, D = c.KC, c.D
        NJ = 6 * KC
        wm, bm = self.dram["w_mod"], self.dram["b_mod"]
        with ExitStack() as st:
            craw = self.sb(st, "craw", [128, KC, 2], F32)
            sc = self.sb(st, "sc", [128, KC, 2], F32)
            bcol = self.sb(st, "bcol", [128, c.NL, NJ], F32)
            ps = [self.ps(st, f"mps{i}", [128, 4, 2], F32) for i in range(2)]
            pst = self.ps(st, "mpst", [128, 128], F32)
            wt = [self.sb(st, f"wmt{i}", [128, KC, 512], F32) for i in range(2)]
            tr = self.sb(st, "mtr", [NJ, 128], F32)
            cc, cx = self.dram["c"], self.dram["c_ctx"]
            p.dma(craw[:, :, 0], cc.t.rearrange("(kc p) -> p kc", p=128), reads=[cc], writes=[craw], allow_slow_non_contiguous=True)
            p.dma(craw[:, :, 1], cx.t.rearrange("(kc p) -> p kc", p=128), reads=[cx], writes=[craw], allow_slow_non_contiguous=True)
            for l in range(c.NL):
                p.dma(bcol[:, l, :], bm.t[l].rearrange("(j p) -> p j", p=128), reads=[bm], writes=[bcol], allow_slow_non_contiguous=True)
            p.op("act", lambda e: e.activation(out=sc[:], in_=craw[:], func=AF.Silu), reads=[craw], writes=[sc])
            k = 0
            for l in range(c.NL):
                wv = wm.t[l].rearrange("(kc p) n -> p kc n", p=128)
                for j0 in range(0, NJ, 4):
                    w = wt[k % 2]; pp = ps[k % 2]; k += 1
                    p.dma(w[:], wv[:, :, j0 * 128:(j0 + 4) * 128], reads=[wm], writes=[w])
                    for jj in range(4):
                        for kc in range(KC):
                            p.op("pe", lambda e, w=w, pp=pp, jj=jj, kc=kc: e.matmul(
                                pp[:, jj, :], w[:, kc, jj * 128:(jj + 1) * 128], sc[:, kc, :],
                                start=(kc == 0), stop=(kc == KC - 1)), reads=[w, sc], writes=[pp])
                    for s_ in range(2):
                        p.op("dve", lambda e, pp=pp, s_=s_, l=l, j0=j0: e.tensor_tensor(
                            out=self.modc[:, l, s_, j0:j0 + 4], in0=pp[:, :, s_], in1=bcol[:, l, j0:j0 + 4], op=ALU.add),
                            reads=[pp, bcol], writes=[self.modc])
                for s_ in range(2):
                    for v in (1, 4):
                        p.op("dve", lambda e, l=l, s_=s_, v=v: e.tensor_scalar_add(
                            out=self.modc[:, l, s_, v * KC:(v + 1) * KC], in0=self.modc[:, l, s_, v * KC:(v + 1) * KC], scalar1=1.0),
                            reads=[self.modc], writes=[self.modc])
                md = self.dram["modd"]
                for s_ in range(2):
                    p.op("pe", lambda e, l=l, s_=s_: e.matmul(pst[0:NJ, :], self.modc[:, l, s_, :], self.mask[:, M_ID, :], start=True, stop=True),
                         reads=[self.modc, self.mask], writes=[pst])
                    p.op("dve", lambda e: e.tensor_copy(out=tr[:], in_=pst[0:NJ, :]), reads=[pst], writes=[tr])
                    p.dma(md.t[l, s_].rearrange("(j p) -> j p", p=128), tr[:], reads=[tr], writes=[md])
        p.barrier()

    def stage_ln(self, l, comb, ada, final=False, src_inputs=False, skip_ctx=False):
        c, p = self.cfg, self.p
        D, KC, L = c.D, c.KC, c.L
        alpha = (2 * c.NL) ** 0.25
        xcur = getattr(self, "xcur", 0)
        xres, xdst = self.dram[f"xres{xcur}"], self.dram[f"xres{1 - xcur}"]
        osub, hT, md = self.dram["osub"], self.dram["hT"], self.dram["modd"]
        if comb and not final:
            self.xcur = 1 - xcur
        nch = (D + 511) // 512
        with ExitStack() as st:
            xt = [self.sb(st, f"lxt{i}", [128, D], F32) for i in range(3)]
            if comb:
                ot = [self.sb(st, f"lot{i}", [128, D], F32) for i in range(3)]
                gate = self.sb(st, "lgate", [128, 2, D], F32)
                gB = self.sb(st, "lgB", [128, D], F32)
                bB = self.sb(st, "lbB", [128, D], F32)
                gv = comb[0]
                for s_ in range(2):
                    p.dma(gate[:, s_, :], md.t[l, s_, gv * D:(gv + 1) * D].partition_broadcast(128), reads=[md], writes=[gate])
                gd, bd = self.dram[comb[1]], self.dram[comb[2]]
                p.dma(gB[:], gd.t[l].partition_broadcast(128), reads=[gd], writes=[gB])
                p.dma(bB[:], bd.t[l].partition_broadcast(128), reads=[bd], writes=[bB])
            stt = [self.sb(st, f"lst{i}", [128, nch, 6], F32) for i in range(3)]
            mv = [self.sb(st, f"lmv{i}", [128, 2], F32) for i in range(3)]
            rs = [self.sb(st, f"lrs{i}", [128, 1], F32) for i in range(3)]
            if ada:
                xb = [self.sb(st, f"lxb{i}", [128, D], BF16) for i in range(3)]
                ht = [self.sb(st, f"lht{i}", [128, KC, 128], BF16) for i in range(3)]
                pt = [self.ps(st, f"lpt{i}", [128, KC, 128], BF16) for i in range(3)]
            def tile(i):
                isctx = i * 128 < L
                s_ = 1 if isctx else 0
                b = i % 3
                X = xt[b]
                if src_inputs:
                    src = self.dram["ctx"] if isctx else self.dram["x"]
                    r0 = i * 128 if isctx else i * 128 - L
                else:
                    src = xres; r0 = i * 128
                p.dma(X[:], src.t[r0:r0 + 128, :], reads=[src], writes=[X])

                def lnstats(X, b):
                    S, MV, R = stt[b], mv[b], rs[b]
                    for ch in range(nch):
                        p.op("dve", lambda e, ch=ch: e.bn_stats(out=S[:, ch, :], in_=X[:, ch * 512:min(D, (ch + 1) * 512)]), reads=[X], writes=[S])
                    p.op("dve", lambda e: e.bn_aggr(out=MV[:], in_=S[:].rearrange("p a b -> p (a b)")), reads=[S], writes=[MV])
                    self.rsqrt(R, R[:], MV, MV[:, 1:2], LN_EPS)
                    return MV, R
                if comb:
                    O = ot[b]
                    p.dma(O[:], osub.t[i * 128:(i + 1) * 128, :], reads=[osub], writes=[O])
                    yield
                    p.op("pool", lambda e, O=O, s_=s_: e.tensor_tensor(out=O[:], in0=O[:], in1=gate[:, s_, :], op=ALU.mult), reads=[O, gate], writes=[O])
                    yield
                    p.op("dve", lambda e, O=O, X=X: e.scalar_tensor_tensor(out=X[:], in0=X[:], scalar=alpha, in1=O[:], op0=ALU.mult, op1=ALU.add), reads=[X, O], writes=[X])
                    MV, R = lnstats(X, b)
                    p.op("dve", lambda e, X=X, MV=MV, R=R: e.tensor_scalar(out=X[:], in0=X[:], scalar1=MV[:, 0:1], scalar2=R[:, 0:1], op0=ALU.subtract, op1=ALU.mult), reads=[X, MV, R], writes=[X])
                    yield
                    p.op("pool", lambda e, X=X: e.tensor_tensor(out=X[:], in0=X[:], in1=gB[:], op=ALU.mult), reads=[X, gB], writes=[X])
                    p.op("pool", lambda e, X=X: e.tensor_tensor(out=X[:], in0=X[:], in1=bB[:], op=ALU.add), reads=[X, bB], writes=[X])
                    if final:
                        if not isctx:
                            y = self.dram["y"]
                            p.dma(y.t[i * 128 - L:(i + 1) * 128 - L, :], X[:], reads=[X], writes=[y], is_out=True)
                    else:
                        p.dma(xdst.t[i * 128:(i + 1) * 128, :], X[:], reads=[X], writes=[xdst])
                if ada:
                    la, sv = ada
                    yield
                    MV, R = lnstats(X, b)
                    XB, HT, PT = xb[b], ht[b], pt[b]
                    p.op("dve", lambda e, X=X, XB=XB, MV=MV, R=R: e.tensor_scalar(out=XB[:], in0=X[:], scalar1=MV[:, 0:1], scalar2=R[:, 0:1], op0=ALU.subtract, op1=ALU.mult), reads=[X, MV, R], writes=[XB])
                    yield
                    for kc in range(KC):
                        p.op("pe", lambda e, kc=kc, XB=XB, PT=PT: e.transpose(PT[:, kc, :], XB[:, kc * 128:(kc + 1) * 128], self.identb[:]), reads=[XB, self.identb], writes=[PT])
                    yield
                    for kc in range(KC):
                        p.op("act", lambda e, kc=kc, HT=HT, PT=PT, s_=s_: e.activation(
                            out=HT[:, kc, :], in_=PT[:, kc, :], func=AF.Identity,
                            bias=self.modc[:, la, s_, sv * KC + kc:sv * KC + kc + 1],
                            scale=self.modc[:, la, s_, (sv + 1) * KC + kc:(sv + 1) * KC + kc + 1]),
                            reads=[PT, self.modc], writes=[HT])
                    p.dma(hT.t.rearrange("(kc p) t -> p kc t", p=128)[:, :, i * 128:(i + 1) * 128], HT[:], reads=[HT], writes=[hT])
            lockstep((tile(i) for i in range(c.NT) if not (skip_ctx and i * 128 < L)), 2)
        p.barrier()


def _proj(self, actT, K, W, wl, sections, tok0, tok1, NB=512, TG=512, tag="pj"):
    c, p = self.cfg, self.p
    KCs = K // 128
    wv = wl.rearrange("(kc p) n -> p kc n", p=128)
    av = actT.t.rearrange("(kc p) t -> p kc t", p=128)
    with ExitStack() as st:
        wbuf = [self.sb(st, f"{tag}w{i}", [128, KCs, NB], BF16) for i in range(2)]
        anyrope = any(s.get("rope") for s in sections)
        if anyrope:
            wperm = [self.sb(st, f"{tag}wp{i}", [128, KCs, min(NB, 512)], BF16) for i in range(2)]
        abuf = [self.sb(st, f"{tag}a{i}", [128, KCs, TG], BF16) for i in range(2)]
        nbank = 6
        banks = [self.ps(st, f"{tag}ps{i}", [128, 512], F32) for i in range(nbank)]
        self.pj_st = st
        for s in sections:
            if s.get("init"):
                s["init"](st)
        wi = 0; ai = 0; bi = 0
        for s in sections:
            mode = s["mode"]
            NBs = min(NB, 512) if s.get("rope") else NB
            for nb0 in range(0, s["n"], NBs):
                nb = min(NBs, s["n"] - nb0)
                wb = wbuf[wi % 2]
                for k0_ in range(0, KCs, 16):
                    k1_ = min(KCs, k0_ + 16)
                    p.dma(wb[:, k0_:k1_, :nb], wv[:, k0_:k1_, s["c0"] + nb0:s["c0"] + nb0 + nb], reads=[W], writes=[wb], eng="pool")
                if s.get("rope"):
                    qs = 32 if s["rope"] == "A" else 16
                    wp = wperm[wi % 2]
                    v1 = wb[:, :, :nb].rearrange("p k (a two q) -> p k a two q", two=2, q=qs)
                    v2 = wp[:, :, :nb].rearrange("p k (a two q) -> p k a two q", two=2, q=qs)
                    for kc in range(KCs):
                        p.op("pool", lambda e, v1=v1, v2=v2, kc=kc: e.tensor_copy(out=v2[:, kc, :, 0, :], in_=v1[:, kc, :, 1, :]), reads=[wb], writes=[wp])
                        p.op("pool", lambda e, v1=v1, v2=v2, kc=kc: e.tensor_copy(out=v2[:, kc, :, 1, :], in_=v1[:, kc, :, 0, :]), reads=[wb], writes=[wp])
                wi += 1
                for g0 in range(tok0, tok1, TG):
                    ntok = min(TG, tok1 - g0)
                    ab = abuf[ai % 2]; ai += 1
                    for k0_ in range(0, KCs, 16):
                        k1_ = min(KCs, k0_ + 16)
                        p.dma(ab[:, k0_:k1_, :ntok], av[:, k0_:k1_, g0:g0 + ntok], reads=[actT], writes=[ab])
                    if s.get("pre"):
                        s["pre"](g0, ntok)
                    for sb0 in range(0, nb, 512):
                        sbn = min(512, nb - sb0)
                        if mode == "TM":
                            for tt in range(ntok // 128):
                                ps = banks[bi % nbank]; bi += 1
                                for kc in range(KCs):
                                    p.op("pe", lambda e, ps=ps, ab=ab, wb=wb, kc=kc, tt=tt: e.matmul(
                                        ps[:, :sbn], ab[:, kc, tt * 128:(tt + 1) * 128], wb[:, kc, sb0:sb0 + sbn],
                                        start=(kc == 0), stop=(kc == KCs - 1)), reads=[ab, wb], writes=[ps])
                                s["epi"](ps, g0 + tt * 128, nb0 + sb0, sbn)
                        else:
                            for cc in range(sb0 // 128, (sb0 + sbn) // 128):
                                ps = banks[bi % nbank]; bi += 1
                                for kc in range(KCs):
                                    p.op("pe", lambda e, ps=ps, ab=ab, wb=wb, kc=kc, cc=cc, ntok=ntok: e.matmul(
                                        ps[:, :ntok], wb[:, kc, cc * 128:(cc + 1) * 128], ab[:, kc, :ntok],
                                        start=(kc == 0), stop=(kc == KCs - 1)), reads=[ab, wb], writes=[ps])
                                ps2 = None
                                if s.get("rope"):
                                    ps2 = banks[bi % nbank]; bi += 1
                                    for kc in range(KCs):
                                        p.op("pe", lambda e, ps2=ps2, ab=ab, wp=wp, kc=kc, cc=cc, ntok=ntok: e.matmul(
                                            ps2[:, :ntok], wp[:, kc, cc * 128:(cc + 1) * 128], ab[:, kc, :ntok],
                                            start=(kc == 0), stop=(kc == KCs - 1)), reads=[ab, wp], writes=[ps2])
                                s["epi"](ps, ps2, g0, ntok, nb0 + cc * 128)
    p.barrier()
B.proj = _proj


def _stage_win(self, l, tok0=0):
    c, p = self.cfg, self.p
    D, T_ = c.D, c.T
    d = self.dram
    W = d["w_in"]
    state = {}

    def init(st):
        state["of"] = [self.sb(st, f"wiof{i}", [128, 512], F32) for i in range(3)]
        state["ob"] = [self.sb(st, f"wiob{i}", [128, 512], BF16) for i in range(3)]
        state["t1"] = [self.sb(st, f"wit1{i}", [128, 512], F32) for i in range(2)]
        state["t2"] = [self.sb(st, f"wit2{i}", [128, 512], F32) for i in range(2)]
        state["cos"] = [self.sb(st, f"wicos{i}", [128, 512], F32) for i in range(2)]
        state["sin"] = [self.sb(st, f"wisin{i}", [128, 512], F32) for i in range(2)]
        state["k"] = 0

    def tm_epi(dst, dcol0, func, bf):
        def epi(ps, tok, nb0, nb):
            k = state["k"]; state["k"] += 1
            o = (state["ob"] if bf else state["of"])[k % 3]
            p.op("act", lambda e: e.activation(out=o[:, :nb], in_=ps[:, :nb], func=func), reads=[ps], writes=[o])
            p.dma(dst.t[tok:tok + 128, dcol0 + nb0:dcol0 + nb0 + nb], o[:, :nb], reads=[o], writes=[dst])
        return epi

    def rope_pre(tabname, idx):
        tab = d[tabname]
        def pre(g0, ntok):
            k = state["k"]; state["k"] += 1
            cs, sn = state["cos"][k % 2], state["sin"][k % 2]
            p.dma(cs[:, :ntok], tab.t[idx, :, g0:g0 + ntok], reads=[tab], writes=[cs])
            p.dma(sn[:, :ntok], tab.t[idx + 1, :, g0:g0 + ntok], reads=[tab], writes=[sn])
            state["cs"] = (cs, sn)
        return pre

    def rope_epi(dst):
        def epi(ps, ps2, g0, ntok, c0):
            k = state["k"]; state["k"] += 1
            cs, sn = state["cs"]
            t1, t2, o = state["t1"][k % 2], state["t2"][k % 2], state["ob"][k % 3]
            p.op("dve", lambda e: e.tensor_tensor(out=t1[:, :ntok], in0=ps[:, :ntok], in1=cs[:, :ntok], op=ALU.mult), reads=[ps, cs], writes=[t1])
            p.op("dve", lambda e: e.tensor_tensor(out=t2[:, :ntok], in0=ps2[:, :ntok], in1=sn[:, :ntok], op=ALU.mult), reads=[ps2, sn], writes=[t2])
            p.op("pool", lambda e: e.tensor_tensor(out=o[:, :ntok], in0=t1[:, :ntok], in1=t2[:, :ntok], op=ALU.add), reads=[t1, t2], writes=[o])
            p.dma(dst.t[c0:c0 + 128, g0:g0 + ntok], o[:, :ntok], reads=[o], writes=[dst])
        return epi

    def fm_epi(dst, func=AF.Copy, bf=False):
        def epi(ps, ps2, g0, ntok, c0):
            k = state["k"]; state["k"] += 1
            o = (state["ob"] if bf else state["of"])[k % 3]
            p.op("act", lambda e: e.activation(out=o[:, :ntok], in_=ps[:, :ntok], func=func), reads=[ps], writes=[o])
            p.dma(dst.t[c0:c0 + 128, g0:g0 + ntok], o[:, :ntok], reads=[o], writes=[dst])
        return epi

    secs = [
        dict(c0=O_QA, n=1024, mode="FM", rope="A", pre=rope_pre("ropeA", 0), epi=rope_epi(d["qaT"]), init=init),
        dict(c0=O_KA, n=256, mode="FM", rope="A", pre=rope_pre("ropeA", 2), epi=rope_epi(d["kaT"])),
        dict(c0=O_QD, n=1024, mode="FM", rope="D", pre=rope_pre("ropeD", 0), epi=rope_epi(d["qdT"])),
        dict(c0=O_KD, n=1024, mode="FM", rope="D", pre=rope_pre("ropeD", 2), epi=rope_epi(d["kdT"])),
        dict(c0=O_QKV, n=3072, mode="FM", epi=fm_epi(d["gpre"])),
        dict(c0=O_VA, n=256, mode="TM", epi=tm_epi(d["va"], 0, AF.Copy, True)),
        dict(c0=O_VD, n=1024, mode="TM", epi=tm_epi(d["vd"], 0, AF.Copy, True)),
        dict(c0=O_Z, n=1024, mode="TM", epi=tm_epi(d["zs"], 0, AF.Silu, False)),
        dict(c0=O_A, n=32, mode="TM", epi=tm_epi(d["ab"], 0, AF.Copy, False)),
        dict(c0=O_G, n=3 * D, mode="FM", epi=fm_epi(d["gtsT"], AF.Sigmoid, True)),
    ]
    self.proj(d["hT"], D, W, W.t[l], secs, tok0, T_, NB=1024, tag="wi")
B.stage_win = _stage_win


def _stage_tm2fm(self, src, dstT, C, tok0=0):
    c, p = self.cfg, self.p
    nck = C // 128
    with ExitStack() as st:
        xt = [self.sb(st, f"tfx{i}", [128, C], BF16) for i in range(2)]
        ot = [self.sb(st, f"tfo{i}", [128, nck, 128], BF16) for i in range(2)]
        pt = [self.ps(st, f"tfp{i}", [128, nck, 128], BF16) for i in range(2)]
        for i in range(tok0 // 128, c.NT):
            X, O, P = xt[i % 2], ot[i % 2], pt[i % 2]
            p.dma(X[:], src.t[i * 128:(i + 1) * 128, :], reads=[src], writes=[X])
            for k in range(nck):
                p.op("pe", lambda e, k=k, X=X, P=P: e.transpose(P[:, k, :], X[:, k * 128:(k + 1) * 128], self.identb[:]), reads=[X, self.identb], writes=[P])
            p.op("act", lambda e, O=O, P=P: e.copy(out=O[:], in_=P[:]), reads=[P], writes=[O])
            p.dma(dstT.t.rearrange("(k p) t -> p k t", p=128)[:, :, i * 128:(i + 1) * 128], O[:], reads=[O], writes=[dstT])
    p.barrier()
B.stage_tm2fm = _stage_tm2fm


def _stage_diff(self, l, ctx_out):
    c, p = self.cfg, self.p
    T_, L, NT = c.T, c.L, c.NT
    d = self.dram
    qdT, kdT, vd, yb = d["qdT"], d["kdT"], d["vd"], d["yb"]
    lam_init = 0.8 - 0.6 * math.exp(-0.3 * l)
    with ExitStack() as st:
        lv = self.sb(st, "dflv", [128, 4, 64], F32)
        for i, n in enumerate(("df_lam_q1", "df_lam_k1", "df_lam_q2", "df_lam_k2")):
            p.dma(lv[:, i, :], d[n].t[l].partition_broadcast(128), reads=[d[n]], writes=[lv])
        lp = self.sb(st, "dflp", [128, 2, 64], F32)
        ls = self.sb(st, "dfls", [128, 2], F32)
        nlam = self.sb(st, "dfnl", [128, 1], F32)
        p.op("dve", lambda e: e.tensor_tensor(out=lp[:, 0, :], in0=lv[:, 0, :], in1=lv[:, 1, :], op=ALU.mult), reads=[lv], writes=[lp])
        p.op("dve", lambda e: e.tensor_tensor(out=lp[:, 1, :], in0=lv[:, 2, :], in1=lv[:, 3, :], op=ALU.mult), reads=[lv], writes=[lp])
        p.op("dve", lambda e: e.reduce_sum(out=ls[:], in_=lp[:], axis=AX.X), reads=[lp], writes=[ls])
        p.op("act", lambda e: e.activation(out=ls[:], in_=ls[:], func=AF.Exp), reads=[ls], writes=[ls])
        p.op("dve", lambda e: e.tensor_tensor(out=nlam[:], in0=ls[:, 1:2], in1=ls[:, 0:1], op=ALU.subtract), reads=[ls], writes=[nlam])
        p.op("dve", lambda e: e.tensor_scalar_add(out=nlam[:], in0=nlam[:], scalar1=-lam_init), reads=[nlam], writes=[nlam])
        sub = self.sb(st, "dfsub", [128, 128], F32)
        p.dma(sub[:], d["df_subln"].t[l].partition_broadcast(128), reads=[d["df_subln"]], writes=[sub])
        p.op("dve", lambda e: e.tensor_scalar_mul(out=sub[:], in0=sub[:], scalar1=1.0 - lam_init), reads=[sub], writes=[sub])

        kT = [self.sb(st, f"dfk{i}", [128, T_], BF16) for i in range(2)]
        vx = [self.sb(st, f"dfv{i}", [128, NT, 132], BF16) for i in range(2)]
        for i in range(2):
            p.op("pool", lambda e, i=i: e.memset(vx[i][:, :, 128:129], 1.0), writes=[vx[i]])
        qt = [self.sb(st, f"dfq{i}", [128, 512], BF16) for i in range(2)]
        pe_ = [[self.sb(st, f"dfp{j}{i}", [128, 512], BF16) for i in range(2)] for j in range(2)]
        sb_ = [[self.ps(st, f"dfs{j}{i}", [128, 512], F32) for i in range(2)] for j in range(2)]
        ob_ = [[self.ps(st, f"dfo{j}{i}", [128, 2, 256], F32) for i in range(2)] for j in range(2)]
        r12 = [self.sb(st, f"dfr{i}", [128, 2], F32) for i in range(2)]
        t1 = [self.sb(st, f"dft{i}", [128, 128], F32) for i in range(2)]
        o_ = [self.sb(st, f"dfoo{i}", [128, 128], F32) for i in range(2)]
        sq = [self.sb(st, f"dfsq{i}", [128, 128], F32) for i in range(2)]
        ss = [self.sb(st, f"dfss{i}", [128, 1], F32) for i in range(2)]
        yo = [self.sb(st, f"dfyo{i}", [128, 128], BF16) for i in range(2)]
        qi = 0; si = 0; ei = 0
        groups = []
        if ctx_out:
            groups.append((0, L, 0, L // 128))
        for g0 in range(L, T_, 512):
            groups.append((g0, min(512, T_ - g0), 0, NT))
        for h in range(8):
            K, V = kT[h % 2], vx[h % 2]
            p.dma(K[:], kdT.t[h * 128:(h + 1) * 128, :], reads=[kdT], writes=[K])
            vv_ = vd.t[:, h * 128:(h + 1) * 128].rearrange("(n p) c -> p n c", p=128)
            for n0_ in range(0, NT, 16):
                n1_ = min(NT, n0_ + 16)
                p.dma(V[:, n0_:n1_, 0:128], vv_[:, n0_:n1_, :], reads=[vd], writes=[V])
            for (g0, ntok, kt0, kt1) in groups:
                Q = qt[qi % 2]; qi += 1
                p.dma(Q[:, :ntok], qdT.t[h * 128:(h + 1) * 128, g0:g0 + ntok], reads=[qdT], writes=[Q])
                nq = ntok // 128

                def emit_S(kt):
                    for j in range(2):
                        S = sb_[j][kt % 2]
                        p.op("pe", lambda e, S=S, K=K, Q=Q, j=j, kt=kt, ntok=ntok: e.matmul(
                            S[:, :ntok], K[j * 64:(j + 1) * 64, kt * 128:(kt + 1) * 128], Q[j * 64:(j + 1) * 64, :ntok], start=True, stop=True),
                            reads=[K, Q], writes=[S])
                emit_S(kt0)
                for kt in range(kt0, kt1):
                    if kt + 1 < kt1:
                        emit_S(kt + 1)
                    for j in range(2):
                        S = sb_[j][kt % 2]; P = pe_[j][kt % 2]
                        p.op("act", lambda e, S=S, P=P, ntok=ntok: e.activation(out=P[:, :ntok], in_=S[:, :ntok], func=AF.Exp), reads=[S], writes=[P])
                        for qs in range(nq):
                            O = ob_[j][qs // 2]
                            p.op("pe", lambda e, O=O, P=P, V=V, qs=qs, kt=kt: e.matmul(
                                O[:, qs % 2, 0:129], P[:, qs * 128:(qs + 1) * 128], V[:, kt, 0:129], start=(kt == kt0 and qs % 2 == 0), stop=(kt == kt1 - 1)),
                                reads=[P, V], writes=[O])
                for qs in range(nq):
                    O1, O2 = ob_[0][qs // 2], ob_[1][qs // 2]
                    R_, T1, OO, SQ, SS, YO = r12[ei % 2], t1[ei % 2], o_[ei % 2], sq[ei % 2], ss[ei % 2], yo[ei % 2]; ei += 1
                    s2 = qs % 2
                    p.op("dve", lambda e, R_=R_, O1=O1, s2=s2: e.reciprocal(out=R_[:, 0:1], in_=O1[:, s2, 128:129]), reads=[O1], writes=[R_])
                    p.op("dve", lambda e, R_=R_, O2=O2, s2=s2: e.reciprocal(out=R_[:, 1:2], in_=O2[:, s2, 128:129]), reads=[O2], writes=[R_])
                    p.op("dve", lambda e, R_=R_: e.tensor_tensor(out=R_[:, 1:2], in0=R_[:, 1:2], in1=nlam[:], op=ALU.mult), reads=[R_, nlam], writes=[R_])
                    p.op("dve", lambda e, R_=R_, O1=O1, T1=T1, s2=s2: e.tensor_scalar_mul(out=T1[:], in0=O1[:, s2, 0:128], scalar1=R_[:, 0:1]), reads=[O1, R_], writes=[T1])
                    p.op("dve", lambda e, R_=R_, O2=O2, T1=T1, OO=OO, s2=s2: e.scalar_tensor_tensor(out=OO[:], in0=O2[:, s2, 0:128], scalar=R_[:, 1:2], in1=T1[:], op0=ALU.mult, op1=ALU.add), reads=[O2, R_, T1], writes=[OO])
                    p.op("act", lambda e, OO=OO, SQ=SQ, SS=SS: e.activation(out=SQ[:], in_=OO[:], func=AF.Square, accum_out=SS[:]), reads=[OO], writes=[SQ, SS])
                    self.rsqrt(SS, SS[:], SS, SS[:], 1e-6, scale=1.0 / 128)
                    p.op("dve", lambda e, OO=OO, SS=SS, YO=YO: e.scalar_tensor_tensor(out=YO[:], in0=OO[:], scalar=SS[:, 0:1], in1=sub[:], op0=ALU.mult, op1=ALU.mult), reads=[OO, SS, sub], writes=[YO])
                    t0 = g0 + qs * 128
                    p.dma(yb.t[t0:t0 + 128, h * 128:(h + 1) * 128], YO[:], reads=[YO], writes=[yb])
    p.barrier()
B.stage_diff = _stage_diff


def _stage_win_attn(self, l, ctx_out):
    c, p = self.cfg, self.p
    T_, L, NT = c.T, c.L, c.NT
    LT = L // 128
    d = self.dram
    qaT, kaT, va, ya = d["qaT"], d["kaT"], d["va"], d["ya"]
    with ExitStack() as st:
        es = self.sb(st, "waes", [128, 8], F32)
        p.dma(es[:], d["wa_sink"].t[l].partition_broadcast(128), reads=[d["wa_sink"]], writes=[es])
        p.op("act", lambda e: e.activation(out=es[:], in_=es[:], func=AF.Exp), reads=[es], writes=[es])
        mk = self.sb(st, "wamk", [128, 2, 4, 128], BF16)
        for h4 in range(4):
            p.op("dve", lambda e, h4=h4: e.tensor_copy(out=mk[:, 0, h4, :], in_=self.mask[:, M_TRILI, :]), reads=[self.mask], writes=[mk])
            p.op("dve", lambda e, h4=h4: e.tensor_copy(out=mk[:, 1, h4, :], in_=self.mask[:, M_TRIUI, :]), reads=[self.mask], writes=[mk])
        K = self.sb(st, "wak", [128, T_], BF16)
        V = self.sb(st, "wav", [128, NT, 132], BF16)
        p.op("pool", lambda e: e.memset(V[:, :, 128:129], 1.0), writes=[V])
        qt = [self.sb(st, f"waq{i}", [128, 4, 128], BF16) for i in range(2)]
        pp = [self.sb(st, f"wap{i}", [128, 512], BF16) for i in range(2)]
        sb_ = [self.ps(st, f"was{i}", [128, 512], F32) for i in range(2)]
        ob_ = [[self.ps(st, f"wao{i}{j}", [128, 2, 256], F32) for j in range(2)] for i in range(2)]
        rr = [self.sb(st, f"war{i}", [128, 1], F32) for i in range(2)]
        yo = [self.sb(st, f"wayo{i}", [128, 128], BF16) for i in range(2)]
        si = 0; ei = 0; bi = 0
        for g in range(2):
            p.dma(K[:], kaT.t[g * 128:(g + 1) * 128, :], reads=[kaT], writes=[K])
            vv_ = va.t[:, g * 128:(g + 1) * 128].rearrange("(n p) c -> p n c", p=128)
            for n0_ in range(0, NT, 16):
                n1_ = min(NT, n0_ + 16)
                p.dma(V[:, n0_:n1_, 0:128], vv_[:, n0_:n1_, :], reads=[va], writes=[V])
            for i in range(0 if ctx_out else LT, NT):
                Q = qt[bi % 2]; OB = ob_[bi % 2]; bi += 1
                for h4 in range(4):
                    h = g * 4 + h4
                    p.dma(Q[:, h4, :], qaT.t[h * 128:(h + 1) * 128, i * 128:(i + 1) * 128], reads=[qaT], writes=[Q])
                if i < LT:
                    kts = [(kt, None) for kt in range(LT)]
                else:
                    kts = [(kt, None) for kt in range(LT)]
                    if i - 1 >= LT:
                        kts.append((i - 1, 0))
                    kts.append((i, None))
                    if i + 1 < NT:
                        kts.append((i + 1, 1))
                def emit_S(n_):
                    kt_ = kts[n_][0]
                    S_ = sb_[(si + n_) % 2]
                    p.op("pe", lambda e, S_=S_, Q=Q, kt_=kt_: e.matmul(S_[:], K[:, kt_ * 128:(kt_ + 1) * 128], Q[:].rearrange("p a b -> p (a b)"), start=True, stop=True),
                         reads=[K, Q], writes=[S_])
                emit_S(0)
                for n_, (kt, m) in enumerate(kts):
                    S = sb_[(si + n_) % 2]; P = pp[(si + n_) % 2]
                    if n_ + 1 < len(kts):
                        emit_S(n_ + 1)
                    p.op("act", lambda e, S=S, P=P: e.activation(out=P[:], in_=S[:], func=AF.Exp), reads=[S], writes=[P])
                    if m is not None:
                        p.op("pool", lambda e, P=P, m=m: e.tensor_tensor(out=P[:], in0=P[:], in1=mk[:, m].rearrange("p a b -> p (a b)"), op=ALU.mult), reads=[P, mk], writes=[P])
                    for h4 in range(4):
                        O = OB[h4 // 2]
                        p.op("pe", lambda e, O=O, P=P, h4=h4, kt=kt, n_=n_, nk=len(kts): e.matmul(
                            O[:, h4 % 2, 0:129], P[:, h4 * 128:(h4 + 1) * 128], V[:, kt, 0:129], start=(n_ == 0 and h4 % 2 == 0), stop=(n_ == nk - 1)),
                            reads=[P, V], writes=[O])
                si += len(kts)
                for h4 in range(4):
                    h = g * 4 + h4
                    O = OB[h4 // 2]
                    R_, YO = rr[ei % 2], yo[ei % 2]; ei += 1
                    p.op("dve", lambda e, R_=R_, O=O, h4=h4, h=h: e.tensor_scalar(out=R_[:], in0=O[:, h4 % 2, 128:129], scalar1=es[:, h:h + 1], scalar2=None, op0=ALU.add), reads=[O, es], writes=[R_])
                    p.op("dve", lambda e, R_=R_: e.reciprocal(out=R_[:], in_=R_[:]), reads=[R_], writes=[R_])
                    p.op("dve", lambda e, R_=R_, O=O, YO=YO, h4=h4: e.tensor_scalar_mul(out=YO[:], in0=O[:, h4 % 2, 0:128], scalar1=R_[:, 0:1]), reads=[O, R_], writes=[YO])
                    p.dma(ya.t[i * 128:(i + 1) * 128, h * 128:(h + 1) * 128], YO[:], reads=[YO], writes=[ya])
    p.barrier()
B.stage_win_attn = _stage_win_attn


def _stage_gdn_conv(self, l):
    c, p = self.cfg, self.p
    T_, L, NT = c.T, c.L, c.NT
    d = self.dram
    gpre, qnT, knT, kn, vn = d["gpre"], d["qnT"], d["knT"], d["kn"], d["vn"]
    with ExitStack() as st:
        cw = self.sb(st, "gcw", [128, 24, 3], F32)
        for k_ in range(3):
            p.dma(cw[:, :, k_], d["dn_conv"].t[l, k_].rearrange("(ch p) -> p ch", p=128), reads=[d["dn_conv"]], writes=[cw], allow_slow_non_contiguous=True)
        G = [self.sb(st, f"gcG{i}", [128, 514], F32) for i in range(4)]
        Y = [self.sb(st, f"gcY{i}", [128, 512], F32) for i in range(4)]
        SQ = [self.sb(st, f"gcS{i}", [128, 512], F32) for i in range(4)]
        RI = [self.sb(st, f"gcR{i}", [128, 512], F32) for i in range(4)]
        YN = [self.sb(st, f"gcN{i}", [128, 512], F32) for i in range(4)]
        TT = [self.sb(st, f"gcT{i}", [128, 4, 128], F32) for i in range(4)]
        pss = [self.ps(st, f"gcps{i}", [128, 512], F32) for i in range(4)]
        ptt = [self.ps(st, f"gcpt{i}", [128, 4, 128], F32) for i in range(4)]
        ones = self.mask[:, M_ONES, :]
        ident = self.mask[:, M_ID, :]
        def iters():
            k = 0
            for ch in range(24):
                for (s0, s1) in ((0, L), (L, T_)):
                    for g0 in range(s0, s1, 512):
                        yield one(k, ch, s0, s1, g0)
                        k += 1

        def one(k, ch, s0, s1, g0):
                    kind = ch // 8
                    hh = ch % 8
                    ntok = min(512, s1 - g0)
                    g, y, sq, ri, yn, tt, ps1, pt1 = G[k % 4], Y[k % 4], SQ[k % 4], RI[k % 4], YN[k % 4], TT[k % 4], pss[k % 4], ptt[k % 4]
                    lo = max(s0, g0 - 1); hi = min(s1, g0 + ntok + 1)
                    if lo > g0 - 1:
                        p.op("pool", lambda e, g=g: e.memset(g[:, 0:1], 0.0), writes=[g])
                    if hi < g0 + ntok + 1:
                        p.op("pool", lambda e, g=g, ntok=ntok: e.memset(g[:, ntok + 1:ntok + 2], 0.0), writes=[g])
                    p.dma(g[:, lo - (g0 - 1):hi - (g0 - 1)], gpre.t[ch * 128:(ch + 1) * 128, lo:hi], reads=[gpre], writes=[g])
                    yield
                    p.op("dve", lambda e, g=g, y=y, ntok=ntok, ch=ch: e.tensor_scalar_mul(out=y[:, :ntok], in0=g[:, 0:ntok], scalar1=cw[:, ch, 0:1]), reads=[g, cw], writes=[y])
                    p.op("dve", lambda e, g=g, y=y, ntok=ntok, ch=ch: e.scalar_tensor_tensor(out=y[:, :ntok], in0=g[:, 1:ntok + 1], scalar=cw[:, ch, 1:2], in1=y[:, :ntok], op0=ALU.mult, op1=ALU.add), reads=[g, cw, y], writes=[y])
                    p.op("dve", lambda e, g=g, y=y, ntok=ntok, ch=ch: e.scalar_tensor_tensor(out=y[:, :ntok], in0=g[:, 2:ntok + 2], scalar=cw[:, ch, 2:3], in1=y[:, :ntok], op0=ALU.mult, op1=ALU.add), reads=[g, cw, y], writes=[y])
                    yield
                    p.op("act", lambda e, y=y, ntok=ntok: e.activation(out=y[:, :ntok], in_=y[:, :ntok], func=AF.Silu), reads=[y], writes=[y])
                    src = y
                    if kind < 2:
                        p.op("act", lambda e, y=y, sq=sq, ntok=ntok: e.activation(out=sq[:, :ntok], in_=y[:, :ntok], func=AF.Square), reads=[y], writes=[sq])
                        yield
                        p.op("pe", lambda e, ps1=ps1, sq=sq, ntok=ntok: e.matmul(ps1[:, :ntok], ones, sq[:, :ntok], start=True, stop=True), reads=[sq, self.mask], writes=[ps1])
                        yield
                        self.rsqrt(ri, ri[:, :ntok], ps1, ps1[:, :ntok], 1e-6)
                        sc_ = (128 ** -0.5) if kind == 0 else 1.0
                        p.op("dve", lambda e, y=y, ri=ri, yn=yn, ntok=ntok, sc_=sc_: e.scalar_tensor_tensor(out=yn[:, :ntok], in0=y[:, :ntok], scalar=sc_, in1=ri[:, :ntok], op0=ALU.mult, op1=ALU.mult), reads=[y, ri], writes=[yn])
                        dst = qnT if kind == 0 else knT
                        p.dma(dst.t[hh * 128:(hh + 1) * 128, g0:g0 + ntok], yn[:, :ntok], reads=[yn], writes=[dst])
                        src = yn
                    if kind >= 1:
                        nq = ntok // 128
                        for q_ in range(nq):
                            p.op("pe", lambda e, pt1=pt1, src=src, q_=q_: e.matmul(pt1[:, q_, :], src[:, q_ * 128:(q_ + 1) * 128], ident, start=True, stop=True), reads=[src, self.mask], writes=[pt1])
                        yield
                        p.op("act", lambda e, tt=tt, pt1=pt1, nq=nq: e.copy(out=tt[:, :nq, :], in_=pt1[:, :nq, :]), reads=[pt1], writes=[tt])
                        dst = kn if kind == 1 else vn
                        p.dma(dst.t[g0:g0 + ntok, hh * 128:(hh + 1) * 128].rearrange("(q p) c -> p q c", p=128), tt[:, :nq, :], reads=[tt], writes=[dst])
        lockstep(iters(), 3)
    p.barrier()
B.stage_gdn_conv = _stage_gdn_conv


def _stage_gdn_scan(self, l, ctx_out):
    c, p = self.cfg, self.p
    T_, L, NT = c.T, c.L, c.NT
    LT = L // 128
    d = self.dram
    qnT, knT, kn, vn, ab, of, zs, yc = d["qnT"], d["knT"], d["kn"], d["vn"], d["ab"], d["of"], d["zs"], d["yc"]
    M = lambda i: self.mask[:, i, :]
    with ExitStack() as st:
        S = self.sb(st, "gsS", [128, 8, 128], F32)
        S_h = [S.sub(f"S{h}") for h in range(8)]
        nal = self.sb(st, "gsnal", [128, 16], F32)
        dtb = self.sb(st, "gsdtb", [128, 16], F32)
        nw = self.sb(st, "gsnw", [128, 128], F32)
        p.dma(nal[:], d["dn_a_log"].t[l].partition_broadcast(128), reads=[d["dn_a_log"]], writes=[nal])
        p.dma(dtb[:], d["dn_dt_bias"].t[l].partition_broadcast(128), reads=[d["dn_dt_bias"]], writes=[dtb])
        p.dma(nw[:], d["dn_norm"].t[l].partition_broadcast(128), reads=[d["dn_norm"]], writes=[nw])
        p.op("act", lambda e: e.activation(out=nal[:], in_=nal[:], func=AF.Exp), reads=[nal], writes=[nal])
        p.op("dve", lambda e: e.tensor_scalar_mul(out=nal[:], in0=nal[:], scalar1=-1.0), reads=[nal], writes=[nal])
        banks = [self.ps(st, f"gsb{i}", [128, 4, 128], F32) for i in range(8)]
        slots = [(bnk, 0, bnk) for bnk in banks]
        sc = {"ps": 0}
        rings = {}

        def tmp(name, shape=(128, 128), n=6, dt=F32):
            if name == "X":
                n = 12
            if name not in rings:
                rings[name] = [[self.sb(st, f"gs_{name}{i}", list(shape), dt) for i in range(n)], 0]
            r = rings[name]
            t = r[0][r[1] % n]; r[1] += 1
            return t

        def mm(lhsT, lT, rhs, rT, ncols=128, acc=None):
            trk, j, bnk = slots[sc["ps"] % len(slots)]; sc["ps"] += 1
            ap = bnk[:, j, 0:ncols]
            p.op("pe", lambda e: e.matmul(ap, lhsT, rhs, start=True, stop=(acc is None)), reads=lT + rT, writes=[trk])
            if acc is not None:
                l2, l2T, r2, r2T = acc
                p.op("pe", lambda e: e.matmul(ap, l2, r2, start=False, stop=True), reads=l2T + r2T, writes=[trk])
            return trk, ap

        evi = {"k": 0}

        def evac(out_t, out_ap, trk, ap):
            k = evi["k"]; evi["k"] += 1
            if k % 2 == 0:
                p.op("act", lambda e: e.copy(out=out_ap, in_=ap), reads=[trk], writes=[out_t])
            else:
                p.op("dve", lambda e: e.tensor_copy(out=out_ap, in_=ap), reads=[trk], writes=[out_t])

        for dirn in range(2):
            cum = M_TRIUI if dirn == 0 else M_TRILI
            mL = M_TRILS if dirn == 0 else M_TRIUS
            mA = M_TRIUI if dirn == 0 else M_TRILI
            p.op("pool", lambda e: e.memset(S[:], 0.0), writes=[S] + S_h)
            if dirn == 0:
                order = list(range(NT))
            else:
                order = list(range(LT - 1, -1, -1)) + list(range(NT - 1, LT - 1, -1))
            for i in order:
                want = (i >= LT) or ctx_out
                tk = slice(i * 128, (i + 1) * 128)
                abt = tmp("abt", (128, 32), 2)
                p.dma(abt[:], ab.t[tk, :], reads=[ab], writes=[abt])
                knt = tmp("knt", (128, 1024), 2); vnt = tmp("vnt", (128, 1024), 2)
                kTt = tmp("kTt", (128, 8, 128), 2); qTt = tmp("qTt", (128, 8, 128), 2)
                p.dma(knt[:], kn.t[tk, :], reads=[kn], writes=[knt])
                p.dma(vnt[:], vn.t[tk, :], reads=[vn], writes=[vnt])
                p.dma(kTt[:], knT.t[:, tk].rearrange("(h p) t -> p h t", p=128), reads=[knT], writes=[kTt])
                p.dma(qTt[:], qnT.t[:, tk].rearrange("(h p) t -> p h t", p=128), reads=[qnT], writes=[qTt])
                gx = tmp("gx", (128, 8), 2); gax = tmp("gax", (128, 8), 2); g = tmp("g", (128, 8), 2); beta = tmp("beta", (128, 8), 2)
                ds = slice(dirn * 8, dirn * 8 + 8)
                p.op("dve", lambda e: e.tensor_tensor(out=gx[:], in0=abt[:, ds], in1=dtb[:, ds], op=ALU.add), reads=[abt, dtb], writes=[gx])
                p.op("act", lambda e: e.activation(out=gax[:], in_=gx[:], func=AF.Abs), reads=[gx], writes=[gax])
                p.op("act", lambda e: e.activation(out=gax[:], in_=gax[:], func=AF.Exp, scale=-1.0), reads=[gax], writes=[gax])
                p.op("act", lambda e: e.activation(out=gax[:], in_=gax[:], func=AF.Ln, bias=self.epsc[1.0], scale=1.0), reads=[gax, self.epst], writes=[gax])
                p.op("dve", lambda e: e.tensor_scalar_max(out=gx[:], in0=gx[:], scalar1=0.0), reads=[gx], writes=[gx])
                p.op("dve", lambda e: e.tensor_tensor(out=gx[:], in0=gx[:], in1=gax[:], op=ALU.add), reads=[gx, gax], writes=[gx])
                p.op("dve", lambda e: e.tensor_tensor(out=g[:], in0=gx[:], in1=nal[:, ds], op=ALU.mult), reads=[gx, nal], writes=[g])
                p.op("act", lambda e: e.activation(out=beta[:], in_=abt[:, 16 + dirn * 8:24 + dirn * 8], func=AF.Sigmoid), reads=[abt], writes=[beta])
                t1, a1 = mm(M(cum), [self.mask], g[:], [g], ncols=8)
                gcum = tmp("gcum", (128, 8), 2)
                evac(gcum, gcum[:], t1, a1)
                t2, a2 = mm(M(M_ONES), [self.mask], g[:], [g], ncols=8)
                gtot = tmp("gtot", (128, 8), 2)
                evac(gtot, gtot[:], t2, a2)
                eg = tmp("eg", (128, 8), 2); ekd = tmp("ekd", (128, 8), 2); egl = tmp("egl", (128, 8), 2); bk = tmp("bk", (128, 8), 2)
                p.op("act", lambda e: e.activation(out=eg[:], in_=gcum[:], func=AF.Exp), reads=[gcum], writes=[eg])
                p.op("dve", lambda e: e.tensor_tensor(out=ekd[:], in0=gtot[:], in1=gcum[:], op=ALU.subtract), reads=[gtot, gcum], writes=[ekd])
                p.op("act", lambda e: e.activation(out=ekd[:], in_=ekd[:], func=AF.Exp), reads=[ekd], writes=[ekd])
                p.op("act", lambda e: e.activation(out=egl[:], in_=gtot[:], func=AF.Exp), reads=[gtot], writes=[egl])
                p.op("dve", lambda e: e.tensor_tensor(out=bk[:], in0=beta[:], in1=eg[:], op=ALU.mult), reads=[beta, eg], writes=[bk])
                if want:
                    ot = tmp("ot", (128, 1024), 2)
                    if dirn == 1:
                        oft = tmp("oft", (128, 1024), 2); zt = tmp("zt", (128, 1024), 2)
                        p.dma(oft[:], of.t[tk, :], reads=[of], writes=[oft])
                        p.dma(zt[:], zs.t[tk, :], reads=[zs], writes=[zt])
                def unit(h):
                    hs = slice(h * 128, (h + 1) * 128)
                    hc = slice(h, h + 1)
                    kT = kTt[:, h, :]; qT = qTt[:, h, :]
                    Ug = tmp("Ug")
                    p.op("pool", lambda e: e.tensor_scalar(out=Ug[:], in0=M(cum), scalar1=g[:, hc], scalar2=None, op0=ALU.mult), reads=[self.mask, g], writes=[Ug])
                    yield
                    tD, aD = mm(M(M_ONES), [self.mask], Ug[:], [Ug])
                    DL = tmp("DL"); DU = tmp("DU")
                    p.op("dve", lambda e: e.tensor_scalar(out=DL[:], in0=aD, scalar1=gcum[:, hc], scalar2=0.0, op0=ALU.subtract, op1=ALU.max), reads=[tD, gcum], writes=[DL])
                    p.op("dve", lambda e: e.tensor_scalar(out=DU[:], in0=aD, scalar1=gcum[:, hc], scalar2=0.0, op0=ALU.subtract, op1=ALU.min), reads=[tD, gcum], writes=[DU])
                    p.op("act", lambda e: e.activation(out=DL[:], in_=DL[:], func=AF.Exp, scale=-1.0), reads=[DL], writes=[DL])
                    p.op("act", lambda e: e.activation(out=DU[:], in_=DU[:], func=AF.Exp), reads=[DU], writes=[DU])
                    p.op("pool", lambda e: e.tensor_tensor(out=DL[:], in0=DL[:], in1=M(mL), op=ALU.mult), reads=[DL, self.mask], writes=[DL])
                    p.op("pool", lambda e: e.tensor_tensor(out=DU[:], in0=DU[:], in1=M(mA), op=ALU.mult), reads=[DU, self.mask], writes=[DU])
                    yield
                    tG, aG = mm(kT, [kTt], kT, [kTt])
                    Lm = tmp("Lm")
                    p.op("dve", lambda e: e.scalar_tensor_tensor(out=Lm[:], in0=aG, scalar=beta[:, hc], in1=DL[:], op0=ALU.mult, op1=ALU.mult), reads=[tG, beta, DL], writes=[Lm])
                    if want:
                        tA, aA = mm(kT, [kTt], qT, [qTt])
                        AT = tmp("AT")
                        p.op("dve", lambda e: e.tensor_tensor(out=AT[:], in0=aA, in1=DU[:], op=ALU.mult), reads=[tA, DU], writes=[AT])
                    yield

                    def trn(src):
                        trk, j, bnk = slots[sc["ps"] % len(slots)]; sc["ps"] += 1
                        ap = bnk[:, j, 0:128]
                        p.op("pe", lambda e: e.transpose(ap, src[:], M(M_ID)), reads=[src, self.mask], writes=[trk])
                        return trk, ap
                    tN, aN = trn(Lm)
                    Nm = tmp("Nm")
                    evac(Nm, Nm[:], tN, aN)
                    L16 = tmp("L16"); N16 = tmp("N16")
                    p.op("pool", lambda e: e.tensor_tensor(out=L16[:], in0=Lm[:], in1=M(M_BD16), op=ALU.mult), reads=[Lm, self.mask], writes=[L16])
                    p.op("pool", lambda e: e.tensor_tensor(out=N16[:], in0=Nm[:], in1=M(M_BD16), op=ALU.mult), reads=[Nm, self.mask], writes=[N16])

                    def mmev(name, lh, lhT, rh, rhT):
                        t_, a_ = mm(lh[:], [lh], rh[:], [rh])
                        o_ = tmp(name)
                        evac(o_, o_[:], t_, a_)
                        return o_
                    yield
                    L2 = mmev("L2", N16, None, L16, None); N2 = mmev("N2", L16, None, N16, None)
                    yield
                    L4 = mmev("L4", N2, None, L2, None); N4 = mmev("N4", L2, None, N2, None)
                    yield
                    L8 = mmev("L8", N4, None, L4, None)
                    Q1 = tmp("P1")
                    p.op("pool", lambda e: e.tensor_tensor(out=Q1[:], in0=M(M_ID), in1=N16[:], op=ALU.subtract), reads=[N16, self.mask], writes=[Q1])

                    def mmadd(name, base, lh, rh):
                        t_, a_ = mm(lh[:], [lh], rh[:], [rh])
                        o_ = tmp(name)
                        p.op("dve", lambda e: e.tensor_tensor(out=o_[:], in0=base[:], in1=a_, op=ALU.add), reads=[base, t_], writes=[o_])
                        return o_
                    yield
                    Q2 = mmadd("P2", Q1, L2, Q1)
                    yield
                    Q3 = mmadd("P3", Q2, L4, Q2)
                    yield
                    Y = mmadd("X", Q3, L8, Q3)
                    for lvl in (M_OFF32, M_OFF64, M_OFF128):
                        Loff = tmp("Noff")
                        p.op("pool", lambda e, Loff=Loff, lvl=lvl: e.tensor_tensor(out=Loff[:], in0=Lm[:], in1=M(lvl), op=ALU.mult), reads=[Lm, self.mask], writes=[Loff])
                        yield
                        Xt_, Xa_ = trn(Y)
                        Xs = tmp("Y"); evac(Xs, Xs[:], Xt_, Xa_)
                        T2t, T2a = mm(Loff[:], [Loff], Y[:], [Y])
                        T2 = tmp("T2"); evac(T2, T2[:], T2t, T2a)
                        yield
                        Zt, Za = mm(Xs[:], [Xs], T2[:], [T2])
                        Yn = tmp("X")
                        p.op("dve", lambda e, Yn=Yn, Y=Y, Za=Za: e.tensor_tensor(out=Yn[:], in0=Y[:], in1=Za, op=ALU.subtract), reads=[Y, Zt], writes=[Yn])
                        Y = Yn
                    yield
                    RU = tmp("RU"); RW = tmp("RW"); KD = tmp("KD")
                    p.op("pool", lambda e: e.tensor_scalar(out=RU[:], in0=vnt[:, hs], scalar1=beta[:, hc], scalar2=None, op0=ALU.mult), reads=[vnt, beta], writes=[RU])
                    p.op("pool", lambda e: e.tensor_scalar(out=RW[:], in0=knt[:, hs], scalar1=bk[:, hc], scalar2=None, op0=ALU.mult), reads=[knt, bk], writes=[RW])
                    p.op("pool", lambda e: e.tensor_scalar(out=KD[:], in0=knt[:, hs], scalar1=ekd[:, hc], scalar2=None, op0=ALU.mult), reads=[knt, ekd], writes=[KD])
                    yield
                    ut, ua = mm(Y[:], [Y], RU[:], [RU])
                    U_ = tmp("U"); evac(U_, U_[:], ut, ua)
                    wt_, wa_ = mm(RW[:], [RW], Y[:], [Y])
                    WT = tmp("WT"); evac(WT, WT[:], wt_, wa_)
                    Sh = S_h[h]
                    yield
                    wst, wsa = mm(WT[:], [WT], S[:, h, :], [Sh])
                    VN = tmp("VN")
                    p.op("dve", lambda e: e.tensor_tensor(out=VN[:], in0=U_[:], in1=wsa, op=ALU.subtract), reads=[U_, wst], writes=[VN])
                    if want:
                        qst, qsa = mm(qT, [qTt], S[:, h, :], [Sh])
                        avt, ava = mm(AT[:], [AT], VN[:], [VN])
                        QS = tmp("QS")
                        p.op("dve", lambda e: e.tensor_scalar(out=QS[:], in0=qsa, scalar1=eg[:, hc], scalar2=None, op0=ALU.mult), reads=[qst, eg], writes=[QS])
                        if dirn == 0:
                            p.op("dve", lambda e: e.tensor_tensor(out=ot[:, hs], in0=QS[:], in1=ava, op=ALU.add), reads=[QS, avt], writes=[ot])
                        else:
                            p.op("dve", lambda e: e.tensor_tensor(out=QS[:], in0=QS[:], in1=ava, op=ALU.add), reads=[QS, avt], writes=[QS])
                            p.op("pool", lambda e: e.tensor_tensor(out=ot[:, hs], in0=QS[:], in1=oft[:, hs], op=ALU.add), reads=[QS, oft], writes=[ot])
                    yield
                    kvt, kva = mm(KD[:], [KD], VN[:], [VN])
                    p.op("dve", lambda e: e.scalar_tensor_tensor(out=S[:, h, :], in0=S[:, h, :], scalar=egl[:, hc], in1=kva, op0=ALU.mult, op1=ALU.add), reads=[Sh, egl, kvt], writes=[Sh])
                for hg in ((0, 1, 2, 3), (4, 5, 6, 7)):
                    gens = [unit(h) for h in hg]
                    while gens:
                        for g_ in list(gens):
                            try:
                                next(g_)
                            except StopIteration:
                                gens.remove(g_)
                if want:
                    if dirn == 0:
                        p.dma(of.t[tk, :], ot[:], reads=[ot], writes=[of])
                    else:
                        sq = tmp("osq", (128, 128), 2); ssq = tmp("ossq", (128, 8), 2)
                        yct = tmp("yct", (128, 1024), 2, BF16)
                        for h in range(8):
                            hs = slice(h * 128, (h + 1) * 128)
                            p.op("act", lambda e, hs=hs, h=h: e.activation(out=sq[:], in_=ot[:, hs], func=AF.Square, accum_out=ssq[:, h:h + 1]), reads=[ot], writes=[sq, ssq])
                        self.rsqrt(ssq, ssq[:], ssq, ssq[:], 1e-6, scale=1.0 / 128)
                        for h in range(8):
                            hs = slice(h * 128, (h + 1) * 128)
                            p.op("dve", lambda e, hs=hs, h=h: e.scalar_tensor_tensor(out=ot[:, hs], in0=ot[:, hs], scalar=ssq[:, h:h + 1], in1=nw[:], op0=ALU.mult, op1=ALU.mult), reads=[ot, ssq, nw], writes=[ot])
                        p.op("pool", lambda e: e.tensor_tensor(out=yct[:], in0=ot[:], in1=zt[:], op=ALU.mult), reads=[ot, zt], writes=[yct])
                        p.dma(yc.t[tk, :], yct[:], reads=[yct], writes=[yc])
    p.barrier()
B.stage_gdn_scan = _stage_gdn_scan


def _stage_merge(self, l):
    c, p = self.cfg, self.p
    D, T_ = c.D, c.T
    d = self.dram
    ys = [d["yaT"], d["ybT"], d["ycT"]]
    ws = [d["w_branch_a"], d["w_branch_b"], d["w_branch_c"]]
    gT, mT = d["gtsT"], d["mT"]
    NB = D
    with ExitStack() as st:
        wb = [[self.sb(st, f"mgw{b_}{i}", [128, 8, NB], BF16) for i in range(1)] for b_ in range(3)]
        ab = [[self.sb(st, f"mga{b_}{i}", [128, 8, 512], BF16) for i in range(2)] for b_ in range(3)]
        gt = [self.sb(st, f"mgg{i}", [128, 3, 512], BF16) for i in range(3)]
        t1 = [self.sb(st, f"mgt1{i}", [128, 512], F32) for i in range(2)]
        t2 = [self.sb(st, f"mgt2{i}", [128, 512], F32) for i in range(2)]
        mo = [self.sb(st, f"mgo{i}", [128, 512], BF16) for i in range(2)]
        banks = [self.ps(st, f"mgps{i}", [128, 512], F32) for i in range(6)]
        wi = ai = bi = k = 0
        for nb0 in range(0, D, NB):
            for b_ in range(3):
                p.dma(wb[b_][0][:], ws[b_].t[l].rearrange("(kc p) n -> p kc n", p=128)[:, :, nb0:nb0 + NB], reads=[ws[b_]], writes=[wb[b_][0]], eng="pool")
            W3 = [wb[b_][0] for b_ in range(3)]; wi += 1
            for g0 in range(0, T_, 512):
                ntok = min(512, T_ - g0)
                A3 = [ab[b_][ai % 2] for b_ in range(3)]; ai += 1
                for b_ in range(3):
                    p.dma(A3[b_][:, :, :ntok], ys[b_].t.rearrange("(kc p) t -> p kc t", p=128)[:, :, g0:g0 + ntok], reads=[ys[b_]], writes=[A3[b_]])
                for cc in range(NB // 128):
                    r0 = nb0 + cc * 128
                    G_ = gt[k % 3]; T1 = t1[k % 2]; T2 = t2[k % 2]; MO = mo[k % 2]; k += 1
                    for b_ in range(3):
                        p.dma(G_[:, b_, :ntok], gT.t[b_ * D + r0:b_ * D + r0 + 128, g0:g0 + ntok], reads=[gT], writes=[G_])
                    P3 = []
                    for b_ in range(3):
                        ps = banks[bi % 6]; bi += 1
                        for kc in range(8):
                            p.op("pe", lambda e, ps=ps, b_=b_, kc=kc: e.matmul(ps[:, :ntok], W3[b_][:, kc, cc * 128:(cc + 1) * 128], A3[b_][:, kc, :ntok],
                                                                              start=(kc == 0), stop=(kc == 7)), reads=[W3[b_], A3[b_]], writes=[ps])
                        P3.append(ps)
                    p.op("dve", lambda e: e.tensor_tensor(out=T1[:, :ntok], in0=P3[0][:, :ntok], in1=G_[:, 0, :ntok], op=ALU.mult), reads=[P3[0], G_], writes=[T1])
                    p.op("dve", lambda e: e.tensor_tensor(out=T2[:, :ntok], in0=P3[1][:, :ntok], in1=G_[:, 1, :ntok], op=ALU.mult), reads=[P3[1], G_], writes=[T2])
                    p.op("pool", lambda e: e.tensor_tensor(out=T1[:, :ntok], in0=T1[:, :ntok], in1=T2[:, :ntok], op=ALU.add), reads=[T1, T2], writes=[T1])
                    p.op("dve", lambda e: e.tensor_tensor(out=T2[:, :ntok], in0=P3[2][:, :ntok], in1=G_[:, 2, :ntok], op=ALU.mult), reads=[P3[2], G_], writes=[T2])
                    p.op("pool", lambda e: e.tensor_tensor(out=MO[:, :ntok], in0=T1[:, :ntok], in1=T2[:, :ntok], op=ALU.add), reads=[T1, T2], writes=[MO])
                    p.dma(mT.t[r0:r0 + 128, g0:g0 + ntok], MO[:, :ntok], reads=[MO], writes=[mT])
    p.barrier()
B.stage_merge = _stage_merge


def _stage_tm_proj(self, l, actname, K, wname, NB=512, TG=512):
    p = self.p
    d = self.dram
    osub = d["osub"]
    state = {"k": 0}

    def init(st):
        state["o"] = [self.sb(st, f"tpo{i}", [128, 512], F32) for i in range(3)]

    def epi(ps, tok, nb0, nb):
        o = state["o"][state["k"] % 3]; state["k"] += 1
        p.op("act", lambda e: e.copy(out=o[:, :nb], in_=ps[:, :nb]), reads=[ps], writes=[o])
        p.dma(osub.t[tok:tok + 128, nb0:nb0 + nb], o[:, :nb], reads=[o], writes=[osub])
    W = d[wname]
    secs = [dict(c0=0, n=self.cfg.D, mode="TM", epi=epi, init=init)]
    self.proj(d[actname], K, W, W.t[l], secs, 0, self.cfg.T, NB=NB, TG=TG, tag="tp")
B.stage_tm_proj = _stage_tm_proj


def _stage_ffn_up(self, l):
    c, p = self.cfg, self.p
    D, T_, L, DFF, KC = c.D, c.T, c.L, c.DFF, c.KC
    d = self.dram
    hT, gT, W = d["hT"], d["gT"], d["w_up"]
    NCH = 2 * DFF // 128
    HC = DFF // 128
    CB = 4 if HC % 4 == 0 else (2 if HC % 2 == 0 else 1)
    TG = 510
    with ExitStack() as st:
        cw = self.sb(st, "fucw", [128, NCH, 3], F32)
        cb = self.sb(st, "fucb", [128, NCH], F32)
        for k_ in range(3):
            p.dma(cw[:, :, k_], d["ffn_conv_w"].t[l, k_].rearrange("(ch p) -> p ch", p=128), reads=[d["ffn_conv_w"]], writes=[cw], allow_slow_non_contiguous=True)
        p.dma(cb[:], d["ffn_conv_b"].t[l].rearrange("(ch p) -> p ch", p=128), reads=[d["ffn_conv_b"]], writes=[cb], allow_slow_non_contiguous=True)
        wA = [self.sb(st, f"fuwa{i}", [128, KC, CB * 128], BF16) for i in range(2)]
        wB = [self.sb(st, f"fuwb{i}", [128, KC, CB * 128], BF16) for i in range(2)]
        ab = [self.sb(st, f"fua{i}", [128, KC, 512], BF16) for i in range(2)]
        ua = [self.sb(st, f"fuua{i}", [128, 512], F32) for i in range(2)]
        ub = [self.sb(st, f"fuub{i}", [128, 512], F32) for i in range(2)]
        go = [self.sb(st, f"fugo{i}", [128, 512], BF16) for i in range(2)]
        banks = [self.ps(st, f"fups{i}", [128, 512], F32) for i in range(6)]
        wv = W.t[l].rearrange("(kc p) n -> p kc n", p=128)
        av = hT.t.rearrange("(kc p) t -> p kc t", p=128)
        wi = ai = bi = k = 0
        for cb0 in range(0, HC, CB):
            WA, WB = wA[wi % 2], wB[wi % 2]; wi += 1
            p.dma(WA[:], wv[:, :, cb0 * 128:(cb0 + CB) * 128], reads=[W], writes=[WA], eng="pool")
            p.dma(WB[:], wv[:, :, DFF + cb0 * 128:DFF + (cb0 + CB) * 128], reads=[W], writes=[WB], eng="pool")
            for (s0, s1) in ((0, L), (L, T_)):
                for g0 in range(s0, s1, TG):
                    n = min(TG, s1 - g0)
                    A = ab[ai % 2]; ai += 1
                    lo = max(s0, g0 - 1); hi = min(s1, g0 + n + 1)
                    if lo > g0 - 1:
                        p.op("pool", lambda e: e.memset(A[:, :, 0:1], 0.0), writes=[A])
                    if hi < g0 + n + 1:
                        p.op("pool", lambda e: e.memset(A[:, :, n + 1:n + 2], 0.0), writes=[A])
                    p.dma(A[:, :, lo - (g0 - 1):hi - (g0 - 1)], av[:, :, lo:hi], reads=[hT], writes=[A])
                    for cc in range(CB):
                        cha = cb0 + cc; chb = HC + cb0 + cc
                        pa = banks[bi % 6]; bi += 1
                        pb = banks[bi % 6]; bi += 1
                        for kc in range(KC):
                            p.op("pe", lambda e, kc=kc: e.matmul(pa[:, :n + 2], WA[:, kc, cc * 128:(cc + 1) * 128], A[:, kc, :n + 2], start=(kc == 0), stop=(kc == KC - 1)), reads=[WA, A], writes=[pa])
                        for kc in range(KC):
                            p.op("pe", lambda e, kc=kc: e.matmul(pb[:, :n + 2], WB[:, kc, cc * 128:(cc + 1) * 128], A[:, kc, :n + 2], start=(kc == 0), stop=(kc == KC - 1)), reads=[WB, A], writes=[pb])
                        UA, UB, GO = ua[k % 2], ub[k % 2], go[k % 2]; k += 1
                        for (U, ps, ch) in ((UA, pa, cha), (UB, pb, chb)):
                            p.op("dve", lambda e, U=U, ps=ps, ch=ch: e.tensor_scalar(out=U[:, :n], in0=ps[:, 0:n], scalar1=cw[:, ch, 0:1], scalar2=cb[:, ch:ch + 1], op0=ALU.mult, op1=ALU.add), reads=[ps, cw, cb], writes=[U])
                            p.op("dve", lambda e, U=U, ps=ps, ch=ch: e.scalar_tensor_tensor(out=U[:, :n], in0=ps[:, 1:n + 1], scalar=cw[:, ch, 1:2], in1=U[:, :n], op0=ALU.mult, op1=ALU.add), reads=[ps, cw, U], writes=[U])
                            p.op("dve", lambda e, U=U, ps=ps, ch=ch: e.scalar_tensor_tensor(out=U[:, :n], in0=ps[:, 2:n + 2], scalar=cw[:, ch, 2:3], in1=U[:, :n], op0=ALU.mult, op1=ALU.add), reads=[ps, cw, U], writes=[U])
                        p.op("act", lambda e: e.activation(out=UA[:, :n], in_=UA[:, :n], func=AF.Silu), reads=[UA], writes=[UA])
                        p.op("pool", lambda e: e.tensor_tensor(out=GO[:, :n], in0=UA[:, :n], in1=UB[:, :n], op=ALU.mult), reads=[UA, UB], writes=[GO])
                        p.dma(gT.t[cha * 128:(cha + 1) * 128, g0:g0 + n], GO[:, :n], reads=[GO], writes=[gT])
    p.barrier()
B.stage_ffn_up = _stage_ffn_up


def _build_all(self, st, upto=None):
    c = self.cfg
    self.declare()
    self.consts(st)
    self.stage_mod()
    self.stage_ln(0, None, (0, 0), src_inputs=True)
    for l in range(c.NL):
        last = (l == c.NL - 1)
        ctx_out = not last
        self.stage_win(l)
        self.stage_win_attn(l, ctx_out)
        self.stage_diff(l, ctx_out)
        self.stage_gdn_conv(l)
        self.stage_gdn_scan(l, ctx_out)
        d = self.dram
        self.stage_tm2fm(d["ya"], d["yaT"], 1024)
        self.stage_tm2fm(d["yb"], d["ybT"], 1024)
        self.stage_tm2fm(d["yc"], d["ycT"], 1024)
        self.stage_merge(l)
        self.stage_tm_proj(l, "mT", c.D, "w_o", NB=min(1024, c.D))
        if upto == "mix" and l == 0:
            break
        self.stage_ln(l, (2, "ln1_g", "ln1_b"), (l, 3), src_inputs=(l == 0))
        self.stage_ffn_up(l)
        kdown = c.DFF
        big = (kdown // 128) > 16
        self.stage_tm_proj(l, "gT", kdown, "w_down", NB=512, TG=256 if big else 512)
        if last:
            self.stage_ln(l, (5, "ln2_g", "ln2_b"), None, final=True)
        else:
            self.stage_ln(l, (5, "ln2_g", "ln2_b"), (l + 1, 0))
    self.p.emit(st)
B.build_all = _build_all


_CACHE = {}


def _get_nc(cfg_key):
    if cfg_key not in _CACHE:
        cfg = Cfg(*cfg_key)
        b = B(cfg)
        st = ExitStack()
        b.build_all(st)
        _CACHE[cfg_key] = (b, st, cfg)
    return _CACHE[cfg_key]


def kernel(**inputs):
    x = np.asarray(inputs["x"])
    bsz, N, D = x.shape
    L = inputs["ctx"].shape[1]
    DFF = inputs["w_down"].shape[1]
    NL = inputs["w_mod"].shape[0]
    b, st, cfg = _get_nc((D, N, L, DFF, NL))
    consts = make_consts(cfg)
    shared = {}
    for k, v in inputs.items():
        if k in ("x", "c", "ctx", "c_ctx"):
            continue
        shared[k] = np.ascontiguousarray(np.asarray(v), dtype=np.float32)
    shared["dn_a_log"] = shared["dn_a_log"].reshape(NL, 16)
    shared["dn_dt_bias"] = shared["dn_dt_bias"].reshape(NL, 16)
    shared["c_ctx"] = np.ascontiguousarray(np.asarray(inputs["c_ctx"]), dtype=np.float32)
    shared.update(consts)
    n_cores = 8 if bsz <= 4 else bsz
    hot = [0, 1, 4, 5][:bsz] if bsz <= 4 else list(range(bsz))
    zx = np.zeros_like(np.ascontiguousarray(x[0], dtype=np.float32))
    zc = np.zeros((D,), np.float32)
    zctx = np.zeros((L, D), np.float32)
    in_maps = []
    for core in range(n_cores):
        m = dict(shared)
        if core in hot:
            i = hot.index(core)
            m["x"] = np.ascontiguousarray(x[i], dtype=np.float32)
            m["c"] = np.ascontiguousarray(np.asarray(inputs["c"])[i], dtype=np.float32)
            m["ctx"] = np.ascontiguousarray(np.asarray(inputs["ctx"])[i], dtype=np.float32)
        else:
            m["x"], m["c"], m["ctx"] = zx, zc, zctx
        in_maps.append(m)
    res = run_bass_kernel_spmd(b.nc, in_maps, core_ids=list(range(n_cores)))
    return np.stack([np.asarray(res.results[core]["y"], dtype=np.float32) for core in hot], axis=0)
```

```python
import numpy as np
from contextlib import ExitStack
import concourse.bass as bass
import concourse.mybir as mybir
from concourse.bass_utils import run_bass_kernel_spmd

F32 = mybir.dt.float32
BF16 = mybir.dt.bfloat16
AF = mybir.ActivationFunctionType
ALU = mybir.AluOpType
AX = mybir.AxisListType

ENGS = ("pe", "act", "dve", "pool", "sp")
EPOCH = 30000
NDMASEM = 40


import types


def freeze(fn):
    if fn is None or fn.__closure__ is None:
        return fn
    cells = tuple(types.CellType(c.cell_contents) for c in fn.__closure__)
    return types.FunctionType(fn.__code__, fn.__globals__, fn.__name__, fn.__defaults__, cells)


class T:
    __slots__ = ("t", "name", "w", "r", "rd")

    def __init__(self, t, name=""):
        self.t = t
        self.name = name
        self.w = None
        self.r = {}
        self.rd = {}

    def sub(self, name=""):
        return T(self.t, name or self.name)

    def __getitem__(self, idx):
        return self.t[idx]


class Op:
    __slots__ = ("eng", "seq", "fn", "deps", "signal", "isdma", "sem", "val", "dsem", "dval", "dprev", "dslot")

    def __init__(self, eng, seq, fn, isdma):
        self.eng = eng
        self.seq = seq
        self.fn = fn
        self.deps = []
        self.signal = False
        self.isdma = isdma
        self.sem = None
        self.val = 0
        self.dsem = None
        self.dval = 0
        self.dprev = 0


class Prog:
    def __init__(self, nc, same_raw=True):
        self.nc = nc
        self.ops = {e: [] for e in ENGS}
        self.seen = {e: {} for e in ENGS}
        self.seen_dma = {e: set() for e in ENGS}
        self.pending = {e: [] for e in ENGS}
        self.ndma = 0
        self.same_raw = same_raw
        self.out_dmas = []
        self.all_dmas_unwaited = []
        self.dcount = {e: 0 for e in ENGS}
        self.lastd = {}

    def _need(self, op, tgt):
        if tgt is None or tgt is op:
            return
        e = op.eng
        if tgt.isdma:
            if id(tgt) in self.seen_dma[e]:
                return
            self.seen_dma[e].add(id(tgt))
            op.deps.append(tgt)
            return
        if tgt.eng == e:
            if e == "pe" or not self.same_raw:
                return
        if self.seen[e].get(tgt.eng, 0) >= tgt.seq:
            return
        self.seen[e][tgt.eng] = tgt.seq
        tgt.signal = True
        op.deps.append(tgt)

    def _record(self, eng, fn, reads, writes, isdma=False, raw_only_same=True):
        ops = self.ops[eng]
        op = Op(eng, len(ops) + 1, fn, isdma)
        if isdma:
            op.dslot = self.dcount[eng] % NDMASEM
            self.dcount[eng] += 1
            self.lastd[(eng, op.dslot)] = op
        for t in self.pending[eng]:
            self._need(op, t)
        self.pending[eng] = []
        for b in reads:
            self._need(op, b.w)
        for b in writes:
            self._need(op, b.w)
            for re_, r in b.r.items():
                if re_ == eng and not isdma:
                    continue
                self._need(op, r)
            for r in b.rd.values():
                self._need(op, r)
        for b in reads:
            if isdma:
                b.rd[(eng, op.dslot)] = op
            else:
                b.r[eng] = op
        for b in writes:
            b.w = op
            b.r = {}
            b.rd = {}
        ops.append(op)
        return op

    def op(self, eng, fn, reads=(), writes=()):
        return self._record(eng, freeze(fn), reads, writes)

    def dma(self, out_ap, in_ap, reads=(), writes=(), eng="sp", is_out=False, **kw):
        def fn(e, out_ap=out_ap, in_ap=in_ap, kw=kw):
            return e.dma_start(out=out_ap, in_=in_ap, **kw)
        op = self._record(eng, fn, reads, writes, isdma=True)
        op.signal = True
        self.ndma += 1
        if is_out:
            self.out_dmas.append(op)
        return op

    def barrier(self, label=None):
        import sys as _sys
        if not hasattr(self, "marks"):
            self.marks = []
        self.marks.append((label or _sys._getframe(1).f_code.co_name, {e: len(self.ops[e]) for e in ENGS}))
        lasts = []
        for e in ENGS:
            for o in reversed(self.ops[e]):
                if not o.isdma:
                    lasts.append(o)
                    break
        dmas = list(self.lastd.values())
        for e in ENGS:
            self.pending[e] = self.pending[e] + lasts + dmas

    def emit(self, stack):
        nc = self.nc
        self.barrier()
        fin = {}
        for e in ENGS:
            op = Op(e, len(self.ops[e]) + 1, None, False)
            for t in self.pending[e]:
                self._need(op, t)
            self.ops[e].append(op)
        for e in ENGS:
            cnt = 0
            sems = []
            for op in self.ops[e]:
                if op.isdma or not op.signal:
                    continue
                ep = cnt // EPOCH
                while len(sems) <= ep:
                    sems.append(stack.enter_context(nc.semaphore(f"c_{e}_{len(sems)}")))
                cnt += 1
                op.sem = sems[ep]
                op.val = cnt - ep * EPOCH
        dpool = {}
        duse = {}
        for e in ENGS:
            k = 0
            for op in self.ops[e]:
                if not op.isdma:
                    continue
                if e not in dpool:
                    dpool[e] = [stack.enter_context(nc.semaphore(f"d_{e}_{i}")) for i in range(NDMASEM)]
                    duse[e] = [0] * NDMASEM
                i = op.dslot
                k += 1
                op.dsem = dpool[e][i]
                op.dprev = 16 * duse[e][i]
                duse[e][i] += 1
                op.dval = 16 * duse[e][i]
        block = stack.enter_context(nc.Block())

        def run(eng, e):
            for op in self.ops[e]:
                for d in op.deps:
                    if d.isdma:
                        eng.wait_ge(d.dsem, d.dval)
                    else:
                        eng.wait_ge(d.sem, d.val)
                if op.fn is None:
                    continue
                if op.isdma:
                    if op.dprev:
                        eng.wait_ge(op.dsem, op.dprev)
                    op.fn(eng).then_inc(op.dsem, 16)
                else:
                    ins = op.fn(eng)
                    if op.signal:
                        ins.then_inc(op.sem, 1)

        block.tensor(lambda eng: run(eng, "pe"))
        block.scalar(lambda eng: run(eng, "act"))
        block.vector(lambda eng: run(eng, "dve"))
        block.gpsimd(lambda eng: run(eng, "pool"))
        block.sync(lambda eng: run(eng, "sp"))

    def stats(self):
        return {e: len(self.ops[e]) for e in ENGS}

import math

class Cfg:
    def __init__(self, D=2048, N=8192, L=256, DFF=5632, NL=2):
        self.D, self.N, self.L, self.DFF, self.NL = D, N, L, DFF, NL
        self.T = N + L
        self.KC = D // 128
        self.NT = self.T // 128
        self.IN_W = 1024 + 256 + 256 + 1024 + 1024 + 1024 + 3072 + 1024 + 16 + 16 + 3 * D

GRID_W = 64
ALPHA = None
LN_EPS = 1e-6

O_QA, O_KA, O_VA, O_QD, O_KD, O_VD, O_QKV, O_Z, O_A, O_B, O_G = (
    0, 1024, 1280, 1536, 2560, 3584, 4608, 7680, 8704, 8720, 8736)

M_ID, M_TRILS, M_TRIUI, M_TRIUS, M_TRILI, M_ONES, M_BD16, M_OFF32, M_OFF64, M_OFF128, M_NEG = range(11)
NMASK = 11


def rope_tables(n, dim):
    rows = n // GRID_W
    row = np.repeat(np.arange(rows, dtype=np.float32), GRID_W)
    col = np.tile(np.arange(GRID_W, dtype=np.float32), rows)
    axis_dim = dim // 2
    inv_freq = (10000.0 ** (-np.arange(0, axis_dim, 2, dtype=np.float32) / axis_dim)).astype(np.float32)
    ang_r = row[:, None] * inv_freq[None]
    ang_c = col[:, None] * inv_freq[None]
    ang = np.concatenate([ang_r, ang_r, ang_c, ang_c], axis=-1).astype(np.float32)
    q = dim // 4
    sgn = np.concatenate([-np.ones(q), np.ones(q), -np.ones(q), np.ones(q)]).astype(np.float32)
    return np.cos(ang).T.astype(np.float32), (np.sin(ang) * sgn[None]).T.astype(np.float32)


def make_consts(cfg):
    T, L = cfg.T, cfg.L
    def tab(dim, rep, scale):
        c, s = rope_tables(cfg.N, dim)
        out = np.zeros((4, 128, T), np.float32)
        c = np.tile(c, (rep, 1)); s = np.tile(s, (rep, 1))
        out[0, :, :L] = scale; out[0, :, L:] = c * scale
        out[1, :, L:] = s * scale
        out[2, :, :L] = 1.0; out[2, :, L:] = c
        out[3, :, L:] = s
        return out
    ropeA = tab(128, 1, 128 ** -0.5)
    ropeD = tab(64, 2, 64 ** -0.5)
    p = np.arange(128)[:, None]; f = np.arange(128)[None, :]
    m = np.zeros((NMASK, 128, 128), np.float32)
    m[M_ID] = (p == f); m[M_TRILS] = (p > f); m[M_TRIUI] = (p <= f); m[M_TRIUS] = (p < f); m[M_TRILI] = (p >= f)
    m[M_ONES] = 1.0
    m[M_NEG] = -1.0
    m[M_BD16] = (p // 16 == f // 16)
    m[M_OFF32] = (p // 32 == f // 32) & (p // 16 != f // 16)
    m[M_OFF64] = (p // 64 == f // 64) & (p // 32 != f // 32)
    m[M_OFF128] = (p // 64 != f // 64)
    return {"ropeA": ropeA, "ropeD": ropeD, "cmask": m}


def lockstep(gens, width):
    active = []
    it = iter(gens)
    done = False
    while True:
        while len(active) < width and not done:
            try:
                active.append(next(it))
            except StopIteration:
                done = True
        if not active:
            break
        for g in list(active):
            try:
                next(g)
            except StopIteration:
                active.remove(g)


class B:
    def __init__(self, cfg, dbg=()):
        self.cfg = cfg
        self.dbg = set(dbg)
        self.nc = bass.Bass("TRN2", target_bir_lowering=False)
        self.p = Prog(self.nc)
        self.dram = {}
        self.outs = []
        self.bank = 0

    def inp(self, name, shape, dt=F32):
        t = T(self.nc.dram_tensor(name, list(shape), dt, kind="ExternalInput").ap(), name)
        self.dram[name] = t
        return t

    def out(self, name, shape, dt=F32):
        t = T(self.nc.dram_tensor(name, list(shape), dt, kind="ExternalOutput").ap(), name)
        self.dram[name] = t
        self.outs.append(name)
        return t

    def scr(self, name, shape, dt):
        kind = "ExternalOutput" if name in self.dbg else "Internal"
        t = T(self.nc.dram_tensor(name, list(shape), dt, kind=kind).ap(), name)
        self.dram[name] = t
        if name in self.dbg:
            self.outs.append(name)
        return t

    def sb(self, st, name, shape, dt=F32):
        self.uid = getattr(self, "uid", 0) + 1
        name = f"{name}_{self.uid}"
        return T(st.enter_context(self.nc.sbuf_tensor(name, list(shape), dt)), name)

    def ps(self, st, name, shape, dt=F32):
        self.uid = getattr(self, "uid", 0) + 1
        name = f"{name}_{self.uid}"
        return T(st.enter_context(self.nc.psum_tensor(name, list(shape), dt)), name)

    def rsqrt(self, OUT, out_ap, IN, in_ap, eps, scale=1.0):
        p = self.p
        p.op("act", lambda e: e.activation(out=out_ap, in_=in_ap, func=AF.Sqrt, bias=self.epsc[eps][:out_ap.shape[0], :], scale=scale), reads=[IN, self.epst], writes=[OUT])
        p.op("dve", lambda e: e.reciprocal(out=out_ap, in_=out_ap), reads=[OUT], writes=[OUT])

    def declare(self):
        c = self.cfg
        D, T_, L, N, DFF, NL = c.D, c.T, c.L, c.N, c.DFF, c.NL
        i = self.inp
        i("x", [N, D]); i("c", [D]); i("ctx", [L, D]); i("c_ctx", [D])
        i("w_mod", [NL, D, 6 * D]); i("b_mod", [NL, 6 * D]); i("w_in", [NL, D, c.IN_W])
        i("wa_sink", [NL, 8])
        for n in ("df_lam_q1", "df_lam_k1", "df_lam_q2", "df_lam_k2"):
            i(n, [NL, 64])
        i("df_subln", [NL, 128]); i("dn_conv", [NL, 3, 3072]); i("dn_a_log", [NL, 16]); i("dn_dt_bias", [NL, 16])
        i("dn_norm", [NL, 128])
        i("w_branch_a", [NL, 1024, D]); i("w_branch_b", [NL, 1024, D]); i("w_branch_c", [NL, 1024, D])
        i("w_o", [NL, D, D]); i("ln1_g", [NL, D]); i("ln1_b", [NL, D])
        i("w_up", [NL, D, 2 * DFF]); i("ffn_conv_w", [NL, 3, 2 * DFF]); i("ffn_conv_b", [NL, 2 * DFF])
        i("w_down", [NL, DFF, D]); i("ln2_g", [NL, D]); i("ln2_b", [NL, D])
        i("ropeA", [4, 128, T_]); i("ropeD", [4, 128, T_]); i("cmask", [NMASK, 128, 128])
        self.out("y", [N, D])
        s = self.scr
        s("xres0", [T_, D], F32); s("xres1", [T_, D], F32); s("hT", [D, T_], BF16); s("modd", [NL, 2, 6 * D], F32)
        s("qaT", [1024, T_], BF16); s("kaT", [256, T_], BF16); s("va", [T_, 256], BF16)
        s("qdT", [1024, T_], BF16); s("kdT", [1024, T_], BF16); s("vd", [T_, 1024], BF16)
        s("gpre", [3072, T_], F32); s("zs", [T_, 1024], F32); s("ab", [T_, 32], F32); s("gtsT", [3 * D, T_], BF16)
        s("osub", [T_, D], F32)
        s("ya", [T_, 1024], BF16); s("yb", [T_, 1024], BF16); s("yc", [T_, 1024], BF16)
        s("qnT", [1024, T_], F32); s("knT", [1024, T_], F32); s("kn", [T_, 1024], F32); s("vn", [T_, 1024], F32)
        s("of", [T_, 1024], F32); s("mT", [D, T_], BF16); s("gT", [DFF, T_], BF16)
        s("yaT", [1024, T_], BF16); s("ybT", [1024, T_], BF16); s("ycT", [1024, T_], BF16)

    def consts(self, st):
        p = self.p
        cm = self.dram["cmask"]
        self.mask = self.sb(st, "mask", [128, NMASK, 128], F32)
        p.dma(self.mask[:], cm.t.rearrange("m p f -> p m f"), reads=[cm], writes=[self.mask])
        self.identb = self.sb(st, "identb", [128, 128], BF16)
        p.op("dve", lambda e: e.tensor_copy(out=self.identb[:], in_=self.mask[:, M_ID, :]), reads=[self.mask], writes=[self.identb])
        self.epst = self.sb(st, "epst", [128, 4], F32)
        self.epsc = {}
        for i_, v_ in enumerate((1e-6, 0.0, 1.0)):
            p.op("pool", lambda e, i_=i_, v_=v_: e.memset(self.epst[:, i_:i_ + 1], v_), writes=[self.epst])
            self.epsc[v_] = self.epst[:, i_:i_ + 1]
        c = self.cfg
        self.modc = self.sb(st, "modc", [128, c.NL, 2, 6 * c.KC], F32)

    def stage_mod(self):
        c, p, nc = self.cfg, self.p, self.nc
        KC, D = c.KC, c.D
        NJ = 6 * KC
        wm, bm = self.dram["w_mod"], self.dram["b_mod"]
        with ExitStack() as st:
            craw = self.sb(st, "craw", [128, KC, 2], F32)
            sc = self.sb(st, "sc", [128, KC, 2], F32)
            bcol = self.sb(st, "bcol", [128, c.NL, NJ], F32)
            ps = [self.ps(st, f"mps{i}", [128, 4, 2], F32) for i in range(2)]
            pst = self.ps(st, "mpst", [128, 128], F32)
            wt = [self.sb(st, f"wmt{i}", [128, KC, 512], F32) for i in range(2)]
            tr = self.sb(st, "mtr", [NJ, 128], F32)
            cc, cx = self.dram["c"], self.dram["c_ctx"]
            p.dma(craw[:, :, 0], cc.t.rearrange("(kc p) -> p kc", p=128), reads=[cc], writes=[craw], allow_slow_non_contiguous=True)
            p.dma(craw[:, :, 1], cx.t.rearrange("(kc p) -> p kc", p=128), reads=[cx], writes=[craw], allow_slow_non_contiguous=True)
            for l in range(c.NL):
                p.dma(bcol[:, l, :], bm.t[l].rearrange("(j p) -> p j", p=128), reads=[bm], writes=[bcol], allow_slow_non_contiguous=True)
            p.op("act", lambda e: e.activation(out=sc[:], in_=craw[:], func=AF.Silu), reads=[craw], writes=[sc])
            k = 0
            for l in range(c.NL):
                wv = wm.t[l].rearrange("(kc p) n -> p kc n", p=128)
                for j0 in range(0, NJ, 4):
                    w = wt[k % 2]; pp = ps[k % 2]; k += 1
                    p.dma(w[:], wv[:, :, j0 * 128:(j0 + 4) * 128], reads=[wm], writes=[w])
                    for jj in range(4):
                        for kc in range(KC):
                            p.op("pe", lambda e, w=w, pp=pp, jj=jj, kc=kc: e.matmul(
                                pp[:, jj, :], w[:, kc, jj * 128:(jj + 1) * 128], sc[:, kc, :],
                                start=(kc == 0), stop=(kc == KC - 1)), reads=[w, sc], writes=[pp])
                    for s_ in range(2):
                        p.op("dve", lambda e, pp=pp, s_=s_, l=l, j0=j0: e.tensor_tensor(
                            out=self.modc[:, l, s_, j0:j0 + 4], in0=pp[:, :, s_], in1=bcol[:, l, j0:j0 + 4], op=ALU.add),
                            reads=[pp, bcol], writes=[self.modc])
                for s_ in range(2):
                    for v in (1, 4):
                        p.op("dve", lambda e, l=l, s_=s_, v=v: e.tensor_scalar_add(
                            out=self.modc[:, l, s_, v * KC:(v + 1) * KC], in0=self.modc[:, l, s_, v * KC:(v + 1) * KC], scalar1=1.0),
                            reads=[self.modc], writes=[self.modc])
                md = self.dram["modd"]
                for s_ in range(2):
                    p.op("pe", lambda e, l=l, s_=s_: e.matmul(pst[0:NJ, :], self.modc[:, l, s_, :], self.mask[:, M_ID, :], start=True, stop=True),
                         reads=[self.modc, self.mask], writes=[pst])
                    p.op("dve", lambda e: e.tensor_copy(out=tr[:], in_=pst[0:NJ, :]), reads=[pst], writes=[tr])
                    p.dma(md.t[l, s_].rearrange("(j p) -> j p", p=128), tr[:], reads=[tr], writes=[md])
        p.barrier()

    def stage_ln(self, l, comb, ada, final=False, src_inputs=False, skip_ctx=False):
        c, p = self.cfg, self.p
        D, KC, L = c.D, c.KC, c.L
        alpha = (2 * c.NL) ** 0.25
        xcur = getattr(self, "xcur", 0)
        xres, xdst = self.dram[f"xres{xcur}"], self.dram[f"xres{1 - xcur}"]
        osub, hT, md = self.dram["osub"], self.dram["hT"], self.dram["modd"]
        if comb and not final:
            self.xcur = 1 - xcur
        nch = (D + 511) // 512
        with ExitStack() as st:
            xt = [self.sb(st, f"lxt{i}", [128, D], F32) for i in range(3)]
            if comb:
                ot = [self.sb(st, f"lot{i}", [128, D], F32) for i in range(3)]
                gate = self.sb(st, "lgate", [128, 2, D], F32)
                gB = self.sb(st, "lgB", [128, D], F32)
                bB = self.sb(st, "lbB", [128, D], F32)
                gv = comb[0]
                for s_ in range(2):
                    p.dma(gate[:, s_, :], md.t[l, s_, gv * D:(gv + 1) * D].partition_broadcast(128), reads=[md], writes=[gate])
                gd, bd = self.dram[comb[1]], self.dram[comb[2]]
                p.dma(gB[:], gd.t[l].partition_broadcast(128), reads=[gd], writes=[gB])
                p.dma(bB[:], bd.t[l].partition_broadcast(128), reads=[bd], writes=[bB])
            stt = [self.sb(st, f"lst{i}", [128, nch, 6], F32) for i in range(3)]
            mv = [self.sb(st, f"lmv{i}", [128, 2], F32) for i in range(3)]
            rs = [self.sb(st, f"lrs{i}", [128, 1], F32) for i in range(3)]
            if ada:
                xb = [self.sb(st, f"lxb{i}", [128, D], BF16) for i in range(3)]
                ht = [self.sb(st, f"lht{i}", [128, KC, 128], BF16) for i in range(3)]
                pt = [self.ps(st, f"lpt{i}", [128, KC, 128], BF16) for i in range(3)]
            def tile(i):
                isctx = i * 128 < L
                s_ = 1 if isctx else 0
                b = i % 3
                X = xt[b]
                if src_inputs:
                    src = self.dram["ctx"] if isctx else self.dram["x"]
                    r0 = i * 128 if isctx else i * 128 - L
                else:
                    src = xres; r0 = i * 128
                p.dma(X[:], src.t[r0:r0 + 128, :], reads=[src], writes=[X])

                def lnstats(X, b):
                    S, MV, R = stt[b], mv[b], rs[b]
                    for ch in range(nch):
                        p.op("dve", lambda e, ch=ch: e.bn_stats(out=S[:, ch, :], in_=X[:, ch * 512:min(D, (ch + 1) * 512)]), reads=[X], writes=[S])
                    p.op("dve", lambda e: e.bn_aggr(out=MV[:], in_=S[:].rearrange("p a b -> p (a b)")), reads=[S], writes=[MV])
                    self.rsqrt(R, R[:], MV, MV[:, 1:2], LN_EPS)
                    return MV, R
                if comb:
                    O = ot[b]
                    p.dma(O[:], osub.t[i * 128:(i + 1) * 128, :], reads=[osub], writes=[O])
                    yield
                    p.op("pool", lambda e, O=O, s_=s_: e.tensor_tensor(out=O[:], in0=O[:], in1=gate[:, s_, :], op=ALU.mult), reads=[O, gate], writes=[O])
                    yield
                    p.op("dve", lambda e, O=O, X=X: e.scalar_tensor_tensor(out=X[:], in0=X[:], scalar=alpha, in1=O[:], op0=ALU.mult, op1=ALU.add), reads=[X, O], writes=[X])
                    MV, R = lnstats(X, b)
                    p.op("dve", lambda e, X=X, MV=MV, R=R: e.tensor_scalar(out=X[:], in0=X[:], scalar1=MV[:, 0:1], scalar2=R[:, 0:1], op0=ALU.subtract, op1=ALU.mult), reads=[X, MV, R], writes=[X])
                    yield
                    p.op("pool", lambda e, X=X: e.tensor_tensor(out=X[:], in0=X[:], in1=gB[:], op=ALU.mult), reads=[X, gB], writes=[X])
                    p.op("pool", lambda e, X=X: e.tensor_tensor(out=X[:], in0=X[:], in1=bB[:], op=ALU.add), reads=[X, bB], writes=[X])
                    if final:
                        if not isctx:
                            y = self.dram["y"]
                            p.dma(y.t[i * 128 - L:(i + 1) * 128 - L, :], X[:], reads=[X], writes=[y], is_out=True)
                    else:
                        p.dma(xdst.t[i * 128:(i + 1) * 128, :], X[:], reads=[X], writes=[xdst])
                if ada:
                    la, sv = ada
                    yield
                    MV, R = lnstats(X, b)
                    XB, HT, PT = xb[b], ht[b], pt[b]
                    p.op("dve", lambda e, X=X, XB=XB, MV=MV, R=R: e.tensor_scalar(out=XB[:], in0=X[:], scalar1=MV[:, 0:1], scalar2=R[:, 0:1], op0=ALU.subtract, op1=ALU.mult), reads=[X, MV, R], writes=[XB])
                    yield
                    for kc in range(KC):
                        p.op("pe", lambda e, kc=kc, XB=XB, PT=PT: e.transpose(PT[:, kc, :], XB[:, kc * 128:(kc + 1) * 128], self.identb[:]), reads=[XB, self.identb], writes=[PT])
                    yield
                    for kc in range(KC):
                        p.op("act", lambda e, kc=kc, HT=HT, PT=PT, s_=s_: e.activation(
                            out=HT[:, kc, :], in_=PT[:, kc, :], func=AF.Identity,
                            bias=self.modc[:, la, s_, sv * KC + kc:sv * KC + kc + 1],
                            scale=self.modc[:, la, s_, (sv + 1) * KC + kc:(sv + 1) * KC + kc + 1]),
                            reads=[PT, self.modc], writes=[HT])
                    p.dma(hT.t.rearrange("(kc p) t -> p kc t", p=128)[:, :, i * 128:(i + 1) * 128], HT[:], reads=[HT], writes=[hT])
            lockstep((tile(i) for i in range(c.NT) if not (skip_ctx and i * 128 < L)), 2)
        p.barrier()


def _proj(self, actT, K, W, wl, sections, tok0, tok1, NB=512, TG=512, tag="pj"):
    c, p = self.cfg, self.p
    KCs = K // 128
    wv = wl.rearrange("(kc p) n -> p kc n", p=128)
    av = actT.t.rearrange("(kc p) t -> p kc t", p=128)
    with ExitStack() as st:
        wbuf = [self.sb(st, f"{tag}w{i}", [128, KCs, NB], BF16) for i in range(2)]
        anyrope = any(s.get("rope") for s in sections)
        if anyrope:
            wperm = [self.sb(st, f"{tag}wp{i}", [128, KCs, min(NB, 512)], BF16) for i in range(2)]
        abuf = [self.sb(st, f"{tag}a{i}", [128, KCs, TG], BF16) for i in range(2)]
        nbank = 6
        banks = [self.ps(st, f"{tag}ps{i}", [128, 512], F32) for i in range(nbank)]
        self.pj_st = st
        for s in sections:
            if s.get("init"):
                s["init"](st)
        wi = 0; ai = 0; bi = 0
        for s in sections:
            mode = s["mode"]
            NBs = min(NB, 512) if s.get("rope") else NB
            for nb0 in range(0, s["n"], NBs):
                nb = min(NBs, s["n"] - nb0)
                wb = wbuf[wi % 2]
                for k0_ in range(0, KCs, 16):
                    k1_ = min(KCs, k0_ + 16)
                    p.dma(wb[:, k0_:k1_, :nb], wv[:, k0_:k1_, s["c0"] + nb0:s["c0"] + nb0 + nb], reads=[W], writes=[wb], eng="pool")
                if s.get("rope"):
                    qs = 32 if s["rope"] == "A" else 16
                    wp = wperm[wi % 2]
                    v1 = wb[:, :, :nb].rearrange("p k (a two q) -> p k a two q", two=2, q=qs)
                    v2 = wp[:, :, :nb].rearrange("p k (a two q) -> p k a two q", two=2, q=qs)
                    for kc in range(KCs):
                        p.op("pool", lambda e, v1=v1, v2=v2, kc=kc: e.tensor_copy(out=v2[:, kc, :, 0, :], in_=v1[:, kc, :, 1, :]), reads=[wb], writes=[wp])
                        p.op("pool", lambda e, v1=v1, v2=v2, kc=kc: e.tensor_copy(out=v2[:, kc, :, 1, :], in_=v1[:, kc, :, 0, :]), reads=[wb], writes=[wp])
                wi += 1
                for g0 in range(tok0, tok1, TG):
                    ntok = min(TG, tok1 - g0)
                    ab = abuf[ai % 2]; ai += 1
                    for k0_ in range(0, KCs, 16):
                        k1_ = min(KCs, k0_ + 16)
                        p.dma(ab[:, k0_:k1_, :ntok], av[:, k0_:k1_, g0:g0 + ntok], reads=[actT], writes=[ab])
                    if s.get("pre"):
                        s["pre"](g0, ntok)
                    for sb0 in range(0, nb, 512):
                        sbn = min(512, nb - sb0)
                        if mode == "TM":
                            for tt in range(ntok // 128):
                                ps = banks[bi % nbank]; bi += 1
                                for kc in range(KCs):
                                    p.op("pe", lambda e, ps=ps, ab=ab, wb=wb, kc=kc, tt=tt: e.matmul(
                                        ps[:, :sbn], ab[:, kc, tt * 128:(tt + 1) * 128], wb[:, kc, sb0:sb0 + sbn],
                                        start=(kc == 0), stop=(kc == KCs - 1)), reads=[ab, wb], writes=[ps])
                                s["epi"](ps, g0 + tt * 128, nb0 + sb0, sbn)
                        else:
                            for cc in range(sb0 // 128, (sb0 + sbn) // 128):
                                ps = banks[bi % nbank]; bi += 1
                                for kc in range(KCs):
                                    p.op("pe", lambda e, ps=ps, ab=ab, wb=wb, kc=kc, cc=cc, ntok=ntok: e.matmul(
                                        ps[:, :ntok], wb[:, kc, cc * 128:(cc + 1) * 128], ab[:, kc, :ntok],
                                        start=(kc == 0), stop=(kc == KCs - 1)), reads=[ab, wb], writes=[ps])
                                ps2 = None
                                if s.get("rope"):
                                    ps2 = banks[bi % nbank]; bi += 1
                                    for kc in range(KCs):
                                        p.op("pe", lambda e, ps2=ps2, ab=ab, wp=wp, kc=kc, cc=cc, ntok=ntok: e.matmul(
                                            ps2[:, :ntok], wp[:, kc, cc * 128:(cc + 1) * 128], ab[:, kc, :ntok],
                                            start=(kc == 0), stop=(kc == KCs - 1)), reads=[ab, wp], writes=[ps2])
                                s["epi"](ps, ps2, g0, ntok, nb0 + cc * 128)
    p.barrier()
B.proj = _proj


def _stage_win(self, l, tok0=0):
    c, p = self.cfg, self.p
    D, T_ = c.D, c.T
    d = self.dram
    W = d["w_in"]
    state = {}

    def init(st):
        state["of"] = [self.sb(st, f"wiof{i}", [128, 512], F32) for i in range(3)]
        state["ob"] = [self.sb(st, f"wiob{i}", [128, 512], BF16) for i in range(3)]
        state["t1"] = [self.sb(st, f"wit1{i}", [128, 512], F32) for i in range(2)]
        state["t2"] = [self.sb(st, f"wit2{i}", [128, 512], F32) for i in range(2)]
        state["cos"] = [self.sb(st, f"wicos{i}", [128, 512], F32) for i in range(2)]
        state["sin"] = [self.sb(st, f"wisin{i}", [128, 512], F32) for i in range(2)]
        state["k"] = 0

    def tm_epi(dst, dcol0, func, bf):
        def epi(ps, tok, nb0, nb):
            k = state["k"]; state["k"] += 1
            o = (state["ob"] if bf else state["of"])[k % 3]
            p.op("act", lambda e: e.activation(out=o[:, :nb], in_=ps[:, :nb], func=func), reads=[ps], writes=[o])
            p.dma(dst.t[tok:tok + 128, dcol0 + nb0:dcol0 + nb0 + nb], o[:, :nb], reads=[o], writes=[dst])
        return epi

    def rope_pre(tabname, idx):
        tab = d[tabname]
        def pre(g0, ntok):
            k = state["k"]; state["k"] += 1
            cs, sn = state["cos"][k % 2], state["sin"][k % 2]
            p.dma(cs[:, :ntok], tab.t[idx, :, g0:g0 + ntok], reads=[tab], writes=[cs])
            p.dma(sn[:, :ntok], tab.t[idx + 1, :, g0:g0 + ntok], reads=[tab], writes=[sn])
            state["cs"] = (cs, sn)
        return pre

    def rope_epi(dst):
        def epi(ps, ps2, g0, ntok, c0):
            k = state["k"]; state["k"] += 1
            cs, sn = state["cs"]
            t1, t2, o = state["t1"][k % 2], state["t2"][k % 2], state["ob"][k % 3]
            p.op("dve", lambda e: e.tensor_tensor(out=t1[:, :ntok], in0=ps[:, :ntok], in1=cs[:, :ntok], op=ALU.mult), reads=[ps, cs], writes=[t1])
            p.op("dve", lambda e: e.tensor_tensor(out=t2[:, :ntok], in0=ps2[:, :ntok], in1=sn[:, :ntok], op=ALU.mult), reads=[ps2, sn], writes=[t2])
            p.op("pool", lambda e: e.tensor_tensor(out=o[:, :ntok], in0=t1[:, :ntok], in1=t2[:, :ntok], op=ALU.add), reads=[t1, t2], writes=[o])
            p.dma(dst.t[c0:c0 + 128, g0:g0 + ntok], o[:, :ntok], reads=[o], writes=[dst])
        return epi

    def fm_epi(dst, func=AF.Copy, bf=False):
        def epi(ps, ps2, g0, ntok, c0):
            k = state["k"]; state["k"] += 1
            o = (state["ob"] if bf else state["of"])[k % 3]
            p.op("act", lambda e: e.activation(out=o[:, :ntok], in_=ps[:, :ntok], func=func), reads=[ps], writes=[o])
            p.dma(dst.t[c0:c0 + 128, g0:g0 + ntok], o[:, :ntok], reads=[o], writes=[dst])
        return epi

    secs = [
        dict(c0=O_QA, n=1024, mode="FM", rope="A", pre=rope_pre("ropeA", 0), epi=rope_epi(d["qaT"]), init=init),
        dict(c0=O_KA, n=256, mode="FM", rope="A", pre=rope_pre("ropeA", 2), epi=rope_epi(d["kaT"])),
        dict(c0=O_QD, n=1024, mode="FM", rope="D", pre=rope_pre("ropeD", 0), epi=rope_epi(d["qdT"])),
        dict(c0=O_KD, n=1024, mode="FM", rope="D", pre=rope_pre("ropeD", 2), epi=rope_epi(d["kdT"])),
        dict(c0=O_QKV, n=3072, mode="FM", epi=fm_epi(d["gpre"])),
        dict(c0=O_VA, n=256, mode="TM", epi=tm_epi(d["va"], 0, AF.Copy, True)),
        dict(c0=O_VD, n=1024, mode="TM", epi=tm_epi(d["vd"], 0, AF.Copy, True)),
        dict(c0=O_Z, n=1024, mode="TM", epi=tm_epi(d["zs"], 0, AF.Silu, False)),
        dict(c0=O_A, n=32, mode="TM", epi=tm_epi(d["ab"], 0, AF.Copy, False)),
        dict(c0=O_G, n=3 * D, mode="FM", epi=fm_epi(d["gtsT"], AF.Sigmoid, True)),
    ]
    self.proj(d["hT"], D, W, W.t[l], secs, tok0, T_, NB=1024, tag="wi")
B.stage_win = _stage_win


def _stage_tm2fm(self, src, dstT, C, tok0=0):
    c, p = self.cfg, self.p
    nck = C // 128
    with ExitStack() as st:
        xt = [self.sb(st, f"tfx{i}", [128, C], BF16) for i in range(2)]
        ot = [self.sb(st, f"tfo{i}", [128, nck, 128], BF16) for i in range(2)]
        pt = [self.ps(st, f"tfp{i}", [128, nck, 128], BF16) for i in range(2)]
        for i in range(tok0 // 128, c.NT):
            X, O, P = xt[i % 2], ot[i % 2], pt[i % 2]
            p.dma(X[:], src.t[i * 128:(i + 1) * 128, :], reads=[src], writes=[X])
            for k in range(nck):
                p.op("pe", lambda e, k=k, X=X, P=P: e.transpose(P[:, k, :], X[:, k * 128:(k + 1) * 128], self.identb[:]), reads=[X, self.identb], writes=[P])
            p.op("act", lambda e, O=O, P=P: e.copy(out=O[:], in_=P[:]), reads=[P], writes=[O])
            p.dma(dstT.t.rearrange("(k p) t -> p k t", p=128)[:, :, i * 128:(i + 1) * 128], O[:], reads=[O], writes=[dstT])
    p.barrier()
B.stage_tm2fm = _stage_tm2fm


def _stage_diff(self, l, ctx_out):
    c, p = self.cfg, self.p
    T_, L, NT = c.T, c.L, c.NT
    d = self.dram
    qdT, kdT, vd, yb = d["qdT"], d["kdT"], d["vd"], d["yb"]
    lam_init = 0.8 - 0.6 * math.exp(-0.3 * l)
    with ExitStack() as st:
        lv = self.sb(st, "dflv", [128, 4, 64], F32)
        for i, n in enumerate(("df_lam_q1", "df_lam_k1", "df_lam_q2", "df_lam_k2")):
            p.dma(lv[:, i, :], d[n].t[l].partition_broadcast(128), reads=[d[n]], writes=[lv])
        lp = self.sb(st, "dflp", [128, 2, 64], F32)
        ls = self.sb(st, "dfls", [128, 2], F32)
        nlam = self.sb(st, "dfnl", [128, 1], F32)
        p.op("dve", lambda e: e.tensor_tensor(out=lp[:, 0, :], in0=lv[:, 0, :], in1=lv[:, 1, :], op=ALU.mult), reads=[lv], writes=[lp])
        p.op("dve", lambda e: e.tensor_tensor(out=lp[:, 1, :], in0=lv[:, 2, :], in1=lv[:, 3, :], op=ALU.mult), reads=[lv], writes=[lp])
        p.op("dve", lambda e: e.reduce_sum(out=ls[:], in_=lp[:], axis=AX.X), reads=[lp], writes=[ls])
        p.op("act", lambda e: e.activation(out=ls[:], in_=ls[:], func=AF.Exp), reads=[ls], writes=[ls])
        p.op("dve", lambda e: e.tensor_tensor(out=nlam[:], in0=ls[:, 1:2], in1=ls[:, 0:1], op=ALU.subtract), reads=[ls], writes=[nlam])
        p.op("dve", lambda e: e.tensor_scalar_add(out=nlam[:], in0=nlam[:], scalar1=-lam_init), reads=[nlam], writes=[nlam])
        sub = self.sb(st, "dfsub", [128, 128], F32)
        p.dma(sub[:], d["df_subln"].t[l].partition_broadcast(128), reads=[d["df_subln"]], writes=[sub])
        p.op("dve", lambda e: e.tensor_scalar_mul(out=sub[:], in0=sub[:], scalar1=1.0 - lam_init), reads=[sub], writes=[sub])

        kT = [self.sb(st, f"dfk{i}", [128, T_], BF16) for i in range(2)]
        vx = [self.sb(st, f"dfv{i}", [128, NT, 132], BF16) for i in range(2)]
        for i in range(2):
            p.op("pool", lambda e, i=i: e.memset(vx[i][:, :, 128:129], 1.0), writes=[vx[i]])
        qt = [self.sb(st, f"dfq{i}", [128, 512], BF16) for i in range(2)]
        pe_ = [[self.sb(st, f"dfp{j}{i}", [128, 512], BF16) for i in range(2)] for j in range(2)]
        sb_ = [[self.ps(st, f"dfs{j}{i}", [128, 512], F32) for i in range(2)] for j in range(2)]
        ob_ = [[self.ps(st, f"dfo{j}{i}", [128, 2, 256], F32) for i in range(2)] for j in range(2)]
        r12 = [self.sb(st, f"dfr{i}", [128, 2], F32) for i in range(2)]
        t1 = [self.sb(st, f"dft{i}", [128, 128], F32) for i in range(2)]
        o_ = [self.sb(st, f"dfoo{i}", [128, 128], F32) for i in range(2)]
        sq = [self.sb(st, f"dfsq{i}", [128, 128], F32) for i in range(2)]
        ss = [self.sb(st, f"dfss{i}", [128, 1], F32) for i in range(2)]
        yo = [self.sb(st, f"dfyo{i}", [128, 128], BF16) for i in range(2)]
        qi = 0; si = 0; ei = 0
        groups = []
        if ctx_out:
            groups.append((0, L, 0, L // 128))
        for g0 in range(L, T_, 512):
            groups.append((g0, min(512, T_ - g0), 0, NT))
        for h in range(8):
            K, V = kT[h % 2], vx[h % 2]
            p.dma(K[:], kdT.t[h * 128:(h + 1) * 128, :], reads=[kdT], writes=[K])
            vv_ = vd.t[:, h * 128:(h + 1) * 128].rearrange("(n p) c -> p n c", p=128)
            for n0_ in range(0, NT, 16):
                n1_ = min(NT, n0_ + 16)
                p.dma(V[:, n0_:n1_, 0:128], vv_[:, n0_:n1_, :], reads=[vd], writes=[V])
            for (g0, ntok, kt0, kt1) in groups:
                Q = qt[qi % 2]; qi += 1
                p.dma(Q[:, :ntok], qdT.t[h * 128:(h + 1) * 128, g0:g0 + ntok], reads=[qdT], writes=[Q])
                nq = ntok // 128

                def emit_S(kt):
                    for j in range(2):
                        S = sb_[j][kt % 2]
                        p.op("pe", lambda e, S=S, K=K, Q=Q, j=j, kt=kt, ntok=ntok: e.matmul(
                            S[:, :ntok], K[j * 64:(j + 1) * 64, kt * 128:(kt + 1) * 128], Q[j * 64:(j + 1) * 64, :ntok], start=True, stop=True),
                            reads=[K, Q], writes=[S])
                emit_S(kt0)
                for kt in range(kt0, kt1):
                    if kt + 1 < kt1:
                        emit_S(kt + 1)
                    for j in range(2):
                        S = sb_[j][kt % 2]; P = pe_[j][kt % 2]
                        p.op("act", lambda e, S=S, P=P, ntok=ntok: e.activation(out=P[:, :ntok], in_=S[:, :ntok], func=AF.Exp), reads=[S], writes=[P])
                        for qs in range(nq):
                            O = ob_[j][qs // 2]
                            p.op("pe", lambda e, O=O, P=P, V=V, qs=qs, kt=kt: e.matmul(
                                O[:, qs % 2, 0:129], P[:, qs * 128:(qs + 1) * 128], V[:, kt, 0:129], start=(kt == kt0 and qs % 2 == 0), stop=(kt == kt1 - 1)),
                                reads=[P, V], writes=[O])
                for qs in range(nq):
                    O1, O2 = ob_[0][qs // 2], ob_[1][qs // 2]
                    R_, T1, OO, SQ, SS, YO = r12[ei % 2], t1[ei % 2], o_[ei % 2], sq[ei % 2], ss[ei % 2], yo[ei % 2]; ei += 1
                    s2 = qs % 2
                    p.op("dve", lambda e, R_=R_, O1=O1, s2=s2: e.reciprocal(out=R_[:, 0:1], in_=O1[:, s2, 128:129]), reads=[O1], writes=[R_])
                    p.op("dve", lambda e, R_=R_, O2=O2, s2=s2: e.reciprocal(out=R_[:, 1:2], in_=O2[:, s2, 128:129]), reads=[O2], writes=[R_])
                    p.op("dve", lambda e, R_=R_: e.tensor_tensor(out=R_[:, 1:2], in0=R_[:, 1:2], in1=nlam[:], op=ALU.mult), reads=[R_, nlam], writes=[R_])
                    p.op("dve", lambda e, R_=R_, O1=O1, T1=T1, s2=s2: e.tensor_scalar_mul(out=T1[:], in0=O1[:, s2, 0:128], scalar1=R_[:, 0:1]), reads=[O1, R_], writes=[T1])
                    p.op("dve", lambda e, R_=R_, O2=O2, T1=T1, OO=OO, s2=s2: e.scalar_tensor_tensor(out=OO[:], in0=O2[:, s2, 0:128], scalar=R_[:, 1:2], in1=T1[:], op0=ALU.mult, op1=ALU.add), reads=[O2, R_, T1], writes=[OO])
                    p.op("act", lambda e, OO=OO, SQ=SQ, SS=SS: e.activation(out=SQ[:], in_=OO[:], func=AF.Square, accum_out=SS[:]), reads=[OO], writes=[SQ, SS])
                    self.rsqrt(SS, SS[:], SS, SS[:], 1e-6, scale=1.0 / 128)
                    p.op("dve", lambda e, OO=OO, SS=SS, YO=YO: e.scalar_tensor_tensor(out=YO[:], in0=OO[:], scalar=SS[:, 0:1], in1=sub[:], op0=ALU.mult, op1=ALU.mult), reads=[OO, SS, sub], writes=[YO])
                    t0 = g0 + qs * 128
                    p.dma(yb.t[t0:t0 + 128, h * 128:(h + 1) * 128], YO[:], reads=[YO], writes=[yb])
    p.barrier()
B.stage_diff = _stage_diff


def _stage_win_attn(self, l, ctx_out):
    c, p = self.cfg, self.p
    T_, L, NT = c.T, c.L, c.NT
    LT = L // 128
    d = self.dram
    qaT, kaT, va, ya = d["qaT"], d["kaT"], d["va"], d["ya"]
    with ExitStack() as st:
        es = self.sb(st, "waes", [128, 8], F32)
        p.dma(es[:], d["wa_sink"].t[l].partition_broadcast(128), reads=[d["wa_sink"]], writes=[es])
        p.op("act", lambda e: e.activation(out=es[:], in_=es[:], func=AF.Exp), reads=[es], writes=[es])
        mk = self.sb(st, "wamk", [128, 2, 4, 128], BF16)
        for h4 in range(4):
            p.op("dve", lambda e, h4=h4: e.tensor_copy(out=mk[:, 0, h4, :], in_=self.mask[:, M_TRILI, :]), reads=[self.mask], writes=[mk])
            p.op("dve", lambda e, h4=h4: e.tensor_copy(out=mk[:, 1, h4, :], in_=self.mask[:, M_TRIUI, :]), reads=[self.mask], writes=[mk])
        K = self.sb(st, "wak", [128, T_], BF16)
        V = self.sb(st, "wav", [128, NT, 132], BF16)
        p.op("pool", lambda e: e.memset(V[:, :, 128:129], 1.0), writes=[V])
        qt = [self.sb(st, f"waq{i}", [128, 4, 128], BF16) for i in range(2)]
        pp = [self.sb(st, f"wap{i}", [128, 512], BF16) for i in range(4)]
        sb_ = [self.ps(st, f"was{i}", [128, 512], F32) for i in range(4)]
        ob_ = [[self.ps(st, f"wao{i}{j}", [128, 2, 256], F32) for j in range(2)] for i in range(2)]
        rr = [self.sb(st, f"war{i}", [128, 1], F32) for i in range(4)]
        yo = [self.sb(st, f"wayo{i}", [128, 128], BF16) for i in range(4)]
        si = 0; ei = 0; bi = 0
        for g in range(2):
            p.dma(K[:], kaT.t[g * 128:(g + 1) * 128, :], reads=[kaT], writes=[K])
            vv_ = va.t[:, g * 128:(g + 1) * 128].rearrange("(n p) c -> p n c", p=128)
            for n0_ in range(0, NT, 16):
                n1_ = min(NT, n0_ + 16)
                p.dma(V[:, n0_:n1_, 0:128], vv_[:, n0_:n1_, :], reads=[va], writes=[V])
            def block(i, bi):
                Q = qt[bi % 2]; OB = ob_[bi % 2]
                SB = sb_[2 * (bi % 2):2 * (bi % 2) + 2]; PP = pp[2 * (bi % 2):2 * (bi % 2) + 2]
                for h4 in range(4):
                    h = g * 4 + h4
                    p.dma(Q[:, h4, :], qaT.t[h * 128:(h + 1) * 128, i * 128:(i + 1) * 128], reads=[qaT], writes=[Q])
                kts = [(kt, None) for kt in range(LT)]
                if i >= LT:
                    if i - 1 >= LT:
                        kts.append((i - 1, 0))
                    kts.append((i, None))
                    if i + 1 < NT:
                        kts.append((i + 1, 1))

                def emit_S(n_):
                    kt_ = kts[n_][0]
                    S_ = SB[n_ % 2]
                    p.op("pe", lambda e, S_=S_, Q=Q, kt_=kt_: e.matmul(S_[:], K[:, kt_ * 128:(kt_ + 1) * 128], Q[:].rearrange("p a b -> p (a b)"), start=True, stop=True),
                         reads=[K, Q], writes=[S_])
                yield
                emit_S(0)
                for n_, (kt, m) in enumerate(kts):
                    S = SB[n_ % 2]; P = PP[n_ % 2]
                    if n_ + 1 < len(kts):
                        emit_S(n_ + 1)
                    p.op("act", lambda e, S=S, P=P: e.activation(out=P[:], in_=S[:], func=AF.Exp), reads=[S], writes=[P])
                    if m is not None:
                        p.op("pool", lambda e, P=P, m=m: e.tensor_tensor(out=P[:], in0=P[:], in1=mk[:, m].rearrange("p a b -> p (a b)"), op=ALU.mult), reads=[P, mk], writes=[P])
                    yield
                    for h4 in range(4):
                        O = OB[h4 // 2]
                        p.op("pe", lambda e, O=O, P=P, h4=h4, kt=kt, n_=n_, nk=len(kts): e.matmul(
                            O[:, h4 % 2, 0:129], P[:, h4 * 128:(h4 + 1) * 128], V[:, kt, 0:129], start=(n_ == 0 and h4 % 2 == 0), stop=(n_ == nk - 1)),
                            reads=[P, V], writes=[O])
                yield
                for h4 in range(4):
                    h = g * 4 + h4
                    O = OB[h4 // 2]
                    R_, YO = rr[(bi * 4 + h4) % 4], yo[(bi * 4 + h4) % 4]
                    p.op("dve", lambda e, R_=R_, O=O, h4=h4, h=h: e.tensor_scalar(out=R_[:], in0=O[:, h4 % 2, 128:129], scalar1=es[:, h:h + 1], scalar2=None, op0=ALU.add), reads=[O, es], writes=[R_])
                    p.op("dve", lambda e, R_=R_: e.reciprocal(out=R_[:], in_=R_[:]), reads=[R_], writes=[R_])
                    p.op("dve", lambda e, R_=R_, O=O, YO=YO, h4=h4: e.tensor_scalar_mul(out=YO[:], in0=O[:, h4 % 2, 0:128], scalar1=R_[:, 0:1]), reads=[O, R_], writes=[YO])
                    p.dma(ya.t[i * 128:(i + 1) * 128, h * 128:(h + 1) * 128], YO[:], reads=[YO], writes=[ya])
            blocks = list(range(0 if ctx_out else LT, NT))
            lockstep((block(i, n) for n, i in enumerate(blocks)), 2)
    p.barrier()
B.stage_win_attn = _stage_win_attn


def _stage_gdn_conv(self, l):
    c, p = self.cfg, self.p
    T_, L, NT = c.T, c.L, c.NT
    d = self.dram
    gpre, qnT, knT, kn, vn = d["gpre"], d["qnT"], d["knT"], d["kn"], d["vn"]
    with ExitStack() as st:
        cw = self.sb(st, "gcw", [128, 24, 3], F32)
        for k_ in range(3):
            p.dma(cw[:, :, k_], d["dn_conv"].t[l, k_].rearrange("(ch p) -> p ch", p=128), reads=[d["dn_conv"]], writes=[cw], allow_slow_non_contiguous=True)
        G = [self.sb(st, f"gcG{i}", [128, 514], F32) for i in range(4)]
        Y = [self.sb(st, f"gcY{i}", [128, 512], F32) for i in range(4)]
        SQ = [self.sb(st, f"gcS{i}", [128, 512], F32) for i in range(4)]
        RI = [self.sb(st, f"gcR{i}", [128, 512], F32) for i in range(4)]
        YN = [self.sb(st, f"gcN{i}", [128, 512], F32) for i in range(4)]
        TT = [self.sb(st, f"gcT{i}", [128, 4, 128], F32) for i in range(4)]
        pss = [self.ps(st, f"gcps{i}", [128, 512], F32) for i in range(4)]
        ptt = [self.ps(st, f"gcpt{i}", [128, 4, 128], F32) for i in range(4)]
        ones = self.mask[:, M_ONES, :]
        ident = self.mask[:, M_ID, :]
        def iters():
            k = 0
            for ch in range(24):
                for (s0, s1) in ((0, L), (L, T_)):
                    for g0 in range(s0, s1, 512):
                        yield one(k, ch, s0, s1, g0)
                        k += 1

        def one(k, ch, s0, s1, g0):
                    kind = ch // 8
                    hh = ch % 8
                    ntok = min(512, s1 - g0)
                    g, y, sq, ri, yn, tt, ps1, pt1 = G[k % 4], Y[k % 4], SQ[k % 4], RI[k % 4], YN[k % 4], TT[k % 4], pss[k % 4], ptt[k % 4]
                    lo = max(s0, g0 - 1); hi = min(s1, g0 + ntok + 1)
                    if lo > g0 - 1:
                        p.op("pool", lambda e, g=g: e.memset(g[:, 0:1], 0.0), writes=[g])
                    if hi < g0 + ntok + 1:
                        p.op("pool", lambda e, g=g, ntok=ntok: e.memset(g[:, ntok + 1:ntok + 2], 0.0), writes=[g])
                    p.dma(g[:, lo - (g0 - 1):hi - (g0 - 1)], gpre.t[ch * 128:(ch + 1) * 128, lo:hi], reads=[gpre], writes=[g])
                    yield
                    p.op("dve", lambda e, g=g, y=y, ntok=ntok, ch=ch: e.tensor_scalar_mul(out=y[:, :ntok], in0=g[:, 0:ntok], scalar1=cw[:, ch, 0:1]), reads=[g, cw], writes=[y])
                    p.op("dve", lambda e, g=g, y=y, ntok=ntok, ch=ch: e.scalar_tensor_tensor(out=y[:, :ntok], in0=g[:, 1:ntok + 1], scalar=cw[:, ch, 1:2], in1=y[:, :ntok], op0=ALU.mult, op1=ALU.add), reads=[g, cw, y], writes=[y])
                    p.op("dve", lambda e, g=g, y=y, ntok=ntok, ch=ch: e.scalar_tensor_tensor(out=y[:, :ntok], in0=g[:, 2:ntok + 2], scalar=cw[:, ch, 2:3], in1=y[:, :ntok], op0=ALU.mult, op1=ALU.add), reads=[g, cw, y], writes=[y])
                    yield
                    p.op("act", lambda e, y=y, ntok=ntok: e.activation(out=y[:, :ntok], in_=y[:, :ntok], func=AF.Silu), reads=[y], writes=[y])
                    src = y
                    if kind < 2:
                        p.op("act", lambda e, y=y, sq=sq, ntok=ntok: e.activation(out=sq[:, :ntok], in_=y[:, :ntok], func=AF.Square), reads=[y], writes=[sq])
                        yield
                        p.op("pe", lambda e, ps1=ps1, sq=sq, ntok=ntok: e.matmul(ps1[:, :ntok], ones, sq[:, :ntok], start=True, stop=True), reads=[sq, self.mask], writes=[ps1])
                        yield
                        self.rsqrt(ri, ri[:, :ntok], ps1, ps1[:, :ntok], 1e-6)
                        sc_ = (128 ** -0.5) if kind == 0 else 1.0
                        p.op("dve", lambda e, y=y, ri=ri, yn=yn, ntok=ntok, sc_=sc_: e.scalar_tensor_tensor(out=yn[:, :ntok], in0=y[:, :ntok], scalar=sc_, in1=ri[:, :ntok], op0=ALU.mult, op1=ALU.mult), reads=[y, ri], writes=[yn])
                        dst = qnT if kind == 0 else knT
                        p.dma(dst.t[hh * 128:(hh + 1) * 128, g0:g0 + ntok], yn[:, :ntok], reads=[yn], writes=[dst])
                        src = yn
                    if kind >= 1:
                        nq = ntok // 128
                        for q_ in range(nq):
                            p.op("pe", lambda e, pt1=pt1, src=src, q_=q_: e.matmul(pt1[:, q_, :], src[:, q_ * 128:(q_ + 1) * 128], ident, start=True, stop=True), reads=[src, self.mask], writes=[pt1])
                        yield
                        p.op("act", lambda e, tt=tt, pt1=pt1, nq=nq: e.copy(out=tt[:, :nq, :], in_=pt1[:, :nq, :]), reads=[pt1], writes=[tt])
                        dst = kn if kind == 1 else vn
                        p.dma(dst.t[g0:g0 + ntok, hh * 128:(hh + 1) * 128].rearrange("(q p) c -> p q c", p=128), tt[:, :nq, :], reads=[tt], writes=[dst])
        lockstep(iters(), 3)
    p.barrier()
B.stage_gdn_conv = _stage_gdn_conv


def _stage_gdn_scan(self, l, ctx_out):
    c, p = self.cfg, self.p
    T_, L, NT = c.T, c.L, c.NT
    LT = L // 128
    d = self.dram
    qnT, knT, kn, vn, ab, of, zs, yc = d["qnT"], d["knT"], d["kn"], d["vn"], d["ab"], d["of"], d["zs"], d["yc"]
    M = lambda i: self.mask[:, i, :]
    with ExitStack() as st:
        S = self.sb(st, "gsS", [128, 8, 128], F32)
        S_h = [S.sub(f"S{h}") for h in range(8)]
        Sb = self.sb(st, "gsSb", [128, 8, 128], BF16)
        Sb_h = [Sb.sub(f"Sb{h}") for h in range(8)]
        nal = self.sb(st, "gsnal", [128, 16], F32)
        dtb = self.sb(st, "gsdtb", [128, 16], F32)
        nw = self.sb(st, "gsnw", [128, 128], F32)
        p.dma(nal[:], d["dn_a_log"].t[l].partition_broadcast(128), reads=[d["dn_a_log"]], writes=[nal])
        p.dma(dtb[:], d["dn_dt_bias"].t[l].partition_broadcast(128), reads=[d["dn_dt_bias"]], writes=[dtb])
        p.dma(nw[:], d["dn_norm"].t[l].partition_broadcast(128), reads=[d["dn_norm"]], writes=[nw])
        p.op("act", lambda e: e.activation(out=nal[:], in_=nal[:], func=AF.Exp), reads=[nal], writes=[nal])
        p.op("dve", lambda e: e.tensor_scalar_mul(out=nal[:], in0=nal[:], scalar1=-1.0), reads=[nal], writes=[nal])
        banks = [self.ps(st, f"gsb{i}", [128, 4, 128], F32) for i in range(8)]
        slots = [(bnk, 0, bnk) for bnk in banks]
        sc = {"ps": 0}
        rings = {}

        def tmp(name, shape=(128, 128), n=6, dt=F32):
            if name == "X":
                n = 12
            if name not in rings:
                rings[name] = [[self.sb(st, f"gs_{name}{i}", list(shape), dt) for i in range(n)], 0]
            r = rings[name]
            t = r[0][r[1] % n]; r[1] += 1
            return t

        def mm(lhsT, lT, rhs, rT, ncols=128, acc=None):
            trk, j, bnk = slots[sc["ps"] % len(slots)]; sc["ps"] += 1
            ap = bnk[:, j, 0:ncols]
            p.op("pe", lambda e: e.matmul(ap, lhsT, rhs, start=True, stop=(acc is None)), reads=lT + rT, writes=[trk])
            if acc is not None:
                l2, l2T, r2, r2T = acc
                p.op("pe", lambda e: e.matmul(ap, l2, r2, start=False, stop=True), reads=l2T + r2T, writes=[trk])
            return trk, ap

        evi = {"k": 0}

        def evac(out_t, out_ap, trk, ap):
            k = evi["k"]; evi["k"] += 1
            if k % 2 == 0:
                p.op("act", lambda e: e.copy(out=out_ap, in_=ap), reads=[trk], writes=[out_t])
            else:
                p.op("dve", lambda e: e.tensor_copy(out=out_ap, in_=ap), reads=[trk], writes=[out_t])

        for dirn in range(2):
            cum = M_TRIUI if dirn == 0 else M_TRILI
            mL = M_TRILS if dirn == 0 else M_TRIUS
            mA = M_TRIUI if dirn == 0 else M_TRILI
            p.op("pool", lambda e: e.memset(S[:], 0.0), writes=[S] + S_h)
            p.op("pool", lambda e: e.memset(Sb[:], 0.0), writes=[Sb] + Sb_h)
            if dirn == 0:
                order = list(range(NT))
            else:
                order = list(range(LT - 1, -1, -1)) + list(range(NT - 1, LT - 1, -1))
            for i in order:
                want = (i >= LT) or ctx_out
                tk = slice(i * 128, (i + 1) * 128)
                abt = tmp("abt", (128, 32), 2)
                p.dma(abt[:], ab.t[tk, :], reads=[ab], writes=[abt])
                knt = tmp("knt", (128, 1024), 2); vnt = tmp("vnt", (128, 1024), 2)
                kTt = tmp("kTt", (128, 8, 128), 2); qTt = tmp("qTt", (128, 8, 128), 2)
                p.dma(knt[:], kn.t[tk, :], reads=[kn], writes=[knt])
                p.dma(vnt[:], vn.t[tk, :], reads=[vn], writes=[vnt])
                p.dma(kTt[:], knT.t[:, tk].rearrange("(h p) t -> p h t", p=128), reads=[knT], writes=[kTt])
                p.dma(qTt[:], qnT.t[:, tk].rearrange("(h p) t -> p h t", p=128), reads=[qnT], writes=[qTt])
                kTb = tmp("kTb", (128, 8, 128), 2, BF16); qTb = tmp("qTb", (128, 8, 128), 2, BF16)
                p.op("act", lambda e: e.copy(out=kTb[:], in_=kTt[:]), reads=[kTt], writes=[kTb])
                p.op("pool", lambda e: e.tensor_copy(out=qTb[:], in_=qTt[:]), reads=[qTt], writes=[qTb])
                gx = tmp("gx", (128, 8), 2); gax = tmp("gax", (128, 8), 2); g = tmp("g", (128, 8), 2); beta = tmp("beta", (128, 8), 2)
                ds = slice(dirn * 8, dirn * 8 + 8)
                p.op("dve", lambda e: e.tensor_tensor(out=gx[:], in0=abt[:, ds], in1=dtb[:, ds], op=ALU.add), reads=[abt, dtb], writes=[gx])
                p.op("act", lambda e: e.activation(out=gax[:], in_=gx[:], func=AF.Abs), reads=[gx], writes=[gax])
                p.op("act", lambda e: e.activation(out=gax[:], in_=gax[:], func=AF.Exp, scale=-1.0), reads=[gax], writes=[gax])
                p.op("act", lambda e: e.activation(out=gax[:], in_=gax[:], func=AF.Ln, bias=self.epsc[1.0], scale=1.0), reads=[gax, self.epst], writes=[gax])
                p.op("dve", lambda e: e.tensor_scalar_max(out=gx[:], in0=gx[:], scalar1=0.0), reads=[gx], writes=[gx])
                p.op("dve", lambda e: e.tensor_tensor(out=gx[:], in0=gx[:], in1=gax[:], op=ALU.add), reads=[gx, gax], writes=[gx])
                p.op("dve", lambda e: e.tensor_tensor(out=g[:], in0=gx[:], in1=nal[:, ds], op=ALU.mult), reads=[gx, nal], writes=[g])
                p.op("act", lambda e: e.activation(out=beta[:], in_=abt[:, 16 + dirn * 8:24 + dirn * 8], func=AF.Sigmoid), reads=[abt], writes=[beta])
                t1, a1 = mm(M(cum), [self.mask], g[:], [g], ncols=8)
                gcum = tmp("gcum", (128, 8), 2)
                evac(gcum, gcum[:], t1, a1)
                t2, a2 = mm(M(M_ONES), [self.mask], g[:], [g], ncols=8)
                gtot = tmp("gtot", (128, 8), 2)
                evac(gtot, gtot[:], t2, a2)
                eg = tmp("eg", (128, 8), 2); ekd = tmp("ekd", (128, 8), 2); egl = tmp("egl", (128, 8), 2); bk = tmp("bk", (128, 8), 2)
                p.op("act", lambda e: e.activation(out=eg[:], in_=gcum[:], func=AF.Exp), reads=[gcum], writes=[eg])
                p.op("dve", lambda e: e.tensor_tensor(out=ekd[:], in0=gtot[:], in1=gcum[:], op=ALU.subtract), reads=[gtot, gcum], writes=[ekd])
                p.op("act", lambda e: e.activation(out=ekd[:], in_=ekd[:], func=AF.Exp), reads=[ekd], writes=[ekd])
                p.op("act", lambda e: e.activation(out=egl[:], in_=gtot[:], func=AF.Exp), reads=[gtot], writes=[egl])
                p.op("dve", lambda e: e.tensor_tensor(out=bk[:], in0=beta[:], in1=eg[:], op=ALU.mult), reads=[beta, eg], writes=[bk])
                if want:
                    ot = tmp("ot", (128, 1024), 2)
                    if dirn == 1:
                        oft = tmp("oft", (128, 1024), 2); zt = tmp("zt", (128, 1024), 2)
                        p.dma(oft[:], of.t[tk, :], reads=[of], writes=[oft])
                        p.dma(zt[:], zs.t[tk, :], reads=[zs], writes=[zt])
                def unit(h):
                    hs = slice(h * 128, (h + 1) * 128)
                    hc = slice(h, h + 1)
                    kT = kTt[:, h, :]; qT = qTt[:, h, :]
                    Ug = tmp("Ug")
                    p.op("pool", lambda e: e.tensor_scalar(out=Ug[:], in0=M(cum), scalar1=g[:, hc], scalar2=None, op0=ALU.mult), reads=[self.mask, g], writes=[Ug])
                    yield
                    tD, aD = mm(M(M_ONES), [self.mask], Ug[:], [Ug])
                    DL = tmp("DL"); DU = tmp("DU")
                    p.op("dve", lambda e: e.tensor_scalar(out=DL[:], in0=aD, scalar1=gcum[:, hc], scalar2=0.0, op0=ALU.subtract, op1=ALU.max), reads=[tD, gcum], writes=[DL])
                    p.op("dve", lambda e: e.tensor_scalar(out=DU[:], in0=aD, scalar1=gcum[:, hc], scalar2=0.0, op0=ALU.subtract, op1=ALU.min), reads=[tD, gcum], writes=[DU])
                    p.op("act", lambda e: e.activation(out=DL[:], in_=DL[:], func=AF.Exp, scale=-1.0), reads=[DL], writes=[DL])
                    p.op("act", lambda e: e.activation(out=DU[:], in_=DU[:], func=AF.Exp), reads=[DU], writes=[DU])
                    p.op("pool", lambda e: e.tensor_tensor(out=DL[:], in0=DL[:], in1=M(mL), op=ALU.mult), reads=[DL, self.mask], writes=[DL])
                    p.op("pool", lambda e: e.tensor_tensor(out=DU[:], in0=DU[:], in1=M(mA), op=ALU.mult), reads=[DU, self.mask], writes=[DU])
                    yield
                    tG, aG = mm(kT, [kTt], kT, [kTt])
                    Lm = tmp("Lm")
                    p.op("dve", lambda e: e.scalar_tensor_tensor(out=Lm[:], in0=aG, scalar=beta[:, hc], in1=DL[:], op0=ALU.mult, op1=ALU.mult), reads=[tG, beta, DL], writes=[Lm])
                    if want:
                        tA, aA = mm(kTb[:, h, :], [kTb], qTb[:, h, :], [qTb])
                        AT = tmp("ATb", dt=BF16)
                        p.op("dve", lambda e: e.tensor_tensor(out=AT[:], in0=aA, in1=DU[:], op=ALU.mult), reads=[tA, DU], writes=[AT])
                    yield

                    def trn(src):
                        trk, j, bnk = slots[sc["ps"] % len(slots)]; sc["ps"] += 1
                        ap = bnk[:, j, 0:128]
                        p.op("pe", lambda e: e.transpose(ap, src[:], M(M_ID)), reads=[src, self.mask], writes=[trk])
                        return trk, ap
                    tN, aN = trn(Lm)
                    Nm = tmp("Nm")
                    evac(Nm, Nm[:], tN, aN)
                    L16 = tmp("L16"); N16 = tmp("N16")
                    p.op("pool", lambda e: e.tensor_tensor(out=L16[:], in0=Lm[:], in1=M(M_BD16), op=ALU.mult), reads=[Lm, self.mask], writes=[L16])
                    p.op("pool", lambda e: e.tensor_tensor(out=N16[:], in0=Nm[:], in1=M(M_BD16), op=ALU.mult), reads=[Nm, self.mask], writes=[N16])

                    def mmev(name, lh, lhT, rh, rhT):
                        t_, a_ = mm(lh[:], [lh], rh[:], [rh])
                        o_ = tmp(name)
                        evac(o_, o_[:], t_, a_)
                        return o_
                    yield
                    L2 = mmev("L2", N16, None, L16, None); N2 = mmev("N2", L16, None, N16, None)
                    yield
                    L4 = mmev("L4", N2, None, L2, None); N4 = mmev("N4", L2, None, N2, None)
                    yield
                    L8 = mmev("L8", N4, None, L4, None)
                    Q1 = tmp("P1")
                    p.op("pool", lambda e: e.tensor_tensor(out=Q1[:], in0=M(M_ID), in1=N16[:], op=ALU.subtract), reads=[N16, self.mask], writes=[Q1])

                    def mmadd(name, base, lh, rh):
                        t_, a_ = mm(lh[:], [lh], rh[:], [rh])
                        o_ = tmp(name)
                        p.op("dve", lambda e: e.tensor_tensor(out=o_[:], in0=base[:], in1=a_, op=ALU.add), reads=[base, t_], writes=[o_])
                        return o_
                    yield
                    Q2 = mmadd("P2", Q1, L2, Q1)
                    yield
                    Q3 = mmadd("P3", Q2, L4, Q2)
                    yield
                    Y = mmadd("X", Q3, L8, Q3)
                    for lvl in (M_OFF32, M_OFF64, M_OFF128):
                        Loff = tmp("Noff")
                        p.op("pool", lambda e, Loff=Loff, lvl=lvl: e.tensor_tensor(out=Loff[:], in0=Lm[:], in1=M(lvl), op=ALU.mult), reads=[Lm, self.mask], writes=[Loff])
                        yield
                        Xt_, Xa_ = trn(Y)
                        Xs = tmp("Y"); evac(Xs, Xs[:], Xt_, Xa_)
                        T2t, T2a = mm(Loff[:], [Loff], Y[:], [Y])
                        T2 = tmp("T2"); evac(T2, T2[:], T2t, T2a)
                        yield
                        Zt, Za = mm(Xs[:], [Xs], T2[:], [T2])
                        Yn = tmp("X")
                        p.op("dve", lambda e, Yn=Yn, Y=Y, Za=Za: e.tensor_tensor(out=Yn[:], in0=Y[:], in1=Za, op=ALU.subtract), reads=[Y, Zt], writes=[Yn])
                        Y = Yn
                    yield
                    RU = tmp("RUb", dt=BF16); RW = tmp("RWb", dt=BF16); KD = tmp("KDb", dt=BF16)
                    Yb = tmp("Yb", dt=BF16)
                    p.op("act", lambda e: e.copy(out=Yb[:], in_=Y[:]), reads=[Y], writes=[Yb])
                    p.op("pool", lambda e: e.tensor_scalar(out=RU[:], in0=vnt[:, hs], scalar1=beta[:, hc], scalar2=None, op0=ALU.mult), reads=[vnt, beta], writes=[RU])
                    p.op("pool", lambda e: e.tensor_scalar(out=RW[:], in0=knt[:, hs], scalar1=bk[:, hc], scalar2=None, op0=ALU.mult), reads=[knt, bk], writes=[RW])
                    p.op("pool", lambda e: e.tensor_scalar(out=KD[:], in0=knt[:, hs], scalar1=ekd[:, hc], scalar2=None, op0=ALU.mult), reads=[knt, ekd], writes=[KD])
                    yield
                    ut, ua = mm(Yb[:], [Yb], RU[:], [RU])
                    U_ = tmp("U"); evac(U_, U_[:], ut, ua)
                    wt_, wa_ = mm(RW[:], [RW], Yb[:], [Yb])
                    WT = tmp("WTb", dt=BF16); evac(WT, WT[:], wt_, wa_)
                    Sh = S_h[h]
                    yield
                    Sbh = Sb_h[h]
                    wst, wsa = mm(WT[:], [WT], Sb[:, h, :], [Sbh])
                    VN = tmp("VNb", dt=BF16)
                    p.op("dve", lambda e: e.tensor_tensor(out=VN[:], in0=U_[:], in1=wsa, op=ALU.subtract), reads=[U_, wst], writes=[VN])
                    if want:
                        qst, qsa = mm(qTb[:, h, :], [qTb], Sb[:, h, :], [Sbh])
                        avt, ava = mm(AT[:], [AT], VN[:], [VN])
                        QS = tmp("QS")
                        p.op("dve", lambda e: e.tensor_scalar(out=QS[:], in0=qsa, scalar1=eg[:, hc], scalar2=None, op0=ALU.mult), reads=[qst, eg], writes=[QS])
                        if dirn == 0:
                            p.op("dve", lambda e: e.tensor_tensor(out=ot[:, hs], in0=QS[:], in1=ava, op=ALU.add), reads=[QS, avt], writes=[ot])
                        else:
                            p.op("dve", lambda e: e.tensor_tensor(out=QS[:], in0=QS[:], in1=ava, op=ALU.add), reads=[QS, avt], writes=[QS])
                            p.op("pool", lambda e: e.tensor_tensor(out=ot[:, hs], in0=QS[:], in1=oft[:, hs], op=ALU.add), reads=[QS, oft], writes=[ot])
                    yield
                    kvt, kva = mm(KD[:], [KD], VN[:], [VN])
                    p.op("dve", lambda e: e.scalar_tensor_tensor(out=S[:, h, :], in0=S[:, h, :], scalar=egl[:, hc], in1=kva, op0=ALU.mult, op1=ALU.add), reads=[Sh, egl, kvt], writes=[Sh])
                    p.op("act", lambda e: e.copy(out=Sb[:, h, :], in_=S[:, h, :]), reads=[Sh], writes=[Sbh])
                for hg in ((0, 1, 2, 3), (4, 5, 6, 7)):
                    gens = [unit(h) for h in hg]
                    while gens:
                        for g_ in list(gens):
                            try:
                                next(g_)
                            except StopIteration:
                                gens.remove(g_)
                if want:
                    if dirn == 0:
                        p.dma(of.t[tk, :], ot[:], reads=[ot], writes=[of])
                    else:
                        sq = tmp("osq", (128, 128), 2); ssq = tmp("ossq", (128, 8), 2)
                        yct = tmp("yct", (128, 1024), 2, BF16)
                        for h in range(8):
                            hs = slice(h * 128, (h + 1) * 128)
                            p.op("act", lambda e, hs=hs, h=h: e.activation(out=sq[:], in_=ot[:, hs], func=AF.Square, accum_out=ssq[:, h:h + 1]), reads=[ot], writes=[sq, ssq])
                        self.rsqrt(ssq, ssq[:], ssq, ssq[:], 1e-6, scale=1.0 / 128)
                        for h in range(8):
                            hs = slice(h * 128, (h + 1) * 128)
                            p.op("dve", lambda e, hs=hs, h=h: e.scalar_tensor_tensor(out=ot[:, hs], in0=ot[:, hs], scalar=ssq[:, h:h + 1], in1=nw[:], op0=ALU.mult, op1=ALU.mult), reads=[ot, ssq, nw], writes=[ot])
                        p.op("pool", lambda e: e.tensor_tensor(out=yct[:], in0=ot[:], in1=zt[:], op=ALU.mult), reads=[ot, zt], writes=[yct])
                        p.dma(yc.t[tk, :], yct[:], reads=[yct], writes=[yc])
    p.barrier()
B.stage_gdn_scan = _stage_gdn_scan


def _stage_merge(self, l):
    c, p = self.cfg, self.p
    D, T_ = c.D, c.T
    d = self.dram
    ys = [d["yaT"], d["ybT"], d["ycT"]]
    ws = [d["w_branch_a"], d["w_branch_b"], d["w_branch_c"]]
    gT, mT = d["gtsT"], d["mT"]
    NB = D
    with ExitStack() as st:
        wb = [[self.sb(st, f"mgw{b_}{i}", [128, 8, NB], BF16) for i in range(1)] for b_ in range(3)]
        ab = [[self.sb(st, f"mga{b_}{i}", [128, 8, 512], BF16) for i in range(2)] for b_ in range(3)]
        gt = [self.sb(st, f"mgg{i}", [128, 3, 512], BF16) for i in range(3)]
        t1 = [self.sb(st, f"mgt1{i}", [128, 512], F32) for i in range(2)]
        t2 = [self.sb(st, f"mgt2{i}", [128, 512], F32) for i in range(2)]
        mo = [self.sb(st, f"mgo{i}", [128, 512], BF16) for i in range(2)]
        banks = [self.ps(st, f"mgps{i}", [128, 512], F32) for i in range(6)]
        wi = ai = bi = k = 0
        for nb0 in range(0, D, NB):
            for b_ in range(3):
                p.dma(wb[b_][0][:], ws[b_].t[l].rearrange("(kc p) n -> p kc n", p=128)[:, :, nb0:nb0 + NB], reads=[ws[b_]], writes=[wb[b_][0]], eng="pool")
            W3 = [wb[b_][0] for b_ in range(3)]; wi += 1
            for g0 in range(0, T_, 512):
                ntok = min(512, T_ - g0)
                A3 = [ab[b_][ai % 2] for b_ in range(3)]; ai += 1
                for b_ in range(3):
                    p.dma(A3[b_][:, :, :ntok], ys[b_].t.rearrange("(kc p) t -> p kc t", p=128)[:, :, g0:g0 + ntok], reads=[ys[b_]], writes=[A3[b_]])
                for cc in range(NB // 128):
                    r0 = nb0 + cc * 128
                    G_ = gt[k % 3]; T1 = t1[k % 2]; T2 = t2[k % 2]; MO = mo[k % 2]; k += 1
                    for b_ in range(3):
                        p.dma(G_[:, b_, :ntok], gT.t[b_ * D + r0:b_ * D + r0 + 128, g0:g0 + ntok], reads=[gT], writes=[G_])
                    P3 = []
                    for b_ in range(3):
                        ps = banks[bi % 6]; bi += 1
                        for kc in range(8):
                            p.op("pe", lambda e, ps=ps, b_=b_, kc=kc: e.matmul(ps[:, :ntok], W3[b_][:, kc, cc * 128:(cc + 1) * 128], A3[b_][:, kc, :ntok],
                                                                              start=(kc == 0), stop=(kc == 7)), reads=[W3[b_], A3[b_]], writes=[ps])
                        P3.append(ps)
                    p.op("dve", lambda e: e.tensor_tensor(out=T1[:, :ntok], in0=P3[0][:, :ntok], in1=G_[:, 0, :ntok], op=ALU.mult), reads=[P3[0], G_], writes=[T1])
                    p.op("dve", lambda e: e.tensor_tensor(out=T2[:, :ntok], in0=P3[1][:, :ntok], in1=G_[:, 1, :ntok], op=ALU.mult), reads=[P3[1], G_], writes=[T2])
                    p.op("pool", lambda e: e.tensor_tensor(out=T1[:, :ntok], in0=T1[:, :ntok], in1=T2[:, :ntok], op=ALU.add), reads=[T1, T2], writes=[T1])
                    p.op("dve", lambda e: e.tensor_tensor(out=T2[:, :ntok], in0=P3[2][:, :ntok], in1=G_[:, 2, :ntok], op=ALU.mult), reads=[P3[2], G_], writes=[T2])
                    p.op("pool", lambda e: e.tensor_tensor(out=MO[:, :ntok], in0=T1[:, :ntok], in1=T2[:, :ntok], op=ALU.add), reads=[T1, T2], writes=[MO])
                    p.dma(mT.t[r0:r0 + 128, g0:g0 + ntok], MO[:, :ntok], reads=[MO], writes=[mT])
    p.barrier()
B.stage_merge = _stage_merge


def _stage_tm_proj(self, l, actname, K, wname, NB=512, TG=512):
    p = self.p
    d = self.dram
    osub = d["osub"]
    state = {"k": 0}

    def init(st):
        state["o"] = [self.sb(st, f"tpo{i}", [128, 512], F32) for i in range(3)]

    def epi(ps, tok, nb0, nb):
        o = state["o"][state["k"] % 3]; state["k"] += 1
        p.op("act", lambda e: e.copy(out=o[:, :nb], in_=ps[:, :nb]), reads=[ps], writes=[o])
        p.dma(osub.t[tok:tok + 128, nb0:nb0 + nb], o[:, :nb], reads=[o], writes=[osub])
    W = d[wname]
    secs = [dict(c0=0, n=self.cfg.D, mode="TM", epi=epi, init=init)]
    self.proj(d[actname], K, W, W.t[l], secs, 0, self.cfg.T, NB=NB, TG=TG, tag="tp")
B.stage_tm_proj = _stage_tm_proj


def _stage_ffn_up(self, l):
    c, p = self.cfg, self.p
    D, T_, L, DFF, KC = c.D, c.T, c.L, c.DFF, c.KC
    d = self.dram
    hT, gT, W = d["hT"], d["gT"], d["w_up"]
    NCH = 2 * DFF // 128
    HC = DFF // 128
    CB = 4 if HC % 4 == 0 else (2 if HC % 2 == 0 else 1)
    TG = 510
    with ExitStack() as st:
        cw = self.sb(st, "fucw", [128, NCH, 3], F32)
        cb = self.sb(st, "fucb", [128, NCH], F32)
        for k_ in range(3):
            p.dma(cw[:, :, k_], d["ffn_conv_w"].t[l, k_].rearrange("(ch p) -> p ch", p=128), reads=[d["ffn_conv_w"]], writes=[cw], allow_slow_non_contiguous=True)
        p.dma(cb[:], d["ffn_conv_b"].t[l].rearrange("(ch p) -> p ch", p=128), reads=[d["ffn_conv_b"]], writes=[cb], allow_slow_non_contiguous=True)
        wA = [self.sb(st, f"fuwa{i}", [128, KC, CB * 128], BF16) for i in range(2)]
        wB = [self.sb(st, f"fuwb{i}", [128, KC, CB * 128], BF16) for i in range(2)]
        ab = [self.sb(st, f"fua{i}", [128, KC, 512], BF16) for i in range(2)]
        ua = [self.sb(st, f"fuua{i}", [128, 512], F32) for i in range(2)]
        ub = [self.sb(st, f"fuub{i}", [128, 512], F32) for i in range(2)]
        go = [self.sb(st, f"fugo{i}", [128, 512], BF16) for i in range(2)]
        banks = [self.ps(st, f"fups{i}", [128, 512], F32) for i in range(6)]
        wv = W.t[l].rearrange("(kc p) n -> p kc n", p=128)
        av = hT.t.rearrange("(kc p) t -> p kc t", p=128)
        wi = ai = bi = k = 0
        for cb0 in range(0, HC, CB):
            WA, WB = wA[wi % 2], wB[wi % 2]; wi += 1
            p.dma(WA[:], wv[:, :, cb0 * 128:(cb0 + CB) * 128], reads=[W], writes=[WA], eng="pool")
            p.dma(WB[:], wv[:, :, DFF + cb0 * 128:DFF + (cb0 + CB) * 128], reads=[W], writes=[WB], eng="pool")
            for (s0, s1) in ((0, L), (L, T_)):
                for g0 in range(s0, s1, TG):
                    n = min(TG, s1 - g0)
                    A = ab[ai % 2]; ai += 1
                    lo = max(s0, g0 - 1); hi = min(s1, g0 + n + 1)
                    if lo > g0 - 1:
                        p.op("pool", lambda e: e.memset(A[:, :, 0:1], 0.0), writes=[A])
                    if hi < g0 + n + 1:
                        p.op("pool", lambda e: e.memset(A[:, :, n + 1:n + 2], 0.0), writes=[A])
                    p.dma(A[:, :, lo - (g0 - 1):hi - (g0 - 1)], av[:, :, lo:hi], reads=[hT], writes=[A])
                    for cc in range(CB):
                        cha = cb0 + cc; chb = HC + cb0 + cc
                        pa = banks[bi % 6]; bi += 1
                        pb = banks[bi % 6]; bi += 1
                        for kc in range(KC):
                            p.op("pe", lambda e, kc=kc: e.matmul(pa[:, :n + 2], WA[:, kc, cc * 128:(cc + 1) * 128], A[:, kc, :n + 2], start=(kc == 0), stop=(kc == KC - 1)), reads=[WA, A], writes=[pa])
                        for kc in range(KC):
                            p.op("pe", lambda e, kc=kc: e.matmul(pb[:, :n + 2], WB[:, kc, cc * 128:(cc + 1) * 128], A[:, kc, :n + 2], start=(kc == 0), stop=(kc == KC - 1)), reads=[WB, A], writes=[pb])
                        UA, UB, GO = ua[k % 2], ub[k % 2], go[k % 2]; k += 1
                        for (U, ps, ch) in ((UA, pa, cha), (UB, pb, chb)):
                            p.op("dve", lambda e, U=U, ps=ps, ch=ch: e.tensor_scalar(out=U[:, :n], in0=ps[:, 0:n], scalar1=cw[:, ch, 0:1], scalar2=cb[:, ch:ch + 1], op0=ALU.mult, op1=ALU.add), reads=[ps, cw, cb], writes=[U])
                            p.op("dve", lambda e, U=U, ps=ps, ch=ch: e.scalar_tensor_tensor(out=U[:, :n], in0=ps[:, 1:n + 1], scalar=cw[:, ch, 1:2], in1=U[:, :n], op0=ALU.mult, op1=ALU.add), reads=[ps, cw, U], writes=[U])
                            p.op("dve", lambda e, U=U, ps=ps, ch=ch: e.scalar_tensor_tensor(out=U[:, :n], in0=ps[:, 2:n + 2], scalar=cw[:, ch, 2:3], in1=U[:, :n], op0=ALU.mult, op1=ALU.add), reads=[ps, cw, U], writes=[U])
                        p.op("act", lambda e: e.activation(out=UA[:, :n], in_=UA[:, :n], func=AF.Silu), reads=[UA], writes=[UA])
                        p.op("pool", lambda e: e.tensor_tensor(out=GO[:, :n], in0=UA[:, :n], in1=UB[:, :n], op=ALU.mult), reads=[UA, UB], writes=[GO])
                        p.dma(gT.t[cha * 128:(cha + 1) * 128, g0:g0 + n], GO[:, :n], reads=[GO], writes=[gT])
    p.barrier()
B.stage_ffn_up = _stage_ffn_up


def _build_all(self, st, upto=None):
    c = self.cfg
    self.declare()
    self.consts(st)
    self.stage_mod()
    self.stage_ln(0, None, (0, 0), src_inputs=True)
    for l in range(c.NL):
        last = (l == c.NL - 1)
        ctx_out = not last
        self.stage_win(l)
        self.stage_win_attn(l, ctx_out)
        self.stage_diff(l, ctx_out)
        self.stage_gdn_conv(l)
        self.stage_gdn_scan(l, ctx_out)
        d = self.dram
        self.stage_tm2fm(d["ya"], d["yaT"], 1024)
        self.stage_tm2fm(d["yb"], d["ybT"], 1024)
        self.stage_tm2fm(d["yc"], d["ycT"], 1024)
        self.stage_merge(l)
        self.stage_tm_proj(l, "mT", c.D, "w_o", NB=min(1024, c.D))
        if upto == "mix" and l == 0:
            break
        self.stage_ln(l, (2, "ln1_g", "ln1_b"), (l, 3), src_inputs=(l == 0))
        self.stage_ffn_up(l)
        kdown = c.DFF
        big = (kdown // 128) > 16
        self.stage_tm_proj(l, "gT", kdown, "w_down", NB=512, TG=256 if big else 512)
        if last:
            self.stage_ln(l, (5, "ln2_g", "ln2_b"), None, final=True)
        else:
            self.stage_ln(l, (5, "ln2_g", "ln2_b"), (l + 1, 0))
    self.p.emit(st)
B.build_all = _build_all


_CACHE = {}


def _get_nc(cfg_key):
    if cfg_key not in _CACHE:
        cfg = Cfg(*cfg_key)
        b = B(cfg)
        st = ExitStack()
        b.build_all(st)
        _CACHE[cfg_key] = (b, st, cfg)
    return _CACHE[cfg_key]


def kernel(**inputs):
    x = np.asarray(inputs["x"])
    bsz, N, D = x.shape
    L = inputs["ctx"].shape[1]
    DFF = inputs["w_down"].shape[1]
    NL = inputs["w_mod"].shape[0]
    b, st, cfg = _get_nc((D, N, L, DFF, NL))
    consts = make_consts(cfg)
    shared = {}
    for k, v in inputs.items():
        if k in ("x", "c", "ctx", "c_ctx"):
            continue
        shared[k] = np.ascontiguousarray(np.asarray(v), dtype=np.float32)
    shared["dn_a_log"] = shared["dn_a_log"].reshape(NL, 16)
    shared["dn_dt_bias"] = shared["dn_dt_bias"].reshape(NL, 16)
    shared["c_ctx"] = np.ascontiguousarray(np.asarray(inputs["c_ctx"]), dtype=np.float32)
    shared.update(consts)
    n_cores = 8 if bsz <= 4 else bsz
    hot = [0, 1, 4, 5][:bsz] if bsz <= 4 else list(range(bsz))
    zx = np.zeros_like(np.ascontiguousarray(x[0], dtype=np.float32))
    zc = np.zeros((D,), np.float32)
    zctx = np.zeros((L, D), np.float32)
    in_maps = []
    for core in range(n_cores):
        m = dict(shared)
        if core in hot:
            i = hot.index(core)
            m["x"] = np.ascontiguousarray(x[i], dtype=np.float32)
            m["c"] = np.ascontiguousarray(np.asarray(inputs["c"])[i], dtype=np.float32)
            m["ctx"] = np.ascontiguousarray(np.asarray(inputs["ctx"])[i], dtype=np.float32)
        else:
            m["x"], m["c"], m["ctx"] = zx, zc, zctx
        in_maps.append(m)
    res = run_bass_kernel_spmd(b.nc, in_maps, core_ids=list(range(n_cores)))
    return np.stack([np.asarray(res.results[core]["y"], dtype=np.float32) for core in hot], axis=0)
```

```python
import numpy as np
from contextlib import ExitStack
import concourse.bass as bass
import concourse.mybir as mybir
from concourse.bass_utils import run_bass_kernel_spmd

F32 = mybir.dt.float32
BF16 = mybir.dt.bfloat16
AF = mybir.ActivationFunctionType
ALU = mybir.AluOpType
AX = mybir.AxisListType

ENGS = ("pe", "act", "dve", "pool", "sp")
EPOCH = 30000
NDMASEM = 40


import types


def freeze(fn):
    if fn is None or fn.__closure__ is None:
        return fn
    cells = tuple(types.CellType(c.cell_contents) for c in fn.__closure__)
    return types.FunctionType(fn.__code__, fn.__globals__, fn.__name__, fn.__defaults__, cells)


class T:
    __slots__ = ("t", "name", "w", "r", "rd")

    def __init__(self, t, name=""):
        self.t = t
        self.name = name
        self.w = None
        self.r = {}
        self.rd = {}

    def sub(self, name=""):
        return T(self.t, name or self.name)

    def __getitem__(self, idx):
        return self.t[idx]


class Op:
    __slots__ = ("eng", "seq", "fn", "deps", "signal", "isdma", "sem", "val", "dsem", "dval", "dprev", "dslot")

    def __init__(self, eng, seq, fn, isdma):
        self.eng = eng
        self.seq = seq
        self.fn = fn
        self.deps = []
        self.signal = False
        self.isdma = isdma
        self.sem = None
        self.val = 0
        self.dsem = None
        self.dval = 0
        self.dprev = 0


class Prog:
    def __init__(self, nc, same_raw=True):
        self.nc = nc
        self.ops = {e: [] for e in ENGS}
        self.seen = {e: {} for e in ENGS}
        self.seen_dma = {e: set() for e in ENGS}
        self.pending = {e: [] for e in ENGS}
        self.ndma = 0
        self.same_raw = same_raw
        self.out_dmas = []
        self.all_dmas_unwaited = []
        self.dcount = {e: 0 for e in ENGS}
        self.lastd = {}

    def _need(self, op, tgt):
        if tgt is None or tgt is op:
            return
        e = op.eng
        if tgt.isdma:
            if id(tgt) in self.seen_dma[e]:
                return
            self.seen_dma[e].add(id(tgt))
            op.deps.append(tgt)
            return
        if tgt.eng == e:
            if e == "pe" or not self.same_raw:
                return
        if self.seen[e].get(tgt.eng, 0) >= tgt.seq:
            return
        self.seen[e][tgt.eng] = tgt.seq
        tgt.signal = True
        op.deps.append(tgt)

    def _record(self, eng, fn, reads, writes, isdma=False, raw_only_same=True):
        ops = self.ops[eng]
        op = Op(eng, len(ops) + 1, fn, isdma)
        if isdma:
            op.dslot = self.dcount[eng] % NDMASEM
            self.dcount[eng] += 1
            self.lastd[(eng, op.dslot)] = op
        for t in self.pending[eng]:
            self._need(op, t)
        self.pending[eng] = []
        for b in reads:
            self._need(op, b.w)
        for b in writes:
            self._need(op, b.w)
            for re_, r in b.r.items():
                if re_ == eng and not isdma:
                    continue
                self._need(op, r)
            for r in b.rd.values():
                self._need(op, r)
        for b in reads:
            if isdma:
                b.rd[(eng, op.dslot)] = op
            else:
                b.r[eng] = op
        for b in writes:
            b.w = op
            b.r = {}
            b.rd = {}
        ops.append(op)
        return op

    def op(self, eng, fn, reads=(), writes=()):
        return self._record(eng, freeze(fn), reads, writes)

    def dma(self, out_ap, in_ap, reads=(), writes=(), eng="sp", is_out=False, **kw):
        def fn(e, out_ap=out_ap, in_ap=in_ap, kw=kw):
            return e.dma_start(out=out_ap, in_=in_ap, **kw)
        op = self._record(eng, fn, reads, writes, isdma=True)
        op.signal = True
        self.ndma += 1
        if is_out:
            self.out_dmas.append(op)
        return op

    def barrier(self, label=None):
        import sys as _sys
        if not hasattr(self, "marks"):
            self.marks = []
        self.marks.append((label or _sys._getframe(1).f_code.co_name, {e: len(self.ops[e]) for e in ENGS}))
        lasts = []
        for e in ENGS:
            for o in reversed(self.ops[e]):
                if not o.isdma:
                    lasts.append(o)
                    break
        dmas = list(self.lastd.values())
        for e in ENGS:
            self.pending[e] = self.pending[e] + lasts + dmas

    def emit(self, stack):
        nc = self.nc
        self.barrier()
        fin = {}
        for e in ENGS:
            op = Op(e, len(self.ops[e]) + 1, None, False)
            for t in self.pending[e]:
                self._need(op, t)
            self.ops[e].append(op)
        for e in ENGS:
            cnt = 0
            sems = []
            for op in self.ops[e]:
                if op.isdma or not op.signal:
                    continue
                ep = cnt // EPOCH
                while len(sems) <= ep:
                    sems.append(stack.enter_context(nc.semaphore(f"c_{e}_{len(sems)}")))
                cnt += 1
                op.sem = sems[ep]
                op.val = cnt - ep * EPOCH
        dpool = {}
        duse = {}
        for e in ENGS:
            k = 0
            for op in self.ops[e]:
                if not op.isdma:
                    continue
                if e not in dpool:
                    dpool[e] = [stack.enter_context(nc.semaphore(f"d_{e}_{i}")) for i in range(NDMASEM)]
                    duse[e] = [0] * NDMASEM
                i = op.dslot
                k += 1
                op.dsem = dpool[e][i]
                op.dprev = 16 * duse[e][i]
                duse[e][i] += 1
                op.dval = 16 * duse[e][i]
        block = stack.enter_context(nc.Block())

        def run(eng, e):
            for op in self.ops[e]:
                for d in op.deps:
                    if d.isdma:
                        eng.wait_ge(d.dsem, d.dval)
                    else:
                        eng.wait_ge(d.sem, d.val)
                if op.fn is None:
                    continue
                if op.isdma:
                    if op.dprev:
                        eng.wait_ge(op.dsem, op.dprev)
                    op.fn(eng).then_inc(op.dsem, 16)
                else:
                    ins = op.fn(eng)
                    if op.signal:
                        ins.then_inc(op.sem, 1)

        block.tensor(lambda eng: run(eng, "pe"))
        block.scalar(lambda eng: run(eng, "act"))
        block.vector(lambda eng: run(eng, "dve"))
        block.gpsimd(lambda eng: run(eng, "pool"))
        block.sync(lambda eng: run(eng, "sp"))

    def stats(self):
        return {e: len(self.ops[e]) for e in ENGS}

import math

class Cfg:
    def __init__(self, D=2048, N=8192, L=256, DFF=5632, NL=2):
        self.D, self.N, self.L, self.DFF, self.NL = D, N, L, DFF, NL
        self.T = N + L
        self.KC = D // 128
        self.NT = self.T // 128
        self.IN_W = 1024 + 256 + 256 + 1024 + 1024 + 1024 + 3072 + 1024 + 16 + 16 + 3 * D

GRID_W = 64
ALPHA = None
LN_EPS = 1e-6

O_QA, O_KA, O_VA, O_QD, O_KD, O_VD, O_QKV, O_Z, O_A, O_B, O_G = (
    0, 1024, 1280, 1536, 2560, 3584, 4608, 7680, 8704, 8720, 8736)

M_ID, M_TRILS, M_TRIUI, M_TRIUS, M_TRILI, M_ONES, M_BD16, M_OFF32, M_OFF64, M_OFF128, M_NEG, M_PA, M_PD = range(13)
NMASK = 13


def rope_tables(n, dim):
    rows = n // GRID_W
    row = np.repeat(np.arange(rows, dtype=np.float32), GRID_W)
    col = np.tile(np.arange(GRID_W, dtype=np.float32), rows)
    axis_dim = dim // 2
    inv_freq = (10000.0 ** (-np.arange(0, axis_dim, 2, dtype=np.float32) / axis_dim)).astype(np.float32)
    ang_r = row[:, None] * inv_freq[None]
    ang_c = col[:, None] * inv_freq[None]
    ang = np.concatenate([ang_r, ang_r, ang_c, ang_c], axis=-1).astype(np.float32)
    q = dim // 4
    sgn = np.concatenate([-np.ones(q), np.ones(q), -np.ones(q), np.ones(q)]).astype(np.float32)
    return np.cos(ang).T.astype(np.float32), (np.sin(ang) * sgn[None]).T.astype(np.float32)


def make_consts(cfg):
    T, L = cfg.T, cfg.L
    def tab(dim, rep, scale):
        c, s = rope_tables(cfg.N, dim)
        out = np.zeros((4, 128, T), np.float32)
        c = np.tile(c, (rep, 1)); s = np.tile(s, (rep, 1))
        out[0, :, :L] = scale; out[0, :, L:] = c * scale
        out[1, :, L:] = s * scale
        out[2, :, :L] = 1.0; out[2, :, L:] = c
        out[3, :, L:] = s
        return out
    ropeA = tab(128, 1, 128 ** -0.5)
    ropeD = tab(64, 2, 64 ** -0.5)
    p = np.arange(128)[:, None]; f = np.arange(128)[None, :]
    m = np.zeros((NMASK, 128, 128), np.float32)
    m[M_ID] = (p == f); m[M_TRILS] = (p > f); m[M_TRIUI] = (p <= f); m[M_TRIUS] = (p < f); m[M_TRILI] = (p >= f)
    m[M_ONES] = 1.0
    m[M_NEG] = -1.0
    m[M_PA] = (p == (f ^ 32))
    m[M_PD] = (p == (f ^ 16))
    m[M_BD16] = (p // 16 == f // 16)
    m[M_OFF32] = (p // 32 == f // 32) & (p // 16 != f // 16)
    m[M_OFF64] = (p // 64 == f // 64) & (p // 32 != f // 32)
    m[M_OFF128] = (p // 64 != f // 64)
    return {"ropeA": ropeA, "ropeD": ropeD, "cmask": m}


def lockstep(gens, width):
    active = []
    it = iter(gens)
    done = False
    while True:
        while len(active) < width and not done:
            try:
                active.append(next(it))
            except StopIteration:
                done = True
        if not active:
            break
        for g in list(active):
            try:
                next(g)
            except StopIteration:
                active.remove(g)


class B:
    def __init__(self, cfg, dbg=()):
        self.cfg = cfg
        self.dbg = set(dbg)
        self.nc = bass.Bass("TRN2", target_bir_lowering=False)
        self.p = Prog(self.nc)
        self.dram = {}
        self.outs = []
        self.bank = 0

    def inp(self, name, shape, dt=F32):
        t = T(self.nc.dram_tensor(name, list(shape), dt, kind="ExternalInput").ap(), name)
        self.dram[name] = t
        return t

    def out(self, name, shape, dt=F32):
        t = T(self.nc.dram_tensor(name, list(shape), dt, kind="ExternalOutput").ap(), name)
        self.dram[name] = t
        self.outs.append(name)
        return t

    def scr(self, name, shape, dt):
        kind = "ExternalOutput" if name in self.dbg else "Internal"
        t = T(self.nc.dram_tensor(name, list(shape), dt, kind=kind).ap(), name)
        self.dram[name] = t
        if name in self.dbg:
            self.outs.append(name)
        return t

    def sb(self, st, name, shape, dt=F32):
        self.uid = getattr(self, "uid", 0) + 1
        name = f"{name}_{self.uid}"
        return T(st.enter_context(self.nc.sbuf_tensor(name, list(shape), dt)), name)

    def ps(self, st, name, shape, dt=F32):
        self.uid = getattr(self, "uid", 0) + 1
        name = f"{name}_{self.uid}"
        return T(st.enter_context(self.nc.psum_tensor(name, list(shape), dt)), name)

    def rsqrt(self, OUT, out_ap, IN, in_ap, eps, scale=1.0):
        p = self.p
        p.op("act", lambda e: e.activation(out=out_ap, in_=in_ap, func=AF.Sqrt, bias=self.epsc[eps][:out_ap.shape[0], :], scale=scale), reads=[IN, self.epst], writes=[OUT])
        p.op("dve", lambda e: e.reciprocal(out=out_ap, in_=out_ap), reads=[OUT], writes=[OUT])

    def declare(self):
        c = self.cfg
        D, T_, L, N, DFF, NL = c.D, c.T, c.L, c.N, c.DFF, c.NL
        i = self.inp
        i("x", [N, D]); i("c", [D]); i("ctx", [L, D]); i("c_ctx", [D])
        i("w_mod", [NL, D, 6 * D]); i("b_mod", [NL, 6 * D]); i("w_in", [NL, D, c.IN_W])
        i("wa_sink", [NL, 8])
        for n in ("df_lam_q1", "df_lam_k1", "df_lam_q2", "df_lam_k2"):
            i(n, [NL, 64])
        i("df_subln", [NL, 128]); i("dn_conv", [NL, 3, 3072]); i("dn_a_log", [NL, 16]); i("dn_dt_bias", [NL, 16])
        i("dn_norm", [NL, 128])
        i("w_branch_a", [NL, 1024, D]); i("w_branch_b", [NL, 1024, D]); i("w_branch_c", [NL, 1024, D])
        i("w_o", [NL, D, D]); i("ln1_g", [NL, D]); i("ln1_b", [NL, D])
        i("w_up", [NL, D, 2 * DFF]); i("ffn_conv_w", [NL, 3, 2 * DFF]); i("ffn_conv_b", [NL, 2 * DFF])
        i("w_down", [NL, DFF, D]); i("ln2_g", [NL, D]); i("ln2_b", [NL, D])
        i("ropeA", [4, 128, T_]); i("ropeD", [4, 128, T_]); i("cmask", [NMASK, 128, 128])
        self.out("y", [N, D])
        s = self.scr
        s("xres0", [T_, D], F32); s("xres1", [T_, D], F32); s("hT", [D, T_], BF16); s("modd", [NL, 2, 6 * D], F32)
        s("qaT", [1024, T_], BF16); s("kaT", [256, T_], BF16); s("va", [T_, 256], BF16)
        s("qdT", [1024, T_], BF16); s("kdT", [1024, T_], BF16); s("vd", [T_, 1024], BF16)
        s("gpre", [3072, T_], F32); s("zs", [T_, 1024], F32); s("ab", [T_, 32], F32); s("gtsT", [3 * D, T_], BF16)
        s("osub", [T_, D], F32)
        s("ya", [T_, 1024], BF16); s("yb", [T_, 1024], BF16); s("yc", [T_, 1024], BF16)
        s("qnT", [1024, T_], F32); s("knT", [1024, T_], F32); s("kn", [T_, 1024], F32); s("vn", [T_, 1024], F32)
        s("of", [T_, 1024], F32); s("mT", [D, T_], BF16); s("gT", [DFF, T_], BF16)
        s("yaT", [1024, T_], BF16); s("ybT", [1024, T_], BF16); s("ycT", [1024, T_], BF16)

    def consts(self, st):
        p = self.p
        cm = self.dram["cmask"]
        self.mask = self.sb(st, "mask", [128, NMASK, 128], F32)
        p.dma(self.mask[:], cm.t.rearrange("m p f -> p m f"), reads=[cm], writes=[self.mask])
        self.identb = self.sb(st, "identb", [128, 128], BF16)
        p.op("dve", lambda e: e.tensor_copy(out=self.identb[:], in_=self.mask[:, M_ID, :]), reads=[self.mask], writes=[self.identb])
        self.permb = self.sb(st, "permb", [128, 2, 128], BF16)
        p.op("dve", lambda e: e.tensor_copy(out=self.permb[:, 0, :], in_=self.mask[:, M_PA, :]), reads=[self.mask], writes=[self.permb])
        p.op("dve", lambda e: e.tensor_copy(out=self.permb[:, 1, :], in_=self.mask[:, M_PD, :]), reads=[self.mask], writes=[self.permb])
        self.epst = self.sb(st, "epst", [128, 4], F32)
        self.epsc = {}
        for i_, v_ in enumerate((1e-6, 0.0, 1.0)):
            p.op("pool", lambda e, i_=i_, v_=v_: e.memset(self.epst[:, i_:i_ + 1], v_), writes=[self.epst])
            self.epsc[v_] = self.epst[:, i_:i_ + 1]
        c = self.cfg
        self.modc = self.sb(st, "modc", [128, c.NL, 2, 6 * c.KC], F32)

    def stage_mod(self):
        c, p, nc = self.cfg, self.p, self.nc
        KC, D = c.KC, c.D
        NJ = 6 * KC
        wm, bm = self.dram["w_mod"], self.dram["b_mod"]
        with ExitStack() as st:
            craw = self.sb(st, "craw", [128, KC, 2], F32)
            sc = self.sb(st, "sc", [128, KC, 2], F32)
            bcol = self.sb(st, "bcol", [128, c.NL, NJ], F32)
            ps = [self.ps(st, f"mps{i}", [128, 4, 2], F32) for i in range(2)]
            pst = self.ps(st, "mpst", [128, 128], F32)
            wt = [self.sb(st, f"wmt{i}", [128, KC, 512], F32) for i in range(2)]
            tr = self.sb(st, "mtr", [NJ, 128], F32)
            cc, cx = self.dram["c"], self.dram["c_ctx"]
            p.dma(craw[:, :, 0], cc.t.rearrange("(kc p) -> p kc", p=128), reads=[cc], writes=[craw], allow_slow_non_contiguous=True)
            p.dma(craw[:, :, 1], cx.t.rearrange("(kc p) -> p kc", p=128), reads=[cx], writes=[craw], allow_slow_non_contiguous=True)
            for l in range(c.NL):
                p.dma(bcol[:, l, :], bm.t[l].rearrange("(j p) -> p j", p=128), reads=[bm], writes=[bcol], allow_slow_non_contiguous=True)
            p.op("act", lambda e: e.activation(out=sc[:], in_=craw[:], func=AF.Silu), reads=[craw], writes=[sc])
            k = 0
            for l in range(c.NL):
                wv = wm.t[l].rearrange("(kc p) n -> p kc n", p=128)
                for j0 in range(0, NJ, 4):
                    w = wt[k % 2]; pp = ps[k % 2]; k += 1
                    p.dma(w[:], wv[:, :, j0 * 128:(j0 + 4) * 128], reads=[wm], writes=[w])
                    for jj in range(4):
                        for kc in range(KC):
                            p.op("pe", lambda e, w=w, pp=pp, jj=jj, kc=kc: e.matmul(
                                pp[:, jj, :], w[:, kc, jj * 128:(jj + 1) * 128], sc[:, kc, :],
                                start=(kc == 0), stop=(kc == KC - 1)), reads=[w, sc], writes=[pp])
                    for s_ in range(2):
                        p.op("dve", lambda e, pp=pp, s_=s_, l=l, j0=j0: e.tensor_tensor(
                            out=self.modc[:, l, s_, j0:j0 + 4], in0=pp[:, :, s_], in1=bcol[:, l, j0:j0 + 4], op=ALU.add),
                            reads=[pp, bcol], writes=[self.modc])
                for s_ in range(2):
                    for v in (1, 4):
                        p.op("dve", lambda e, l=l, s_=s_, v=v: e.tensor_scalar_add(
                            out=self.modc[:, l, s_, v * KC:(v + 1) * KC], in0=self.modc[:, l, s_, v * KC:(v + 1) * KC], scalar1=1.0),
                            reads=[self.modc], writes=[self.modc])
                md = self.dram["modd"]
                for s_ in range(2):
                    p.op("pe", lambda e, l=l, s_=s_: e.matmul(pst[0:NJ, :], self.modc[:, l, s_, :], self.mask[:, M_ID, :], start=True, stop=True),
                         reads=[self.modc, self.mask], writes=[pst])
                    p.op("dve", lambda e: e.tensor_copy(out=tr[:], in_=pst[0:NJ, :]), reads=[pst], writes=[tr])
                    p.dma(md.t[l, s_].rearrange("(j p) -> j p", p=128), tr[:], reads=[tr], writes=[md])
        p.barrier()

    def stage_ln(self, l, comb, ada, final=False, src_inputs=False, skip_ctx=False):
        c, p = self.cfg, self.p
        D, KC, L = c.D, c.KC, c.L
        alpha = (2 * c.NL) ** 0.25
        xcur = getattr(self, "xcur", 0)
        xres, xdst = self.dram[f"xres{xcur}"], self.dram[f"xres{1 - xcur}"]
        osub, hT, md = self.dram["osub"], self.dram["hT"], self.dram["modd"]
        if comb and not final:
            self.xcur = 1 - xcur
        nch = (D + 511) // 512
        with ExitStack() as st:
            xt = [self.sb(st, f"lxt{i}", [128, D], F32) for i in range(3)]
            if comb:
                ot = [self.sb(st, f"lot{i}", [128, D], F32) for i in range(3)]
                gate = self.sb(st, "lgate", [128, 2, D], F32)
                gB = self.sb(st, "lgB", [128, D], F32)
                bB = self.sb(st, "lbB", [128, D], F32)
                gv = comb[0]
                for s_ in range(2):
                    p.dma(gate[:, s_, :], md.t[l, s_, gv * D:(gv + 1) * D].partition_broadcast(128), reads=[md], writes=[gate])
                gd, bd = self.dram[comb[1]], self.dram[comb[2]]
                p.dma(gB[:], gd.t[l].partition_broadcast(128), reads=[gd], writes=[gB])
                p.dma(bB[:], bd.t[l].partition_broadcast(128), reads=[bd], writes=[bB])
            stt = [self.sb(st, f"lst{i}", [128, nch, 6], F32) for i in range(3)]
            mv = [self.sb(st, f"lmv{i}", [128, 2], F32) for i in range(3)]
            rs = [self.sb(st, f"lrs{i}", [128, 1], F32) for i in range(3)]
            if ada:
                xb = [self.sb(st, f"lxb{i}", [128, D], BF16) for i in range(3)]
                ht = [self.sb(st, f"lht{i}", [128, KC, 128], BF16) for i in range(3)]
                pt = [self.ps(st, f"lpt{i}", [128, KC, 128], BF16) for i in range(3)]
            def tile(i):
                isctx = i * 128 < L
                s_ = 1 if isctx else 0
                b = i % 3
                X = xt[b]
                if src_inputs:
                    src = self.dram["ctx"] if isctx else self.dram["x"]
                    r0 = i * 128 if isctx else i * 128 - L
                else:
                    src = xres; r0 = i * 128
                p.dma(X[:], src.t[r0:r0 + 128, :], reads=[src], writes=[X])

                def lnstats(X, b):
                    S, MV, R = stt[b], mv[b], rs[b]
                    for ch in range(nch):
                        p.op("dve", lambda e, ch=ch: e.bn_stats(out=S[:, ch, :], in_=X[:, ch * 512:min(D, (ch + 1) * 512)]), reads=[X], writes=[S])
                    p.op("dve", lambda e: e.bn_aggr(out=MV[:], in_=S[:].rearrange("p a b -> p (a b)")), reads=[S], writes=[MV])
                    self.rsqrt(R, R[:], MV, MV[:, 1:2], LN_EPS)
                    return MV, R
                if comb:
                    O = ot[b]
                    p.dma(O[:], osub.t[i * 128:(i + 1) * 128, :], reads=[osub], writes=[O])
                    yield
                    p.op("pool", lambda e, O=O, s_=s_: e.tensor_tensor(out=O[:], in0=O[:], in1=gate[:, s_, :], op=ALU.mult), reads=[O, gate], writes=[O])
                    yield
                    p.op("dve", lambda e, O=O, X=X: e.scalar_tensor_tensor(out=X[:], in0=X[:], scalar=alpha, in1=O[:], op0=ALU.mult, op1=ALU.add), reads=[X, O], writes=[X])
                    MV, R = lnstats(X, b)
                    p.op("dve", lambda e, X=X, MV=MV, R=R: e.tensor_scalar(out=X[:], in0=X[:], scalar1=MV[:, 0:1], scalar2=R[:, 0:1], op0=ALU.subtract, op1=ALU.mult), reads=[X, MV, R], writes=[X])
                    yield
                    p.op("pool", lambda e, X=X: e.tensor_tensor(out=X[:], in0=X[:], in1=gB[:], op=ALU.mult), reads=[X, gB], writes=[X])
                    p.op("pool", lambda e, X=X: e.tensor_tensor(out=X[:], in0=X[:], in1=bB[:], op=ALU.add), reads=[X, bB], writes=[X])
                    if final:
                        if not isctx:
                            y = self.dram["y"]
                            p.dma(y.t[i * 128 - L:(i + 1) * 128 - L, :], X[:], reads=[X], writes=[y], is_out=True)
                    else:
                        p.dma(xdst.t[i * 128:(i + 1) * 128, :], X[:], reads=[X], writes=[xdst])
                if ada:
                    la, sv = ada
                    yield
                    MV, R = lnstats(X, b)
                    XB, HT, PT = xb[b], ht[b], pt[b]
                    p.op("dve", lambda e, X=X, XB=XB, MV=MV, R=R: e.tensor_scalar(out=XB[:], in0=X[:], scalar1=MV[:, 0:1], scalar2=R[:, 0:1], op0=ALU.subtract, op1=ALU.mult), reads=[X, MV, R], writes=[XB])
                    yield
                    for kc in range(KC):
                        p.op("pe", lambda e, kc=kc, XB=XB, PT=PT: e.transpose(PT[:, kc, :], XB[:, kc * 128:(kc + 1) * 128], self.identb[:]), reads=[XB, self.identb], writes=[PT])
                    yield
                    for kc in range(KC):
                        p.op("act", lambda e, kc=kc, HT=HT, PT=PT, s_=s_: e.activation(
                            out=HT[:, kc, :], in_=PT[:, kc, :], func=AF.Identity,
                            bias=self.modc[:, la, s_, sv * KC + kc:sv * KC + kc + 1],
                            scale=self.modc[:, la, s_, (sv + 1) * KC + kc:(sv + 1) * KC + kc + 1]),
                            reads=[PT, self.modc], writes=[HT])
                    p.dma(hT.t.rearrange("(kc p) t -> p kc t", p=128)[:, :, i * 128:(i + 1) * 128], HT[:], reads=[HT], writes=[hT])
            lockstep((tile(i) for i in range(c.NT) if not (skip_ctx and i * 128 < L)), 2)
        p.barrier()


def _proj(self, actT, K, W, wl, sections, tok0, tok1, NB=512, TG=512, tag="pj"):
    c, p = self.cfg, self.p
    KCs = K // 128
    wv = wl.rearrange("(kc p) n -> p kc n", p=128)
    av = actT.t.rearrange("(kc p) t -> p kc t", p=128)
    with ExitStack() as st:
        wbuf = [self.sb(st, f"{tag}w{i}", [128, KCs, NB], BF16) for i in range(2)]
        abuf = [self.sb(st, f"{tag}a{i}", [128, KCs, TG], BF16) for i in range(2)]
        nbank = 6
        banks = [self.ps(st, f"{tag}ps{i}", [128, 512], F32) for i in range(nbank)]
        self.pj_st = st
        for s in sections:
            if s.get("init"):
                s["init"](st)
        wi = 0; ai = 0; bi = 0
        bstate = {"bi": 0}

        def _alloc():
            b_ = banks[bstate["bi"] % nbank]; bstate["bi"] += 1
            return b_
        self.pj_alloc = _alloc
        for s in sections:
            mode = s["mode"]
            NBs = NB
            for nb0 in range(0, s["n"], NBs):
                nb = min(NBs, s["n"] - nb0)
                wb = wbuf[wi % 2]
                for k0_ in range(0, KCs, 16):
                    k1_ = min(KCs, k0_ + 16)
                    p.dma(wb[:, k0_:k1_, :nb], wv[:, k0_:k1_, s["c0"] + nb0:s["c0"] + nb0 + nb], reads=[W], writes=[wb], eng="pool")
                wi += 1
                for g0 in range(tok0, tok1, TG):
                    ntok = min(TG, tok1 - g0)
                    ab = abuf[ai % 2]; ai += 1
                    for k0_ in range(0, KCs, 16):
                        k1_ = min(KCs, k0_ + 16)
                        p.dma(ab[:, k0_:k1_, :ntok], av[:, k0_:k1_, g0:g0 + ntok], reads=[actT], writes=[ab])
                    if s.get("pre"):
                        s["pre"](g0, ntok)
                    for sb0 in range(0, nb, 512):
                        sbn = min(512, nb - sb0)
                        if mode == "TM":
                            for tt in range(ntok // 128):
                                ps = _alloc()
                                for kc in range(KCs):
                                    p.op("pe", lambda e, ps=ps, ab=ab, wb=wb, kc=kc, tt=tt: e.matmul(
                                        ps[:, :sbn], ab[:, kc, tt * 128:(tt + 1) * 128], wb[:, kc, sb0:sb0 + sbn],
                                        start=(kc == 0), stop=(kc == KCs - 1)), reads=[ab, wb], writes=[ps])
                                s["epi"](ps, g0 + tt * 128, nb0 + sb0, sbn)
                        else:
                            for cc in range(sb0 // 128, (sb0 + sbn) // 128):
                                ps = _alloc()
                                for kc in range(KCs):
                                    p.op("pe", lambda e, ps=ps, ab=ab, wb=wb, kc=kc, cc=cc, ntok=ntok: e.matmul(
                                        ps[:, :ntok], wb[:, kc, cc * 128:(cc + 1) * 128], ab[:, kc, :ntok],
                                        start=(kc == 0), stop=(kc == KCs - 1)), reads=[ab, wb], writes=[ps])
                                s["epi"](ps, None, g0, ntok, nb0 + cc * 128)
            if s.get("flush"):
                s["flush"]()
    p.barrier()
B.proj = _proj


def _stage_win(self, l, tok0=0):
    c, p = self.cfg, self.p
    D, T_ = c.D, c.T
    d = self.dram
    W = d["w_in"]
    state = {}

    def init(st):
        state["of"] = [self.sb(st, f"wiof{i}", [128, 512], F32) for i in range(3)]
        state["ob"] = [self.sb(st, f"wiob{i}", [128, 512], BF16) for i in range(3)]
        state["t1"] = [self.sb(st, f"wit1{i}", [128, 512], F32) for i in range(2)]
        state["t2"] = [self.sb(st, f"wit2{i}", [128, 512], F32) for i in range(2)]
        state["cos"] = [self.sb(st, f"wicos{i}", [128, 512], F32) for i in range(3)]
        state["sin"] = [self.sb(st, f"wisin{i}", [128, 512], F32) for i in range(3)]
        state["qb"] = [self.sb(st, f"wiqb{i}", [128, 512], BF16) for i in range(3)]
        state["k"] = 0; state["kc"] = 0; state["kq"] = 0; state["pend"] = None

    def tm_epi(dst, dcol0, func, bf):
        def epi(ps, tok, nb0, nb):
            k = state["k"]; state["k"] += 1
            o = (state["ob"] if bf else state["of"])[k % 3]
            p.op("act", lambda e: e.activation(out=o[:, :nb], in_=ps[:, :nb], func=func), reads=[ps], writes=[o])
            p.dma(dst.t[tok:tok + 128, dcol0 + nb0:dcol0 + nb0 + nb], o[:, :nb], reads=[o], writes=[dst])
        return epi

    def rope_pre(tabname, idx):
        tab = d[tabname]
        def pre(g0, ntok):
            k = state["kc"]; state["kc"] += 1
            cs, sn = state["cos"][k % 3], state["sin"][k % 3]
            p.dma(cs[:, :ntok], tab.t[idx, :, g0:g0 + ntok], reads=[tab], writes=[cs])
            p.dma(sn[:, :ntok], tab.t[idx + 1, :, g0:g0 + ntok], reads=[tab], writes=[sn])
            state["cs"] = (cs, sn)
        return pre

    def rope_finish():
        it = state["pend"]
        if it is None:
            return
        state["pend"] = None
        ps, qb, cs, sn, g0, ntok, c0, dst, which = it
        k = state["k"]; state["k"] += 1
        t1, t2, o = state["t1"][k % 2], state["t2"][k % 2], state["ob"][k % 3]
        ps2 = self.pj_alloc()
        p.op("pe", lambda e: e.matmul(ps2[:, :ntok], self.permb[:, which, :], qb[:, :ntok], start=True, stop=True), reads=[self.permb, qb], writes=[ps2])
        p.op("dve", lambda e: e.tensor_tensor(out=t1[:, :ntok], in0=ps[:, :ntok], in1=cs[:, :ntok], op=ALU.mult), reads=[ps, cs], writes=[t1])
        p.op("dve", lambda e: e.tensor_tensor(out=t2[:, :ntok], in0=ps2[:, :ntok], in1=sn[:, :ntok], op=ALU.mult), reads=[ps2, sn], writes=[t2])
        p.op("pool", lambda e: e.tensor_tensor(out=o[:, :ntok], in0=t1[:, :ntok], in1=t2[:, :ntok], op=ALU.add), reads=[t1, t2], writes=[o])
        p.dma(dst.t[c0:c0 + 128, g0:g0 + ntok], o[:, :ntok], reads=[o], writes=[dst])

    def rope_epi(dst, which):
        def epi(ps, ps2, g0, ntok, c0):
            rope_finish()
            kq = state["kq"]; state["kq"] += 1
            qb = state["qb"][kq % 3]
            cs, sn = state["cs"]
            p.op("dve", lambda e: e.tensor_copy(out=qb[:, :ntok], in_=ps[:, :ntok]), reads=[ps], writes=[qb])
            state["pend"] = (ps, qb, cs, sn, g0, ntok, c0, dst, which)
        return epi

    def fm_epi(dst, func=AF.Copy, bf=False):
        def epi(ps, ps2, g0, ntok, c0):
            k = state["k"]; state["k"] += 1
            o = (state["ob"] if bf else state["of"])[k % 3]
            p.op("act", lambda e: e.activation(out=o[:, :ntok], in_=ps[:, :ntok], func=func), reads=[ps], writes=[o])
            p.dma(dst.t[c0:c0 + 128, g0:g0 + ntok], o[:, :ntok], reads=[o], writes=[dst])
        return epi

    secs = [
        dict(c0=O_QA, n=1024, mode="FM", rope="A", pre=rope_pre("ropeA", 0), epi=rope_epi(d["qaT"], 0), flush=rope_finish, init=init),
        dict(c0=O_KA, n=256, mode="FM", rope="A", pre=rope_pre("ropeA", 2), epi=rope_epi(d["kaT"], 0), flush=rope_finish),
        dict(c0=O_QD, n=1024, mode="FM", rope="D", pre=rope_pre("ropeD", 0), epi=rope_epi(d["qdT"], 1), flush=rope_finish),
        dict(c0=O_KD, n=1024, mode="FM", rope="D", pre=rope_pre("ropeD", 2), epi=rope_epi(d["kdT"], 1), flush=rope_finish),
        dict(c0=O_QKV, n=3072, mode="FM", epi=fm_epi(d["gpre"])),
        dict(c0=O_VA, n=256, mode="TM", epi=tm_epi(d["va"], 0, AF.Copy, True)),
        dict(c0=O_VD, n=1024, mode="TM", epi=tm_epi(d["vd"], 0, AF.Copy, True)),
        dict(c0=O_Z, n=1024, mode="TM", epi=tm_epi(d["zs"], 0, AF.Silu, False)),
        dict(c0=O_A, n=32, mode="TM", epi=tm_epi(d["ab"], 0, AF.Copy, False)),
        dict(c0=O_G, n=3 * D, mode="FM", epi=fm_epi(d["gtsT"], AF.Sigmoid, True)),
    ]
    self.proj(d["hT"], D, W, W.t[l], secs, tok0, T_, NB=1024, tag="wi")
B.stage_win = _stage_win


def _stage_tm2fm(self, src, dstT, C, tok0=0):
    c, p = self.cfg, self.p
    nck = C // 128
    with ExitStack() as st:
        xt = [self.sb(st, f"tfx{i}", [128, C], BF16) for i in range(2)]
        ot = [self.sb(st, f"tfo{i}", [128, nck, 128], BF16) for i in range(2)]
        pt = [self.ps(st, f"tfp{i}", [128, nck, 128], BF16) for i in range(2)]
        for i in range(tok0 // 128, c.NT):
            X, O, P = xt[i % 2], ot[i % 2], pt[i % 2]
            p.dma(X[:], src.t[i * 128:(i + 1) * 128, :], reads=[src], writes=[X])
            for k in range(nck):
                p.op("pe", lambda e, k=k, X=X, P=P: e.transpose(P[:, k, :], X[:, k * 128:(k + 1) * 128], self.identb[:]), reads=[X, self.identb], writes=[P])
            p.op("act", lambda e, O=O, P=P: e.copy(out=O[:], in_=P[:]), reads=[P], writes=[O])
            p.dma(dstT.t.rearrange("(k p) t -> p k t", p=128)[:, :, i * 128:(i + 1) * 128], O[:], reads=[O], writes=[dstT])
    p.barrier()
B.stage_tm2fm = _stage_tm2fm


def _stage_diff(self, l, ctx_out):
    c, p = self.cfg, self.p
    T_, L, NT = c.T, c.L, c.NT
    d = self.dram
    qdT, kdT, vd, yb = d["qdT"], d["kdT"], d["vd"], d["yb"]
    lam_init = 0.8 - 0.6 * math.exp(-0.3 * l)
    with ExitStack() as st:
        lv = self.sb(st, "dflv", [128, 4, 64], F32)
        for i, n in enumerate(("df_lam_q1", "df_lam_k1", "df_lam_q2", "df_lam_k2")):
            p.dma(lv[:, i, :], d[n].t[l].partition_broadcast(128), reads=[d[n]], writes=[lv])
        lp = self.sb(st, "dflp", [128, 2, 64], F32)
        ls = self.sb(st, "dfls", [128, 2], F32)
        nlam = self.sb(st, "dfnl", [128, 1], F32)
        p.op("dve", lambda e: e.tensor_tensor(out=lp[:, 0, :], in0=lv[:, 0, :], in1=lv[:, 1, :], op=ALU.mult), reads=[lv], writes=[lp])
        p.op("dve", lambda e: e.tensor_tensor(out=lp[:, 1, :], in0=lv[:, 2, :], in1=lv[:, 3, :], op=ALU.mult), reads=[lv], writes=[lp])
        p.op("dve", lambda e: e.reduce_sum(out=ls[:], in_=lp[:], axis=AX.X), reads=[lp], writes=[ls])
        p.op("act", lambda e: e.activation(out=ls[:], in_=ls[:], func=AF.Exp), reads=[ls], writes=[ls])
        p.op("dve", lambda e: e.tensor_tensor(out=nlam[:], in0=ls[:, 1:2], in1=ls[:, 0:1], op=ALU.subtract), reads=[ls], writes=[nlam])
        p.op("dve", lambda e: e.tensor_scalar_add(out=nlam[:], in0=nlam[:], scalar1=-lam_init), reads=[nlam], writes=[nlam])
        sub = self.sb(st, "dfsub", [128, 128], F32)
        p.dma(sub[:], d["df_subln"].t[l].partition_broadcast(128), reads=[d["df_subln"]], writes=[sub])
        p.op("dve", lambda e: e.tensor_scalar_mul(out=sub[:], in0=sub[:], scalar1=1.0 - lam_init), reads=[sub], writes=[sub])

        kT = [self.sb(st, f"dfk{i}", [128, T_], BF16) for i in range(2)]
        vx = [self.sb(st, f"dfv{i}", [128, NT, 132], BF16) for i in range(2)]
        for i in range(2):
            p.op("pool", lambda e, i=i: e.memset(vx[i][:, :, 128:129], 1.0), writes=[vx[i]])
        qt = [self.sb(st, f"dfq{i}", [128, 512], BF16) for i in range(2)]
        pe_ = [[self.sb(st, f"dfp{j}{i}", [128, 512], BF16) for i in range(2)] for j in range(2)]
        sb_ = [[self.ps(st, f"dfs{j}{i}", [128, 512], F32) for i in range(2)] for j in range(2)]
        ob_ = [[self.ps(st, f"dfo{j}{i}", [128, 2, 256], F32) for i in range(2)] for j in range(2)]
        r12 = [self.sb(st, f"dfr{i}", [128, 2], F32) for i in range(2)]
        t1 = [self.sb(st, f"dft{i}", [128, 128], F32) for i in range(2)]
        o_ = [self.sb(st, f"dfoo{i}", [128, 128], F32) for i in range(2)]
        sq = [self.sb(st, f"dfsq{i}", [128, 128], F32) for i in range(2)]
        ss = [self.sb(st, f"dfss{i}", [128, 1], F32) for i in range(2)]
        yo = [self.sb(st, f"dfyo{i}", [128, 128], BF16) for i in range(2)]
        qi = 0; si = 0; ei = 0
        groups = []
        if ctx_out:
            groups.append((0, L, 0, L // 128))
        for g0 in range(L, T_, 512):
            groups.append((g0, min(512, T_ - g0), 0, NT))
        for h in range(8):
            K, V = kT[h % 2], vx[h % 2]
            p.dma(K[:], kdT.t[h * 128:(h + 1) * 128, :], reads=[kdT], writes=[K])
            vv_ = vd.t[:, h * 128:(h + 1) * 128].rearrange("(n p) c -> p n c", p=128)
            for n0_ in range(0, NT, 16):
                n1_ = min(NT, n0_ + 16)
                p.dma(V[:, n0_:n1_, 0:128], vv_[:, n0_:n1_, :], reads=[vd], writes=[V])
            for (g0, ntok, kt0, kt1) in groups:
                Q = qt[qi % 2]; qi += 1
                p.dma(Q[:, :ntok], qdT.t[h * 128:(h + 1) * 128, g0:g0 + ntok], reads=[qdT], writes=[Q])
                nq = ntok // 128

                def emit_S(kt):
                    for j in range(2):
                        S = sb_[j][kt % 2]
                        p.op("pe", lambda e, S=S, K=K, Q=Q, j=j, kt=kt, ntok=ntok: e.matmul(
                            S[:, :ntok], K[j * 64:(j + 1) * 64, kt * 128:(kt + 1) * 128], Q[j * 64:(j + 1) * 64, :ntok], start=True, stop=True),
                            reads=[K, Q], writes=[S])
                emit_S(kt0)
                for kt in range(kt0, kt1):
                    if kt + 1 < kt1:
                        emit_S(kt + 1)
                    for j in range(2):
                        S = sb_[j][kt % 2]; P = pe_[j][kt % 2]
                        p.op("act", lambda e, S=S, P=P, ntok=ntok: e.activation(out=P[:, :ntok], in_=S[:, :ntok], func=AF.Exp), reads=[S], writes=[P])
                        for qs in range(nq):
                            O = ob_[j][qs // 2]
                            p.op("pe", lambda e, O=O, P=P, V=V, qs=qs, kt=kt: e.matmul(
                                O[:, qs % 2, 0:129], P[:, qs * 128:(qs + 1) * 128], V[:, kt, 0:129], start=(kt == kt0 and qs % 2 == 0), stop=(kt == kt1 - 1)),
                                reads=[P, V], writes=[O])
                for qs in range(nq):
                    O1, O2 = ob_[0][qs // 2], ob_[1][qs // 2]
                    R_, T1, OO, SQ, SS, YO = r12[ei % 2], t1[ei % 2], o_[ei % 2], sq[ei % 2], ss[ei % 2], yo[ei % 2]; ei += 1
                    s2 = qs % 2
                    p.op("dve", lambda e, R_=R_, O1=O1, s2=s2: e.reciprocal(out=R_[:, 0:1], in_=O1[:, s2, 128:129]), reads=[O1], writes=[R_])
                    p.op("dve", lambda e, R_=R_, O2=O2, s2=s2: e.reciprocal(out=R_[:, 1:2], in_=O2[:, s2, 128:129]), reads=[O2], writes=[R_])
                    p.op("dve", lambda e, R_=R_: e.tensor_tensor(out=R_[:, 1:2], in0=R_[:, 1:2], in1=nlam[:], op=ALU.mult), reads=[R_, nlam], writes=[R_])
                    p.op("dve", lambda e, R_=R_, O1=O1, T1=T1, s2=s2: e.tensor_scalar_mul(out=T1[:], in0=O1[:, s2, 0:128], scalar1=R_[:, 0:1]), reads=[O1, R_], writes=[T1])
                    p.op("dve", lambda e, R_=R_, O2=O2, T1=T1, OO=OO, s2=s2: e.scalar_tensor_tensor(out=OO[:], in0=O2[:, s2, 0:128], scalar=R_[:, 1:2], in1=T1[:], op0=ALU.mult, op1=ALU.add), reads=[O2, R_, T1], writes=[OO])
                    p.op("act", lambda e, OO=OO, SQ=SQ, SS=SS: e.activation(out=SQ[:], in_=OO[:], func=AF.Square, accum_out=SS[:]), reads=[OO], writes=[SQ, SS])
                    self.rsqrt(SS, SS[:], SS, SS[:], 1e-6, scale=1.0 / 128)
                    p.op("dve", lambda e, OO=OO, SS=SS, YO=YO: e.scalar_tensor_tensor(out=YO[:], in0=OO[:], scalar=SS[:, 0:1], in1=sub[:], op0=ALU.mult, op1=ALU.mult), reads=[OO, SS, sub], writes=[YO])
                    t0 = g0 + qs * 128
                    p.dma(yb.t[t0:t0 + 128, h * 128:(h + 1) * 128], YO[:], reads=[YO], writes=[yb])
    p.barrier()
B.stage_diff = _stage_diff


def _stage_win_attn(self, l, ctx_out):
    c, p = self.cfg, self.p
    T_, L, NT = c.T, c.L, c.NT
    LT = L // 128
    d = self.dram
    qaT, kaT, va, ya = d["qaT"], d["kaT"], d["va"], d["ya"]
    with ExitStack() as st:
        es = self.sb(st, "waes", [128, 8], F32)
        p.dma(es[:], d["wa_sink"].t[l].partition_broadcast(128), reads=[d["wa_sink"]], writes=[es])
        p.op("act", lambda e: e.activation(out=es[:], in_=es[:], func=AF.Exp), reads=[es], writes=[es])
        mk = self.sb(st, "wamk", [128, 2, 4, 128], BF16)
        for h4 in range(4):
            p.op("dve", lambda e, h4=h4: e.tensor_copy(out=mk[:, 0, h4, :], in_=self.mask[:, M_TRILI, :]), reads=[self.mask], writes=[mk])
            p.op("dve", lambda e, h4=h4: e.tensor_copy(out=mk[:, 1, h4, :], in_=self.mask[:, M_TRIUI, :]), reads=[self.mask], writes=[mk])
        K = self.sb(st, "wak", [128, T_], BF16)
        V = self.sb(st, "wav", [128, NT, 132], BF16)
        p.op("pool", lambda e: e.memset(V[:, :, 128:129], 1.0), writes=[V])
        qt = [self.sb(st, f"waq{i}", [128, 4, 128], BF16) for i in range(2)]
        pp = [self.sb(st, f"wap{i}", [128, 512], BF16) for i in range(4)]
        sb_ = [self.ps(st, f"was{i}", [128, 512], F32) for i in range(4)]
        ob_ = [[self.ps(st, f"wao{i}{j}", [128, 2, 256], F32) for j in range(2)] for i in range(2)]
        rr = [self.sb(st, f"war{i}", [128, 1], F32) for i in range(4)]
        yo = [self.sb(st, f"wayo{i}", [128, 128], BF16) for i in range(4)]
        si = 0; ei = 0; bi = 0
        for g in range(2):
            p.dma(K[:], kaT.t[g * 128:(g + 1) * 128, :], reads=[kaT], writes=[K])
            vv_ = va.t[:, g * 128:(g + 1) * 128].rearrange("(n p) c -> p n c", p=128)
            for n0_ in range(0, NT, 16):
                n1_ = min(NT, n0_ + 16)
                p.dma(V[:, n0_:n1_, 0:128], vv_[:, n0_:n1_, :], reads=[va], writes=[V])
            def block(i, bi):
                Q = qt[bi % 2]; OB = ob_[bi % 2]
                SB = sb_[2 * (bi % 2):2 * (bi % 2) + 2]; PP = pp[2 * (bi % 2):2 * (bi % 2) + 2]
                for h4 in range(4):
                    h = g * 4 + h4
                    p.dma(Q[:, h4, :], qaT.t[h * 128:(h + 1) * 128, i * 128:(i + 1) * 128], reads=[qaT], writes=[Q])
                kts = [(kt, None) for kt in range(LT)]
                if i >= LT:
                    if i - 1 >= LT:
                        kts.append((i - 1, 0))
                    kts.append((i, None))
                    if i + 1 < NT:
                        kts.append((i + 1, 1))

                def emit_S(n_):
                    kt_ = kts[n_][0]
                    S_ = SB[n_ % 2]
                    p.op("pe", lambda e, S_=S_, Q=Q, kt_=kt_: e.matmul(S_[:], K[:, kt_ * 128:(kt_ + 1) * 128], Q[:].rearrange("p a b -> p (a b)"), start=True, stop=True),
                         reads=[K, Q], writes=[S_])
                yield
                emit_S(0)
                for n_, (kt, m) in enumerate(kts):
                    S = SB[n_ % 2]; P = PP[n_ % 2]
                    if n_ + 1 < len(kts):
                        emit_S(n_ + 1)
                    p.op("act", lambda e, S=S, P=P: e.activation(out=P[:], in_=S[:], func=AF.Exp), reads=[S], writes=[P])
                    if m is not None:
                        p.op("pool", lambda e, P=P, m=m: e.tensor_tensor(out=P[:], in0=P[:], in1=mk[:, m].rearrange("p a b -> p (a b)"), op=ALU.mult), reads=[P, mk], writes=[P])
                    yield
                    for h4 in range(4):
                        O = OB[h4 // 2]
                        p.op("pe", lambda e, O=O, P=P, h4=h4, kt=kt, n_=n_, nk=len(kts): e.matmul(
                            O[:, h4 % 2, 0:129], P[:, h4 * 128:(h4 + 1) * 128], V[:, kt, 0:129], start=(n_ == 0 and h4 % 2 == 0), stop=(n_ == nk - 1)),
                            reads=[P, V], writes=[O])
                yield
                for h4 in range(4):
                    h = g * 4 + h4
                    O = OB[h4 // 2]
                    R_, YO = rr[(bi * 4 + h4) % 4], yo[(bi * 4 + h4) % 4]
                    p.op("dve", lambda e, R_=R_, O=O, h4=h4, h=h: e.tensor_scalar(out=R_[:], in0=O[:, h4 % 2, 128:129], scalar1=es[:, h:h + 1], scalar2=None, op0=ALU.add), reads=[O, es], writes=[R_])
                    p.op("dve", lambda e, R_=R_: e.reciprocal(out=R_[:], in_=R_[:]), reads=[R_], writes=[R_])
                    p.op("dve", lambda e, R_=R_, O=O, YO=YO, h4=h4: e.tensor_scalar_mul(out=YO[:], in0=O[:, h4 % 2, 0:128], scalar1=R_[:, 0:1]), reads=[O, R_], writes=[YO])
                    p.dma(ya.t[i * 128:(i + 1) * 128, h * 128:(h + 1) * 128], YO[:], reads=[YO], writes=[ya])
            blocks = list(range(0 if ctx_out else LT, NT))
            lockstep((block(i, n) for n, i in enumerate(blocks)), 2)
    p.barrier()
B.stage_win_attn = _stage_win_attn


def _stage_gdn_conv(self, l):
    c, p = self.cfg, self.p
    T_, L, NT = c.T, c.L, c.NT
    d = self.dram
    gpre, qnT, knT, kn, vn = d["gpre"], d["qnT"], d["knT"], d["kn"], d["vn"]
    with ExitStack() as st:
        cw = self.sb(st, "gcw", [128, 24, 3], F32)
        for k_ in range(3):
            p.dma(cw[:, :, k_], d["dn_conv"].t[l, k_].rearrange("(ch p) -> p ch", p=128), reads=[d["dn_conv"]], writes=[cw], allow_slow_non_contiguous=True)
        G = [self.sb(st, f"gcG{i}", [128, 514], F32) for i in range(4)]
        Y = [self.sb(st, f"gcY{i}", [128, 512], F32) for i in range(4)]
        SQ = [self.sb(st, f"gcS{i}", [128, 512], F32) for i in range(4)]
        RI = [self.sb(st, f"gcR{i}", [128, 512], F32) for i in range(4)]
        YN = [self.sb(st, f"gcN{i}", [128, 512], F32) for i in range(4)]
        TT = [self.sb(st, f"gcT{i}", [128, 4, 128], F32) for i in range(4)]
        pss = [self.ps(st, f"gcps{i}", [128, 512], F32) for i in range(4)]
        ptt = [self.ps(st, f"gcpt{i}", [128, 4, 128], F32) for i in range(4)]
        ones = self.mask[:, M_ONES, :]
        ident = self.mask[:, M_ID, :]
        def iters():
            k = 0
            for ch in range(24):
                for (s0, s1) in ((0, L), (L, T_)):
                    for g0 in range(s0, s1, 512):
                        yield one(k, ch, s0, s1, g0)
                        k += 1

        def one(k, ch, s0, s1, g0):
                    kind = ch // 8
                    hh = ch % 8
                    ntok = min(512, s1 - g0)
                    g, y, sq, ri, yn, tt, ps1, pt1 = G[k % 4], Y[k % 4], SQ[k % 4], RI[k % 4], YN[k % 4], TT[k % 4], pss[k % 4], ptt[k % 4]
                    lo = max(s0, g0 - 1); hi = min(s1, g0 + ntok + 1)
                    if lo > g0 - 1:
                        p.op("pool", lambda e, g=g: e.memset(g[:, 0:1], 0.0), writes=[g])
                    if hi < g0 + ntok + 1:
                        p.op("pool", lambda e, g=g, ntok=ntok: e.memset(g[:, ntok + 1:ntok + 2], 0.0), writes=[g])
                    p.dma(g[:, lo - (g0 - 1):hi - (g0 - 1)], gpre.t[ch * 128:(ch + 1) * 128, lo:hi], reads=[gpre], writes=[g])
                    yield
                    p.op("dve", lambda e, g=g, y=y, ntok=ntok, ch=ch: e.tensor_scalar_mul(out=y[:, :ntok], in0=g[:, 0:ntok], scalar1=cw[:, ch, 0:1]), reads=[g, cw], writes=[y])
                    p.op("dve", lambda e, g=g, y=y, ntok=ntok, ch=ch: e.scalar_tensor_tensor(out=y[:, :ntok], in0=g[:, 1:ntok + 1], scalar=cw[:, ch, 1:2], in1=y[:, :ntok], op0=ALU.mult, op1=ALU.add), reads=[g, cw, y], writes=[y])
                    p.op("dve", lambda e, g=g, y=y, ntok=ntok, ch=ch: e.scalar_tensor_tensor(out=y[:, :ntok], in0=g[:, 2:ntok + 2], scalar=cw[:, ch, 2:3], in1=y[:, :ntok], op0=ALU.mult, op1=ALU.add), reads=[g, cw, y], writes=[y])
                    yield
                    p.op("act", lambda e, y=y, ntok=ntok: e.activation(out=y[:, :ntok], in_=y[:, :ntok], func=AF.Silu), reads=[y], writes=[y])
                    src = y
                    if kind < 2:
                        p.op("act", lambda e, y=y, sq=sq, ntok=ntok: e.activation(out=sq[:, :ntok], in_=y[:, :ntok], func=AF.Square), reads=[y], writes=[sq])
                        yield
                        p.op("pe", lambda e, ps1=ps1, sq=sq, ntok=ntok: e.matmul(ps1[:, :ntok], ones, sq[:, :ntok], start=True, stop=True), reads=[sq, self.mask], writes=[ps1])
                        yield
                        self.rsqrt(ri, ri[:, :ntok], ps1, ps1[:, :ntok], 1e-6)
                        sc_ = (128 ** -0.5) if kind == 0 else 1.0
                        p.op("dve", lambda e, y=y, ri=ri, yn=yn, ntok=ntok, sc_=sc_: e.scalar_tensor_tensor(out=yn[:, :ntok], in0=y[:, :ntok], scalar=sc_, in1=ri[:, :ntok], op0=ALU.mult, op1=ALU.mult), reads=[y, ri], writes=[yn])
                        dst = qnT if kind == 0 else knT
                        p.dma(dst.t[hh * 128:(hh + 1) * 128, g0:g0 + ntok], yn[:, :ntok], reads=[yn], writes=[dst])
                        src = yn
                    if kind >= 1:
                        nq = ntok // 128
                        for q_ in range(nq):
                            p.op("pe", lambda e, pt1=pt1, src=src, q_=q_: e.matmul(pt1[:, q_, :], src[:, q_ * 128:(q_ + 1) * 128], ident, start=True, stop=True), reads=[src, self.mask], writes=[pt1])
                        yield
                        p.op("act", lambda e, tt=tt, pt1=pt1, nq=nq: e.copy(out=tt[:, :nq, :], in_=pt1[:, :nq, :]), reads=[pt1], writes=[tt])
                        dst = kn if kind == 1 else vn
                        p.dma(dst.t[g0:g0 + ntok, hh * 128:(hh + 1) * 128].rearrange("(q p) c -> p q c", p=128), tt[:, :nq, :], reads=[tt], writes=[dst])
        lockstep(iters(), 3)
    p.barrier()
B.stage_gdn_conv = _stage_gdn_conv


def _stage_gdn_scan(self, l, ctx_out):
    c, p = self.cfg, self.p
    T_, L, NT = c.T, c.L, c.NT
    LT = L // 128
    d = self.dram
    qnT, knT, kn, vn, ab, of, zs, yc = d["qnT"], d["knT"], d["kn"], d["vn"], d["ab"], d["of"], d["zs"], d["yc"]
    M = lambda i: self.mask[:, i, :]
    with ExitStack() as st:
        S = self.sb(st, "gsS", [128, 8, 128], F32)
        S_h = [S.sub(f"S{h}") for h in range(8)]
        nal = self.sb(st, "gsnal", [128, 16], F32)
        dtb = self.sb(st, "gsdtb", [128, 16], F32)
        nw = self.sb(st, "gsnw", [128, 128], F32)
        p.dma(nal[:], d["dn_a_log"].t[l].partition_broadcast(128), reads=[d["dn_a_log"]], writes=[nal])
        p.dma(dtb[:], d["dn_dt_bias"].t[l].partition_broadcast(128), reads=[d["dn_dt_bias"]], writes=[dtb])
        p.dma(nw[:], d["dn_norm"].t[l].partition_broadcast(128), reads=[d["dn_norm"]], writes=[nw])
        p.op("act", lambda e: e.activation(out=nal[:], in_=nal[:], func=AF.Exp), reads=[nal], writes=[nal])
        p.op("dve", lambda e: e.tensor_scalar_mul(out=nal[:], in0=nal[:], scalar1=-1.0), reads=[nal], writes=[nal])
        banks = [self.ps(st, f"gsb{i}", [128, 4, 128], F32) for i in range(8)]
        slots = [(bnk, 0, bnk) for bnk in banks]
        sc = {"ps": 0}
        rings = {}

        def tmp(name, shape=(128, 128), n=6, dt=F32):
            if name == "X":
                n = 12
            if name not in rings:
                rings[name] = [[self.sb(st, f"gs_{name}{i}", list(shape), dt) for i in range(n)], 0]
            r = rings[name]
            t = r[0][r[1] % n]; r[1] += 1
            return t

        def mm(lhsT, lT, rhs, rT, ncols=128, acc=None):
            trk, j, bnk = slots[sc["ps"] % len(slots)]; sc["ps"] += 1
            ap = bnk[:, j, 0:ncols]
            p.op("pe", lambda e: e.matmul(ap, lhsT, rhs, start=True, stop=(acc is None)), reads=lT + rT, writes=[trk])
            if acc is not None:
                l2, l2T, r2, r2T = acc
                p.op("pe", lambda e: e.matmul(ap, l2, r2, start=False, stop=True), reads=l2T + r2T, writes=[trk])
            return trk, ap

        evi = {"k": 0}

        def evac(out_t, out_ap, trk, ap):
            k = evi["k"]; evi["k"] += 1
            if k % 2 == 0:
                p.op("act", lambda e: e.copy(out=out_ap, in_=ap), reads=[trk], writes=[out_t])
            else:
                p.op("dve", lambda e: e.tensor_copy(out=out_ap, in_=ap), reads=[trk], writes=[out_t])

        for dirn in range(2):
            cum = M_TRIUI if dirn == 0 else M_TRILI
            mL = M_TRILS if dirn == 0 else M_TRIUS
            mA = M_TRIUI if dirn == 0 else M_TRILI
            p.op("pool", lambda e: e.memset(S[:], 0.0), writes=[S] + S_h)
            if dirn == 0:
                order = list(range(NT))
            else:
                order = list(range(LT - 1, -1, -1)) + list(range(NT - 1, LT - 1, -1))
            for i in order:
                want = (i >= LT) or ctx_out
                tk = slice(i * 128, (i + 1) * 128)
                abt = tmp("abt", (128, 32), 2)
                p.dma(abt[:], ab.t[tk, :], reads=[ab], writes=[abt])
                knt = tmp("knt", (128, 1024), 2); vnt = tmp("vnt", (128, 1024), 2)
                kTt = tmp("kTt", (128, 8, 128), 2); qTt = tmp("qTt", (128, 8, 128), 2)
                p.dma(knt[:], kn.t[tk, :], reads=[kn], writes=[knt])
                p.dma(vnt[:], vn.t[tk, :], reads=[vn], writes=[vnt])
                p.dma(kTt[:], knT.t[:, tk].rearrange("(h p) t -> p h t", p=128), reads=[knT], writes=[kTt])
                p.dma(qTt[:], qnT.t[:, tk].rearrange("(h p) t -> p h t", p=128), reads=[qnT], writes=[qTt])
                gx = tmp("gx", (128, 8), 2); gax = tmp("gax", (128, 8), 2); g = tmp("g", (128, 8), 2); beta = tmp("beta", (128, 8), 2)
                ds = slice(dirn * 8, dirn * 8 + 8)
                p.op("dve", lambda e: e.tensor_tensor(out=gx[:], in0=abt[:, ds], in1=dtb[:, ds], op=ALU.add), reads=[abt, dtb], writes=[gx])
                p.op("act", lambda e: e.activation(out=gax[:], in_=gx[:], func=AF.Abs), reads=[gx], writes=[gax])
                p.op("act", lambda e: e.activation(out=gax[:], in_=gax[:], func=AF.Exp, scale=-1.0), reads=[gax], writes=[gax])
                p.op("act", lambda e: e.activation(out=gax[:], in_=gax[:], func=AF.Ln, bias=self.epsc[1.0], scale=1.0), reads=[gax, self.epst], writes=[gax])
                p.op("dve", lambda e: e.tensor_scalar_max(out=gx[:], in0=gx[:], scalar1=0.0), reads=[gx], writes=[gx])
                p.op("dve", lambda e: e.tensor_tensor(out=gx[:], in0=gx[:], in1=gax[:], op=ALU.add), reads=[gx, gax], writes=[gx])
                p.op("dve", lambda e: e.tensor_tensor(out=g[:], in0=gx[:], in1=nal[:, ds], op=ALU.mult), reads=[gx, nal], writes=[g])
                p.op("act", lambda e: e.activation(out=beta[:], in_=abt[:, 16 + dirn * 8:24 + dirn * 8], func=AF.Sigmoid), reads=[abt], writes=[beta])
                t1, a1 = mm(M(cum), [self.mask], g[:], [g], ncols=8)
                gcum = tmp("gcum", (128, 8), 2)
                evac(gcum, gcum[:], t1, a1)
                t2, a2 = mm(M(M_ONES), [self.mask], g[:], [g], ncols=8)
                gtot = tmp("gtot", (128, 8), 2)
                evac(gtot, gtot[:], t2, a2)
                eg = tmp("eg", (128, 8), 2); ekd = tmp("ekd", (128, 8), 2); egl = tmp("egl", (128, 8), 2); bk = tmp("bk", (128, 8), 2)
                p.op("act", lambda e: e.activation(out=eg[:], in_=gcum[:], func=AF.Exp), reads=[gcum], writes=[eg])
                p.op("dve", lambda e: e.tensor_tensor(out=ekd[:], in0=gtot[:], in1=gcum[:], op=ALU.subtract), reads=[gtot, gcum], writes=[ekd])
                p.op("act", lambda e: e.activation(out=ekd[:], in_=ekd[:], func=AF.Exp), reads=[ekd], writes=[ekd])
                p.op("act", lambda e: e.activation(out=egl[:], in_=gtot[:], func=AF.Exp), reads=[gtot], writes=[egl])
                p.op("dve", lambda e: e.tensor_tensor(out=bk[:], in0=beta[:], in1=eg[:], op=ALU.mult), reads=[beta, eg], writes=[bk])
                if want:
                    ot = tmp("ot", (128, 1024), 2)
                    if dirn == 1:
                        oft = tmp("oft", (128, 1024), 2); zt = tmp("zt", (128, 1024), 2)
                        p.dma(oft[:], of.t[tk, :], reads=[of], writes=[oft])
                        p.dma(zt[:], zs.t[tk, :], reads=[zs], writes=[zt])
                def unit(h):
                    hs = slice(h * 128, (h + 1) * 128)
                    hc = slice(h, h + 1)
                    kT = kTt[:, h, :]; qT = qTt[:, h, :]
                    Ug = tmp("Ug")
                    p.op("pool", lambda e: e.tensor_scalar(out=Ug[:], in0=M(cum), scalar1=g[:, hc], scalar2=None, op0=ALU.mult), reads=[self.mask, g], writes=[Ug])
                    yield
                    tD, aD = mm(M(M_ONES), [self.mask], Ug[:], [Ug])
                    DL = tmp("DL"); DU = tmp("DU")
                    p.op("dve", lambda e: e.tensor_scalar(out=DL[:], in0=aD, scalar1=gcum[:, hc], scalar2=0.0, op0=ALU.subtract, op1=ALU.max), reads=[tD, gcum], writes=[DL])
                    p.op("dve", lambda e: e.tensor_scalar(out=DU[:], in0=aD, scalar1=gcum[:, hc], scalar2=0.0, op0=ALU.subtract, op1=ALU.min), reads=[tD, gcum], writes=[DU])
                    p.op("act", lambda e: e.activation(out=DL[:], in_=DL[:], func=AF.Exp, scale=-1.0), reads=[DL], writes=[DL])
                    p.op("act", lambda e: e.activation(out=DU[:], in_=DU[:], func=AF.Exp), reads=[DU], writes=[DU])
                    p.op("pool", lambda e: e.tensor_tensor(out=DL[:], in0=DL[:], in1=M(mL), op=ALU.mult), reads=[DL, self.mask], writes=[DL])
                    p.op("pool", lambda e: e.tensor_tensor(out=DU[:], in0=DU[:], in1=M(mA), op=ALU.mult), reads=[DU, self.mask], writes=[DU])
                    yield
                    tG, aG = mm(kT, [kTt], kT, [kTt])
                    Lm = tmp("Lm")
                    p.op("dve", lambda e: e.scalar_tensor_tensor(out=Lm[:], in0=aG, scalar=beta[:, hc], in1=DL[:], op0=ALU.mult, op1=ALU.mult), reads=[tG, beta, DL], writes=[Lm])
                    if want:
                        tA, aA = mm(kT, [kTt], qT, [qTt])
                        AT = tmp("AT")
                        p.op("dve", lambda e: e.tensor_tensor(out=AT[:], in0=aA, in1=DU[:], op=ALU.mult), reads=[tA, DU], writes=[AT])
                    yield

                    def trn(src):
                        trk, j, bnk = slots[sc["ps"] % len(slots)]; sc["ps"] += 1
                        ap = bnk[:, j, 0:128]
                        p.op("pe", lambda e: e.transpose(ap, src[:], M(M_ID)), reads=[src, self.mask], writes=[trk])
                        return trk, ap
                    tN, aN = trn(Lm)
                    Nm = tmp("Nm")
                    evac(Nm, Nm[:], tN, aN)
                    L16 = tmp("L16"); N16 = tmp("N16")
                    p.op("pool", lambda e: e.tensor_tensor(out=L16[:], in0=Lm[:], in1=M(M_BD16), op=ALU.mult), reads=[Lm, self.mask], writes=[L16])
                    p.op("pool", lambda e: e.tensor_tensor(out=N16[:], in0=Nm[:], in1=M(M_BD16), op=ALU.mult), reads=[Nm, self.mask], writes=[N16])

                    def mmev(name, lh, lhT, rh, rhT):
                        t_, a_ = mm(lh[:], [lh], rh[:], [rh])
                        o_ = tmp(name)
                        evac(o_, o_[:], t_, a_)
                        return o_
                    yield
                    L2 = mmev("L2", N16, None, L16, None); N2 = mmev("N2", L16, None, N16, None)
                    yield
                    L4 = mmev("L4", N2, None, L2, None); N4 = mmev("N4", L2, None, N2, None)
                    yield
                    L8 = mmev("L8", N4, None, L4, None)
                    Q1 = tmp("P1")
                    p.op("pool", lambda e: e.tensor_tensor(out=Q1[:], in0=M(M_ID), in1=N16[:], op=ALU.subtract), reads=[N16, self.mask], writes=[Q1])

                    def mmadd(name, base, lh, rh):
                        t_, a_ = mm(lh[:], [lh], rh[:], [rh])
                        o_ = tmp(name)
                        p.op("dve", lambda e: e.tensor_tensor(out=o_[:], in0=base[:], in1=a_, op=ALU.add), reads=[base, t_], writes=[o_])
                        return o_
                    yield
                    Q2 = mmadd("P2", Q1, L2, Q1)
                    yield
                    Q3 = mmadd("P3", Q2, L4, Q2)
                    yield
                    Y = mmadd("X", Q3, L8, Q3)
                    for lvl in (M_OFF32, M_OFF64, M_OFF128):
                        Loff = tmp("Noff")
                        p.op("pool", lambda e, Loff=Loff, lvl=lvl: e.tensor_tensor(out=Loff[:], in0=Lm[:], in1=M(lvl), op=ALU.mult), reads=[Lm, self.mask], writes=[Loff])
                        yield
                        Xt_, Xa_ = trn(Y)
                        Xs = tmp("Y"); evac(Xs, Xs[:], Xt_, Xa_)
                        T2t, T2a = mm(Loff[:], [Loff], Y[:], [Y])
                        T2 = tmp("T2"); evac(T2, T2[:], T2t, T2a)
                        yield
                        Zt, Za = mm(Xs[:], [Xs], T2[:], [T2])
                        Yn = tmp("X")
                        p.op("dve", lambda e, Yn=Yn, Y=Y, Za=Za: e.tensor_tensor(out=Yn[:], in0=Y[:], in1=Za, op=ALU.subtract), reads=[Y, Zt], writes=[Yn])
                        Y = Yn
                    yield
                    RU = tmp("RU"); RW = tmp("RW"); KD = tmp("KD")
                    p.op("pool", lambda e: e.tensor_scalar(out=RU[:], in0=vnt[:, hs], scalar1=beta[:, hc], scalar2=None, op0=ALU.mult), reads=[vnt, beta], writes=[RU])
                    p.op("pool", lambda e: e.tensor_scalar(out=RW[:], in0=knt[:, hs], scalar1=bk[:, hc], scalar2=None, op0=ALU.mult), reads=[knt, bk], writes=[RW])
                    p.op("pool", lambda e: e.tensor_scalar(out=KD[:], in0=knt[:, hs], scalar1=ekd[:, hc], scalar2=None, op0=ALU.mult), reads=[knt, ekd], writes=[KD])
                    yield
                    ut, ua = mm(Y[:], [Y], RU[:], [RU])
                    U_ = tmp("U"); evac(U_, U_[:], ut, ua)
                    wt_, wa_ = mm(RW[:], [RW], Y[:], [Y])
                    WT = tmp("WT"); evac(WT, WT[:], wt_, wa_)
                    Sh = S_h[h]
                    yield
                    wst, wsa = mm(WT[:], [WT], S[:, h, :], [Sh])
                    VN = tmp("VN")
                    p.op("dve", lambda e: e.tensor_tensor(out=VN[:], in0=U_[:], in1=wsa, op=ALU.subtract), reads=[U_, wst], writes=[VN])
                    if want:
                        qst, qsa = mm(qT, [qTt], S[:, h, :], [Sh])
                        avt, ava = mm(AT[:], [AT], VN[:], [VN])
                        QS = tmp("QS")
                        p.op("dve", lambda e: e.tensor_scalar(out=QS[:], in0=qsa, scalar1=eg[:, hc], scalar2=None, op0=ALU.mult), reads=[qst, eg], writes=[QS])
                        if dirn == 0:
                            p.op("dve", lambda e: e.tensor_tensor(out=ot[:, hs], in0=QS[:], in1=ava, op=ALU.add), reads=[QS, avt], writes=[ot])
                        else:
                            p.op("dve", lambda e: e.tensor_tensor(out=QS[:], in0=QS[:], in1=ava, op=ALU.add), reads=[QS, avt], writes=[QS])
                            p.op("pool", lambda e: e.tensor_tensor(out=ot[:, hs], in0=QS[:], in1=oft[:, hs], op=ALU.add), reads=[QS, oft], writes=[ot])
                    yield
                    kvt, kva = mm(KD[:], [KD], VN[:], [VN])
                    p.op("dve", lambda e: e.scalar_tensor_tensor(out=S[:, h, :], in0=S[:, h, :], scalar=egl[:, hc], in1=kva, op0=ALU.mult, op1=ALU.add), reads=[Sh, egl, kvt], writes=[Sh])
                for hg in ((0, 1, 2, 3), (4, 5, 6, 7)):
                    gens = [unit(h) for h in hg]
                    while gens:
                        for g_ in list(gens):
                            try:
                                next(g_)
                            except StopIteration:
                                gens.remove(g_)
                if want:
                    if dirn == 0:
                        p.dma(of.t[tk, :], ot[:], reads=[ot], writes=[of])
                    else:
                        sq = tmp("osq", (128, 128), 2); ssq = tmp("ossq", (128, 8), 2)
                        yct = tmp("yct", (128, 1024), 2, BF16)
                        for h in range(8):
                            hs = slice(h * 128, (h + 1) * 128)
                            p.op("act", lambda e, hs=hs, h=h: e.activation(out=sq[:], in_=ot[:, hs], func=AF.Square, accum_out=ssq[:, h:h + 1]), reads=[ot], writes=[sq, ssq])
                        self.rsqrt(ssq, ssq[:], ssq, ssq[:], 1e-6, scale=1.0 / 128)
                        for h in range(8):
                            hs = slice(h * 128, (h + 1) * 128)
                            p.op("dve", lambda e, hs=hs, h=h: e.scalar_tensor_tensor(out=ot[:, hs], in0=ot[:, hs], scalar=ssq[:, h:h + 1], in1=nw[:], op0=ALU.mult, op1=ALU.mult), reads=[ot, ssq, nw], writes=[ot])
                        p.op("pool", lambda e: e.tensor_tensor(out=yct[:], in0=ot[:], in1=zt[:], op=ALU.mult), reads=[ot, zt], writes=[yct])
                        p.dma(yc.t[tk, :], yct[:], reads=[yct], writes=[yc])
    p.barrier()
B.stage_gdn_scan = _stage_gdn_scan


def _stage_merge(self, l):
    c, p = self.cfg, self.p
    D, T_ = c.D, c.T
    d = self.dram
    ys = [d["yaT"], d["ybT"], d["ycT"]]
    ws = [d["w_branch_a"], d["w_branch_b"], d["w_branch_c"]]
    gT, mT = d["gtsT"], d["mT"]
    NB = D
    with ExitStack() as st:
        wb = [[self.sb(st, f"mgw{b_}{i}", [128, 8, NB], BF16) for i in range(1)] for b_ in range(3)]
        ab = [[self.sb(st, f"mga{b_}{i}", [128, 8, 512], BF16) for i in range(2)] for b_ in range(3)]
        gt = [self.sb(st, f"mgg{i}", [128, 3, 512], BF16) for i in range(3)]
        t1 = [self.sb(st, f"mgt1{i}", [128, 512], F32) for i in range(2)]
        t2 = [self.sb(st, f"mgt2{i}", [128, 512], F32) for i in range(2)]
        mo = [self.sb(st, f"mgo{i}", [128, 512], BF16) for i in range(2)]
        banks = [self.ps(st, f"mgps{i}", [128, 512], F32) for i in range(6)]
        wi = ai = bi = k = 0
        for nb0 in range(0, D, NB):
            for b_ in range(3):
                p.dma(wb[b_][0][:], ws[b_].t[l].rearrange("(kc p) n -> p kc n", p=128)[:, :, nb0:nb0 + NB], reads=[ws[b_]], writes=[wb[b_][0]], eng="pool")
            W3 = [wb[b_][0] for b_ in range(3)]; wi += 1
            for g0 in range(0, T_, 512):
                ntok = min(512, T_ - g0)
                A3 = [ab[b_][ai % 2] for b_ in range(3)]; ai += 1
                for b_ in range(3):
                    p.dma(A3[b_][:, :, :ntok], ys[b_].t.rearrange("(kc p) t -> p kc t", p=128)[:, :, g0:g0 + ntok], reads=[ys[b_]], writes=[A3[b_]])
                for cc in range(NB // 128):
                    r0 = nb0 + cc * 128
                    G_ = gt[k % 3]; T1 = t1[k % 2]; T2 = t2[k % 2]; MO = mo[k % 2]; k += 1
                    for b_ in range(3):
                        p.dma(G_[:, b_, :ntok], gT.t[b_ * D + r0:b_ * D + r0 + 128, g0:g0 + ntok], reads=[gT], writes=[G_])
                    P3 = []
                    for b_ in range(3):
                        ps = banks[bi % 6]; bi += 1
                        for kc in range(8):
                            p.op("pe", lambda e, ps=ps, b_=b_, kc=kc: e.matmul(ps[:, :ntok], W3[b_][:, kc, cc * 128:(cc + 1) * 128], A3[b_][:, kc, :ntok],
                                                                              start=(kc == 0), stop=(kc == 7)), reads=[W3[b_], A3[b_]], writes=[ps])
                        P3.append(ps)
                    p.op("dve", lambda e: e.tensor_tensor(out=T1[:, :ntok], in0=P3[0][:, :ntok], in1=G_[:, 0, :ntok], op=ALU.mult), reads=[P3[0], G_], writes=[T1])
                    p.op("dve", lambda e: e.tensor_tensor(out=T2[:, :ntok], in0=P3[1][:, :ntok], in1=G_[:, 1, :ntok], op=ALU.mult), reads=[P3[1], G_], writes=[T2])
                    p.op("pool", lambda e: e.tensor_tensor(out=T1[:, :ntok], in0=T1[:, :ntok], in1=T2[:, :ntok], op=ALU.add), reads=[T1, T2], writes=[T1])
                    p.op("dve", lambda e: e.tensor_tensor(out=T2[:, :ntok], in0=P3[2][:, :ntok], in1=G_[:, 2, :ntok], op=ALU.mult), reads=[P3[2], G_], writes=[T2])
                    p.op("pool", lambda e: e.tensor_tensor(out=MO[:, :ntok], in0=T1[:, :ntok], in1=T2[:, :ntok], op=ALU.add), reads=[T1, T2], writes=[MO])
                    p.dma(mT.t[r0:r0 + 128, g0:g0 + ntok], MO[:, :ntok], reads=[MO], writes=[mT])
    p.barrier()
B.stage_merge = _stage_merge


def _stage_tm_proj(self, l, actname, K, wname, NB=512, TG=512):
    p = self.p
    d = self.dram
    osub = d["osub"]
    state = {"k": 0}

    def init(st):
        state["o"] = [self.sb(st, f"tpo{i}", [128, 512], F32) for i in range(3)]

    def epi(ps, tok, nb0, nb):
        o = state["o"][state["k"] % 3]; state["k"] += 1
        p.op("act", lambda e: e.copy(out=o[:, :nb], in_=ps[:, :nb]), reads=[ps], writes=[o])
        p.dma(osub.t[tok:tok + 128, nb0:nb0 + nb], o[:, :nb], reads=[o], writes=[osub])
    W = d[wname]
    secs = [dict(c0=0, n=self.cfg.D, mode="TM", epi=epi, init=init)]
    self.proj(d[actname], K, W, W.t[l], secs, 0, self.cfg.T, NB=NB, TG=TG, tag="tp")
B.stage_tm_proj = _stage_tm_proj


def _stage_ffn_up(self, l):
    c, p = self.cfg, self.p
    D, T_, L, DFF, KC = c.D, c.T, c.L, c.DFF, c.KC
    d = self.dram
    hT, gT, W = d["hT"], d["gT"], d["w_up"]
    NCH = 2 * DFF // 128
    HC = DFF // 128
    CB = 4 if HC % 4 == 0 else (2 if HC % 2 == 0 else 1)
    TG = 510
    with ExitStack() as st:
        cw = self.sb(st, "fucw", [128, NCH, 3], F32)
        cb = self.sb(st, "fucb", [128, NCH], F32)
        for k_ in range(3):
            p.dma(cw[:, :, k_], d["ffn_conv_w"].t[l, k_].rearrange("(ch p) -> p ch", p=128), reads=[d["ffn_conv_w"]], writes=[cw], allow_slow_non_contiguous=True)
        p.dma(cb[:], d["ffn_conv_b"].t[l].rearrange("(ch p) -> p ch", p=128), reads=[d["ffn_conv_b"]], writes=[cb], allow_slow_non_contiguous=True)
        wA = [self.sb(st, f"fuwa{i}", [128, KC, CB * 128], BF16) for i in range(2)]
        wB = [self.sb(st, f"fuwb{i}", [128, KC, CB * 128], BF16) for i in range(2)]
        ab = [self.sb(st, f"fua{i}", [128, KC, 512], BF16) for i in range(2)]
        ua = [self.sb(st, f"fuua{i}", [128, 512], F32) for i in range(2)]
        ub = [self.sb(st, f"fuub{i}", [128, 512], F32) for i in range(2)]
        go = [self.sb(st, f"fugo{i}", [128, 512], BF16) for i in range(2)]
        banks = [self.ps(st, f"fups{i}", [128, 512], F32) for i in range(6)]
        wv = W.t[l].rearrange("(kc p) n -> p kc n", p=128)
        av = hT.t.rearrange("(kc p) t -> p kc t", p=128)
        wi = ai = bi = k = 0
        for cb0 in range(0, HC, CB):
            WA, WB = wA[wi % 2], wB[wi % 2]; wi += 1
            p.dma(WA[:], wv[:, :, cb0 * 128:(cb0 + CB) * 128], reads=[W], writes=[WA], eng="pool")
            p.dma(WB[:], wv[:, :, DFF + cb0 * 128:DFF + (cb0 + CB) * 128], reads=[W], writes=[WB], eng="pool")
            for (s0, s1) in ((0, L), (L, T_)):
                for g0 in range(s0, s1, TG):
                    n = min(TG, s1 - g0)
                    A = ab[ai % 2]; ai += 1
                    lo = max(s0, g0 - 1); hi = min(s1, g0 + n + 1)
                    if lo > g0 - 1:
                        p.op("pool", lambda e: e.memset(A[:, :, 0:1], 0.0), writes=[A])
                    if hi < g0 + n + 1:
                        p.op("pool", lambda e: e.memset(A[:, :, n + 1:n + 2], 0.0), writes=[A])
                    p.dma(A[:, :, lo - (g0 - 1):hi - (g0 - 1)], av[:, :, lo:hi], reads=[hT], writes=[A])
                    for cc in range(CB):
                        cha = cb0 + cc; chb = HC + cb0 + cc
                        pa = banks[bi % 6]; bi += 1
                        pb = banks[bi % 6]; bi += 1
                        for kc in range(KC):
                            p.op("pe", lambda e, kc=kc: e.matmul(pa[:, :n + 2], WA[:, kc, cc * 128:(cc + 1) * 128], A[:, kc, :n + 2], start=(kc == 0), stop=(kc == KC - 1)), reads=[WA, A], writes=[pa])
                        for kc in range(KC):
                            p.op("pe", lambda e, kc=kc: e.matmul(pb[:, :n + 2], WB[:, kc, cc * 128:(cc + 1) * 128], A[:, kc, :n + 2], start=(kc == 0), stop=(kc == KC - 1)), reads=[WB, A], writes=[pb])
                        UA, UB, GO = ua[k % 2], ub[k % 2], go[k % 2]; k += 1
                        for (U, ps, ch) in ((UA, pa, cha), (UB, pb, chb)):
                            p.op("dve", lambda e, U=U, ps=ps, ch=ch: e.tensor_scalar(out=U[:, :n], in0=ps[:, 0:n], scalar1=cw[:, ch, 0:1], scalar2=cb[:, ch:ch + 1], op0=ALU.mult, op1=ALU.add), reads=[ps, cw, cb], writes=[U])
                            p.op("dve", lambda e, U=U, ps=ps, ch=ch: e.scalar_tensor_tensor(out=U[:, :n], in0=ps[:, 1:n + 1], scalar=cw[:, ch, 1:2], in1=U[:, :n], op0=ALU.mult, op1=ALU.add), reads=[ps, cw, U], writes=[U])
                            p.op("dve", lambda e, U=U, ps=ps, ch=ch: e.scalar_tensor_tensor(out=U[:, :n], in0=ps[:, 2:n + 2], scalar=cw[:, ch, 2:3], in1=U[:, :n], op0=ALU.mult, op1=ALU.add), reads=[ps, cw, U], writes=[U])
                        p.op("act", lambda e: e.activation(out=UA[:, :n], in_=UA[:, :n], func=AF.Silu), reads=[UA], writes=[UA])
                        p.op("pool", lambda e: e.tensor_tensor(out=GO[:, :n], in0=UA[:, :n], in1=UB[:, :n], op=ALU.mult), reads=[UA, UB], writes=[GO])
                        p.dma(gT.t[cha * 128:(cha + 1) * 128, g0:g0 + n], GO[:, :n], reads=[GO], writes=[gT])
    p.barrier()
B.stage_ffn_up = _stage_ffn_up


def _build_all(self, st, upto=None):
    c = self.cfg
    self.declare()
    self.consts(st)
    self.stage_mod()
    self.stage_ln(0, None, (0, 0), src_inputs=True)
    for l in range(c.NL):
        last = (l == c.NL - 1)
        ctx_out = not last
        self.stage_win(l)
        self.stage_win_attn(l, ctx_out)
        self.stage_diff(l, ctx_out)
        self.stage_gdn_conv(l)
        self.stage_gdn_scan(l, ctx_out)
        d = self.dram
        self.stage_tm2fm(d["ya"], d["yaT"], 1024)
        self.stage_tm2fm(d["yb"], d["ybT"], 1024)
        self.stage_tm2fm(d["yc"], d["ycT"], 1024)
        self.stage_merge(l)
        self.stage_tm_proj(l, "mT", c.D, "w_o", NB=min(1024, c.D))
        if upto == "mix" and l == 0:
            break
        self.stage_ln(l, (2, "ln1_g", "ln1_b"), (l, 3), src_inputs=(l == 0))
        self.stage_ffn_up(l)
        kdown = c.DFF
        big = (kdown // 128) > 16
        self.stage_tm_proj(l, "gT", kdown, "w_down", NB=512, TG=256 if big else 512)
        if last:
            self.stage_ln(l, (5, "ln2_g", "ln2_b"), None, final=True)
        else:
            self.stage_ln(l, (5, "ln2_g", "ln2_b"), (l + 1, 0))
    self.p.emit(st)
B.build_all = _build_all


_CACHE = {}


def _get_nc(cfg_key):
    if cfg_key not in _CACHE:
        cfg = Cfg(*cfg_key)
        b = B(cfg)
        st = ExitStack()
        b.build_all(st)
        _CACHE[cfg_key] = (b, st, cfg)
    return _CACHE[cfg_key]


def kernel(**inputs):
    x = np.asarray(inputs["x"])
    bsz, N, D = x.shape
    L = inputs["ctx"].shape[1]
    DFF = inputs["w_down"].shape[1]
    NL = inputs["w_mod"].shape[0]
    b, st, cfg = _get_nc((D, N, L, DFF, NL))
    consts = make_consts(cfg)
    shared = {}
    for k, v in inputs.items():
        if k in ("x", "c", "ctx", "c_ctx"):
            continue
        shared[k] = np.ascontiguousarray(np.asarray(v), dtype=np.float32)
    shared["dn_a_log"] = shared["dn_a_log"].reshape(NL, 16)
    shared["dn_dt_bias"] = shared["dn_dt_bias"].reshape(NL, 16)
    shared["c_ctx"] = np.ascontiguousarray(np.asarray(inputs["c_ctx"]), dtype=np.float32)
    shared.update(consts)
    n_cores = 8 if bsz <= 4 else bsz
    hot = [0, 1, 4, 5][:bsz] if bsz <= 4 else list(range(bsz))
    zx = np.zeros_like(np.ascontiguousarray(x[0], dtype=np.float32))
    zc = np.zeros((D,), np.float32)
    zctx = np.zeros((L, D), np.float32)
    in_maps = []
    for core in range(n_cores):
        m = dict(shared)
        if core in hot:
            i = hot.index(core)
            m["x"] = np.ascontiguousarray(x[i], dtype=np.float32)
            m["c"] = np.ascontiguousarray(np.asarray(inputs["c"])[i], dtype=np.float32)
            m["ctx"] = np.ascontiguousarray(np.asarray(inputs["ctx"])[i], dtype=np.float32)
        else:
            m["x"], m["c"], m["ctx"] = zx, zc, zctx
        in_maps.append(m)
    res = run_bass_kernel_spmd(b.nc, in_maps, core_ids=list(range(n_cores)))
    return np.stack([np.asarray(res.results[core]["y"], dtype=np.float32) for core in hot], axis=0)
```

```python
import numpy as np
from contextlib import ExitStack
import concourse.bass as bass
import concourse.mybir as mybir
from concourse.bass_utils import run_bass_kernel_spmd

F32 = mybir.dt.float32
BF16 = mybir.dt.bfloat16
AF = mybir.ActivationFunctionType
ALU = mybir.AluOpType
AX = mybir.AxisListType

ENGS = ("pe", "act", "dve", "pool", "sp")
EPOCH = 30000
NDMASEM = 40


import types


def freeze(fn):
    if fn is None or fn.__closure__ is None:
        return fn
    cells = tuple(types.CellType(c.cell_contents) for c in fn.__closure__)
    return types.FunctionType(fn.__code__, fn.__globals__, fn.__name__, fn.__defaults__, cells)


class T:
    __slots__ = ("t", "name", "w", "r", "rd")

    def __init__(self, t, name=""):
        self.t = t
        self.name = name
        self.w = None
        self.r = {}
        self.rd = {}

    def sub(self, name=""):
        return T(self.t, name or self.name)

    def __getitem__(self, idx):
        return self.t[idx]


class Op:
    __slots__ = ("eng", "seq", "fn", "deps", "signal", "isdma", "sem", "val", "dsem", "dval", "dprev", "dslot")

    def __init__(self, eng, seq, fn, isdma):
        self.eng = eng
        self.seq = seq
        self.fn = fn
        self.deps = []
        self.signal = False
        self.isdma = isdma
        self.sem = None
        self.val = 0
        self.dsem = None
        self.dval = 0
        self.dprev = 0


class Prog:
    def __init__(self, nc, same_raw=True):
        self.nc = nc
        self.ops = {e: [] for e in ENGS}
        self.seen = {e: {} for e in ENGS}
        self.seen_dma = {e: set() for e in ENGS}
        self.pending = {e: [] for e in ENGS}
        self.ndma = 0
        self.same_raw = same_raw
        self.out_dmas = []
        self.all_dmas_unwaited = []
        self.dcount = {e: 0 for e in ENGS}
        self.lastd = {}

    def _need(self, op, tgt):
        if tgt is None or tgt is op:
            return
        e = op.eng
        if tgt.isdma:
            if id(tgt) in self.seen_dma[e]:
                return
            self.seen_dma[e].add(id(tgt))
            op.deps.append(tgt)
            return
        if tgt.eng == e:
            if e == "pe" or not self.same_raw:
                return
        if self.seen[e].get(tgt.eng, 0) >= tgt.seq:
            return
        self.seen[e][tgt.eng] = tgt.seq
        tgt.signal = True
        op.deps.append(tgt)

    def _record(self, eng, fn, reads, writes, isdma=False, raw_only_same=True):
        ops = self.ops[eng]
        op = Op(eng, len(ops) + 1, fn, isdma)
        if isdma:
            op.dslot = self.dcount[eng] % NDMASEM
            self.dcount[eng] += 1
            self.lastd[(eng, op.dslot)] = op
        for t in self.pending[eng]:
            self._need(op, t)
        self.pending[eng] = []
        for b in reads:
            self._need(op, b.w)
        for b in writes:
            self._need(op, b.w)
            for re_, r in b.r.items():
                if re_ == eng and not isdma:
                    continue
                self._need(op, r)
            for r in b.rd.values():
                self._need(op, r)
        for b in reads:
            if isdma:
                b.rd[(eng, op.dslot)] = op
            else:
                b.r[eng] = op
        for b in writes:
            b.w = op
            b.r = {}
            b.rd = {}
        ops.append(op)
        return op

    def op(self, eng, fn, reads=(), writes=()):
        return self._record(eng, freeze(fn), reads, writes)

    def dma(self, out_ap, in_ap, reads=(), writes=(), eng="sp", is_out=False, **kw):
        def fn(e, out_ap=out_ap, in_ap=in_ap, kw=kw):
            return e.dma_start(out=out_ap, in_=in_ap, **kw)
        op = self._record(eng, fn, reads, writes, isdma=True)
        op.signal = True
        self.ndma += 1
        if is_out:
            self.out_dmas.append(op)
        return op

    def barrier(self, label=None):
        import sys as _sys
        if not hasattr(self, "marks"):
            self.marks = []
        self.marks.append((label or _sys._getframe(1).f_code.co_name, {e: len(self.ops[e]) for e in ENGS}))
        lasts = []
        for e in ENGS:
            for o in reversed(self.ops[e]):
                if not o.isdma:
                    lasts.append(o)
                    break
        dmas = list(self.lastd.values())
        for e in ENGS:
            self.pending[e] = self.pending[e] + lasts + dmas

    def emit(self, stack):
        nc = self.nc
        self.barrier()
        fin = {}
        for e in ENGS:
            op = Op(e, len(self.ops[e]) + 1, None, False)
            for t in self.pending[e]:
                self._need(op, t)
            self.ops[e].append(op)
        for e in ENGS:
            cnt = 0
            sems = []
            for op in self.ops[e]:
                if op.isdma or not op.signal:
                    continue
                ep = cnt // EPOCH
                while len(sems) <= ep:
                    sems.append(stack.enter_context(nc.semaphore(f"c_{e}_{len(sems)}")))
                cnt += 1
                op.sem = sems[ep]
                op.val = cnt - ep * EPOCH
        dpool = {}
        duse = {}
        for e in ENGS:
            k = 0
            for op in self.ops[e]:
                if not op.isdma:
                    continue
                if e not in dpool:
                    dpool[e] = [stack.enter_context(nc.semaphore(f"d_{e}_{i}")) for i in range(NDMASEM)]
                    duse[e] = [0] * NDMASEM
                i = op.dslot
                k += 1
                op.dsem = dpool[e][i]
                op.dprev = 16 * duse[e][i]
                duse[e][i] += 1
                op.dval = 16 * duse[e][i]
        block = stack.enter_context(nc.Block())

        def run(eng, e):
            for op in self.ops[e]:
                for d in op.deps:
                    if d.isdma:
                        eng.wait_ge(d.dsem, d.dval)
                    else:
                        eng.wait_ge(d.sem, d.val)
                if op.fn is None:
                    continue
                if op.isdma:
                    if op.dprev:
                        eng.wait_ge(op.dsem, op.dprev)
                    op.fn(eng).then_inc(op.dsem, 16)
                else:
                    ins = op.fn(eng)
                    if op.signal:
                        ins.then_inc(op.sem, 1)

        block.tensor(lambda eng: run(eng, "pe"))
        block.scalar(lambda eng: run(eng, "act"))
        block.vector(lambda eng: run(eng, "dve"))
        block.gpsimd(lambda eng: run(eng, "pool"))
        block.sync(lambda eng: run(eng, "sp"))

    def stats(self):
        return {e: len(self.ops[e]) for e in ENGS}

import math

class Cfg:
    def __init__(self, D=2048, N=8192, L=256, DFF=5632, NL=2):
        self.D, self.N, self.L, self.DFF, self.NL = D, N, L, DFF, NL
        self.T = N + L
        self.KC = D // 128
        self.NT = self.T // 128
        self.IN_W = 1024 + 256 + 256 + 1024 + 1024 + 1024 + 3072 + 1024 + 16 + 16 + 3 * D

GRID_W = 64
ALPHA = None
LN_EPS = 1e-6

O_QA, O_KA, O_VA, O_QD, O_KD, O_VD, O_QKV, O_Z, O_A, O_B, O_G = (
    0, 1024, 1280, 1536, 2560, 3584, 4608, 7680, 8704, 8720, 8736)

M_ID, M_TRILS, M_TRIUI, M_TRIUS, M_TRILI, M_ONES, M_BD16, M_OFF32, M_OFF64, M_OFF128, M_NEG, M_PA, M_PD = range(13)
NMASK = 13


def rope_tables(n, dim):
    rows = n // GRID_W
    row = np.repeat(np.arange(rows, dtype=np.float32), GRID_W)
    col = np.tile(np.arange(GRID_W, dtype=np.float32), rows)
    axis_dim = dim // 2
    inv_freq = (10000.0 ** (-np.arange(0, axis_dim, 2, dtype=np.float32) / axis_dim)).astype(np.float32)
    ang_r = row[:, None] * inv_freq[None]
    ang_c = col[:, None] * inv_freq[None]
    ang = np.concatenate([ang_r, ang_r, ang_c, ang_c], axis=-1).astype(np.float32)
    q = dim // 4
    sgn = np.concatenate([-np.ones(q), np.ones(q), -np.ones(q), np.ones(q)]).astype(np.float32)
    return np.cos(ang).T.astype(np.float32), (np.sin(ang) * sgn[None]).T.astype(np.float32)


def make_consts(cfg):
    T, L = cfg.T, cfg.L
    def tab(dim, rep, scale):
        c, s = rope_tables(cfg.N, dim)
        out = np.zeros((4, 128, T), np.float32)
        c = np.tile(c, (rep, 1)); s = np.tile(s, (rep, 1))
        out[0, :, :L] = scale; out[0, :, L:] = c * scale
        out[1, :, L:] = s * scale
        out[2, :, :L] = 1.0; out[2, :, L:] = c
        out[3, :, L:] = s
        return out
    ropeA = tab(128, 1, 128 ** -0.5)
    ropeD = tab(64, 2, 64 ** -0.5)
    p = np.arange(128)[:, None]; f = np.arange(128)[None, :]
    m = np.zeros((NMASK, 128, 128), np.float32)
    m[M_ID] = (p == f); m[M_TRILS] = (p > f); m[M_TRIUI] = (p <= f); m[M_TRIUS] = (p < f); m[M_TRILI] = (p >= f)
    m[M_ONES] = 1.0
    m[M_NEG] = -1.0
    m[M_PA] = (p == (f ^ 32))
    m[M_PD] = (p == (f ^ 16))
    m[M_BD16] = (p // 16 == f // 16)
    m[M_OFF32] = (p // 32 == f // 32) & (p // 16 != f // 16)
    m[M_OFF64] = (p // 64 == f // 64) & (p // 32 != f // 32)
    m[M_OFF128] = (p // 64 != f // 64)
    return {"ropeA": ropeA, "ropeD": ropeD, "cmask": m}


def lockstep(gens, width):
    active = []
    it = iter(gens)
    done = False
    while True:
        while len(active) < width and not done:
            try:
                active.append(next(it))
            except StopIteration:
                done = True
        if not active:
            break
        for g in list(active):
            try:
                next(g)
            except StopIteration:
                active.remove(g)


class B:
    def __init__(self, cfg, dbg=()):
        self.cfg = cfg
        self.dbg = set(dbg)
        self.nc = bass.Bass("TRN2", target_bir_lowering=False)
        self.p = Prog(self.nc)
        self.dram = {}
        self.outs = []
        self.bank = 0

    def inp(self, name, shape, dt=F32):
        t = T(self.nc.dram_tensor(name, list(shape), dt, kind="ExternalInput").ap(), name)
        self.dram[name] = t
        return t

    def out(self, name, shape, dt=F32):
        t = T(self.nc.dram_tensor(name, list(shape), dt, kind="ExternalOutput").ap(), name)
        self.dram[name] = t
        self.outs.append(name)
        return t

    def scr(self, name, shape, dt):
        kind = "ExternalOutput" if name in self.dbg else "Internal"
        t = T(self.nc.dram_tensor(name, list(shape), dt, kind=kind).ap(), name)
        self.dram[name] = t
        if name in self.dbg:
            self.outs.append(name)
        return t

    def sb(self, st, name, shape, dt=F32):
        self.uid = getattr(self, "uid", 0) + 1
        name = f"{name}_{self.uid}"
        return T(st.enter_context(self.nc.sbuf_tensor(name, list(shape), dt)), name)

    def ps(self, st, name, shape, dt=F32):
        self.uid = getattr(self, "uid", 0) + 1
        name = f"{name}_{self.uid}"
        return T(st.enter_context(self.nc.psum_tensor(name, list(shape), dt)), name)

    def rsqrt(self, OUT, out_ap, IN, in_ap, eps, scale=1.0):
        p = self.p
        p.op("act", lambda e: e.activation(out=out_ap, in_=in_ap, func=AF.Sqrt, bias=self.epsc[eps][:out_ap.shape[0], :], scale=scale), reads=[IN, self.epst], writes=[OUT])
        p.op("dve", lambda e: e.reciprocal(out=out_ap, in_=out_ap), reads=[OUT], writes=[OUT])

    def declare(self):
        c = self.cfg
        D, T_, L, N, DFF, NL = c.D, c.T, c.L, c.N, c.DFF, c.NL
        i = self.inp
        i("x", [N, D]); i("c", [D]); i("ctx", [L, D]); i("c_ctx", [D])
        i("w_mod", [NL, D, 6 * D]); i("b_mod", [NL, 6 * D]); i("w_in", [NL, D, c.IN_W])
        i("wa_sink", [NL, 8])
        for n in ("df_lam_q1", "df_lam_k1", "df_lam_q2", "df_lam_k2"):
            i(n, [NL, 64])
        i("df_subln", [NL, 128]); i("dn_conv", [NL, 3, 3072]); i("dn_a_log", [NL, 16]); i("dn_dt_bias", [NL, 16])
        i("dn_norm", [NL, 128])
        i("w_branch_a", [NL, 1024, D]); i("w_branch_b", [NL, 1024, D]); i("w_branch_c", [NL, 1024, D])
        i("w_o", [NL, D, D]); i("ln1_g", [NL, D]); i("ln1_b", [NL, D])
        i("w_up", [NL, D, 2 * DFF]); i("ffn_conv_w", [NL, 3, 2 * DFF]); i("ffn_conv_b", [NL, 2 * DFF])
        i("w_down", [NL, DFF, D]); i("ln2_g", [NL, D]); i("ln2_b", [NL, D])
        i("ropeA", [4, 128, T_]); i("ropeD", [4, 128, T_]); i("cmask", [NMASK, 128, 128])
        self.out("y", [N, D])
        s = self.scr
        s("xres0", [T_, D], F32); s("xres1", [T_, D], F32); s("hT", [D, T_], BF16); s("modd", [NL, 2, 6 * D], F32)
        s("qaT", [1024, T_], BF16); s("kaT", [256, T_], BF16); s("va", [T_, 256], BF16)
        s("qdT", [1024, T_], BF16); s("kdT", [1024, T_], BF16); s("vd", [T_, 1024], BF16)
        s("gpre", [3072, T_], F32); s("zs", [T_, 1024], F32); s("ab", [T_, 32], F32); s("gtsT", [3 * D, T_], BF16)
        s("osub", [T_, D], F32)
        s("ya", [T_, 1024], BF16); s("yb", [T_, 1024], BF16); s("yc", [T_, 1024], BF16)
        s("qnT", [1024, T_], F32); s("knT", [1024, T_], F32); s("kn", [T_, 1024], F32); s("vn", [T_, 1024], F32)
        s("of", [T_, 1024], F32); s("mT", [D, T_], BF16); s("gT", [DFF, T_], BF16)
        s("yaT", [1024, T_], BF16); s("ybT", [1024, T_], BF16); s("ycT", [1024, T_], BF16)

    def consts(self, st):
        p = self.p
        cm = self.dram["cmask"]
        self.mask = self.sb(st, "mask", [128, NMASK, 128], F32)
        p.dma(self.mask[:], cm.t.rearrange("m p f -> p m f"), reads=[cm], writes=[self.mask])
        self.identb = self.sb(st, "identb", [128, 128], BF16)
        p.op("dve", lambda e: e.tensor_copy(out=self.identb[:], in_=self.mask[:, M_ID, :]), reads=[self.mask], writes=[self.identb])
        self.permb = self.sb(st, "permb", [128, 2, 128], BF16)
        p.op("dve", lambda e: e.tensor_copy(out=self.permb[:, 0, :], in_=self.mask[:, M_PA, :]), reads=[self.mask], writes=[self.permb])
        p.op("dve", lambda e: e.tensor_copy(out=self.permb[:, 1, :], in_=self.mask[:, M_PD, :]), reads=[self.mask], writes=[self.permb])
        self.epst = self.sb(st, "epst", [128, 4], F32)
        self.epsc = {}
        for i_, v_ in enumerate((1e-6, 0.0, 1.0)):
            p.op("pool", lambda e, i_=i_, v_=v_: e.memset(self.epst[:, i_:i_ + 1], v_), writes=[self.epst])
            self.epsc[v_] = self.epst[:, i_:i_ + 1]
        c = self.cfg
        self.modc = self.sb(st, "modc", [128, c.NL, 2, 6 * c.KC], F32)

    def stage_mod(self):
        c, p, nc = self.cfg, self.p, self.nc
        KC, D = c.KC, c.D
        NJ = 6 * KC
        wm, bm = self.dram["w_mod"], self.dram["b_mod"]
        with ExitStack() as st:
            craw = self.sb(st, "craw", [128, KC, 2], F32)
            sc = self.sb(st, "sc", [128, KC, 2], F32)
            bcol = self.sb(st, "bcol", [128, c.NL, NJ], F32)
            ps = [self.ps(st, f"mps{i}", [128, 4, 2], F32) for i in range(2)]
            pst = self.ps(st, "mpst", [128, 128], F32)
            wt = [self.sb(st, f"wmt{i}", [128, KC, 512], F32) for i in range(2)]
            tr = self.sb(st, "mtr", [NJ, 128], F32)
            cc, cx = self.dram["c"], self.dram["c_ctx"]
            p.dma(craw[:, :, 0], cc.t.rearrange("(kc p) -> p kc", p=128), reads=[cc], writes=[craw], allow_slow_non_contiguous=True)
            p.dma(craw[:, :, 1], cx.t.rearrange("(kc p) -> p kc", p=128), reads=[cx], writes=[craw], allow_slow_non_contiguous=True)
            for l in range(c.NL):
                p.dma(bcol[:, l, :], bm.t[l].rearrange("(j p) -> p j", p=128), reads=[bm], writes=[bcol], allow_slow_non_contiguous=True)
            p.op("act", lambda e: e.activation(out=sc[:], in_=craw[:], func=AF.Silu), reads=[craw], writes=[sc])
            k = 0
            for l in range(c.NL):
                wv = wm.t[l].rearrange("(kc p) n -> p kc n", p=128)
                for j0 in range(0, NJ, 4):
                    w = wt[k % 2]; pp = ps[k % 2]; k += 1
                    p.dma(w[:], wv[:, :, j0 * 128:(j0 + 4) * 128], reads=[wm], writes=[w])
                    for jj in range(4):
                        for kc in range(KC):
                            p.op("pe", lambda e, w=w, pp=pp, jj=jj, kc=kc: e.matmul(
                                pp[:, jj, :], w[:, kc, jj * 128:(jj + 1) * 128], sc[:, kc, :],
                                start=(kc == 0), stop=(kc == KC - 1)), reads=[w, sc], writes=[pp])
                    for s_ in range(2):
                        p.op("dve", lambda e, pp=pp, s_=s_, l=l, j0=j0: e.tensor_tensor(
                            out=self.modc[:, l, s_, j0:j0 + 4], in0=pp[:, :, s_], in1=bcol[:, l, j0:j0 + 4], op=ALU.add),
                            reads=[pp, bcol], writes=[self.modc])
                for s_ in range(2):
                    for v in (1, 4):
                        p.op("dve", lambda e, l=l, s_=s_, v=v: e.tensor_scalar_add(
                            out=self.modc[:, l, s_, v * KC:(v + 1) * KC], in0=self.modc[:, l, s_, v * KC:(v + 1) * KC], scalar1=1.0),
                            reads=[self.modc], writes=[self.modc])
                md = self.dram["modd"]
                for s_ in range(2):
                    p.op("pe", lambda e, l=l, s_=s_: e.matmul(pst[0:NJ, :], self.modc[:, l, s_, :], self.mask[:, M_ID, :], start=True, stop=True),
                         reads=[self.modc, self.mask], writes=[pst])
                    p.op("dve", lambda e: e.tensor_copy(out=tr[:], in_=pst[0:NJ, :]), reads=[pst], writes=[tr])
                    p.dma(md.t[l, s_].rearrange("(j p) -> j p", p=128), tr[:], reads=[tr], writes=[md])
        p.barrier()

    def stage_ln(self, l, comb, ada, final=False, src_inputs=False, skip_ctx=False):
        c, p = self.cfg, self.p
        D, KC, L = c.D, c.KC, c.L
        alpha = (2 * c.NL) ** 0.25
        xcur = getattr(self, "xcur", 0)
        xres, xdst = self.dram[f"xres{xcur}"], self.dram[f"xres{1 - xcur}"]
        osub, hT, md = self.dram["osub"], self.dram["hT"], self.dram["modd"]
        if comb and not final:
            self.xcur = 1 - xcur
        nch = (D + 511) // 512
        with ExitStack() as st:
            xt = [self.sb(st, f"lxt{i}", [128, D], F32) for i in range(4)]
            if comb:
                ot = [self.sb(st, f"lot{i}", [128, D], F32) for i in range(4)]
                gate = self.sb(st, "lgate", [128, 2, D], F32)
                gB = self.sb(st, "lgB", [128, D], F32)
                bB = self.sb(st, "lbB", [128, D], F32)
                gv = comb[0]
                for s_ in range(2):
                    p.dma(gate[:, s_, :], md.t[l, s_, gv * D:(gv + 1) * D].partition_broadcast(128), reads=[md], writes=[gate])
                gd, bd = self.dram[comb[1]], self.dram[comb[2]]
                p.dma(gB[:], gd.t[l].partition_broadcast(128), reads=[gd], writes=[gB])
                p.dma(bB[:], bd.t[l].partition_broadcast(128), reads=[bd], writes=[bB])
            stt = [self.sb(st, f"lst{i}", [128, nch, 6], F32) for i in range(4)]
            mv = [self.sb(st, f"lmv{i}", [128, 2], F32) for i in range(4)]
            rs = [self.sb(st, f"lrs{i}", [128, 1], F32) for i in range(4)]
            if ada:
                xb = [self.sb(st, f"lxb{i}", [128, D], BF16) for i in range(4)]
                ht = [self.sb(st, f"lht{i}", [128, KC, 128], BF16) for i in range(4)]
                pt = [self.ps(st, f"lpt{i}", [128, KC, 128], BF16) for i in range(4)]
            def tile(i):
                isctx = i * 128 < L
                s_ = 1 if isctx else 0
                b = i % 4
                X = xt[b]
                if src_inputs:
                    src = self.dram["ctx"] if isctx else self.dram["x"]
                    r0 = i * 128 if isctx else i * 128 - L
                else:
                    src = xres; r0 = i * 128
                p.dma(X[:], src.t[r0:r0 + 128, :], reads=[src], writes=[X])

                def lnstats(X, b):
                    S, MV, R = stt[b], mv[b], rs[b]
                    for ch in range(nch):
                        p.op("dve", lambda e, ch=ch: e.bn_stats(out=S[:, ch, :], in_=X[:, ch * 512:min(D, (ch + 1) * 512)]), reads=[X], writes=[S])
                    p.op("dve", lambda e: e.bn_aggr(out=MV[:], in_=S[:].rearrange("p a b -> p (a b)")), reads=[S], writes=[MV])
                    self.rsqrt(R, R[:], MV, MV[:, 1:2], LN_EPS)
                    return MV, R
                if comb:
                    O = ot[b]
                    p.dma(O[:], osub.t[i * 128:(i + 1) * 128, :], reads=[osub], writes=[O])
                    yield
                    p.op("pool", lambda e, O=O, s_=s_: e.tensor_tensor(out=O[:], in0=O[:], in1=gate[:, s_, :], op=ALU.mult), reads=[O, gate], writes=[O])
                    yield
                    p.op("dve", lambda e, O=O, X=X: e.scalar_tensor_tensor(out=X[:], in0=X[:], scalar=alpha, in1=O[:], op0=ALU.mult, op1=ALU.add), reads=[X, O], writes=[X])
                    MV, R = lnstats(X, b)
                    p.op("dve", lambda e, X=X, MV=MV, R=R: e.tensor_scalar(out=X[:], in0=X[:], scalar1=MV[:, 0:1], scalar2=R[:, 0:1], op0=ALU.subtract, op1=ALU.mult), reads=[X, MV, R], writes=[X])
                    yield
                    p.op("pool", lambda e, X=X: e.tensor_tensor(out=X[:], in0=X[:], in1=gB[:], op=ALU.mult), reads=[X, gB], writes=[X])
                    p.op("pool", lambda e, X=X: e.tensor_tensor(out=X[:], in0=X[:], in1=bB[:], op=ALU.add), reads=[X, bB], writes=[X])
                    if final:
                        if not isctx:
                            y = self.dram["y"]
                            p.dma(y.t[i * 128 - L:(i + 1) * 128 - L, :], X[:], reads=[X], writes=[y], is_out=True)
                    else:
                        p.dma(xdst.t[i * 128:(i + 1) * 128, :], X[:], reads=[X], writes=[xdst])
                if ada:
                    la, sv = ada
                    yield
                    MV, R = lnstats(X, b)
                    XB, HT, PT = xb[b], ht[b], pt[b]
                    p.op("dve", lambda e, X=X, XB=XB, MV=MV, R=R: e.tensor_scalar(out=XB[:], in0=X[:], scalar1=MV[:, 0:1], scalar2=R[:, 0:1], op0=ALU.subtract, op1=ALU.mult), reads=[X, MV, R], writes=[XB])
                    yield
                    for kc in range(KC):
                        p.op("pe", lambda e, kc=kc, XB=XB, PT=PT: e.transpose(PT[:, kc, :], XB[:, kc * 128:(kc + 1) * 128], self.identb[:]), reads=[XB, self.identb], writes=[PT])
                    yield
                    for kc in range(KC):
                        p.op("act", lambda e, kc=kc, HT=HT, PT=PT, s_=s_: e.activation(
                            out=HT[:, kc, :], in_=PT[:, kc, :], func=AF.Identity,
                            bias=self.modc[:, la, s_, sv * KC + kc:sv * KC + kc + 1],
                            scale=self.modc[:, la, s_, (sv + 1) * KC + kc:(sv + 1) * KC + kc + 1]),
                            reads=[PT, self.modc], writes=[HT])
                    p.dma(hT.t.rearrange("(kc p) t -> p kc t", p=128)[:, :, i * 128:(i + 1) * 128], HT[:], reads=[HT], writes=[hT])
            lockstep((tile(i) for i in range(c.NT) if not (skip_ctx and i * 128 < L)), 3)
        p.barrier()


def _proj(self, actT, K, W, wl, sections, tok0, tok1, NB=512, TG=512, tag="pj"):
    c, p = self.cfg, self.p
    KCs = K // 128
    wv = wl.rearrange("(kc p) n -> p kc n", p=128)
    av = actT.t.rearrange("(kc p) t -> p kc t", p=128)
    with ExitStack() as st:
        wbuf = [self.sb(st, f"{tag}w{i}", [128, KCs, NB], BF16) for i in range(2)]
        abuf = [self.sb(st, f"{tag}a{i}", [128, KCs, TG], BF16) for i in range(2)]
        nbank = 6
        banks = [self.ps(st, f"{tag}ps{i}", [128, 512], F32) for i in range(nbank)]
        self.pj_st = st
        for s in sections:
            if s.get("init"):
                s["init"](st)
        wi = 0; ai = 0; bi = 0
        bstate = {"bi": 0}

        def _alloc():
            b_ = banks[bstate["bi"] % nbank]; bstate["bi"] += 1
            return b_
        self.pj_alloc = _alloc
        for s in sections:
            mode = s["mode"]
            NBs = NB
            for nb0 in range(0, s["n"], NBs):
                nb = min(NBs, s["n"] - nb0)
                wb = wbuf[wi % 2]
                for k0_ in range(0, KCs, 16):
                    k1_ = min(KCs, k0_ + 16)
                    p.dma(wb[:, k0_:k1_, :nb], wv[:, k0_:k1_, s["c0"] + nb0:s["c0"] + nb0 + nb], reads=[W], writes=[wb], eng="pool")
                wi += 1
                for g0 in range(tok0, tok1, TG):
                    ntok = min(TG, tok1 - g0)
                    ab = abuf[ai % 2]; ai += 1
                    for k0_ in range(0, KCs, 16):
                        k1_ = min(KCs, k0_ + 16)
                        p.dma(ab[:, k0_:k1_, :ntok], av[:, k0_:k1_, g0:g0 + ntok], reads=[actT], writes=[ab])
                    if s.get("pre"):
                        s["pre"](g0, ntok)
                    for sb0 in range(0, nb, 512):
                        sbn = min(512, nb - sb0)
                        if mode == "TM":
                            for tt in range(ntok // 128):
                                ps = _alloc()
                                for kc in range(KCs):
                                    p.op("pe", lambda e, ps=ps, ab=ab, wb=wb, kc=kc, tt=tt: e.matmul(
                                        ps[:, :sbn], ab[:, kc, tt * 128:(tt + 1) * 128], wb[:, kc, sb0:sb0 + sbn],
                                        start=(kc == 0), stop=(kc == KCs - 1)), reads=[ab, wb], writes=[ps])
                                s["epi"](ps, g0 + tt * 128, nb0 + sb0, sbn)
                        else:
                            for cc in range(sb0 // 128, (sb0 + sbn) // 128):
                                ps = _alloc()
                                for kc in range(KCs):
                                    p.op("pe", lambda e, ps=ps, ab=ab, wb=wb, kc=kc, cc=cc, ntok=ntok: e.matmul(
                                        ps[:, :ntok], wb[:, kc, cc * 128:(cc + 1) * 128], ab[:, kc, :ntok],
                                        start=(kc == 0), stop=(kc == KCs - 1)), reads=[ab, wb], writes=[ps])
                                s["epi"](ps, None, g0, ntok, nb0 + cc * 128)
            if s.get("flush"):
                s["flush"]()
    p.barrier()
B.proj = _proj


def _stage_win(self, l, tok0=0):
    c, p = self.cfg, self.p
    D, T_ = c.D, c.T
    d = self.dram
    W = d["w_in"]
    state = {}

    def init(st):
        state["of"] = [self.sb(st, f"wiof{i}", [128, 512], F32) for i in range(3)]
        state["ob"] = [self.sb(st, f"wiob{i}", [128, 512], BF16) for i in range(3)]
        state["t1"] = [self.sb(st, f"wit1{i}", [128, 512], F32) for i in range(2)]
        state["t2"] = [self.sb(st, f"wit2{i}", [128, 512], F32) for i in range(2)]
        state["cos"] = [self.sb(st, f"wicos{i}", [128, 512], F32) for i in range(3)]
        state["sin"] = [self.sb(st, f"wisin{i}", [128, 512], F32) for i in range(3)]
        state["qb"] = [self.sb(st, f"wiqb{i}", [128, 512], BF16) for i in range(3)]
        state["k"] = 0; state["kc"] = 0; state["kq"] = 0; state["pend"] = None

    def tm_epi(dst, dcol0, func, bf):
        def epi(ps, tok, nb0, nb):
            k = state["k"]; state["k"] += 1
            o = (state["ob"] if bf else state["of"])[k % 3]
            p.op("act", lambda e: e.activation(out=o[:, :nb], in_=ps[:, :nb], func=func), reads=[ps], writes=[o])
            p.dma(dst.t[tok:tok + 128, dcol0 + nb0:dcol0 + nb0 + nb], o[:, :nb], reads=[o], writes=[dst])
        return epi

    def rope_pre(tabname, idx):
        tab = d[tabname]
        def pre(g0, ntok):
            k = state["kc"]; state["kc"] += 1
            cs, sn = state["cos"][k % 3], state["sin"][k % 3]
            p.dma(cs[:, :ntok], tab.t[idx, :, g0:g0 + ntok], reads=[tab], writes=[cs])
            p.dma(sn[:, :ntok], tab.t[idx + 1, :, g0:g0 + ntok], reads=[tab], writes=[sn])
            state["cs"] = (cs, sn)
        return pre

    def rope_finish():
        it = state["pend"]
        if it is None:
            return
        state["pend"] = None
        ps, qb, cs, sn, g0, ntok, c0, dst, which = it
        k = state["k"]; state["k"] += 1
        t1, t2, o = state["t1"][k % 2], state["t2"][k % 2], state["ob"][k % 3]
        ps2 = self.pj_alloc()
        p.op("pe", lambda e: e.matmul(ps2[:, :ntok], self.permb[:, which, :], qb[:, :ntok], start=True, stop=True), reads=[self.permb, qb], writes=[ps2])
        p.op("dve", lambda e: e.tensor_tensor(out=t1[:, :ntok], in0=ps[:, :ntok], in1=cs[:, :ntok], op=ALU.mult), reads=[ps, cs], writes=[t1])
        p.op("dve", lambda e: e.tensor_tensor(out=t2[:, :ntok], in0=ps2[:, :ntok], in1=sn[:, :ntok], op=ALU.mult), reads=[ps2, sn], writes=[t2])
        p.op("pool", lambda e: e.tensor_tensor(out=o[:, :ntok], in0=t1[:, :ntok], in1=t2[:, :ntok], op=ALU.add), reads=[t1, t2], writes=[o])
        p.dma(dst.t[c0:c0 + 128, g0:g0 + ntok], o[:, :ntok], reads=[o], writes=[dst])

    def rope_epi(dst, which):
        def epi(ps, ps2, g0, ntok, c0):
            rope_finish()
            kq = state["kq"]; state["kq"] += 1
            qb = state["qb"][kq % 3]
            cs, sn = state["cs"]
            p.op("dve", lambda e: e.tensor_copy(out=qb[:, :ntok], in_=ps[:, :ntok]), reads=[ps], writes=[qb])
            state["pend"] = (ps, qb, cs, sn, g0, ntok, c0, dst, which)
        return epi

    def fm_epi(dst, func=AF.Copy, bf=False):
        def epi(ps, ps2, g0, ntok, c0):
            k = state["k"]; state["k"] += 1
            o = (state["ob"] if bf else state["of"])[k % 3]
            p.op("act", lambda e: e.activation(out=o[:, :ntok], in_=ps[:, :ntok], func=func), reads=[ps], writes=[o])
            p.dma(dst.t[c0:c0 + 128, g0:g0 + ntok], o[:, :ntok], reads=[o], writes=[dst])
        return epi

    secs = [
        dict(c0=O_QA, n=1024, mode="FM", rope="A", pre=rope_pre("ropeA", 0), epi=rope_epi(d["qaT"], 0), flush=rope_finish, init=init),
        dict(c0=O_KA, n=256, mode="FM", rope="A", pre=rope_pre("ropeA", 2), epi=rope_epi(d["kaT"], 0), flush=rope_finish),
        dict(c0=O_QD, n=1024, mode="FM", rope="D", pre=rope_pre("ropeD", 0), epi=rope_epi(d["qdT"], 1), flush=rope_finish),
        dict(c0=O_KD, n=1024, mode="FM", rope="D", pre=rope_pre("ropeD", 2), epi=rope_epi(d["kdT"], 1), flush=rope_finish),
        dict(c0=O_QKV, n=3072, mode="FM", epi=fm_epi(d["gpre"])),
        dict(c0=O_VA, n=256, mode="TM", epi=tm_epi(d["va"], 0, AF.Copy, True)),
        dict(c0=O_VD, n=1024, mode="TM", epi=tm_epi(d["vd"], 0, AF.Copy, True)),
        dict(c0=O_Z, n=1024, mode="TM", epi=tm_epi(d["zs"], 0, AF.Silu, False)),
        dict(c0=O_A, n=32, mode="TM", epi=tm_epi(d["ab"], 0, AF.Copy, False)),
        dict(c0=O_G, n=3 * D, mode="FM", epi=fm_epi(d["gtsT"], AF.Sigmoid, True)),
    ]
    self.proj(d["hT"], D, W, W.t[l], secs, tok0, T_, NB=1024, tag="wi")
B.stage_win = _stage_win


def _stage_tm2fm(self, src, dstT, C, tok0=0):
    c, p = self.cfg, self.p
    nck = C // 128
    with ExitStack() as st:
        xt = [self.sb(st, f"tfx{i}", [128, C], BF16) for i in range(2)]
        ot = [self.sb(st, f"tfo{i}", [128, nck, 128], BF16) for i in range(2)]
        pt = [self.ps(st, f"tfp{i}", [128, nck, 128], BF16) for i in range(2)]
        for i in range(tok0 // 128, c.NT):
            X, O, P = xt[i % 2], ot[i % 2], pt[i % 2]
            p.dma(X[:], src.t[i * 128:(i + 1) * 128, :], reads=[src], writes=[X])
            for k in range(nck):
                p.op("pe", lambda e, k=k, X=X, P=P: e.transpose(P[:, k, :], X[:, k * 128:(k + 1) * 128], self.identb[:]), reads=[X, self.identb], writes=[P])
            p.op("act", lambda e, O=O, P=P: e.copy(out=O[:], in_=P[:]), reads=[P], writes=[O])
            p.dma(dstT.t.rearrange("(k p) t -> p k t", p=128)[:, :, i * 128:(i + 1) * 128], O[:], reads=[O], writes=[dstT])
    p.barrier()
B.stage_tm2fm = _stage_tm2fm


def _stage_diff(self, l, ctx_out):
    c, p = self.cfg, self.p
    T_, L, NT = c.T, c.L, c.NT
    d = self.dram
    qdT, kdT, vd, yb = d["qdT"], d["kdT"], d["vd"], d["yb"]
    lam_init = 0.8 - 0.6 * math.exp(-0.3 * l)
    with ExitStack() as st:
        lv = self.sb(st, "dflv", [128, 4, 64], F32)
        for i, n in enumerate(("df_lam_q1", "df_lam_k1", "df_lam_q2", "df_lam_k2")):
            p.dma(lv[:, i, :], d[n].t[l].partition_broadcast(128), reads=[d[n]], writes=[lv])
        lp = self.sb(st, "dflp", [128, 2, 64], F32)
        ls = self.sb(st, "dfls", [128, 2], F32)
        nlam = self.sb(st, "dfnl", [128, 1], F32)
        p.op("dve", lambda e: e.tensor_tensor(out=lp[:, 0, :], in0=lv[:, 0, :], in1=lv[:, 1, :], op=ALU.mult), reads=[lv], writes=[lp])
        p.op("dve", lambda e: e.tensor_tensor(out=lp[:, 1, :], in0=lv[:, 2, :], in1=lv[:, 3, :], op=ALU.mult), reads=[lv], writes=[lp])
        p.op("dve", lambda e: e.reduce_sum(out=ls[:], in_=lp[:], axis=AX.X), reads=[lp], writes=[ls])
        p.op("act", lambda e: e.activation(out=ls[:], in_=ls[:], func=AF.Exp), reads=[ls], writes=[ls])
        p.op("dve", lambda e: e.tensor_tensor(out=nlam[:], in0=ls[:, 1:2], in1=ls[:, 0:1], op=ALU.subtract), reads=[ls], writes=[nlam])
        p.op("dve", lambda e: e.tensor_scalar_add(out=nlam[:], in0=nlam[:], scalar1=-lam_init), reads=[nlam], writes=[nlam])
        sub = self.sb(st, "dfsub", [128, 128], F32)
        p.dma(sub[:], d["df_subln"].t[l].partition_broadcast(128), reads=[d["df_subln"]], writes=[sub])
        p.op("dve", lambda e: e.tensor_scalar_mul(out=sub[:], in0=sub[:], scalar1=1.0 - lam_init), reads=[sub], writes=[sub])

        kT = [self.sb(st, f"dfk{i}", [128, T_], BF16) for i in range(2)]
        vx = [self.sb(st, f"dfv{i}", [128, NT, 132], BF16) for i in range(2)]
        for i in range(2):
            p.op("pool", lambda e, i=i: e.memset(vx[i][:, :, 128:129], 1.0), writes=[vx[i]])
        qt = [self.sb(st, f"dfq{i}", [128, 512], BF16) for i in range(2)]
        pe_ = [[self.sb(st, f"dfp{j}{i}", [128, 512], BF16) for i in range(2)] for j in range(2)]
        sb_ = [[self.ps(st, f"dfs{j}{i}", [128, 512], F32) for i in range(2)] for j in range(2)]
        ob_ = [[self.ps(st, f"dfo{j}{i}", [128, 2, 256], F32) for i in range(2)] for j in range(2)]
        r12 = [self.sb(st, f"dfr{i}", [128, 2], F32) for i in range(2)]
        t1 = [self.sb(st, f"dft{i}", [128, 128], F32) for i in range(2)]
        o_ = [self.sb(st, f"dfoo{i}", [128, 128], F32) for i in range(2)]
        sq = [self.sb(st, f"dfsq{i}", [128, 128], F32) for i in range(2)]
        ss = [self.sb(st, f"dfss{i}", [128, 1], F32) for i in range(2)]
        yo = [self.sb(st, f"dfyo{i}", [128, 128], BF16) for i in range(2)]
        qi = 0; si = 0; ei = 0
        groups = []
        if ctx_out:
            groups.append((0, L, 0, L // 128))
        for g0 in range(L, T_, 512):
            groups.append((g0, min(512, T_ - g0), 0, NT))
        for h in range(8):
            K, V = kT[h % 2], vx[h % 2]
            p.dma(K[:], kdT.t[h * 128:(h + 1) * 128, :], reads=[kdT], writes=[K])
            vv_ = vd.t[:, h * 128:(h + 1) * 128].rearrange("(n p) c -> p n c", p=128)
            for n0_ in range(0, NT, 16):
                n1_ = min(NT, n0_ + 16)
                p.dma(V[:, n0_:n1_, 0:128], vv_[:, n0_:n1_, :], reads=[vd], writes=[V])
            for (g0, ntok, kt0, kt1) in groups:
                Q = qt[qi % 2]; qi += 1
                p.dma(Q[:, :ntok], qdT.t[h * 128:(h + 1) * 128, g0:g0 + ntok], reads=[qdT], writes=[Q])
                nq = ntok // 128

                def emit_S(kt):
                    for j in range(2):
                        S = sb_[j][kt % 2]
                        p.op("pe", lambda e, S=S, K=K, Q=Q, j=j, kt=kt, ntok=ntok: e.matmul(
                            S[:, :ntok], K[j * 64:(j + 1) * 64, kt * 128:(kt + 1) * 128], Q[j * 64:(j + 1) * 64, :ntok], start=True, stop=True),
                            reads=[K, Q], writes=[S])
                emit_S(kt0)
                for kt in range(kt0, kt1):
                    if kt + 1 < kt1:
                        emit_S(kt + 1)
                    for j in range(2):
                        S = sb_[j][kt % 2]; P = pe_[j][kt % 2]
                        p.op("act", lambda e, S=S, P=P, ntok=ntok: e.activation(out=P[:, :ntok], in_=S[:, :ntok], func=AF.Exp), reads=[S], writes=[P])
                        for qs in range(nq):
                            O = ob_[j][qs // 2]
                            p.op("pe", lambda e, O=O, P=P, V=V, qs=qs, kt=kt: e.matmul(
                                O[:, qs % 2, 0:129], P[:, qs * 128:(qs + 1) * 128], V[:, kt, 0:129], start=(kt == kt0 and qs % 2 == 0), stop=(kt == kt1 - 1)),
                                reads=[P, V], writes=[O])
                for qs in range(nq):
                    O1, O2 = ob_[0][qs // 2], ob_[1][qs // 2]
                    R_, T1, OO, SQ, SS, YO = r12[ei % 2], t1[ei % 2], o_[ei % 2], sq[ei % 2], ss[ei % 2], yo[ei % 2]; ei += 1
                    s2 = qs % 2
                    p.op("dve", lambda e, R_=R_, O1=O1, s2=s2: e.reciprocal(out=R_[:, 0:1], in_=O1[:, s2, 128:129]), reads=[O1], writes=[R_])
                    p.op("dve", lambda e, R_=R_, O2=O2, s2=s2: e.reciprocal(out=R_[:, 1:2], in_=O2[:, s2, 128:129]), reads=[O2], writes=[R_])
                    p.op("dve", lambda e, R_=R_: e.tensor_tensor(out=R_[:, 1:2], in0=R_[:, 1:2], in1=nlam[:], op=ALU.mult), reads=[R_, nlam], writes=[R_])
                    p.op("dve", lambda e, R_=R_, O1=O1, T1=T1, s2=s2: e.tensor_scalar_mul(out=T1[:], in0=O1[:, s2, 0:128], scalar1=R_[:, 0:1]), reads=[O1, R_], writes=[T1])
                    p.op("dve", lambda e, R_=R_, O2=O2, T1=T1, OO=OO, s2=s2: e.scalar_tensor_tensor(out=OO[:], in0=O2[:, s2, 0:128], scalar=R_[:, 1:2], in1=T1[:], op0=ALU.mult, op1=ALU.add), reads=[O2, R_, T1], writes=[OO])
                    p.op("act", lambda e, OO=OO, SQ=SQ, SS=SS: e.activation(out=SQ[:], in_=OO[:], func=AF.Square, accum_out=SS[:]), reads=[OO], writes=[SQ, SS])
                    self.rsqrt(SS, SS[:], SS, SS[:], 1e-6, scale=1.0 / 128)
                    p.op("dve", lambda e, OO=OO, SS=SS, YO=YO: e.scalar_tensor_tensor(out=YO[:], in0=OO[:], scalar=SS[:, 0:1], in1=sub[:], op0=ALU.mult, op1=ALU.mult), reads=[OO, SS, sub], writes=[YO])
                    t0 = g0 + qs * 128
                    p.dma(yb.t[t0:t0 + 128, h * 128:(h + 1) * 128], YO[:], reads=[YO], writes=[yb])
    p.barrier()
B.stage_diff = _stage_diff


def _stage_win_attn(self, l, ctx_out):
    c, p = self.cfg, self.p
    T_, L, NT = c.T, c.L, c.NT
    LT = L // 128
    d = self.dram
    qaT, kaT, va, ya = d["qaT"], d["kaT"], d["va"], d["ya"]
    with ExitStack() as st:
        es = self.sb(st, "waes", [128, 8], F32)
        p.dma(es[:], d["wa_sink"].t[l].partition_broadcast(128), reads=[d["wa_sink"]], writes=[es])
        p.op("act", lambda e: e.activation(out=es[:], in_=es[:], func=AF.Exp), reads=[es], writes=[es])
        mk = self.sb(st, "wamk", [128, 2, 4, 128], BF16)
        for h4 in range(4):
            p.op("dve", lambda e, h4=h4: e.tensor_copy(out=mk[:, 0, h4, :], in_=self.mask[:, M_TRILI, :]), reads=[self.mask], writes=[mk])
            p.op("dve", lambda e, h4=h4: e.tensor_copy(out=mk[:, 1, h4, :], in_=self.mask[:, M_TRIUI, :]), reads=[self.mask], writes=[mk])
        K = self.sb(st, "wak", [128, T_], BF16)
        V = self.sb(st, "wav", [128, NT, 132], BF16)
        p.op("pool", lambda e: e.memset(V[:, :, 128:129], 1.0), writes=[V])
        qt = [self.sb(st, f"waq{i}", [128, 4, 128], BF16) for i in range(2)]
        pp = [self.sb(st, f"wap{i}", [128, 512], BF16) for i in range(4)]
        sb_ = [self.ps(st, f"was{i}", [128, 512], F32) for i in range(4)]
        ob_ = [[self.ps(st, f"wao{i}{j}", [128, 2, 256], F32) for j in range(2)] for i in range(2)]
        rr = [self.sb(st, f"war{i}", [128, 1], F32) for i in range(4)]
        yo = [self.sb(st, f"wayo{i}", [128, 128], BF16) for i in range(4)]
        si = 0; ei = 0; bi = 0
        for g in range(2):
            p.dma(K[:], kaT.t[g * 128:(g + 1) * 128, :], reads=[kaT], writes=[K])
            vv_ = va.t[:, g * 128:(g + 1) * 128].rearrange("(n p) c -> p n c", p=128)
            for n0_ in range(0, NT, 16):
                n1_ = min(NT, n0_ + 16)
                p.dma(V[:, n0_:n1_, 0:128], vv_[:, n0_:n1_, :], reads=[va], writes=[V])
            def block(i, bi):
                Q = qt[bi % 2]; OB = ob_[bi % 2]
                SB = sb_[2 * (bi % 2):2 * (bi % 2) + 2]; PP = pp[2 * (bi % 2):2 * (bi % 2) + 2]
                for h4 in range(4):
                    h = g * 4 + h4
                    p.dma(Q[:, h4, :], qaT.t[h * 128:(h + 1) * 128, i * 128:(i + 1) * 128], reads=[qaT], writes=[Q])
                kts = [(kt, None) for kt in range(LT)]
                if i >= LT:
                    if i - 1 >= LT:
                        kts.append((i - 1, 0))
                    kts.append((i, None))
                    if i + 1 < NT:
                        kts.append((i + 1, 1))

                def emit_S(n_):
                    kt_ = kts[n_][0]
                    S_ = SB[n_ % 2]
                    p.op("pe", lambda e, S_=S_, Q=Q, kt_=kt_: e.matmul(S_[:], K[:, kt_ * 128:(kt_ + 1) * 128], Q[:].rearrange("p a b -> p (a b)"), start=True, stop=True),
                         reads=[K, Q], writes=[S_])
                yield
                emit_S(0)
                for n_, (kt, m) in enumerate(kts):
                    S = SB[n_ % 2]; P = PP[n_ % 2]
                    if n_ + 1 < len(kts):
                        emit_S(n_ + 1)
                    p.op("act", lambda e, S=S, P=P: e.activation(out=P[:], in_=S[:], func=AF.Exp), reads=[S], writes=[P])
                    if m is not None:
                        p.op("pool", lambda e, P=P, m=m: e.tensor_tensor(out=P[:], in0=P[:], in1=mk[:, m].rearrange("p a b -> p (a b)"), op=ALU.mult), reads=[P, mk], writes=[P])
                    yield
                    for h4 in range(4):
                        O = OB[h4 // 2]
                        p.op("pe", lambda e, O=O, P=P, h4=h4, kt=kt, n_=n_, nk=len(kts): e.matmul(
                            O[:, h4 % 2, 0:129], P[:, h4 * 128:(h4 + 1) * 128], V[:, kt, 0:129], start=(n_ == 0 and h4 % 2 == 0), stop=(n_ == nk - 1)),
                            reads=[P, V], writes=[O])
                yield
                for h4 in range(4):
                    h = g * 4 + h4
                    O = OB[h4 // 2]
                    R_, YO = rr[(bi * 4 + h4) % 4], yo[(bi * 4 + h4) % 4]
                    p.op("dve", lambda e, R_=R_, O=O, h4=h4, h=h: e.tensor_scalar(out=R_[:], in0=O[:, h4 % 2, 128:129], scalar1=es[:, h:h + 1], scalar2=None, op0=ALU.add), reads=[O, es], writes=[R_])
                    p.op("dve", lambda e, R_=R_: e.reciprocal(out=R_[:], in_=R_[:]), reads=[R_], writes=[R_])
                    p.op("dve", lambda e, R_=R_, O=O, YO=YO, h4=h4: e.tensor_scalar_mul(out=YO[:], in0=O[:, h4 % 2, 0:128], scalar1=R_[:, 0:1]), reads=[O, R_], writes=[YO])
                    p.dma(ya.t[i * 128:(i + 1) * 128, h * 128:(h + 1) * 128], YO[:], reads=[YO], writes=[ya])
            blocks = list(range(0 if ctx_out else LT, NT))
            lockstep((block(i, n) for n, i in enumerate(blocks)), 2)
    p.barrier()
B.stage_win_attn = _stage_win_attn


def _stage_gdn_conv(self, l):
    c, p = self.cfg, self.p
    T_, L, NT = c.T, c.L, c.NT
    d = self.dram
    gpre, qnT, knT, kn, vn = d["gpre"], d["qnT"], d["knT"], d["kn"], d["vn"]
    with ExitStack() as st:
        cw = self.sb(st, "gcw", [128, 24, 3], F32)
        for k_ in range(3):
            p.dma(cw[:, :, k_], d["dn_conv"].t[l, k_].rearrange("(ch p) -> p ch", p=128), reads=[d["dn_conv"]], writes=[cw], allow_slow_non_contiguous=True)
        G = [self.sb(st, f"gcG{i}", [128, 514], F32) for i in range(4)]
        Y = [self.sb(st, f"gcY{i}", [128, 512], F32) for i in range(4)]
        SQ = [self.sb(st, f"gcS{i}", [128, 512], F32) for i in range(4)]
        RI = [self.sb(st, f"gcR{i}", [128, 512], F32) for i in range(4)]
        YN = [self.sb(st, f"gcN{i}", [128, 512], F32) for i in range(4)]
        TT = [self.sb(st, f"gcT{i}", [128, 4, 128], F32) for i in range(4)]
        pss = [self.ps(st, f"gcps{i}", [128, 512], F32) for i in range(4)]
        ptt = [self.ps(st, f"gcpt{i}", [128, 4, 128], F32) for i in range(4)]
        ones = self.mask[:, M_ONES, :]
        ident = self.mask[:, M_ID, :]
        def iters():
            k = 0
            for ch in range(24):
                for (s0, s1) in ((0, L), (L, T_)):
                    for g0 in range(s0, s1, 512):
                        yield one(k, ch, s0, s1, g0)
                        k += 1

        def one(k, ch, s0, s1, g0):
                    kind = ch // 8
                    hh = ch % 8
                    ntok = min(512, s1 - g0)
                    g, y, sq, ri, yn, tt, ps1, pt1 = G[k % 4], Y[k % 4], SQ[k % 4], RI[k % 4], YN[k % 4], TT[k % 4], pss[k % 4], ptt[k % 4]
                    lo = max(s0, g0 - 1); hi = min(s1, g0 + ntok + 1)
                    if lo > g0 - 1:
                        p.op("pool", lambda e, g=g: e.memset(g[:, 0:1], 0.0), writes=[g])
                    if hi < g0 + ntok + 1:
                        p.op("pool", lambda e, g=g, ntok=ntok: e.memset(g[:, ntok + 1:ntok + 2], 0.0), writes=[g])
                    p.dma(g[:, lo - (g0 - 1):hi - (g0 - 1)], gpre.t[ch * 128:(ch + 1) * 128, lo:hi], reads=[gpre], writes=[g])
                    yield
                    p.op("dve", lambda e, g=g, y=y, ntok=ntok, ch=ch: e.tensor_scalar_mul(out=y[:, :ntok], in0=g[:, 0:ntok], scalar1=cw[:, ch, 0:1]), reads=[g, cw], writes=[y])
                    p.op("dve", lambda e, g=g, y=y, ntok=ntok, ch=ch: e.scalar_tensor_tensor(out=y[:, :ntok], in0=g[:, 1:ntok + 1], scalar=cw[:, ch, 1:2], in1=y[:, :ntok], op0=ALU.mult, op1=ALU.add), reads=[g, cw, y], writes=[y])
                    p.op("dve", lambda e, g=g, y=y, ntok=ntok, ch=ch: e.scalar_tensor_tensor(out=y[:, :ntok], in0=g[:, 2:ntok + 2], scalar=cw[:, ch, 2:3], in1=y[:, :ntok], op0=ALU.mult, op1=ALU.add), reads=[g, cw, y], writes=[y])
                    yield
                    p.op("act", lambda e, y=y, ntok=ntok: e.activation(out=y[:, :ntok], in_=y[:, :ntok], func=AF.Silu), reads=[y], writes=[y])
                    src = y
                    if kind < 2:
                        p.op("act", lambda e, y=y, sq=sq, ntok=ntok: e.activation(out=sq[:, :ntok], in_=y[:, :ntok], func=AF.Square), reads=[y], writes=[sq])
                        yield
                        p.op("pe", lambda e, ps1=ps1, sq=sq, ntok=ntok: e.matmul(ps1[:, :ntok], ones, sq[:, :ntok], start=True, stop=True), reads=[sq, self.mask], writes=[ps1])
                        yield
                        self.rsqrt(ri, ri[:, :ntok], ps1, ps1[:, :ntok], 1e-6)
                        sc_ = (128 ** -0.5) if kind == 0 else 1.0
                        p.op("dve", lambda e, y=y, ri=ri, yn=yn, ntok=ntok, sc_=sc_: e.scalar_tensor_tensor(out=yn[:, :ntok], in0=y[:, :ntok], scalar=sc_, in1=ri[:, :ntok], op0=ALU.mult, op1=ALU.mult), reads=[y, ri], writes=[yn])
                        dst = qnT if kind == 0 else knT
                        p.dma(dst.t[hh * 128:(hh + 1) * 128, g0:g0 + ntok], yn[:, :ntok], reads=[yn], writes=[dst])
                        src = yn
                    if kind >= 1:
                        nq = ntok // 128
                        for q_ in range(nq):
                            p.op("pe", lambda e, pt1=pt1, src=src, q_=q_: e.matmul(pt1[:, q_, :], src[:, q_ * 128:(q_ + 1) * 128], ident, start=True, stop=True), reads=[src, self.mask], writes=[pt1])
                        yield
                        p.op("act", lambda e, tt=tt, pt1=pt1, nq=nq: e.copy(out=tt[:, :nq, :], in_=pt1[:, :nq, :]), reads=[pt1], writes=[tt])
                        dst = kn if kind == 1 else vn
                        p.dma(dst.t[g0:g0 + ntok, hh * 128:(hh + 1) * 128].rearrange("(q p) c -> p q c", p=128), tt[:, :nq, :], reads=[tt], writes=[dst])
        lockstep(iters(), 3)
    p.barrier()
B.stage_gdn_conv = _stage_gdn_conv


def _stage_gdn_scan(self, l, ctx_out):
    c, p = self.cfg, self.p
    T_, L, NT = c.T, c.L, c.NT
    LT = L // 128
    d = self.dram
    qnT, knT, kn, vn, ab, of, zs, yc = d["qnT"], d["knT"], d["kn"], d["vn"], d["ab"], d["of"], d["zs"], d["yc"]
    M = lambda i: self.mask[:, i, :]
    with ExitStack() as st:
        S = self.sb(st, "gsS", [128, 8, 128], F32)
        S_h = [S.sub(f"S{h}") for h in range(8)]
        nal = self.sb(st, "gsnal", [128, 16], F32)
        dtb = self.sb(st, "gsdtb", [128, 16], F32)
        nw = self.sb(st, "gsnw", [128, 128], F32)
        p.dma(nal[:], d["dn_a_log"].t[l].partition_broadcast(128), reads=[d["dn_a_log"]], writes=[nal])
        p.dma(dtb[:], d["dn_dt_bias"].t[l].partition_broadcast(128), reads=[d["dn_dt_bias"]], writes=[dtb])
        p.dma(nw[:], d["dn_norm"].t[l].partition_broadcast(128), reads=[d["dn_norm"]], writes=[nw])
        p.op("act", lambda e: e.activation(out=nal[:], in_=nal[:], func=AF.Exp), reads=[nal], writes=[nal])
        p.op("dve", lambda e: e.tensor_scalar_mul(out=nal[:], in0=nal[:], scalar1=-1.0), reads=[nal], writes=[nal])
        banks = [self.ps(st, f"gsb{i}", [128, 4, 128], F32) for i in range(8)]
        slots = [(bnk, 0, bnk) for bnk in banks]
        sc = {"ps": 0}
        rings = {}

        def tmp(name, shape=(128, 128), n=6, dt=F32):
            if name == "X":
                n = 12
            if name not in rings:
                rings[name] = [[self.sb(st, f"gs_{name}{i}", list(shape), dt) for i in range(n)], 0]
            r = rings[name]
            t = r[0][r[1] % n]; r[1] += 1
            return t

        def mm(lhsT, lT, rhs, rT, ncols=128, acc=None):
            trk, j, bnk = slots[sc["ps"] % len(slots)]; sc["ps"] += 1
            ap = bnk[:, j, 0:ncols]
            p.op("pe", lambda e: e.matmul(ap, lhsT, rhs, start=True, stop=(acc is None)), reads=lT + rT, writes=[trk])
            if acc is not None:
                l2, l2T, r2, r2T = acc
                p.op("pe", lambda e: e.matmul(ap, l2, r2, start=False, stop=True), reads=l2T + r2T, writes=[trk])
            return trk, ap

        evi = {"k": 0}

        def evac(out_t, out_ap, trk, ap):
            k = evi["k"]; evi["k"] += 1
            if k % 2 == 0:
                p.op("act", lambda e: e.copy(out=out_ap, in_=ap), reads=[trk], writes=[out_t])
            else:
                p.op("dve", lambda e: e.tensor_copy(out=out_ap, in_=ap), reads=[trk], writes=[out_t])

        for dirn in range(2):
            cum = M_TRIUI if dirn == 0 else M_TRILI
            mL = M_TRILS if dirn == 0 else M_TRIUS
            mA = M_TRIUI if dirn == 0 else M_TRILI
            p.op("pool", lambda e: e.memset(S[:], 0.0), writes=[S] + S_h)
            if dirn == 0:
                order = list(range(NT))
            else:
                order = list(range(LT - 1, -1, -1)) + list(range(NT - 1, LT - 1, -1))
            def loads(i):
                want = (i >= LT) or ctx_out
                tk = slice(i * 128, (i + 1) * 128)
                abt = tmp("abt", (128, 32), 2)
                p.dma(abt[:], ab.t[tk, :], reads=[ab], writes=[abt])
                knt = tmp("knt", (128, 1024), 2); vnt = tmp("vnt", (128, 1024), 2)
                kTt = tmp("kTt", (128, 8, 128), 2); qTt = tmp("qTt", (128, 8, 128), 2)
                p.dma(knt[:], kn.t[tk, :], reads=[kn], writes=[knt])
                p.dma(vnt[:], vn.t[tk, :], reads=[vn], writes=[vnt])
                p.dma(kTt[:], knT.t[:, tk].rearrange("(h p) t -> p h t", p=128), reads=[knT], writes=[kTt])
                p.dma(qTt[:], qnT.t[:, tk].rearrange("(h p) t -> p h t", p=128), reads=[qnT], writes=[qTt])
                oft = zt = None
                if want and dirn == 1:
                    oft = tmp("oft", (128, 1024), 2); zt = tmp("zt", (128, 1024), 2)
                    p.dma(oft[:], of.t[tk, :], reads=[of], writes=[oft])
                    p.dma(zt[:], zs.t[tk, :], reads=[zs], writes=[zt])
                return (abt, knt, vnt, kTt, qTt, oft, zt)
            nxt = loads(order[0])
            for oi, i in enumerate(order):
                want = (i >= LT) or ctx_out
                tk = slice(i * 128, (i + 1) * 128)
                abt, knt, vnt, kTt, qTt, oft, zt = nxt
                if oi + 1 < len(order):
                    nxt = loads(order[oi + 1])
                gx = tmp("gx", (128, 8), 2); gax = tmp("gax", (128, 8), 2); g = tmp("g", (128, 8), 2); beta = tmp("beta", (128, 8), 2)
                ds = slice(dirn * 8, dirn * 8 + 8)
                p.op("dve", lambda e: e.tensor_tensor(out=gx[:], in0=abt[:, ds], in1=dtb[:, ds], op=ALU.add), reads=[abt, dtb], writes=[gx])
                p.op("act", lambda e: e.activation(out=gax[:], in_=gx[:], func=AF.Abs), reads=[gx], writes=[gax])
                p.op("act", lambda e: e.activation(out=gax[:], in_=gax[:], func=AF.Exp, scale=-1.0), reads=[gax], writes=[gax])
                p.op("act", lambda e: e.activation(out=gax[:], in_=gax[:], func=AF.Ln, bias=self.epsc[1.0], scale=1.0), reads=[gax, self.epst], writes=[gax])
                p.op("dve", lambda e: e.tensor_scalar_max(out=gx[:], in0=gx[:], scalar1=0.0), reads=[gx], writes=[gx])
                p.op("dve", lambda e: e.tensor_tensor(out=gx[:], in0=gx[:], in1=gax[:], op=ALU.add), reads=[gx, gax], writes=[gx])
                p.op("dve", lambda e: e.tensor_tensor(out=g[:], in0=gx[:], in1=nal[:, ds], op=ALU.mult), reads=[gx, nal], writes=[g])
                p.op("act", lambda e: e.activation(out=beta[:], in_=abt[:, 16 + dirn * 8:24 + dirn * 8], func=AF.Sigmoid), reads=[abt], writes=[beta])
                t1, a1 = mm(M(cum), [self.mask], g[:], [g], ncols=8)
                gcum = tmp("gcum", (128, 8), 2)
                evac(gcum, gcum[:], t1, a1)
                t2, a2 = mm(M(M_ONES), [self.mask], g[:], [g], ncols=8)
                gtot = tmp("gtot", (128, 8), 2)
                evac(gtot, gtot[:], t2, a2)
                eg = tmp("eg", (128, 8), 2); ekd = tmp("ekd", (128, 8), 2); egl = tmp("egl", (128, 8), 2); bk = tmp("bk", (128, 8), 2)
                p.op("act", lambda e: e.activation(out=eg[:], in_=gcum[:], func=AF.Exp), reads=[gcum], writes=[eg])
                p.op("dve", lambda e: e.tensor_tensor(out=ekd[:], in0=gtot[:], in1=gcum[:], op=ALU.subtract), reads=[gtot, gcum], writes=[ekd])
                p.op("act", lambda e: e.activation(out=ekd[:], in_=ekd[:], func=AF.Exp), reads=[ekd], writes=[ekd])
                p.op("act", lambda e: e.activation(out=egl[:], in_=gtot[:], func=AF.Exp), reads=[gtot], writes=[egl])
                p.op("dve", lambda e: e.tensor_tensor(out=bk[:], in0=beta[:], in1=eg[:], op=ALU.mult), reads=[beta, eg], writes=[bk])
                if want:
                    ot = tmp("ot", (128, 1024), 2)
                def unit(h):
                    hs = slice(h * 128, (h + 1) * 128)
                    hc = slice(h, h + 1)
                    kT = kTt[:, h, :]; qT = qTt[:, h, :]
                    Ug = tmp("Ug")
                    p.op("pool", lambda e: e.tensor_scalar(out=Ug[:], in0=M(cum), scalar1=g[:, hc], scalar2=None, op0=ALU.mult), reads=[self.mask, g], writes=[Ug])
                    yield
                    tD, aD = mm(M(M_ONES), [self.mask], Ug[:], [Ug])
                    DL = tmp("DL"); DU = tmp("DU")
                    p.op("dve", lambda e: e.tensor_scalar(out=DL[:], in0=aD, scalar1=gcum[:, hc], scalar2=0.0, op0=ALU.subtract, op1=ALU.max), reads=[tD, gcum], writes=[DL])
                    p.op("dve", lambda e: e.tensor_scalar(out=DU[:], in0=aD, scalar1=gcum[:, hc], scalar2=0.0, op0=ALU.subtract, op1=ALU.min), reads=[tD, gcum], writes=[DU])
                    p.op("act", lambda e: e.activation(out=DL[:], in_=DL[:], func=AF.Exp, scale=-1.0), reads=[DL], writes=[DL])
                    p.op("act", lambda e: e.activation(out=DU[:], in_=DU[:], func=AF.Exp), reads=[DU], writes=[DU])
                    p.op("pool", lambda e: e.tensor_tensor(out=DL[:], in0=DL[:], in1=M(mL), op=ALU.mult), reads=[DL, self.mask], writes=[DL])
                    p.op("pool", lambda e: e.tensor_tensor(out=DU[:], in0=DU[:], in1=M(mA), op=ALU.mult), reads=[DU, self.mask], writes=[DU])
                    yield
                    tG, aG = mm(kT, [kTt], kT, [kTt])
                    Lm = tmp("Lm")
                    p.op("dve", lambda e: e.scalar_tensor_tensor(out=Lm[:], in0=aG, scalar=beta[:, hc], in1=DL[:], op0=ALU.mult, op1=ALU.mult), reads=[tG, beta, DL], writes=[Lm])
                    if want:
                        tA, aA = mm(kT, [kTt], qT, [qTt])
                        AT = tmp("AT")
                        p.op("dve", lambda e: e.tensor_tensor(out=AT[:], in0=aA, in1=DU[:], op=ALU.mult), reads=[tA, DU], writes=[AT])
                    yield

                    def trn(src):
                        trk, j, bnk = slots[sc["ps"] % len(slots)]; sc["ps"] += 1
                        ap = bnk[:, j, 0:128]
                        p.op("pe", lambda e: e.transpose(ap, src[:], M(M_ID)), reads=[src, self.mask], writes=[trk])
                        return trk, ap
                    tN, aN = trn(Lm)
                    Nm = tmp("Nm")
                    evac(Nm, Nm[:], tN, aN)
                    L16 = tmp("L16"); N16 = tmp("N16")
                    p.op("pool", lambda e: e.tensor_tensor(out=L16[:], in0=Lm[:], in1=M(M_BD16), op=ALU.mult), reads=[Lm, self.mask], writes=[L16])
                    p.op("pool", lambda e: e.tensor_tensor(out=N16[:], in0=Nm[:], in1=M(M_BD16), op=ALU.mult), reads=[Nm, self.mask], writes=[N16])

                    def mmev(name, lh, lhT, rh, rhT):
                        t_, a_ = mm(lh[:], [lh], rh[:], [rh])
                        o_ = tmp(name)
                        evac(o_, o_[:], t_, a_)
                        return o_
                    yield
                    L2 = mmev("L2", N16, None, L16, None); N2 = mmev("N2", L16, None, N16, None)
                    yield
                    L4 = mmev("L4", N2, None, L2, None); N4 = mmev("N4", L2, None, N2, None)
                    yield
                    L8 = mmev("L8", N4, None, L4, None)
                    Q1 = tmp("P1")
                    p.op("pool", lambda e: e.tensor_tensor(out=Q1[:], in0=M(M_ID), in1=N16[:], op=ALU.subtract), reads=[N16, self.mask], writes=[Q1])

                    def mmadd(name, base, lh, rh):
                        t_, a_ = mm(lh[:], [lh], rh[:], [rh])
                        o_ = tmp(name)
                        p.op("dve", lambda e: e.tensor_tensor(out=o_[:], in0=base[:], in1=a_, op=ALU.add), reads=[base, t_], writes=[o_])
                        return o_
                    yield
                    Q2 = mmadd("P2", Q1, L2, Q1)
                    yield
                    Q3 = mmadd("P3", Q2, L4, Q2)
                    yield
                    Y = mmadd("X", Q3, L8, Q3)
                    for lvl in (M_OFF32, M_OFF64, M_OFF128):
                        Loff = tmp("Noff")
                        p.op("pool", lambda e, Loff=Loff, lvl=lvl: e.tensor_tensor(out=Loff[:], in0=Lm[:], in1=M(lvl), op=ALU.mult), reads=[Lm, self.mask], writes=[Loff])
                        yield
                        Xt_, Xa_ = trn(Y)
                        Xs = tmp("Y"); evac(Xs, Xs[:], Xt_, Xa_)
                        T2t, T2a = mm(Loff[:], [Loff], Y[:], [Y])
                        T2 = tmp("T2"); evac(T2, T2[:], T2t, T2a)
                        yield
                        Zt, Za = mm(Xs[:], [Xs], T2[:], [T2])
                        Yn = tmp("X")
                        p.op("dve", lambda e, Yn=Yn, Y=Y, Za=Za: e.tensor_tensor(out=Yn[:], in0=Y[:], in1=Za, op=ALU.subtract), reads=[Y, Zt], writes=[Yn])
                        Y = Yn
                    yield
                    RU = tmp("RU"); RW = tmp("RW"); KD = tmp("KD")
                    p.op("pool", lambda e: e.tensor_scalar(out=RU[:], in0=vnt[:, hs], scalar1=beta[:, hc], scalar2=None, op0=ALU.mult), reads=[vnt, beta], writes=[RU])
                    p.op("pool", lambda e: e.tensor_scalar(out=RW[:], in0=knt[:, hs], scalar1=bk[:, hc], scalar2=None, op0=ALU.mult), reads=[knt, bk], writes=[RW])
                    p.op("pool", lambda e: e.tensor_scalar(out=KD[:], in0=knt[:, hs], scalar1=ekd[:, hc], scalar2=None, op0=ALU.mult), reads=[knt, ekd], writes=[KD])
                    yield
                    ut, ua = mm(Y[:], [Y], RU[:], [RU])
                    U_ = tmp("U"); evac(U_, U_[:], ut, ua)
                    wt_, wa_ = mm(RW[:], [RW], Y[:], [Y])
                    WT = tmp("WT"); evac(WT, WT[:], wt_, wa_)
                    Sh = S_h[h]
                    yield
                    wst, wsa = mm(WT[:], [WT], S[:, h, :], [Sh])
                    VN = tmp("VN")
                    p.op("dve", lambda e: e.tensor_tensor(out=VN[:], in0=U_[:], in1=wsa, op=ALU.subtract), reads=[U_, wst], writes=[VN])
                    if want:
                        qst, qsa = mm(qT, [qTt], S[:, h, :], [Sh])
                        avt, ava = mm(AT[:], [AT], VN[:], [VN])
                        QS = tmp("QS")
                        p.op("dve", lambda e: e.tensor_scalar(out=QS[:], in0=qsa, scalar1=eg[:, hc], scalar2=None, op0=ALU.mult), reads=[qst, eg], writes=[QS])
                        if dirn == 0:
                            p.op("dve", lambda e: e.tensor_tensor(out=ot[:, hs], in0=QS[:], in1=ava, op=ALU.add), reads=[QS, avt], writes=[ot])
                        else:
                            p.op("dve", lambda e: e.tensor_tensor(out=QS[:], in0=QS[:], in1=ava, op=ALU.add), reads=[QS, avt], writes=[QS])
                            p.op("pool", lambda e: e.tensor_tensor(out=ot[:, hs], in0=QS[:], in1=oft[:, hs], op=ALU.add), reads=[QS, oft], writes=[ot])
                    yield
                    kvt, kva = mm(KD[:], [KD], VN[:], [VN])
                    p.op("dve", lambda e: e.scalar_tensor_tensor(out=S[:, h, :], in0=S[:, h, :], scalar=egl[:, hc], in1=kva, op0=ALU.mult, op1=ALU.add), reads=[Sh, egl, kvt], writes=[Sh])
                for hg in ((0, 1, 2, 3), (4, 5, 6, 7)):
                    gens = [unit(h) for h in hg]
                    while gens:
                        for g_ in list(gens):
                            try:
                                next(g_)
                            except StopIteration:
                                gens.remove(g_)
                if want:
                    if dirn == 0:
                        p.dma(of.t[tk, :], ot[:], reads=[ot], writes=[of])
                    else:
                        sq = tmp("osq", (128, 128), 2); ssq = tmp("ossq", (128, 8), 2)
                        yct = tmp("yct", (128, 1024), 2, BF16)
                        for h in range(8):
                            hs = slice(h * 128, (h + 1) * 128)
                            p.op("act", lambda e, hs=hs, h=h: e.activation(out=sq[:], in_=ot[:, hs], func=AF.Square, accum_out=ssq[:, h:h + 1]), reads=[ot], writes=[sq, ssq])
                        self.rsqrt(ssq, ssq[:], ssq, ssq[:], 1e-6, scale=1.0 / 128)
                        for h in range(8):
                            hs = slice(h * 128, (h + 1) * 128)
                            p.op("dve", lambda e, hs=hs, h=h: e.scalar_tensor_tensor(out=ot[:, hs], in0=ot[:, hs], scalar=ssq[:, h:h + 1], in1=nw[:], op0=ALU.mult, op1=ALU.mult), reads=[ot, ssq, nw], writes=[ot])
                        p.op("pool", lambda e: e.tensor_tensor(out=yct[:], in0=ot[:], in1=zt[:], op=ALU.mult), reads=[ot, zt], writes=[yct])
                        p.dma(yc.t[tk, :], yct[:], reads=[yct], writes=[yc])
    p.barrier()
B.stage_gdn_scan = _stage_gdn_scan


def _stage_merge(self, l):
    c, p = self.cfg, self.p
    D, T_ = c.D, c.T
    d = self.dram
    ys = [d["yaT"], d["ybT"], d["ycT"]]
    ws = [d["w_branch_a"], d["w_branch_b"], d["w_branch_c"]]
    gT, mT = d["gtsT"], d["mT"]
    NB = D
    with ExitStack() as st:
        wb = [[self.sb(st, f"mgw{b_}{i}", [128, 8, NB], BF16) for i in range(1)] for b_ in range(3)]
        ab = [[self.sb(st, f"mga{b_}{i}", [128, 8, 512], BF16) for i in range(2)] for b_ in range(3)]
        gt = [self.sb(st, f"mgg{i}", [128, 3, 512], BF16) for i in range(3)]
        t1 = [self.sb(st, f"mgt1{i}", [128, 512], F32) for i in range(2)]
        t2 = [self.sb(st, f"mgt2{i}", [128, 512], F32) for i in range(2)]
        mo = [self.sb(st, f"mgo{i}", [128, 512], BF16) for i in range(2)]
        banks = [self.ps(st, f"mgps{i}", [128, 512], F32) for i in range(6)]
        wi = ai = bi = k = 0
        for nb0 in range(0, D, NB):
            for b_ in range(3):
                p.dma(wb[b_][0][:], ws[b_].t[l].rearrange("(kc p) n -> p kc n", p=128)[:, :, nb0:nb0 + NB], reads=[ws[b_]], writes=[wb[b_][0]], eng="pool")
            W3 = [wb[b_][0] for b_ in range(3)]; wi += 1
            for g0 in range(0, T_, 512):
                ntok = min(512, T_ - g0)
                A3 = [ab[b_][ai % 2] for b_ in range(3)]; ai += 1
                for b_ in range(3):
                    p.dma(A3[b_][:, :, :ntok], ys[b_].t.rearrange("(kc p) t -> p kc t", p=128)[:, :, g0:g0 + ntok], reads=[ys[b_]], writes=[A3[b_]])
                for cc in range(NB // 128):
                    r0 = nb0 + cc * 128
                    G_ = gt[k % 3]; T1 = t1[k % 2]; T2 = t2[k % 2]; MO = mo[k % 2]; k += 1
                    for b_ in range(3):
                        p.dma(G_[:, b_, :ntok], gT.t[b_ * D + r0:b_ * D + r0 + 128, g0:g0 + ntok], reads=[gT], writes=[G_])
                    P3 = []
                    for b_ in range(3):
                        ps = banks[bi % 6]; bi += 1
                        for kc in range(8):
                            p.op("pe", lambda e, ps=ps, b_=b_, kc=kc: e.matmul(ps[:, :ntok], W3[b_][:, kc, cc * 128:(cc + 1) * 128], A3[b_][:, kc, :ntok],
                                                                              start=(kc == 0), stop=(kc == 7)), reads=[W3[b_], A3[b_]], writes=[ps])
                        P3.append(ps)
                    p.op("dve", lambda e: e.tensor_tensor(out=T1[:, :ntok], in0=P3[0][:, :ntok], in1=G_[:, 0, :ntok], op=ALU.mult), reads=[P3[0], G_], writes=[T1])
                    p.op("dve", lambda e: e.tensor_tensor(out=T2[:, :ntok], in0=P3[1][:, :ntok], in1=G_[:, 1, :ntok], op=ALU.mult), reads=[P3[1], G_], writes=[T2])
                    p.op("pool", lambda e: e.tensor_tensor(out=T1[:, :ntok], in0=T1[:, :ntok], in1=T2[:, :ntok], op=ALU.add), reads=[T1, T2], writes=[T1])
                    p.op("dve", lambda e: e.tensor_tensor(out=T2[:, :ntok], in0=P3[2][:, :ntok], in1=G_[:, 2, :ntok], op=ALU.mult), reads=[P3[2], G_], writes=[T2])
                    p.op("pool", lambda e: e.tensor_tensor(out=MO[:, :ntok], in0=T1[:, :ntok], in1=T2[:, :ntok], op=ALU.add), reads=[T1, T2], writes=[MO])
                    p.dma(mT.t[r0:r0 + 128, g0:g0 + ntok], MO[:, :ntok], reads=[MO], writes=[mT])
    p.barrier()
B.stage_merge = _stage_merge


def _stage_tm_proj(self, l, actname, K, wname, NB=512, TG=512):
    p = self.p
    d = self.dram
    osub = d["osub"]
    state = {"k": 0}

    def init(st):
        state["o"] = [self.sb(st, f"tpo{i}", [128, 512], F32) for i in range(3)]

    def epi(ps, tok, nb0, nb):
        o = state["o"][state["k"] % 3]; state["k"] += 1
        p.op("act", lambda e: e.copy(out=o[:, :nb], in_=ps[:, :nb]), reads=[ps], writes=[o])
        p.dma(osub.t[tok:tok + 128, nb0:nb0 + nb], o[:, :nb], reads=[o], writes=[osub])
    W = d[wname]
    secs = [dict(c0=0, n=self.cfg.D, mode="TM", epi=epi, init=init)]
    self.proj(d[actname], K, W, W.t[l], secs, 0, self.cfg.T, NB=NB, TG=TG, tag="tp")
B.stage_tm_proj = _stage_tm_proj


def _stage_ffn_up(self, l):
    c, p = self.cfg, self.p
    D, T_, L, DFF, KC = c.D, c.T, c.L, c.DFF, c.KC
    d = self.dram
    hT, gT, W = d["hT"], d["gT"], d["w_up"]
    NCH = 2 * DFF // 128
    HC = DFF // 128
    CB = 4 if HC % 4 == 0 else (2 if HC % 2 == 0 else 1)
    TG = 510
    with ExitStack() as st:
        cw = self.sb(st, "fucw", [128, NCH, 3], F32)
        cb = self.sb(st, "fucb", [128, NCH], F32)
        for k_ in range(3):
            p.dma(cw[:, :, k_], d["ffn_conv_w"].t[l, k_].rearrange("(ch p) -> p ch", p=128), reads=[d["ffn_conv_w"]], writes=[cw], allow_slow_non_contiguous=True)
        p.dma(cb[:], d["ffn_conv_b"].t[l].rearrange("(ch p) -> p ch", p=128), reads=[d["ffn_conv_b"]], writes=[cb], allow_slow_non_contiguous=True)
        wA = [self.sb(st, f"fuwa{i}", [128, KC, CB * 128], BF16) for i in range(2)]
        wB = [self.sb(st, f"fuwb{i}", [128, KC, CB * 128], BF16) for i in range(2)]
        ab = [self.sb(st, f"fua{i}", [128, KC, 512], BF16) for i in range(2)]
        ua = [self.sb(st, f"fuua{i}", [128, 512], F32) for i in range(2)]
        ub = [self.sb(st, f"fuub{i}", [128, 512], F32) for i in range(2)]
        go = [self.sb(st, f"fugo{i}", [128, 512], BF16) for i in range(2)]
        banks = [self.ps(st, f"fups{i}", [128, 512], F32) for i in range(6)]
        wv = W.t[l].rearrange("(kc p) n -> p kc n", p=128)
        av = hT.t.rearrange("(kc p) t -> p kc t", p=128)
        wi = ai = bi = k = 0
        for cb0 in range(0, HC, CB):
            WA, WB = wA[wi % 2], wB[wi % 2]; wi += 1
            p.dma(WA[:], wv[:, :, cb0 * 128:(cb0 + CB) * 128], reads=[W], writes=[WA], eng="pool")
            p.dma(WB[:], wv[:, :, DFF + cb0 * 128:DFF + (cb0 + CB) * 128], reads=[W], writes=[WB], eng="pool")
            for (s0, s1) in ((0, L), (L, T_)):
                for g0 in range(s0, s1, TG):
                    n = min(TG, s1 - g0)
                    A = ab[ai % 2]; ai += 1
                    lo = max(s0, g0 - 1); hi = min(s1, g0 + n + 1)
                    if lo > g0 - 1:
                        p.op("pool", lambda e: e.memset(A[:, :, 0:1], 0.0), writes=[A])
                    if hi < g0 + n + 1:
                        p.op("pool", lambda e: e.memset(A[:, :, n + 1:n + 2], 0.0), writes=[A])
                    p.dma(A[:, :, lo - (g0 - 1):hi - (g0 - 1)], av[:, :, lo:hi], reads=[hT], writes=[A])
                    for cc in range(CB):
                        cha = cb0 + cc; chb = HC + cb0 + cc
                        pa = banks[bi % 6]; bi += 1
                        pb = banks[bi % 6]; bi += 1
                        for kc in range(KC):
                            p.op("pe", lambda e, kc=kc: e.matmul(pa[:, :n + 2], WA[:, kc, cc * 128:(cc + 1) * 128], A[:, kc, :n + 2], start=(kc == 0), stop=(kc == KC - 1)), reads=[WA, A], writes=[pa])
                        for kc in range(KC):
                            p.op("pe", lambda e, kc=kc: e.matmul(pb[:, :n + 2], WB[:, kc, cc * 128:(cc + 1) * 128], A[:, kc, :n + 2], start=(kc == 0), stop=(kc == KC - 1)), reads=[WB, A], writes=[pb])
                        UA, UB, GO = ua[k % 2], ub[k % 2], go[k % 2]; k += 1
                        for (U, ps, ch) in ((UA, pa, cha), (UB, pb, chb)):
                            p.op("dve", lambda e, U=U, ps=ps, ch=ch: e.tensor_scalar(out=U[:, :n], in0=ps[:, 0:n], scalar1=cw[:, ch, 0:1], scalar2=cb[:, ch:ch + 1], op0=ALU.mult, op1=ALU.add), reads=[ps, cw, cb], writes=[U])
                            p.op("dve", lambda e, U=U, ps=ps, ch=ch: e.scalar_tensor_tensor(out=U[:, :n], in0=ps[:, 1:n + 1], scalar=cw[:, ch, 1:2], in1=U[:, :n], op0=ALU.mult, op1=ALU.add), reads=[ps, cw, U], writes=[U])
                            p.op("dve", lambda e, U=U, ps=ps, ch=ch: e.scalar_tensor_tensor(out=U[:, :n], in0=ps[:, 2:n + 2], scalar=cw[:, ch, 2:3], in1=U[:, :n], op0=ALU.mult, op1=ALU.add), reads=[ps, cw, U], writes=[U])
                        p.op("act", lambda e: e.activation(out=UA[:, :n], in_=UA[:, :n], func=AF.Silu), reads=[UA], writes=[UA])
                        p.op("pool", lambda e: e.tensor_tensor(out=GO[:, :n], in0=UA[:, :n], in1=UB[:, :n], op=ALU.mult), reads=[UA, UB], writes=[GO])
                        p.dma(gT.t[cha * 128:(cha + 1) * 128, g0:g0 + n], GO[:, :n], reads=[GO], writes=[gT])
    p.barrier()
B.stage_ffn_up = _stage_ffn_up


def _build_all(self, st, upto=None):
    c = self.cfg
    self.declare()
    self.consts(st)
    self.stage_mod()
    self.stage_ln(0, None, (0, 0), src_inputs=True)
    for l in range(c.NL):
        last = (l == c.NL - 1)
        ctx_out = not last
        self.stage_win(l)
        self.stage_win_attn(l, ctx_out)
        self.stage_diff(l, ctx_out)
        self.stage_gdn_conv(l)
        self.stage_gdn_scan(l, ctx_out)
        d = self.dram
        self.stage_tm2fm(d["ya"], d["yaT"], 1024)
        self.stage_tm2fm(d["yb"], d["ybT"], 1024)
        self.stage_tm2fm(d["yc"], d["ycT"], 1024)
        self.stage_merge(l)
        self.stage_tm_proj(l, "mT", c.D, "w_o", NB=min(1024, c.D))
        if upto == "mix" and l == 0:
            break
        self.stage_ln(l, (2, "ln1_g", "ln1_b"), (l, 3), src_inputs=(l == 0))
        self.stage_ffn_up(l)
        kdown = c.DFF
        big = (kdown // 128) > 16
        self.stage_tm_proj(l, "gT", kdown, "w_down", NB=512, TG=256 if big else 512)
        if last:
            self.stage_ln(l, (5, "ln2_g", "ln2_b"), None, final=True)
        else:
            self.stage_ln(l, (5, "ln2_g", "ln2_b"), (l + 1, 0))
    self.p.emit(st)
B.build_all = _build_all


_CACHE = {}


def _get_nc(cfg_key):
    if cfg_key not in _CACHE:
        cfg = Cfg(*cfg_key)
        b = B(cfg)
        st = ExitStack()
        b.build_all(st)
        _CACHE[cfg_key] = (b, st, cfg)
    return _CACHE[cfg_key]


def kernel(**inputs):
    x = np.asarray(inputs["x"])
    bsz, N, D = x.shape
    L = inputs["ctx"].shape[1]
    DFF = inputs["w_down"].shape[1]
    NL = inputs["w_mod"].shape[0]
    b, st, cfg = _get_nc((D, N, L, DFF, NL))
    consts = make_consts(cfg)
    shared = {}
    for k, v in inputs.items():
        if k in ("x", "c", "ctx", "c_ctx"):
            continue
        shared[k] = np.ascontiguousarray(np.asarray(v), dtype=np.float32)
    shared["dn_a_log"] = shared["dn_a_log"].reshape(NL, 16)
    shared["dn_dt_bias"] = shared["dn_dt_bias"].reshape(NL, 16)
    shared["c_ctx"] = np.ascontiguousarray(np.asarray(inputs["c_ctx"]), dtype=np.float32)
    shared.update(consts)
    n_cores = 8 if bsz <= 4 else bsz
    hot = [0, 1, 4, 5][:bsz] if bsz <= 4 else list(range(bsz))
    zx = np.zeros_like(np.ascontiguousarray(x[0], dtype=np.float32))
    zc = np.zeros((D,), np.float32)
    zctx = np.zeros((L, D), np.float32)
    in_maps = []
    for core in range(n_cores):
        m = dict(shared)
        if core in hot:
            i = hot.index(core)
            m["x"] = np.ascontiguousarray(x[i], dtype=np.float32)
            m["c"] = np.ascontiguousarray(np.asarray(inputs["c"])[i], dtype=np.float32)
            m["ctx"] = np.ascontiguousarray(np.asarray(inputs["ctx"])[i], dtype=np.float32)
        else:
            m["x"], m["c"], m["ctx"] = zx, zc, zctx
        in_maps.append(m)
    res = run_bass_kernel_spmd(b.nc, in_maps, core_ids=list(range(n_cores)))
    return np.stack([np.asarray(res.results[core]["y"], dtype=np.float32) for core in hot], axis=0)
```

```python
import numpy as np
from contextlib import ExitStack
import concourse.bass as bass
import concourse.mybir as mybir
from concourse.bass_utils import run_bass_kernel_spmd

F32 = mybir.dt.float32
BF16 = mybir.dt.bfloat16
AF = mybir.ActivationFunctionType
ALU = mybir.AluOpType
AX = mybir.AxisListType

ENGS = ("pe", "act", "dve", "pool", "sp")
EPOCH = 30000
NDMASEM = 40


import types


def freeze(fn):
    if fn is None or fn.__closure__ is None:
        return fn
    cells = tuple(types.CellType(c.cell_contents) for c in fn.__closure__)
    return types.FunctionType(fn.__code__, fn.__globals__, fn.__name__, fn.__defaults__, cells)


class T:
    __slots__ = ("t", "name", "w", "r", "rd")

    def __init__(self, t, name=""):
        self.t = t
        self.name = name
        self.w = None
        self.r = {}
        self.rd = {}

    def sub(self, name=""):
        return T(self.t, name or self.name)

    def __getitem__(self, idx):
        return self.t[idx]


class Op:
    __slots__ = ("eng", "seq", "fn", "deps", "signal", "isdma", "sem", "val", "dsem", "dval", "dprev", "dslot")

    def __init__(self, eng, seq, fn, isdma):
        self.eng = eng
        self.seq = seq
        self.fn = fn
        self.deps = []
        self.signal = False
        self.isdma = isdma
        self.sem = None
        self.val = 0
        self.dsem = None
        self.dval = 0
        self.dprev = 0


class Prog:
    def __init__(self, nc, same_raw=True):
        self.nc = nc
        self.ops = {e: [] for e in ENGS}
        self.seen = {e: {} for e in ENGS}
        self.seen_dma = {e: set() for e in ENGS}
        self.pending = {e: [] for e in ENGS}
        self.ndma = 0
        self.same_raw = same_raw
        self.out_dmas = []
        self.all_dmas_unwaited = []
        self.dcount = {e: 0 for e in ENGS}
        self.lastd = {}

    def _need(self, op, tgt):
        if tgt is None or tgt is op:
            return
        e = op.eng
        if tgt.isdma:
            if id(tgt) in self.seen_dma[e]:
                return
            self.seen_dma[e].add(id(tgt))
            op.deps.append(tgt)
            return
        if tgt.eng == e:
            if e == "pe" or not self.same_raw:
                return
        if self.seen[e].get(tgt.eng, 0) >= tgt.seq:
            return
        self.seen[e][tgt.eng] = tgt.seq
        tgt.signal = True
        op.deps.append(tgt)

    def _record(self, eng, fn, reads, writes, isdma=False, raw_only_same=True):
        ops = self.ops[eng]
        op = Op(eng, len(ops) + 1, fn, isdma)
        if isdma:
            op.dslot = self.dcount[eng] % NDMASEM
            self.dcount[eng] += 1
            self.lastd[(eng, op.dslot)] = op
        for t in self.pending[eng]:
            self._need(op, t)
        self.pending[eng] = []
        for b in reads:
            self._need(op, b.w)
        for b in writes:
            self._need(op, b.w)
            for re_, r in b.r.items():
                if re_ == eng and not isdma:
                    continue
                self._need(op, r)
            for r in b.rd.values():
                self._need(op, r)
        for b in reads:
            if isdma:
                b.rd[(eng, op.dslot)] = op
            else:
                b.r[eng] = op
        for b in writes:
            b.w = op
            b.r = {}
            b.rd = {}
        ops.append(op)
        return op

    def op(self, eng, fn, reads=(), writes=()):
        return self._record(eng, freeze(fn), reads, writes)

    def dma(self, out_ap, in_ap, reads=(), writes=(), eng="sp", is_out=False, **kw):
        def fn(e, out_ap=out_ap, in_ap=in_ap, kw=kw):
            return e.dma_start(out=out_ap, in_=in_ap, **kw)
        op = self._record(eng, fn, reads, writes, isdma=True)
        op.signal = True
        self.ndma += 1
        if is_out:
            self.out_dmas.append(op)
        return op

    def barrier(self, label=None):
        import sys as _sys
        if not hasattr(self, "marks"):
            self.marks = []
        self.marks.append((label or _sys._getframe(1).f_code.co_name, {e: len(self.ops[e]) for e in ENGS}))
        lasts = []
        for e in ENGS:
            for o in reversed(self.ops[e]):
                if not o.isdma:
                    lasts.append(o)
                    break
        dmas = list(self.lastd.values())
        for e in ENGS:
            self.pending[e] = self.pending[e] + lasts + dmas

    def emit(self, stack):
        nc = self.nc
        self.barrier()
        fin = {}
        for e in ENGS:
            op = Op(e, len(self.ops[e]) + 1, None, False)
            for t in self.pending[e]:
                self._need(op, t)
            self.ops[e].append(op)
        for e in ENGS:
            cnt = 0
            sems = []
            for op in self.ops[e]:
                if op.isdma or not op.signal:
                    continue
                ep = cnt // EPOCH
                while len(sems) <= ep:
                    sems.append(stack.enter_context(nc.semaphore(f"c_{e}_{len(sems)}")))
                cnt += 1
                op.sem = sems[ep]
                op.val = cnt - ep * EPOCH
        dpool = {}
        duse = {}
        for e in ENGS:
            k = 0
            for op in self.ops[e]:
                if not op.isdma:
                    continue
                if e not in dpool:
                    dpool[e] = [stack.enter_context(nc.semaphore(f"d_{e}_{i}")) for i in range(NDMASEM)]
                    duse[e] = [0] * NDMASEM
                i = op.dslot
                k += 1
                op.dsem = dpool[e][i]
                op.dprev = 16 * duse[e][i]
                duse[e][i] += 1
                op.dval = 16 * duse[e][i]
        block = stack.enter_context(nc.Block())

        def run(eng, e):
            for op in self.ops[e]:
                for d in op.deps:
                    if d.isdma:
                        eng.wait_ge(d.dsem, d.dval)
                    else:
                        eng.wait_ge(d.sem, d.val)
                if op.fn is None:
                    continue
                if op.isdma:
                    if op.dprev:
                        eng.wait_ge(op.dsem, op.dprev)
                    op.fn(eng).then_inc(op.dsem, 16)
                else:
                    ins = op.fn(eng)
                    if op.signal:
                        ins.then_inc(op.sem, 1)

        block.tensor(lambda eng: run(eng, "pe"))
        block.scalar(lambda eng: run(eng, "act"))
        block.vector(lambda eng: run(eng, "dve"))
        block.gpsimd(lambda eng: run(eng, "pool"))
        block.sync(lambda eng: run(eng, "sp"))

    def stats(self):
        return {e: len(self.ops[e]) for e in ENGS}

import math

class Cfg:
    def __init__(self, D=2048, N=8192, L=256, DFF=5632, NL=2):
        self.D, self.N, self.L, self.DFF, self.NL = D, N, L, DFF, NL
        self.T = N + L
        self.KC = D // 128
        self.NT = self.T // 128
        self.IN_W = 1024 + 256 + 256 + 1024 + 1024 + 1024 + 3072 + 1024 + 16 + 16 + 3 * D

GRID_W = 64
ALPHA = None
LN_EPS = 1e-6

O_QA, O_KA, O_VA, O_QD, O_KD, O_VD, O_QKV, O_Z, O_A, O_B, O_G = (
    0, 1024, 1280, 1536, 2560, 3584, 4608, 7680, 8704, 8720, 8736)

M_ID, M_TRILS, M_TRIUI, M_TRIUS, M_TRILI, M_ONES, M_BD16, M_OFF32, M_OFF64, M_OFF128, M_NEG, M_PA, M_PD = range(13)
NMASK = 13


def rope_tables(n, dim):
    rows = n // GRID_W
    row = np.repeat(np.arange(rows, dtype=np.float32), GRID_W)
    col = np.tile(np.arange(GRID_W, dtype=np.float32), rows)
    axis_dim = dim // 2
    inv_freq = (10000.0 ** (-np.arange(0, axis_dim, 2, dtype=np.float32) / axis_dim)).astype(np.float32)
    ang_r = row[:, None] * inv_freq[None]
    ang_c = col[:, None] * inv_freq[None]
    ang = np.concatenate([ang_r, ang_r, ang_c, ang_c], axis=-1).astype(np.float32)
    q = dim // 4
    sgn = np.concatenate([-np.ones(q), np.ones(q), -np.ones(q), np.ones(q)]).astype(np.float32)
    return np.cos(ang).T.astype(np.float32), (np.sin(ang) * sgn[None]).T.astype(np.float32)


def make_consts(cfg):
    T, L = cfg.T, cfg.L
    def tab(dim, rep, scale):
        c, s = rope_tables(cfg.N, dim)
        out = np.zeros((4, 128, T), np.float32)
        c = np.tile(c, (rep, 1)); s = np.tile(s, (rep, 1))
        out[0, :, :L] = scale; out[0, :, L:] = c * scale
        out[1, :, L:] = s * scale
        out[2, :, :L] = 1.0; out[2, :, L:] = c
        out[3, :, L:] = s
        return out
    ropeA = tab(128, 1, 128 ** -0.5)
    ropeD = tab(64, 2, 64 ** -0.5)
    p = np.arange(128)[:, None]; f = np.arange(128)[None, :]
    m = np.zeros((NMASK, 128, 128), np.float32)
    m[M_ID] = (p == f); m[M_TRILS] = (p > f); m[M_TRIUI] = (p <= f); m[M_TRIUS] = (p < f); m[M_TRILI] = (p >= f)
    m[M_ONES] = 1.0
    m[M_NEG] = -1.0
    m[M_PA] = (p == (f ^ 32))
    m[M_PD] = (p == (f ^ 16))
    m[M_BD16] = (p // 16 == f // 16)
    m[M_OFF32] = (p // 32 == f // 32) & (p // 16 != f // 16)
    m[M_OFF64] = (p // 64 == f // 64) & (p // 32 != f // 32)
    m[M_OFF128] = (p // 64 != f // 64)
    return {"ropeA": ropeA, "ropeD": ropeD, "cmask": m}


def lockstep(gens, width):
    active = []
    it = iter(gens)
    done = False
    while True:
        while len(active) < width and not done:
            try:
                active.append(next(it))
            except StopIteration:
                done = True
        if not active:
            break
        for g in list(active):
            try:
                next(g)
            except StopIteration:
                active.remove(g)


class B:
    def __init__(self, cfg, dbg=()):
        self.cfg = cfg
        self.dbg = set(dbg)
        self.nc = bass.Bass("TRN2", target_bir_lowering=False)
        self.p = Prog(self.nc)
        self.dram = {}
        self.outs = []
        self.bank = 0

    def inp(self, name, shape, dt=F32):
        t = T(self.nc.dram_tensor(name, list(shape), dt, kind="ExternalInput").ap(), name)
        self.dram[name] = t
        return t

    def out(self, name, shape, dt=F32):
        t = T(self.nc.dram_tensor(name, list(shape), dt, kind="ExternalOutput").ap(), name)
        self.dram[name] = t
        self.outs.append(name)
        return t

    def scr(self, name, shape, dt):
        kind = "ExternalOutput" if name in self.dbg else "Internal"
        t = T(self.nc.dram_tensor(name, list(shape), dt, kind=kind).ap(), name)
        self.dram[name] = t
        if name in self.dbg:
            self.outs.append(name)
        return t

    def sb(self, st, name, shape, dt=F32):
        self.uid = getattr(self, "uid", 0) + 1
        name = f"{name}_{self.uid}"
        return T(st.enter_context(self.nc.sbuf_tensor(name, list(shape), dt)), name)

    def ps(self, st, name, shape, dt=F32):
        self.uid = getattr(self, "uid", 0) + 1
        name = f"{name}_{self.uid}"
        return T(st.enter_context(self.nc.psum_tensor(name, list(shape), dt)), name)

    def rsqrt(self, OUT, out_ap, IN, in_ap, eps, scale=1.0):
        p = self.p
        p.op("act", lambda e: e.activation(out=out_ap, in_=in_ap, func=AF.Sqrt, bias=self.epsc[eps][:out_ap.shape[0], :], scale=scale), reads=[IN, self.epst], writes=[OUT])
        p.op("dve", lambda e: e.reciprocal(out=out_ap, in_=out_ap), reads=[OUT], writes=[OUT])

    def declare(self):
        c = self.cfg
        D, T_, L, N, DFF, NL = c.D, c.T, c.L, c.N, c.DFF, c.NL
        i = self.inp
        i("x", [N, D]); i("c", [D]); i("ctx", [L, D]); i("c_ctx", [D])
        i("w_mod", [NL, D, 6 * D]); i("b_mod", [NL, 6 * D]); i("w_in", [NL, D, c.IN_W])
        i("wa_sink", [NL, 8])
        for n in ("df_lam_q1", "df_lam_k1", "df_lam_q2", "df_lam_k2"):
            i(n, [NL, 64])
        i("df_subln", [NL, 128]); i("dn_conv", [NL, 3, 3072]); i("dn_a_log", [NL, 16]); i("dn_dt_bias", [NL, 16])
        i("dn_norm", [NL, 128])
        i("w_branch_a", [NL, 1024, D]); i("w_branch_b", [NL, 1024, D]); i("w_branch_c", [NL, 1024, D])
        i("w_o", [NL, D, D]); i("ln1_g", [NL, D]); i("ln1_b", [NL, D])
        i("w_up", [NL, D, 2 * DFF]); i("ffn_conv_w", [NL, 3, 2 * DFF]); i("ffn_conv_b", [NL, 2 * DFF])
        i("w_down", [NL, DFF, D]); i("ln2_g", [NL, D]); i("ln2_b", [NL, D])
        i("ropeA", [4, 128, T_]); i("ropeD", [4, 128, T_]); i("cmask", [NMASK, 128, 128])
        self.out("y", [N, D])
        s = self.scr
        s("xres0", [T_, D], F32); s("xres1", [T_, D], F32); s("hT", [D, T_], BF16); s("modd", [NL, 2, 6 * D], F32)
        s("qaT", [1024, T_], BF16); s("kaT", [256, T_], BF16); s("va", [T_, 256], BF16)
        s("qdT", [1024, T_], BF16); s("kdT", [1024, T_], BF16); s("vd", [T_, 1024], BF16)
        s("gpre", [3072, T_], F32); s("zs", [T_, 1024], F32); s("ab", [T_, 32], F32); s("gtsT", [3 * D, T_], BF16)
        s("osub", [T_, D], F32)
        s("ya", [T_, 1024], BF16); s("yb", [T_, 1024], BF16); s("yc", [T_, 1024], BF16)
        s("qnT", [1024, T_], F32); s("knT", [1024, T_], F32); s("kn", [T_, 1024], F32); s("vn", [T_, 1024], F32)
        s("of", [T_, 1024], F32); s("mT", [D, T_], BF16); s("gT", [DFF, T_], BF16)
        s("yaT", [1024, T_], BF16); s("ybT", [1024, T_], BF16); s("ycT", [1024, T_], BF16)

    def consts(self, st):
        p = self.p
        cm = self.dram["cmask"]
        self.mask = self.sb(st, "mask", [128, NMASK, 128], F32)
        p.dma(self.mask[:], cm.t.rearrange("m p f -> p m f"), reads=[cm], writes=[self.mask])
        self.identb = self.sb(st, "identb", [128, 128], BF16)
        p.op("dve", lambda e: e.tensor_copy(out=self.identb[:], in_=self.mask[:, M_ID, :]), reads=[self.mask], writes=[self.identb])
        self.permb = self.sb(st, "permb", [128, 2, 128], BF16)
        p.op("dve", lambda e: e.tensor_copy(out=self.permb[:, 0, :], in_=self.mask[:, M_PA, :]), reads=[self.mask], writes=[self.permb])
        p.op("dve", lambda e: e.tensor_copy(out=self.permb[:, 1, :], in_=self.mask[:, M_PD, :]), reads=[self.mask], writes=[self.permb])
        self.epst = self.sb(st, "epst", [128, 4], F32)
        self.epsc = {}
        for i_, v_ in enumerate((1e-6, 0.0, 1.0)):
            p.op("pool", lambda e, i_=i_, v_=v_: e.memset(self.epst[:, i_:i_ + 1], v_), writes=[self.epst])
            self.epsc[v_] = self.epst[:, i_:i_ + 1]
        c = self.cfg
        self.modc = self.sb(st, "modc", [128, c.NL, 2, 6 * c.KC], F32)

    def stage_mod(self):
        c, p, nc = self.cfg, self.p, self.nc
        KC, D = c.KC, c.D
        NJ = 6 * KC
        wm, bm = self.dram["w_mod"], self.dram["b_mod"]
        with ExitStack() as st:
            craw = self.sb(st, "craw", [128, KC, 2], F32)
            sc = self.sb(st, "sc", [128, KC, 2], F32)
            bcol = self.sb(st, "bcol", [128, c.NL, NJ], F32)
            ps = [self.ps(st, f"mps{i}", [128, 4, 2], F32) for i in range(2)]
            pst = self.ps(st, "mpst", [128, 128], F32)
            wt = [self.sb(st, f"wmt{i}", [128, KC, 512], F32) for i in range(2)]
            tr = self.sb(st, "mtr", [NJ, 128], F32)
            cc, cx = self.dram["c"], self.dram["c_ctx"]
            p.dma(craw[:, :, 0], cc.t.rearrange("(kc p) -> p kc", p=128), reads=[cc], writes=[craw], allow_slow_non_contiguous=True)
            p.dma(craw[:, :, 1], cx.t.rearrange("(kc p) -> p kc", p=128), reads=[cx], writes=[craw], allow_slow_non_contiguous=True)
            for l in range(c.NL):
                p.dma(bcol[:, l, :], bm.t[l].rearrange("(j p) -> p j", p=128), reads=[bm], writes=[bcol], allow_slow_non_contiguous=True)
            p.op("act", lambda e: e.activation(out=sc[:], in_=craw[:], func=AF.Silu), reads=[craw], writes=[sc])
            k = 0
            for l in range(c.NL):
                wv = wm.t[l].rearrange("(kc p) n -> p kc n", p=128)
                for j0 in range(0, NJ, 4):
                    w = wt[k % 2]; pp = ps[k % 2]; k += 1
                    p.dma(w[:], wv[:, :, j0 * 128:(j0 + 4) * 128], reads=[wm], writes=[w])
                    for jj in range(4):
                        for kc in range(KC):
                            p.op("pe", lambda e, w=w, pp=pp, jj=jj, kc=kc: e.matmul(
                                pp[:, jj, :], w[:, kc, jj * 128:(jj + 1) * 128], sc[:, kc, :],
                                start=(kc == 0), stop=(kc == KC - 1)), reads=[w, sc], writes=[pp])
                    for s_ in range(2):
                        p.op("dve", lambda e, pp=pp, s_=s_, l=l, j0=j0: e.tensor_tensor(
                            out=self.modc[:, l, s_, j0:j0 + 4], in0=pp[:, :, s_], in1=bcol[:, l, j0:j0 + 4], op=ALU.add),
                            reads=[pp, bcol], writes=[self.modc])
                for s_ in range(2):
                    for v in (1, 4):
                        p.op("dve", lambda e, l=l, s_=s_, v=v: e.tensor_scalar_add(
                            out=self.modc[:, l, s_, v * KC:(v + 1) * KC], in0=self.modc[:, l, s_, v * KC:(v + 1) * KC], scalar1=1.0),
                            reads=[self.modc], writes=[self.modc])
                md = self.dram["modd"]
                for s_ in range(2):
                    p.op("pe", lambda e, l=l, s_=s_: e.matmul(pst[0:NJ, :], self.modc[:, l, s_, :], self.mask[:, M_ID, :], start=True, stop=True),
                         reads=[self.modc, self.mask], writes=[pst])
                    p.op("dve", lambda e: e.tensor_copy(out=tr[:], in_=pst[0:NJ, :]), reads=[pst], writes=[tr])
                    p.dma(md.t[l, s_].rearrange("(j p) -> j p", p=128), tr[:], reads=[tr], writes=[md])
        p.barrier()

    def stage_ln(self, l, comb, ada, final=False, src_inputs=False, skip_ctx=False):
        c, p = self.cfg, self.p
        D, KC, L = c.D, c.KC, c.L
        alpha = (2 * c.NL) ** 0.25
        xcur = getattr(self, "xcur", 0)
        xres, xdst = self.dram[f"xres{xcur}"], self.dram[f"xres{1 - xcur}"]
        osub, hT, md = self.dram["osub"], self.dram["hT"], self.dram["modd"]
        if comb and not final:
            self.xcur = 1 - xcur
        nch = (D + 511) // 512
        with ExitStack() as st:
            xt = [self.sb(st, f"lxt{i}", [128, D], F32) for i in range(3)]
            if comb:
                ot = [self.sb(st, f"lot{i}", [128, D], F32) for i in range(3)]
                gate = self.sb(st, "lgate", [128, 2, D], F32)
                gB = self.sb(st, "lgB", [128, D], F32)
                bB = self.sb(st, "lbB", [128, D], F32)
                gv = comb[0]
                for s_ in range(2):
                    p.dma(gate[:, s_, :], md.t[l, s_, gv * D:(gv + 1) * D].partition_broadcast(128), reads=[md], writes=[gate])
                gd, bd = self.dram[comb[1]], self.dram[comb[2]]
                p.dma(gB[:], gd.t[l].partition_broadcast(128), reads=[gd], writes=[gB])
                p.dma(bB[:], bd.t[l].partition_broadcast(128), reads=[bd], writes=[bB])
            stt = [self.sb(st, f"lst{i}", [128, nch, 6], F32) for i in range(3)]
            mv = [self.sb(st, f"lmv{i}", [128, 2], F32) for i in range(3)]
            rs = [self.sb(st, f"lrs{i}", [128, 1], F32) for i in range(3)]
            if ada:
                xb = [self.sb(st, f"lxb{i}", [128, D], BF16) for i in range(3)]
                ht = [self.sb(st, f"lht{i}", [128, KC, 128], BF16) for i in range(3)]
                pt = [self.ps(st, f"lpt{i}", [128, KC, 128], BF16) for i in range(3)]
            def tile(i):
                isctx = i * 128 < L
                s_ = 1 if isctx else 0
                b = i % 3
                X = xt[b]
                if src_inputs:
                    src = self.dram["ctx"] if isctx else self.dram["x"]
                    r0 = i * 128 if isctx else i * 128 - L
                else:
                    src = xres; r0 = i * 128
                p.dma(X[:], src.t[r0:r0 + 128, :], reads=[src], writes=[X])

                def lnstats(X, b):
                    S, MV, R = stt[b], mv[b], rs[b]
                    for ch in range(nch):
                        p.op("dve", lambda e, ch=ch: e.bn_stats(out=S[:, ch, :], in_=X[:, ch * 512:min(D, (ch + 1) * 512)]), reads=[X], writes=[S])
                    p.op("dve", lambda e: e.bn_aggr(out=MV[:], in_=S[:].rearrange("p a b -> p (a b)")), reads=[S], writes=[MV])
                    self.rsqrt(R, R[:], MV, MV[:, 1:2], LN_EPS)
                    return MV, R
                if comb:
                    O = ot[b]
                    p.dma(O[:], osub.t[i * 128:(i + 1) * 128, :], reads=[osub], writes=[O])
                    yield
                    p.op("pool", lambda e, O=O, s_=s_: e.tensor_tensor(out=O[:], in0=O[:], in1=gate[:, s_, :], op=ALU.mult), reads=[O, gate], writes=[O])
                    yield
                    p.op("dve", lambda e, O=O, X=X: e.scalar_tensor_tensor(out=X[:], in0=X[:], scalar=alpha, in1=O[:], op0=ALU.mult, op1=ALU.add), reads=[X, O], writes=[X])
                    MV, R = lnstats(X, b)
                    p.op("dve", lambda e, X=X, MV=MV, R=R: e.tensor_scalar(out=X[:], in0=X[:], scalar1=MV[:, 0:1], scalar2=R[:, 0:1], op0=ALU.subtract, op1=ALU.mult), reads=[X, MV, R], writes=[X])
                    yield
                    p.op("pool", lambda e, X=X: e.tensor_tensor(out=X[:], in0=X[:], in1=gB[:], op=ALU.mult), reads=[X, gB], writes=[X])
                    p.op("pool", lambda e, X=X: e.tensor_tensor(out=X[:], in0=X[:], in1=bB[:], op=ALU.add), reads=[X, bB], writes=[X])
                    if final:
                        if not isctx:
                            y = self.dram["y"]
                            p.dma(y.t[i * 128 - L:(i + 1) * 128 - L, :], X[:], reads=[X], writes=[y], is_out=True)
                    else:
                        p.dma(xdst.t[i * 128:(i + 1) * 128, :], X[:], reads=[X], writes=[xdst])
                if ada:
                    la, sv = ada
                    yield
                    MV, R = lnstats(X, b)
                    XB, HT, PT = xb[b], ht[b], pt[b]
                    p.op("dve", lambda e, X=X, XB=XB, MV=MV, R=R: e.tensor_scalar(out=XB[:], in0=X[:], scalar1=MV[:, 0:1], scalar2=R[:, 0:1], op0=ALU.subtract, op1=ALU.mult), reads=[X, MV, R], writes=[XB])
                    yield
                    for kc in range(KC):
                        p.op("pe", lambda e, kc=kc, XB=XB, PT=PT: e.transpose(PT[:, kc, :], XB[:, kc * 128:(kc + 1) * 128], self.identb[:]), reads=[XB, self.identb], writes=[PT])
                    yield
                    for kc in range(KC):
                        p.op("act", lambda e, kc=kc, HT=HT, PT=PT, s_=s_: e.activation(
                            out=HT[:, kc, :], in_=PT[:, kc, :], func=AF.Identity,
                            bias=self.modc[:, la, s_, sv * KC + kc:sv * KC + kc + 1],
                            scale=self.modc[:, la, s_, (sv + 1) * KC + kc:(sv + 1) * KC + kc + 1]),
                            reads=[PT, self.modc], writes=[HT])
                    p.dma(hT.t.rearrange("(kc p) t -> p kc t", p=128)[:, :, i * 128:(i + 1) * 128], HT[:], reads=[HT], writes=[hT])
            lockstep((tile(i) for i in range(c.NT) if not (skip_ctx and i * 128 < L)), 2)
        p.barrier()


def _proj(self, actT, K, W, wl, sections, tok0, tok1, NB=512, TG=512, tag="pj"):
    c, p = self.cfg, self.p
    KCs = K // 128
    wv = wl.rearrange("(kc p) n -> p kc n", p=128)
    av = actT.t.rearrange("(kc p) t -> p kc t", p=128)
    with ExitStack() as st:
        wbuf = [self.sb(st, f"{tag}w{i}", [128, KCs, NB], BF16) for i in range(2)]
        abuf = [self.sb(st, f"{tag}a{i}", [128, KCs, TG], BF16) for i in range(2)]
        nbank = 6
        banks = [self.ps(st, f"{tag}ps{i}", [128, 512], F32) for i in range(nbank)]
        self.pj_st = st
        for s in sections:
            if s.get("init"):
                s["init"](st)
        wi = 0; ai = 0; bi = 0
        bstate = {"bi": 0}

        def _alloc():
            b_ = banks[bstate["bi"] % nbank]; bstate["bi"] += 1
            return b_
        self.pj_alloc = _alloc
        blocks = []
        for s in sections:
            for nb0 in range(0, s["n"], NB):
                blocks.append((s, nb0, min(NB, s["n"] - nb0)))
        groups = [(g0, min(TG, tok1 - g0)) for g0 in range(tok0, tok1, TG)]
        nitems = len(blocks) * len(groups)

        def load_w(bx):
            s, nb0, nb = blocks[bx]
            wb = wbuf[bx % 2]
            for k0_ in range(0, KCs, 16):
                k1_ = min(KCs, k0_ + 16)
                p.dma(wb[:, k0_:k1_, :nb], wv[:, k0_:k1_, s["c0"] + nb0:s["c0"] + nb0 + nb], reads=[W], writes=[wb], eng="pool")
            return wb

        def load_a(kx):
            g0, ntok = groups[kx % len(groups)]
            ab = abuf[kx % 2]
            for k0_ in range(0, KCs, 16):
                k1_ = min(KCs, k0_ + 16)
                p.dma(ab[:, k0_:k1_, :ntok], av[:, k0_:k1_, g0:g0 + ntok], reads=[actT], writes=[ab])
            return ab
        wb_next = load_w(0); ab_next = load_a(0)
        kx = 0
        for bx, (s, nb0, nb) in enumerate(blocks):
            mode = s["mode"]
            wb = wb_next
            if bx + 1 < len(blocks):
                wb_next = load_w(bx + 1)
            for (g0, ntok) in groups:
                    ab = ab_next
                    if kx + 1 < nitems:
                        ab_next = load_a(kx + 1)
                    kx += 1
                    if s.get("pre"):
                        s["pre"](g0, ntok)
                    for sb0 in range(0, nb, 512):
                        sbn = min(512, nb - sb0)
                        if mode == "TM":
                            for tt in range(ntok // 128):
                                ps = _alloc()
                                for kc in range(KCs):
                                    p.op("pe", lambda e, ps=ps, ab=ab, wb=wb, kc=kc, tt=tt: e.matmul(
                                        ps[:, :sbn], ab[:, kc, tt * 128:(tt + 1) * 128], wb[:, kc, sb0:sb0 + sbn],
                                        start=(kc == 0), stop=(kc == KCs - 1)), reads=[ab, wb], writes=[ps])
                                s["epi"](ps, g0 + tt * 128, nb0 + sb0, sbn)
                        else:
                            for cc in range(sb0 // 128, (sb0 + sbn) // 128):
                                ps = _alloc()
                                for kc in range(KCs):
                                    p.op("pe", lambda e, ps=ps, ab=ab, wb=wb, kc=kc, cc=cc, ntok=ntok: e.matmul(
                                        ps[:, :ntok], wb[:, kc, cc * 128:(cc + 1) * 128], ab[:, kc, :ntok],
                                        start=(kc == 0), stop=(kc == KCs - 1)), reads=[ab, wb], writes=[ps])
                                s["epi"](ps, None, g0, ntok, nb0 + cc * 128)
            if s.get("flush") and (bx + 1 == len(blocks) or blocks[bx + 1][0] is not s):
                s["flush"]()
    p.barrier()
B.proj = _proj


def _stage_win(self, l, tok0=0):
    c, p = self.cfg, self.p
    D, T_ = c.D, c.T
    d = self.dram
    W = d["w_in"]
    state = {}

    def init(st):
        state["of"] = [self.sb(st, f"wiof{i}", [128, 512], F32) for i in range(3)]
        state["ob"] = [self.sb(st, f"wiob{i}", [128, 512], BF16) for i in range(3)]
        state["t1"] = [self.sb(st, f"wit1{i}", [128, 512], F32) for i in range(2)]
        state["t2"] = [self.sb(st, f"wit2{i}", [128, 512], F32) for i in range(2)]
        state["cos"] = [self.sb(st, f"wicos{i}", [128, 512], F32) for i in range(3)]
        state["sin"] = [self.sb(st, f"wisin{i}", [128, 512], F32) for i in range(3)]
        state["qb"] = [self.sb(st, f"wiqb{i}", [128, 512], BF16) for i in range(3)]
        state["k"] = 0; state["kc"] = 0; state["kq"] = 0; state["pend"] = None

    def tm_epi(dst, dcol0, func, bf):
        def epi(ps, tok, nb0, nb):
            k = state["k"]; state["k"] += 1
            o = (state["ob"] if bf else state["of"])[k % 3]
            p.op("act", lambda e: e.activation(out=o[:, :nb], in_=ps[:, :nb], func=func), reads=[ps], writes=[o])
            p.dma(dst.t[tok:tok + 128, dcol0 + nb0:dcol0 + nb0 + nb], o[:, :nb], reads=[o], writes=[dst])
        return epi

    def rope_pre(tabname, idx):
        tab = d[tabname]
        def pre(g0, ntok):
            k = state["kc"]; state["kc"] += 1
            cs, sn = state["cos"][k % 3], state["sin"][k % 3]
            p.dma(cs[:, :ntok], tab.t[idx, :, g0:g0 + ntok], reads=[tab], writes=[cs])
            p.dma(sn[:, :ntok], tab.t[idx + 1, :, g0:g0 + ntok], reads=[tab], writes=[sn])
            state["cs"] = (cs, sn)
        return pre

    def rope_finish():
        it = state["pend"]
        if it is None:
            return
        state["pend"] = None
        ps, qb, cs, sn, g0, ntok, c0, dst, which = it
        k = state["k"]; state["k"] += 1
        t1, t2, o = state["t1"][k % 2], state["t2"][k % 2], state["ob"][k % 3]
        ps2 = self.pj_alloc()
        p.op("pe", lambda e: e.matmul(ps2[:, :ntok], self.permb[:, which, :], qb[:, :ntok], start=True, stop=True), reads=[self.permb, qb], writes=[ps2])
        p.op("dve", lambda e: e.tensor_tensor(out=t1[:, :ntok], in0=ps[:, :ntok], in1=cs[:, :ntok], op=ALU.mult), reads=[ps, cs], writes=[t1])
        p.op("dve", lambda e: e.tensor_tensor(out=t2[:, :ntok], in0=ps2[:, :ntok], in1=sn[:, :ntok], op=ALU.mult), reads=[ps2, sn], writes=[t2])
        p.op("pool", lambda e: e.tensor_tensor(out=o[:, :ntok], in0=t1[:, :ntok], in1=t2[:, :ntok], op=ALU.add), reads=[t1, t2], writes=[o])
        p.dma(dst.t[c0:c0 + 128, g0:g0 + ntok], o[:, :ntok], reads=[o], writes=[dst])

    def rope_epi(dst, which):
        def epi(ps, ps2, g0, ntok, c0):
            rope_finish()
            kq = state["kq"]; state["kq"] += 1
            qb = state["qb"][kq % 3]
            cs, sn = state["cs"]
            p.op("dve", lambda e: e.tensor_copy(out=qb[:, :ntok], in_=ps[:, :ntok]), reads=[ps], writes=[qb])
            state["pend"] = (ps, qb, cs, sn, g0, ntok, c0, dst, which)
        return epi

    def fm_epi(dst, func=AF.Copy, bf=False):
        def epi(ps, ps2, g0, ntok, c0):
            k = state["k"]; state["k"] += 1
            o = (state["ob"] if bf else state["of"])[k % 3]
            p.op("act", lambda e: e.activation(out=o[:, :ntok], in_=ps[:, :ntok], func=func), reads=[ps], writes=[o])
            p.dma(dst.t[c0:c0 + 128, g0:g0 + ntok], o[:, :ntok], reads=[o], writes=[dst])
        return epi

    secs = [
        dict(c0=O_QA, n=1024, mode="FM", rope="A", pre=rope_pre("ropeA", 0), epi=rope_epi(d["qaT"], 0), flush=rope_finish, init=init),
        dict(c0=O_KA, n=256, mode="FM", rope="A", pre=rope_pre("ropeA", 2), epi=rope_epi(d["kaT"], 0), flush=rope_finish),
        dict(c0=O_QD, n=1024, mode="FM", rope="D", pre=rope_pre("ropeD", 0), epi=rope_epi(d["qdT"], 1), flush=rope_finish),
        dict(c0=O_KD, n=1024, mode="FM", rope="D", pre=rope_pre("ropeD", 2), epi=rope_epi(d["kdT"], 1), flush=rope_finish),
        dict(c0=O_QKV, n=3072, mode="FM", epi=fm_epi(d["gpre"])),
        dict(c0=O_VA, n=256, mode="TM", epi=tm_epi(d["va"], 0, AF.Copy, True)),
        dict(c0=O_VD, n=1024, mode="TM", epi=tm_epi(d["vd"], 0, AF.Copy, True)),
        dict(c0=O_Z, n=1024, mode="TM", epi=tm_epi(d["zs"], 0, AF.Silu, False)),
        dict(c0=O_A, n=32, mode="TM", epi=tm_epi(d["ab"], 0, AF.Copy, False)),
        dict(c0=O_G, n=3 * D, mode="FM", epi=fm_epi(d["gtsT"], AF.Sigmoid, True)),
    ]
    self.proj(d["hT"], D, W, W.t[l], secs, tok0, T_, NB=1024, tag="wi")
B.stage_win = _stage_win


def _stage_tm2fm(self, src, dstT, C, tok0=0):
    c, p = self.cfg, self.p
    nck = C // 128
    with ExitStack() as st:
        xt = [self.sb(st, f"tfx{i}", [128, C], BF16) for i in range(2)]
        ot = [self.sb(st, f"tfo{i}", [128, nck, 128], BF16) for i in range(2)]
        pt = [self.ps(st, f"tfp{i}", [128, nck, 128], BF16) for i in range(2)]
        for i in range(tok0 // 128, c.NT):
            X, O, P = xt[i % 2], ot[i % 2], pt[i % 2]
            p.dma(X[:], src.t[i * 128:(i + 1) * 128, :], reads=[src], writes=[X])
            for k in range(nck):
                p.op("pe", lambda e, k=k, X=X, P=P: e.transpose(P[:, k, :], X[:, k * 128:(k + 1) * 128], self.identb[:]), reads=[X, self.identb], writes=[P])
            p.op("act", lambda e, O=O, P=P: e.copy(out=O[:], in_=P[:]), reads=[P], writes=[O])
            p.dma(dstT.t.rearrange("(k p) t -> p k t", p=128)[:, :, i * 128:(i + 1) * 128], O[:], reads=[O], writes=[dstT])
    p.barrier()
B.stage_tm2fm = _stage_tm2fm


def _stage_diff(self, l, ctx_out):
    c, p = self.cfg, self.p
    T_, L, NT = c.T, c.L, c.NT
    d = self.dram
    qdT, kdT, vd, yb = d["qdT"], d["kdT"], d["vd"], d["yb"]
    lam_init = 0.8 - 0.6 * math.exp(-0.3 * l)
    with ExitStack() as st:
        lv = self.sb(st, "dflv", [128, 4, 64], F32)
        for i, n in enumerate(("df_lam_q1", "df_lam_k1", "df_lam_q2", "df_lam_k2")):
            p.dma(lv[:, i, :], d[n].t[l].partition_broadcast(128), reads=[d[n]], writes=[lv])
        lp = self.sb(st, "dflp", [128, 2, 64], F32)
        ls = self.sb(st, "dfls", [128, 2], F32)
        nlam = self.sb(st, "dfnl", [128, 1], F32)
        p.op("dve", lambda e: e.tensor_tensor(out=lp[:, 0, :], in0=lv[:, 0, :], in1=lv[:, 1, :], op=ALU.mult), reads=[lv], writes=[lp])
        p.op("dve", lambda e: e.tensor_tensor(out=lp[:, 1, :], in0=lv[:, 2, :], in1=lv[:, 3, :], op=ALU.mult), reads=[lv], writes=[lp])
        p.op("dve", lambda e: e.reduce_sum(out=ls[:], in_=lp[:], axis=AX.X), reads=[lp], writes=[ls])
        p.op("act", lambda e: e.activation(out=ls[:], in_=ls[:], func=AF.Exp), reads=[ls], writes=[ls])
        p.op("dve", lambda e: e.tensor_tensor(out=nlam[:], in0=ls[:, 1:2], in1=ls[:, 0:1], op=ALU.subtract), reads=[ls], writes=[nlam])
        p.op("dve", lambda e: e.tensor_scalar_add(out=nlam[:], in0=nlam[:], scalar1=-lam_init), reads=[nlam], writes=[nlam])
        sub = self.sb(st, "dfsub", [128, 128], F32)
        p.dma(sub[:], d["df_subln"].t[l].partition_broadcast(128), reads=[d["df_subln"]], writes=[sub])
        p.op("dve", lambda e: e.tensor_scalar_mul(out=sub[:], in0=sub[:], scalar1=1.0 - lam_init), reads=[sub], writes=[sub])

        kT = [self.sb(st, f"dfk{i}", [128, T_], BF16) for i in range(2)]
        vx = [self.sb(st, f"dfv{i}", [128, NT, 132], BF16) for i in range(2)]
        for i in range(2):
            p.op("pool", lambda e, i=i: e.memset(vx[i][:, :, 128:129], 1.0), writes=[vx[i]])
        qt = [self.sb(st, f"dfq{i}", [128, 512], BF16) for i in range(2)]
        pe_ = [[self.sb(st, f"dfp{j}{i}", [128, 512], BF16) for i in range(2)] for j in range(2)]
        sb_ = [[self.ps(st, f"dfs{j}{i}", [128, 512], F32) for i in range(2)] for j in range(2)]
        ob_ = [[self.ps(st, f"dfo{j}{i}", [128, 2, 256], F32) for i in range(2)] for j in range(2)]
        r12 = [self.sb(st, f"dfr{i}", [128, 2], F32) for i in range(2)]
        t1 = [self.sb(st, f"dft{i}", [128, 128], F32) for i in range(2)]
        o_ = [self.sb(st, f"dfoo{i}", [128, 128], F32) for i in range(2)]
        sq = [self.sb(st, f"dfsq{i}", [128, 128], F32) for i in range(2)]
        ss = [self.sb(st, f"dfss{i}", [128, 1], F32) for i in range(2)]
        yo = [self.sb(st, f"dfyo{i}", [128, 128], BF16) for i in range(2)]
        qi = 0; si = 0; ei = 0
        groups = []
        if ctx_out:
            groups.append((0, L, 0, L // 128))
        for g0 in range(L, T_, 512):
            groups.append((g0, min(512, T_ - g0), 0, NT))
        for h in range(8):
            K, V = kT[h % 2], vx[h % 2]
            p.dma(K[:], kdT.t[h * 128:(h + 1) * 128, :], reads=[kdT], writes=[K])
            vv_ = vd.t[:, h * 128:(h + 1) * 128].rearrange("(n p) c -> p n c", p=128)
            for n0_ in range(0, NT, 16):
                n1_ = min(NT, n0_ + 16)
                p.dma(V[:, n0_:n1_, 0:128], vv_[:, n0_:n1_, :], reads=[vd], writes=[V])
            for (g0, ntok, kt0, kt1) in groups:
                Q = qt[qi % 2]; qi += 1
                p.dma(Q[:, :ntok], qdT.t[h * 128:(h + 1) * 128, g0:g0 + ntok], reads=[qdT], writes=[Q])
                nq = ntok // 128

                def emit_S(kt):
                    for j in range(2):
                        S = sb_[j][kt % 2]
                        p.op("pe", lambda e, S=S, K=K, Q=Q, j=j, kt=kt, ntok=ntok: e.matmul(
                            S[:, :ntok], K[j * 64:(j + 1) * 64, kt * 128:(kt + 1) * 128], Q[j * 64:(j + 1) * 64, :ntok], start=True, stop=True),
                            reads=[K, Q], writes=[S])
                emit_S(kt0)
                for kt in range(kt0, kt1):
                    if kt + 1 < kt1:
                        emit_S(kt + 1)
                    for j in range(2):
                        S = sb_[j][kt % 2]; P = pe_[j][kt % 2]
                        p.op("act", lambda e, S=S, P=P, ntok=ntok: e.activation(out=P[:, :ntok], in_=S[:, :ntok], func=AF.Exp), reads=[S], writes=[P])
                        for qs in range(nq):
                            O = ob_[j][qs // 2]
                            p.op("pe", lambda e, O=O, P=P, V=V, qs=qs, kt=kt: e.matmul(
                                O[:, qs % 2, 0:129], P[:, qs * 128:(qs + 1) * 128], V[:, kt, 0:129], start=(kt == kt0 and qs % 2 == 0), stop=(kt == kt1 - 1)),
                                reads=[P, V], writes=[O])
                for qs in range(nq):
                    O1, O2 = ob_[0][qs // 2], ob_[1][qs // 2]
                    R_, T1, OO, SQ, SS, YO = r12[ei % 2], t1[ei % 2], o_[ei % 2], sq[ei % 2], ss[ei % 2], yo[ei % 2]; ei += 1
                    s2 = qs % 2
                    p.op("dve", lambda e, R_=R_, O1=O1, s2=s2: e.reciprocal(out=R_[:, 0:1], in_=O1[:, s2, 128:129]), reads=[O1], writes=[R_])
                    p.op("dve", lambda e, R_=R_, O2=O2, s2=s2: e.reciprocal(out=R_[:, 1:2], in_=O2[:, s2, 128:129]), reads=[O2], writes=[R_])
                    p.op("dve", lambda e, R_=R_: e.tensor_tensor(out=R_[:, 1:2], in0=R_[:, 1:2], in1=nlam[:], op=ALU.mult), reads=[R_, nlam], writes=[R_])
                    p.op("dve", lambda e, R_=R_, O1=O1, T1=T1, s2=s2: e.tensor_scalar_mul(out=T1[:], in0=O1[:, s2, 0:128], scalar1=R_[:, 0:1]), reads=[O1, R_], writes=[T1])
                    p.op("dve", lambda e, R_=R_, O2=O2, T1=T1, OO=OO, s2=s2: e.scalar_tensor_tensor(out=OO[:], in0=O2[:, s2, 0:128], scalar=R_[:, 1:2], in1=T1[:], op0=ALU.mult, op1=ALU.add), reads=[O2, R_, T1], writes=[OO])
                    p.op("act", lambda e, OO=OO, SQ=SQ, SS=SS: e.activation(out=SQ[:], in_=OO[:], func=AF.Square, accum_out=SS[:]), reads=[OO], writes=[SQ, SS])
                    self.rsqrt(SS, SS[:], SS, SS[:], 1e-6, scale=1.0 / 128)
                    p.op("dve", lambda e, OO=OO, SS=SS, YO=YO: e.scalar_tensor_tensor(out=YO[:], in0=OO[:], scalar=SS[:, 0:1], in1=sub[:], op0=ALU.mult, op1=ALU.mult), reads=[OO, SS, sub], writes=[YO])
                    t0 = g0 + qs * 128
                    p.dma(yb.t[t0:t0 + 128, h * 128:(h + 1) * 128], YO[:], reads=[YO], writes=[yb])
    p.barrier()
B.stage_diff = _stage_diff


def _stage_win_attn(self, l, ctx_out):
    c, p = self.cfg, self.p
    T_, L, NT = c.T, c.L, c.NT
    LT = L // 128
    d = self.dram
    qaT, kaT, va, ya = d["qaT"], d["kaT"], d["va"], d["ya"]
    with ExitStack() as st:
        es = self.sb(st, "waes", [128, 8], F32)
        p.dma(es[:], d["wa_sink"].t[l].partition_broadcast(128), reads=[d["wa_sink"]], writes=[es])
        p.op("act", lambda e: e.activation(out=es[:], in_=es[:], func=AF.Exp), reads=[es], writes=[es])
        mk = self.sb(st, "wamk", [128, 2, 4, 128], BF16)
        for h4 in range(4):
            p.op("dve", lambda e, h4=h4: e.tensor_copy(out=mk[:, 0, h4, :], in_=self.mask[:, M_TRILI, :]), reads=[self.mask], writes=[mk])
            p.op("dve", lambda e, h4=h4: e.tensor_copy(out=mk[:, 1, h4, :], in_=self.mask[:, M_TRIUI, :]), reads=[self.mask], writes=[mk])
        K = self.sb(st, "wak", [128, T_], BF16)
        V = self.sb(st, "wav", [128, NT, 132], BF16)
        p.op("pool", lambda e: e.memset(V[:, :, 128:129], 1.0), writes=[V])
        qt = [self.sb(st, f"waq{i}", [128, 4, 128], BF16) for i in range(2)]
        pp = [self.sb(st, f"wap{i}", [128, 512], BF16) for i in range(4)]
        sb_ = [self.ps(st, f"was{i}", [128, 512], F32) for i in range(4)]
        ob_ = [[self.ps(st, f"wao{i}{j}", [128, 2, 256], F32) for j in range(2)] for i in range(2)]
        rr = [self.sb(st, f"war{i}", [128, 1], F32) for i in range(4)]
        yo = [self.sb(st, f"wayo{i}", [128, 128], BF16) for i in range(4)]
        si = 0; ei = 0; bi = 0
        for g in range(2):
            p.dma(K[:], kaT.t[g * 128:(g + 1) * 128, :], reads=[kaT], writes=[K])
            vv_ = va.t[:, g * 128:(g + 1) * 128].rearrange("(n p) c -> p n c", p=128)
            for n0_ in range(0, NT, 16):
                n1_ = min(NT, n0_ + 16)
                p.dma(V[:, n0_:n1_, 0:128], vv_[:, n0_:n1_, :], reads=[va], writes=[V])
            def block(i, bi):
                Q = qt[bi % 2]; OB = ob_[bi % 2]
                SB = sb_[2 * (bi % 2):2 * (bi % 2) + 2]; PP = pp[2 * (bi % 2):2 * (bi % 2) + 2]
                for h4 in range(4):
                    h = g * 4 + h4
                    p.dma(Q[:, h4, :], qaT.t[h * 128:(h + 1) * 128, i * 128:(i + 1) * 128], reads=[qaT], writes=[Q])
                kts = [(kt, None) for kt in range(LT)]
                if i >= LT:
                    if i - 1 >= LT:
                        kts.append((i - 1, 0))
                    kts.append((i, None))
                    if i + 1 < NT:
                        kts.append((i + 1, 1))

                def emit_S(n_):
                    kt_ = kts[n_][0]
                    S_ = SB[n_ % 2]
                    p.op("pe", lambda e, S_=S_, Q=Q, kt_=kt_: e.matmul(S_[:], K[:, kt_ * 128:(kt_ + 1) * 128], Q[:].rearrange("p a b -> p (a b)"), start=True, stop=True),
                         reads=[K, Q], writes=[S_])
                yield
                emit_S(0)
                for n_, (kt, m) in enumerate(kts):
                    S = SB[n_ % 2]; P = PP[n_ % 2]
                    if n_ + 1 < len(kts):
                        emit_S(n_ + 1)
                    p.op("act", lambda e, S=S, P=P: e.activation(out=P[:], in_=S[:], func=AF.Exp), reads=[S], writes=[P])
                    if m is not None:
                        p.op("pool", lambda e, P=P, m=m: e.tensor_tensor(out=P[:], in0=P[:], in1=mk[:, m].rearrange("p a b -> p (a b)"), op=ALU.mult), reads=[P, mk], writes=[P])
                    yield
                    for h4 in range(4):
                        O = OB[h4 // 2]
                        p.op("pe", lambda e, O=O, P=P, h4=h4, kt=kt, n_=n_, nk=len(kts): e.matmul(
                            O[:, h4 % 2, 0:129], P[:, h4 * 128:(h4 + 1) * 128], V[:, kt, 0:129], start=(n_ == 0 and h4 % 2 == 0), stop=(n_ == nk - 1)),
                            reads=[P, V], writes=[O])
                yield
                for h4 in range(4):
                    h = g * 4 + h4
                    O = OB[h4 // 2]
                    R_, YO = rr[(bi * 4 + h4) % 4], yo[(bi * 4 + h4) % 4]
                    p.op("dve", lambda e, R_=R_, O=O, h4=h4, h=h: e.tensor_scalar(out=R_[:], in0=O[:, h4 % 2, 128:129], scalar1=es[:, h:h + 1], scalar2=None, op0=ALU.add), reads=[O, es], writes=[R_])
                    p.op("dve", lambda e, R_=R_: e.reciprocal(out=R_[:], in_=R_[:]), reads=[R_], writes=[R_])
                    p.op("dve", lambda e, R_=R_, O=O, YO=YO, h4=h4: e.tensor_scalar_mul(out=YO[:], in0=O[:, h4 % 2, 0:128], scalar1=R_[:, 0:1]), reads=[O, R_], writes=[YO])
                    p.dma(ya.t[i * 128:(i + 1) * 128, h * 128:(h + 1) * 128], YO[:], reads=[YO], writes=[ya])
            blocks = list(range(0 if ctx_out else LT, NT))
            lockstep((block(i, n) for n, i in enumerate(blocks)), 2)
    p.barrier()
B.stage_win_attn = _stage_win_attn


def _stage_gdn_conv(self, l):
    c, p = self.cfg, self.p
    T_, L, NT = c.T, c.L, c.NT
    d = self.dram
    gpre, qnT, knT, kn, vn = d["gpre"], d["qnT"], d["knT"], d["kn"], d["vn"]
    with ExitStack() as st:
        cw = self.sb(st, "gcw", [128, 24, 3], F32)
        for k_ in range(3):
            p.dma(cw[:, :, k_], d["dn_conv"].t[l, k_].rearrange("(ch p) -> p ch", p=128), reads=[d["dn_conv"]], writes=[cw], allow_slow_non_contiguous=True)
        G = [self.sb(st, f"gcG{i}", [128, 514], F32) for i in range(4)]
        Y = [self.sb(st, f"gcY{i}", [128, 512], F32) for i in range(4)]
        SQ = [self.sb(st, f"gcS{i}", [128, 512], F32) for i in range(4)]
        RI = [self.sb(st, f"gcR{i}", [128, 512], F32) for i in range(4)]
        YN = [self.sb(st, f"gcN{i}", [128, 512], F32) for i in range(4)]
        TT = [self.sb(st, f"gcT{i}", [128, 4, 128], F32) for i in range(4)]
        pss = [self.ps(st, f"gcps{i}", [128, 512], F32) for i in range(4)]
        ptt = [self.ps(st, f"gcpt{i}", [128, 4, 128], F32) for i in range(4)]
        ones = self.mask[:, M_ONES, :]
        ident = self.mask[:, M_ID, :]
        def iters():
            k = 0
            for ch in range(24):
                for (s0, s1) in ((0, L), (L, T_)):
                    for g0 in range(s0, s1, 512):
                        yield one(k, ch, s0, s1, g0)
                        k += 1

        def one(k, ch, s0, s1, g0):
                    kind = ch // 8
                    hh = ch % 8
                    ntok = min(512, s1 - g0)
                    g, y, sq, ri, yn, tt, ps1, pt1 = G[k % 4], Y[k % 4], SQ[k % 4], RI[k % 4], YN[k % 4], TT[k % 4], pss[k % 4], ptt[k % 4]
                    lo = max(s0, g0 - 1); hi = min(s1, g0 + ntok + 1)
                    if lo > g0 - 1:
                        p.op("pool", lambda e, g=g: e.memset(g[:, 0:1], 0.0), writes=[g])
                    if hi < g0 + ntok + 1:
                        p.op("pool", lambda e, g=g, ntok=ntok: e.memset(g[:, ntok + 1:ntok + 2], 0.0), writes=[g])
                    p.dma(g[:, lo - (g0 - 1):hi - (g0 - 1)], gpre.t[ch * 128:(ch + 1) * 128, lo:hi], reads=[gpre], writes=[g])
                    yield
                    p.op("dve", lambda e, g=g, y=y, ntok=ntok, ch=ch: e.tensor_scalar_mul(out=y[:, :ntok], in0=g[:, 0:ntok], scalar1=cw[:, ch, 0:1]), reads=[g, cw], writes=[y])
                    p.op("dve", lambda e, g=g, y=y, ntok=ntok, ch=ch: e.scalar_tensor_tensor(out=y[:, :ntok], in0=g[:, 1:ntok + 1], scalar=cw[:, ch, 1:2], in1=y[:, :ntok], op0=ALU.mult, op1=ALU.add), reads=[g, cw, y], writes=[y])
                    p.op("dve", lambda e, g=g, y=y, ntok=ntok, ch=ch: e.scalar_tensor_tensor(out=y[:, :ntok], in0=g[:, 2:ntok + 2], scalar=cw[:, ch, 2:3], in1=y[:, :ntok], op0=ALU.mult, op1=ALU.add), reads=[g, cw, y], writes=[y])
                    yield
                    p.op("act", lambda e, y=y, ntok=ntok: e.activation(out=y[:, :ntok], in_=y[:, :ntok], func=AF.Silu), reads=[y], writes=[y])
                    src = y
                    if kind < 2:
                        p.op("act", lambda e, y=y, sq=sq, ntok=ntok: e.activation(out=sq[:, :ntok], in_=y[:, :ntok], func=AF.Square), reads=[y], writes=[sq])
                        yield
                        p.op("pe", lambda e, ps1=ps1, sq=sq, ntok=ntok: e.matmul(ps1[:, :ntok], ones, sq[:, :ntok], start=True, stop=True), reads=[sq, self.mask], writes=[ps1])
                        yield
                        self.rsqrt(ri, ri[:, :ntok], ps1, ps1[:, :ntok], 1e-6)
                        sc_ = (128 ** -0.5) if kind == 0 else 1.0
                        p.op("dve", lambda e, y=y, ri=ri, yn=yn, ntok=ntok, sc_=sc_: e.scalar_tensor_tensor(out=yn[:, :ntok], in0=y[:, :ntok], scalar=sc_, in1=ri[:, :ntok], op0=ALU.mult, op1=ALU.mult), reads=[y, ri], writes=[yn])
                        dst = qnT if kind == 0 else knT
                        p.dma(dst.t[hh * 128:(hh + 1) * 128, g0:g0 + ntok], yn[:, :ntok], reads=[yn], writes=[dst])
                        src = yn
                    if kind >= 1:
                        nq = ntok // 128
                        for q_ in range(nq):
                            p.op("pe", lambda e, pt1=pt1, src=src, q_=q_: e.matmul(pt1[:, q_, :], src[:, q_ * 128:(q_ + 1) * 128], ident, start=True, stop=True), reads=[src, self.mask], writes=[pt1])
                        yield
                        p.op("act", lambda e, tt=tt, pt1=pt1, nq=nq: e.copy(out=tt[:, :nq, :], in_=pt1[:, :nq, :]), reads=[pt1], writes=[tt])
                        dst = kn if kind == 1 else vn
                        p.dma(dst.t[g0:g0 + ntok, hh * 128:(hh + 1) * 128].rearrange("(q p) c -> p q c", p=128), tt[:, :nq, :], reads=[tt], writes=[dst])
        lockstep(iters(), 3)
    p.barrier()
B.stage_gdn_conv = _stage_gdn_conv


def _stage_gdn_scan(self, l, ctx_out):
    c, p = self.cfg, self.p
    T_, L, NT = c.T, c.L, c.NT
    LT = L // 128
    d = self.dram
    qnT, knT, kn, vn, ab, of, zs, yc = d["qnT"], d["knT"], d["kn"], d["vn"], d["ab"], d["of"], d["zs"], d["yc"]
    M = lambda i: self.mask[:, i, :]
    with ExitStack() as st:
        S = self.sb(st, "gsS", [128, 8, 128], F32)
        S_h = [S.sub(f"S{h}") for h in range(8)]
        nal = self.sb(st, "gsnal", [128, 16], F32)
        dtb = self.sb(st, "gsdtb", [128, 16], F32)
        nw = self.sb(st, "gsnw", [128, 128], F32)
        p.dma(nal[:], d["dn_a_log"].t[l].partition_broadcast(128), reads=[d["dn_a_log"]], writes=[nal])
        p.dma(dtb[:], d["dn_dt_bias"].t[l].partition_broadcast(128), reads=[d["dn_dt_bias"]], writes=[dtb])
        p.dma(nw[:], d["dn_norm"].t[l].partition_broadcast(128), reads=[d["dn_norm"]], writes=[nw])
        p.op("act", lambda e: e.activation(out=nal[:], in_=nal[:], func=AF.Exp), reads=[nal], writes=[nal])
        p.op("dve", lambda e: e.tensor_scalar_mul(out=nal[:], in0=nal[:], scalar1=-1.0), reads=[nal], writes=[nal])
        banks = [self.ps(st, f"gsb{i}", [128, 4, 128], F32) for i in range(8)]
        slots = [(bnk, 0, bnk) for bnk in banks]
        sc = {"ps": 0}
        rings = {}

        def tmp(name, shape=(128, 128), n=6, dt=F32):
            if name == "X":
                n = 12
            if name not in rings:
                rings[name] = [[self.sb(st, f"gs_{name}{i}", list(shape), dt) for i in range(n)], 0]
            r = rings[name]
            t = r[0][r[1] % n]; r[1] += 1
            return t

        def mm(lhsT, lT, rhs, rT, ncols=128, acc=None):
            trk, j, bnk = slots[sc["ps"] % len(slots)]; sc["ps"] += 1
            ap = bnk[:, j, 0:ncols]
            p.op("pe", lambda e: e.matmul(ap, lhsT, rhs, start=True, stop=(acc is None)), reads=lT + rT, writes=[trk])
            if acc is not None:
                l2, l2T, r2, r2T = acc
                p.op("pe", lambda e: e.matmul(ap, l2, r2, start=False, stop=True), reads=l2T + r2T, writes=[trk])
            return trk, ap

        evi = {"k": 0}

        def evac(out_t, out_ap, trk, ap):
            k = evi["k"]; evi["k"] += 1
            if k % 2 == 0:
                p.op("act", lambda e: e.copy(out=out_ap, in_=ap), reads=[trk], writes=[out_t])
            else:
                p.op("dve", lambda e: e.tensor_copy(out=out_ap, in_=ap), reads=[trk], writes=[out_t])

        for dirn in range(2):
            cum = M_TRIUI if dirn == 0 else M_TRILI
            mL = M_TRILS if dirn == 0 else M_TRIUS
            mA = M_TRIUI if dirn == 0 else M_TRILI
            p.op("pool", lambda e: e.memset(S[:], 0.0), writes=[S] + S_h)
            if dirn == 0:
                order = list(range(NT))
            else:
                order = list(range(LT - 1, -1, -1)) + list(range(NT - 1, LT - 1, -1))
            for i in order:
                want = (i >= LT) or ctx_out
                tk = slice(i * 128, (i + 1) * 128)
                abt = tmp("abt", (128, 32), 2)
                p.dma(abt[:], ab.t[tk, :], reads=[ab], writes=[abt])
                knt = tmp("knt", (128, 1024), 2); vnt = tmp("vnt", (128, 1024), 2)
                kTt = tmp("kTt", (128, 8, 128), 2); qTt = tmp("qTt", (128, 8, 128), 2)
                p.dma(knt[:], kn.t[tk, :], reads=[kn], writes=[knt])
                p.dma(vnt[:], vn.t[tk, :], reads=[vn], writes=[vnt])
                p.dma(kTt[:], knT.t[:, tk].rearrange("(h p) t -> p h t", p=128), reads=[knT], writes=[kTt])
                p.dma(qTt[:], qnT.t[:, tk].rearrange("(h p) t -> p h t", p=128), reads=[qnT], writes=[qTt])
                gx = tmp("gx", (128, 8), 2); gax = tmp("gax", (128, 8), 2); g = tmp("g", (128, 8), 2); beta = tmp("beta", (128, 8), 2)
                ds = slice(dirn * 8, dirn * 8 + 8)
                p.op("dve", lambda e: e.tensor_tensor(out=gx[:], in0=abt[:, ds], in1=dtb[:, ds], op=ALU.add), reads=[abt, dtb], writes=[gx])
                p.op("act", lambda e: e.activation(out=gax[:], in_=gx[:], func=AF.Abs), reads=[gx], writes=[gax])
                p.op("act", lambda e: e.activation(out=gax[:], in_=gax[:], func=AF.Exp, scale=-1.0), reads=[gax], writes=[gax])
                p.op("act", lambda e: e.activation(out=gax[:], in_=gax[:], func=AF.Ln, bias=self.epsc[1.0], scale=1.0), reads=[gax, self.epst], writes=[gax])
                p.op("dve", lambda e: e.tensor_scalar_max(out=gx[:], in0=gx[:], scalar1=0.0), reads=[gx], writes=[gx])
                p.op("dve", lambda e: e.tensor_tensor(out=gx[:], in0=gx[:], in1=gax[:], op=ALU.add), reads=[gx, gax], writes=[gx])
                p.op("dve", lambda e: e.tensor_tensor(out=g[:], in0=gx[:], in1=nal[:, ds], op=ALU.mult), reads=[gx, nal], writes=[g])
                p.op("act", lambda e: e.activation(out=beta[:], in_=abt[:, 16 + dirn * 8:24 + dirn * 8], func=AF.Sigmoid), reads=[abt], writes=[beta])
                t1, a1 = mm(M(cum), [self.mask], g[:], [g], ncols=8)
                gcum = tmp("gcum", (128, 8), 2)
                evac(gcum, gcum[:], t1, a1)
                t2, a2 = mm(M(M_ONES), [self.mask], g[:], [g], ncols=8)
                gtot = tmp("gtot", (128, 8), 2)
                evac(gtot, gtot[:], t2, a2)
                eg = tmp("eg", (128, 8), 2); ekd = tmp("ekd", (128, 8), 2); egl = tmp("egl", (128, 8), 2); bk = tmp("bk", (128, 8), 2)
                p.op("act", lambda e: e.activation(out=eg[:], in_=gcum[:], func=AF.Exp), reads=[gcum], writes=[eg])
                p.op("dve", lambda e: e.tensor_tensor(out=ekd[:], in0=gtot[:], in1=gcum[:], op=ALU.subtract), reads=[gtot, gcum], writes=[ekd])
                p.op("act", lambda e: e.activation(out=ekd[:], in_=ekd[:], func=AF.Exp), reads=[ekd], writes=[ekd])
                p.op("act", lambda e: e.activation(out=egl[:], in_=gtot[:], func=AF.Exp), reads=[gtot], writes=[egl])
                p.op("dve", lambda e: e.tensor_tensor(out=bk[:], in0=beta[:], in1=eg[:], op=ALU.mult), reads=[beta, eg], writes=[bk])
                if want:
                    ot = tmp("ot", (128, 1024), 2)
                    if dirn == 1:
                        oft = tmp("oft", (128, 1024), 2); zt = tmp("zt", (128, 1024), 2)
                        p.dma(oft[:], of.t[tk, :], reads=[of], writes=[oft])
                        p.dma(zt[:], zs.t[tk, :], reads=[zs], writes=[zt])
                def unit(h):
                    hs = slice(h * 128, (h + 1) * 128)
                    hc = slice(h, h + 1)
                    kT = kTt[:, h, :]; qT = qTt[:, h, :]
                    Ug = tmp("Ug")
                    p.op("pool", lambda e: e.tensor_scalar(out=Ug[:], in0=M(cum), scalar1=g[:, hc], scalar2=None, op0=ALU.mult), reads=[self.mask, g], writes=[Ug])
                    yield
                    tD, aD = mm(M(M_ONES), [self.mask], Ug[:], [Ug])
                    DL = tmp("DL"); DU = tmp("DU")
                    p.op("dve", lambda e: e.tensor_scalar(out=DL[:], in0=aD, scalar1=gcum[:, hc], scalar2=0.0, op0=ALU.subtract, op1=ALU.max), reads=[tD, gcum], writes=[DL])
                    p.op("dve", lambda e: e.tensor_scalar(out=DU[:], in0=aD, scalar1=gcum[:, hc], scalar2=0.0, op0=ALU.subtract, op1=ALU.min), reads=[tD, gcum], writes=[DU])
                    p.op("act", lambda e: e.activation(out=DL[:], in_=DL[:], func=AF.Exp, scale=-1.0), reads=[DL], writes=[DL])
                    p.op("act", lambda e: e.activation(out=DU[:], in_=DU[:], func=AF.Exp), reads=[DU], writes=[DU])
                    p.op("pool", lambda e: e.tensor_tensor(out=DL[:], in0=DL[:], in1=M(mL), op=ALU.mult), reads=[DL, self.mask], writes=[DL])
                    p.op("pool", lambda e: e.tensor_tensor(out=DU[:], in0=DU[:], in1=M(mA), op=ALU.mult), reads=[DU, self.mask], writes=[DU])
                    yield
                    tG, aG = mm(kT, [kTt], kT, [kTt])
                    Lm = tmp("Lm")
                    p.op("dve", lambda e: e.scalar_tensor_tensor(out=Lm[:], in0=aG, scalar=beta[:, hc], in1=DL[:], op0=ALU.mult, op1=ALU.mult), reads=[tG, beta, DL], writes=[Lm])
                    if want:
                        tA, aA = mm(kT, [kTt], qT, [qTt])
                        AT = tmp("AT")
                        p.op("dve", lambda e: e.tensor_tensor(out=AT[:], in0=aA, in1=DU[:], op=ALU.mult), reads=[tA, DU], writes=[AT])
                    yield

                    def trn(src):
                        trk, j, bnk = slots[sc["ps"] % len(slots)]; sc["ps"] += 1
                        ap = bnk[:, j, 0:128]
                        p.op("pe", lambda e: e.transpose(ap, src[:], M(M_ID)), reads=[src, self.mask], writes=[trk])
                        return trk, ap
                    tN, aN = trn(Lm)
                    Nm = tmp("Nm")
                    evac(Nm, Nm[:], tN, aN)
                    L16 = tmp("L16"); N16 = tmp("N16")
                    p.op("pool", lambda e: e.tensor_tensor(out=L16[:], in0=Lm[:], in1=M(M_BD16), op=ALU.mult), reads=[Lm, self.mask], writes=[L16])
                    p.op("pool", lambda e: e.tensor_tensor(out=N16[:], in0=Nm[:], in1=M(M_BD16), op=ALU.mult), reads=[Nm, self.mask], writes=[N16])

                    def mmev(name, lh, lhT, rh, rhT):
                        t_, a_ = mm(lh[:], [lh], rh[:], [rh])
                        o_ = tmp(name)
                        evac(o_, o_[:], t_, a_)
                        return o_
                    yield
                    L2 = mmev("L2", N16, None, L16, None); N2 = mmev("N2", L16, None, N16, None)
                    yield
                    L4 = mmev("L4", N2, None, L2, None); N4 = mmev("N4", L2, None, N2, None)
                    yield
                    L8 = mmev("L8", N4, None, L4, None)
                    Q1 = tmp("P1")
                    p.op("pool", lambda e: e.tensor_tensor(out=Q1[:], in0=M(M_ID), in1=N16[:], op=ALU.subtract), reads=[N16, self.mask], writes=[Q1])

                    def mmadd(name, base, lh, rh):
                        t_, a_ = mm(lh[:], [lh], rh[:], [rh])
                        o_ = tmp(name)
                        p.op("dve", lambda e: e.tensor_tensor(out=o_[:], in0=base[:], in1=a_, op=ALU.add), reads=[base, t_], writes=[o_])
                        return o_
                    yield
                    Q2 = mmadd("P2", Q1, L2, Q1)
                    yield
                    Q3 = mmadd("P3", Q2, L4, Q2)
                    yield
                    Y = mmadd("X", Q3, L8, Q3)
                    for lvl in (M_OFF32, M_OFF64, M_OFF128):
                        Loff = tmp("Noff")
                        p.op("pool", lambda e, Loff=Loff, lvl=lvl: e.tensor_tensor(out=Loff[:], in0=Lm[:], in1=M(lvl), op=ALU.mult), reads=[Lm, self.mask], writes=[Loff])
                        yield
                        Xt_, Xa_ = trn(Y)
                        Xs = tmp("Y"); evac(Xs, Xs[:], Xt_, Xa_)
                        T2t, T2a = mm(Loff[:], [Loff], Y[:], [Y])
                        T2 = tmp("T2"); evac(T2, T2[:], T2t, T2a)
                        yield
                        Zt, Za = mm(Xs[:], [Xs], T2[:], [T2])
                        Yn = tmp("X")
                        p.op("dve", lambda e, Yn=Yn, Y=Y, Za=Za: e.tensor_tensor(out=Yn[:], in0=Y[:], in1=Za, op=ALU.subtract), reads=[Y, Zt], writes=[Yn])
                        Y = Yn
                    yield
                    RU = tmp("RU"); RW = tmp("RW"); KD = tmp("KD")
                    p.op("pool", lambda e: e.tensor_scalar(out=RU[:], in0=vnt[:, hs], scalar1=beta[:, hc], scalar2=None, op0=ALU.mult), reads=[vnt, beta], writes=[RU])
                    p.op("pool", lambda e: e.tensor_scalar(out=RW[:], in0=knt[:, hs], scalar1=bk[:, hc], scalar2=None, op0=ALU.mult), reads=[knt, bk], writes=[RW])
                    p.op("pool", lambda e: e.tensor_scalar(out=KD[:], in0=knt[:, hs], scalar1=ekd[:, hc], scalar2=None, op0=ALU.mult), reads=[knt, ekd], writes=[KD])
                    yield
                    ut, ua = mm(Y[:], [Y], RU[:], [RU])
                    U_ = tmp("U"); evac(U_, U_[:], ut, ua)
                    wt_, wa_ = mm(RW[:], [RW], Y[:], [Y])
                    WT = tmp("WT"); evac(WT, WT[:], wt_, wa_)
                    Sh = S_h[h]
                    yield
                    wst, wsa = mm(WT[:], [WT], S[:, h, :], [Sh])
                    VN = tmp("VN")
                    p.op("dve", lambda e: e.tensor_tensor(out=VN[:], in0=U_[:], in1=wsa, op=ALU.subtract), reads=[U_, wst], writes=[VN])
                    if want:
                        qst, qsa = mm(qT, [qTt], S[:, h, :], [Sh])
                        avt, ava = mm(AT[:], [AT], VN[:], [VN])
                        QS = tmp("QS")
                        p.op("dve", lambda e: e.tensor_scalar(out=QS[:], in0=qsa, scalar1=eg[:, hc], scalar2=None, op0=ALU.mult), reads=[qst, eg], writes=[QS])
                        if dirn == 0:
                            p.op("dve", lambda e: e.tensor_tensor(out=ot[:, hs], in0=QS[:], in1=ava, op=ALU.add), reads=[QS, avt], writes=[ot])
                        else:
                            p.op("dve", lambda e: e.tensor_tensor(out=QS[:], in0=QS[:], in1=ava, op=ALU.add), reads=[QS, avt], writes=[QS])
                            p.op("pool", lambda e: e.tensor_tensor(out=ot[:, hs], in0=QS[:], in1=oft[:, hs], op=ALU.add), reads=[QS, oft], writes=[ot])
                    yield
                    kvt, kva = mm(KD[:], [KD], VN[:], [VN])
                    p.op("dve", lambda e: e.scalar_tensor_tensor(out=S[:, h, :], in0=S[:, h, :], scalar=egl[:, hc], in1=kva, op0=ALU.mult, op1=ALU.add), reads=[Sh, egl, kvt], writes=[Sh])
                for hg in ((0, 1, 2, 3), (4, 5, 6, 7)):
                    gens = [unit(h) for h in hg]
                    while gens:
                        for g_ in list(gens):
                            try:
                                next(g_)
                            except StopIteration:
                                gens.remove(g_)
                if want:
                    if dirn == 0:
                        p.dma(of.t[tk, :], ot[:], reads=[ot], writes=[of])
                    else:
                        sq = tmp("osq", (128, 128), 2); ssq = tmp("ossq", (128, 8), 2)
                        yct = tmp("yct", (128, 1024), 2, BF16)
                        for h in range(8):
                            hs = slice(h * 128, (h + 1) * 128)
                            p.op("act", lambda e, hs=hs, h=h: e.activation(out=sq[:], in_=ot[:, hs], func=AF.Square, accum_out=ssq[:, h:h + 1]), reads=[ot], writes=[sq, ssq])
                        self.rsqrt(ssq, ssq[:], ssq, ssq[:], 1e-6, scale=1.0 / 128)
                        for h in range(8):
                            hs = slice(h * 128, (h + 1) * 128)
                            p.op("dve", lambda e, hs=hs, h=h: e.scalar_tensor_tensor(out=ot[:, hs], in0=ot[:, hs], scalar=ssq[:, h:h + 1], in1=nw[:], op0=ALU.mult, op1=ALU.mult), reads=[ot, ssq, nw], writes=[ot])
                        p.op("pool", lambda e: e.tensor_tensor(out=yct[:], in0=ot[:], in1=zt[:], op=ALU.mult), reads=[ot, zt], writes=[yct])
                        p.dma(yc.t[tk, :], yct[:], reads=[yct], writes=[yc])
    p.barrier()
B.stage_gdn_scan = _stage_gdn_scan


def _stage_merge(self, l):
    c, p = self.cfg, self.p
    D, T_ = c.D, c.T
    d = self.dram
    ys = [d["yaT"], d["ybT"], d["ycT"]]
    ws = [d["w_branch_a"], d["w_branch_b"], d["w_branch_c"]]
    gT, mT = d["gtsT"], d["mT"]
    NB = D
    with ExitStack() as st:
        wb = [[self.sb(st, f"mgw{b_}{i}", [128, 8, NB], BF16) for i in range(1)] for b_ in range(3)]
        ab = [[self.sb(st, f"mga{b_}{i}", [128, 8, 512], BF16) for i in range(2)] for b_ in range(3)]
        gt = [self.sb(st, f"mgg{i}", [128, 3, 512], BF16) for i in range(3)]
        t1 = [self.sb(st, f"mgt1{i}", [128, 512], F32) for i in range(2)]
        t2 = [self.sb(st, f"mgt2{i}", [128, 512], F32) for i in range(2)]
        mo = [self.sb(st, f"mgo{i}", [128, 512], BF16) for i in range(2)]
        banks = [self.ps(st, f"mgps{i}", [128, 512], F32) for i in range(6)]
        wi = ai = bi = k = 0
        for nb0 in range(0, D, NB):
            for b_ in range(3):
                p.dma(wb[b_][0][:], ws[b_].t[l].rearrange("(kc p) n -> p kc n", p=128)[:, :, nb0:nb0 + NB], reads=[ws[b_]], writes=[wb[b_][0]], eng="pool")
            W3 = [wb[b_][0] for b_ in range(3)]; wi += 1
            for g0 in range(0, T_, 512):
                ntok = min(512, T_ - g0)
                A3 = [ab[b_][ai % 2] for b_ in range(3)]; ai += 1
                for b_ in range(3):
                    p.dma(A3[b_][:, :, :ntok], ys[b_].t.rearrange("(kc p) t -> p kc t", p=128)[:, :, g0:g0 + ntok], reads=[ys[b_]], writes=[A3[b_]])
                for cc in range(NB // 128):
                    r0 = nb0 + cc * 128
                    G_ = gt[k % 3]; T1 = t1[k % 2]; T2 = t2[k % 2]; MO = mo[k % 2]; k += 1
                    for b_ in range(3):
                        p.dma(G_[:, b_, :ntok], gT.t[b_ * D + r0:b_ * D + r0 + 128, g0:g0 + ntok], reads=[gT], writes=[G_])
                    P3 = []
                    for b_ in range(3):
                        ps = banks[bi % 6]; bi += 1
                        for kc in range(8):
                            p.op("pe", lambda e, ps=ps, b_=b_, kc=kc: e.matmul(ps[:, :ntok], W3[b_][:, kc, cc * 128:(cc + 1) * 128], A3[b_][:, kc, :ntok],
                                                                              start=(kc == 0), stop=(kc == 7)), reads=[W3[b_], A3[b_]], writes=[ps])
                        P3.append(ps)
                    p.op("dve", lambda e: e.tensor_tensor(out=T1[:, :ntok], in0=P3[0][:, :ntok], in1=G_[:, 0, :ntok], op=ALU.mult), reads=[P3[0], G_], writes=[T1])
                    p.op("dve", lambda e: e.tensor_tensor(out=T2[:, :ntok], in0=P3[1][:, :ntok], in1=G_[:, 1, :ntok], op=ALU.mult), reads=[P3[1], G_], writes=[T2])
                    p.op("pool", lambda e: e.tensor_tensor(out=T1[:, :ntok], in0=T1[:, :ntok], in1=T2[:, :ntok], op=ALU.add), reads=[T1, T2], writes=[T1])
                    p.op("dve", lambda e: e.tensor_tensor(out=T2[:, :ntok], in0=P3[2][:, :ntok], in1=G_[:, 2, :ntok], op=ALU.mult), reads=[P3[2], G_], writes=[T2])
                    p.op("pool", lambda e: e.tensor_tensor(out=MO[:, :ntok], in0=T1[:, :ntok], in1=T2[:, :ntok], op=ALU.add), reads=[T1, T2], writes=[MO])
                    p.dma(mT.t[r0:r0 + 128, g0:g0 + ntok], MO[:, :ntok], reads=[MO], writes=[mT])
    p.barrier()
B.stage_merge = _stage_merge


def _stage_tm_proj(self, l, actname, K, wname, NB=512, TG=512):
    p = self.p
    d = self.dram
    osub = d["osub"]
    state = {"k": 0}

    def init(st):
        state["o"] = [self.sb(st, f"tpo{i}", [128, 512], F32) for i in range(3)]

    def epi(ps, tok, nb0, nb):
        o = state["o"][state["k"] % 3]; state["k"] += 1
        p.op("act", lambda e: e.copy(out=o[:, :nb], in_=ps[:, :nb]), reads=[ps], writes=[o])
        p.dma(osub.t[tok:tok + 128, nb0:nb0 + nb], o[:, :nb], reads=[o], writes=[osub])
    W = d[wname]
    secs = [dict(c0=0, n=self.cfg.D, mode="TM", epi=epi, init=init)]
    self.proj(d[actname], K, W, W.t[l], secs, 0, self.cfg.T, NB=NB, TG=TG, tag="tp")
B.stage_tm_proj = _stage_tm_proj


def _stage_ffn_up(self, l):
    c, p = self.cfg, self.p
    D, T_, L, DFF, KC = c.D, c.T, c.L, c.DFF, c.KC
    d = self.dram
    hT, gT, W = d["hT"], d["gT"], d["w_up"]
    NCH = 2 * DFF // 128
    HC = DFF // 128
    CB = 4 if HC % 4 == 0 else (2 if HC % 2 == 0 else 1)
    TG = 510
    with ExitStack() as st:
        cw = self.sb(st, "fucw", [128, NCH, 3], F32)
        cb = self.sb(st, "fucb", [128, NCH], F32)
        for k_ in range(3):
            p.dma(cw[:, :, k_], d["ffn_conv_w"].t[l, k_].rearrange("(ch p) -> p ch", p=128), reads=[d["ffn_conv_w"]], writes=[cw], allow_slow_non_contiguous=True)
        p.dma(cb[:], d["ffn_conv_b"].t[l].rearrange("(ch p) -> p ch", p=128), reads=[d["ffn_conv_b"]], writes=[cb], allow_slow_non_contiguous=True)
        wA = [self.sb(st, f"fuwa{i}", [128, KC, CB * 128], BF16) for i in range(2)]
        wB = [self.sb(st, f"fuwb{i}", [128, KC, CB * 128], BF16) for i in range(2)]
        ab = [self.sb(st, f"fua{i}", [128, KC, 512], BF16) for i in range(2)]
        ua = [self.sb(st, f"fuua{i}", [128, 512], F32) for i in range(2)]
        ub = [self.sb(st, f"fuub{i}", [128, 512], F32) for i in range(2)]
        go = [self.sb(st, f"fugo{i}", [128, 512], BF16) for i in range(2)]
        banks = [self.ps(st, f"fups{i}", [128, 512], F32) for i in range(6)]
        wv = W.t[l].rearrange("(kc p) n -> p kc n", p=128)
        av = hT.t.rearrange("(kc p) t -> p kc t", p=128)
        wi = ai = bi = k = 0
        for cb0 in range(0, HC, CB):
            WA, WB = wA[wi % 2], wB[wi % 2]; wi += 1
            p.dma(WA[:], wv[:, :, cb0 * 128:(cb0 + CB) * 128], reads=[W], writes=[WA], eng="pool")
            p.dma(WB[:], wv[:, :, DFF + cb0 * 128:DFF + (cb0 + CB) * 128], reads=[W], writes=[WB], eng="pool")
            for (s0, s1) in ((0, L), (L, T_)):
                for g0 in range(s0, s1, TG):
                    n = min(TG, s1 - g0)
                    A = ab[ai % 2]; ai += 1
                    lo = max(s0, g0 - 1); hi = min(s1, g0 + n + 1)
                    if lo > g0 - 1:
                        p.op("pool", lambda e: e.memset(A[:, :, 0:1], 0.0), writes=[A])
                    if hi < g0 + n + 1:
                        p.op("pool", lambda e: e.memset(A[:, :, n + 1:n + 2], 0.0), writes=[A])
                    p.dma(A[:, :, lo - (g0 - 1):hi - (g0 - 1)], av[:, :, lo:hi], reads=[hT], writes=[A])
                    for cc in range(CB):
                        cha = cb0 + cc; chb = HC + cb0 + cc
                        pa = banks[bi % 6]; bi += 1
                        pb = banks[bi % 6]; bi += 1
                        for kc in range(KC):
                            p.op("pe", lambda e, kc=kc: e.matmul(pa[:, :n + 2], WA[:, kc, cc * 128:(cc + 1) * 128], A[:, kc, :n + 2], start=(kc == 0), stop=(kc == KC - 1)), reads=[WA, A], writes=[pa])
                        for kc in range(KC):
                            p.op("pe", lambda e, kc=kc: e.matmul(pb[:, :n + 2], WB[:, kc, cc * 128:(cc + 1) * 128], A[:, kc, :n + 2], start=(kc == 0), stop=(kc == KC - 1)), reads=[WB, A], writes=[pb])
                        UA, UB, GO = ua[k % 2], ub[k % 2], go[k % 2]; k += 1
                        for (U, ps, ch) in ((UA, pa, cha), (UB, pb, chb)):
                            p.op("dve", lambda e, U=U, ps=ps, ch=ch: e.tensor_scalar(out=U[:, :n], in0=ps[:, 0:n], scalar1=cw[:, ch, 0:1], scalar2=cb[:, ch:ch + 1], op0=ALU.mult, op1=ALU.add), reads=[ps, cw, cb], writes=[U])
                            p.op("dve", lambda e, U=U, ps=ps, ch=ch: e.scalar_tensor_tensor(out=U[:, :n], in0=ps[:, 1:n + 1], scalar=cw[:, ch, 1:2], in1=U[:, :n], op0=ALU.mult, op1=ALU.add), reads=[ps, cw, U], writes=[U])
                            p.op("dve", lambda e, U=U, ps=ps, ch=ch: e.scalar_tensor_tensor(out=U[:, :n], in0=ps[:, 2:n + 2], scalar=cw[:, ch, 2:3], in1=U[:, :n], op0=ALU.mult, op1=ALU.add), reads=[ps, cw, U], writes=[U])
                        p.op("act", lambda e: e.activation(out=UA[:, :n], in_=UA[:, :n], func=AF.Silu), reads=[UA], writes=[UA])
                        p.op("pool", lambda e: e.tensor_tensor(out=GO[:, :n], in0=UA[:, :n], in1=UB[:, :n], op=ALU.mult), reads=[UA, UB], writes=[GO])
                        p.dma(gT.t[cha * 128:(cha + 1) * 128, g0:g0 + n], GO[:, :n], reads=[GO], writes=[gT])
    p.barrier()
B.stage_ffn_up = _stage_ffn_up


def _build_all(self, st, upto=None):
    c = self.cfg
    self.declare()
    self.consts(st)
    self.stage_mod()
    self.stage_ln(0, None, (0, 0), src_inputs=True)
    for l in range(c.NL):
        last = (l == c.NL - 1)
        ctx_out = not last
        self.stage_win(l)
        self.stage_win_attn(l, ctx_out)
        self.stage_diff(l, ctx_out)
        self.stage_gdn_conv(l)
        self.stage_gdn_scan(l, ctx_out)
        d = self.dram
        self.stage_tm2fm(d["ya"], d["yaT"], 1024)
        self.stage_tm2fm(d["yb"], d["ybT"], 1024)
        self.stage_tm2fm(d["yc"], d["ycT"], 1024)
        self.stage_merge(l)
        self.stage_tm_proj(l, "mT", c.D, "w_o", NB=min(1024, c.D))
        if upto == "mix" and l == 0:
            break
        self.stage_ln(l, (2, "ln1_g", "ln1_b"), (l, 3), src_inputs=(l == 0))
        self.stage_ffn_up(l)
        kdown = c.DFF
        big = (kdown // 128) > 16
        self.stage_tm_proj(l, "gT", kdown, "w_down", NB=512, TG=256 if big else 512)
        if last:
            self.stage_ln(l, (5, "ln2_g", "ln2_b"), None, final=True)
        else:
            self.stage_ln(l, (5, "ln2_g", "ln2_b"), (l + 1, 0))
    self.p.emit(st)
B.build_all = _build_all


_CACHE = {}


def _get_nc(cfg_key):
    if cfg_key not in _CACHE:
        cfg = Cfg(*cfg_key)
        b = B(cfg)
        st = ExitStack()
        b.build_all(st)
        _CACHE[cfg_key] = (b, st, cfg)
    return _CACHE[cfg_key]


def kernel(**inputs):
    x = np.asarray(inputs["x"])
    bsz, N, D = x.shape
    L = inputs["ctx"].shape[1]
    DFF = inputs["w_down"].shape[1]
    NL = inputs["w_mod"].shape[0]
    b, st, cfg = _get_nc((D, N, L, DFF, NL))
    consts = make_consts(cfg)
    shared = {}
    for k, v in inputs.items():
        if k in ("x", "c", "ctx", "c_ctx"):
            continue
        shared[k] = np.ascontiguousarray(np.asarray(v), dtype=np.float32)
    shared["dn_a_log"] = shared["dn_a_log"].reshape(NL, 16)
    shared["dn_dt_bias"] = shared["dn_dt_bias"].reshape(NL, 16)
    shared["c_ctx"] = np.ascontiguousarray(np.asarray(inputs["c_ctx"]), dtype=np.float32)
    shared.update(consts)
    n_cores = 8 if bsz <= 4 else bsz
    hot = [0, 1, 4, 5][:bsz] if bsz <= 4 else list(range(bsz))
    zx = np.zeros_like(np.ascontiguousarray(x[0], dtype=np.float32))
    zc = np.zeros((D,), np.float32)
    zctx = np.zeros((L, D), np.float32)
    in_maps = []
    for core in range(n_cores):
        m = dict(shared)
        if core in hot:
            i = hot.index(core)
            m["x"] = np.ascontiguousarray(x[i], dtype=np.float32)
            m["c"] = np.ascontiguousarray(np.asarray(inputs["c"])[i], dtype=np.float32)
            m["ctx"] = np.ascontiguousarray(np.asarray(inputs["ctx"])[i], dtype=np.float32)
        else:
            m["x"], m["c"], m["ctx"] = zx, zc, zctx
        in_maps.append(m)
    res = run_bass_kernel_spmd(b.nc, in_maps, core_ids=list(range(n_cores)))
    return np.stack([np.asarray(res.results[core]["y"], dtype=np.float32) for core in hot], axis=0)
```

```python
import numpy as np
from contextlib import ExitStack
import concourse.bass as bass
import concourse.mybir as mybir
from concourse.bass_utils import run_bass_kernel_spmd

F32 = mybir.dt.float32
BF16 = mybir.dt.bfloat16
AF = mybir.ActivationFunctionType
ALU = mybir.AluOpType
AX = mybir.AxisListType

ENGS = ("pe", "act", "dve", "pool", "sp")
EPOCH = 30000
NDMASEM = 40


import types


def freeze(fn):
    if fn is None or fn.__closure__ is None:
        return fn
    cells = tuple(types.CellType(c.cell_contents) for c in fn.__closure__)
    return types.FunctionType(fn.__code__, fn.__globals__, fn.__name__, fn.__defaults__, cells)


class T:
    __slots__ = ("t", "name", "w", "r", "rd")

    def __init__(self, t, name=""):
        self.t = t
        self.name = name
        self.w = None
        self.r = {}
        self.rd = {}

    def sub(self, name=""):
        return T(self.t, name or self.name)

    def __getitem__(self, idx):
        return self.t[idx]


class Op:
    __slots__ = ("eng", "seq", "fn", "deps", "signal", "isdma", "sem", "val", "dsem", "dval", "dprev", "dslot")

    def __init__(self, eng, seq, fn, isdma):
        self.eng = eng
        self.seq = seq
        self.fn = fn
        self.deps = []
        self.signal = False
        self.isdma = isdma
        self.sem = None
        self.val = 0
        self.dsem = None
        self.dval = 0
        self.dprev = 0


class Prog:
    def __init__(self, nc, same_raw=True):
        self.nc = nc
        self.ops = {e: [] for e in ENGS}
        self.seen = {e: {} for e in ENGS}
        self.seen_dma = {e: set() for e in ENGS}
        self.pending = {e: [] for e in ENGS}
        self.ndma = 0
        self.same_raw = same_raw
        self.out_dmas = []
        self.all_dmas_unwaited = []
        self.dcount = {e: 0 for e in ENGS}
        self.lastd = {}

    def _need(self, op, tgt):
        if tgt is None or tgt is op:
            return
        e = op.eng
        if tgt.isdma:
            if id(tgt) in self.seen_dma[e]:
                return
            self.seen_dma[e].add(id(tgt))
            op.deps.append(tgt)
            return
        if tgt.eng == e:
            if e == "pe" or not self.same_raw:
                return
        if self.seen[e].get(tgt.eng, 0) >= tgt.seq:
            return
        self.seen[e][tgt.eng] = tgt.seq
        tgt.signal = True
        op.deps.append(tgt)

    def _record(self, eng, fn, reads, writes, isdma=False, raw_only_same=True):
        ops = self.ops[eng]
        op = Op(eng, len(ops) + 1, fn, isdma)
        if isdma:
            op.dslot = self.dcount[eng] % NDMASEM
            self.dcount[eng] += 1
            self.lastd[(eng, op.dslot)] = op
        for t in self.pending[eng]:
            self._need(op, t)
        self.pending[eng] = []
        for b in reads:
            self._need(op, b.w)
        for b in writes:
            self._need(op, b.w)
            for re_, r in b.r.items():
                if re_ == eng and not isdma:
                    continue
                self._need(op, r)
            for r in b.rd.values():
                self._need(op, r)
        for b in reads:
            if isdma:
                b.rd[(eng, op.dslot)] = op
            else:
                b.r[eng] = op
        for b in writes:
            b.w = op
            b.r = {}
            b.rd = {}
        ops.append(op)
        return op

    def op(self, eng, fn, reads=(), writes=()):
        return self._record(eng, freeze(fn), reads, writes)

    def dma(self, out_ap, in_ap, reads=(), writes=(), eng="sp", is_out=False, **kw):
        def fn(e, out_ap=out_ap, in_ap=in_ap, kw=kw):
            return e.dma_start(out=out_ap, in_=in_ap, **kw)
        op = self._record(eng, fn, reads, writes, isdma=True)
        op.signal = True
        self.ndma += 1
        if is_out:
            self.out_dmas.append(op)
        return op

    def barrier(self, label=None):
        import sys as _sys
        if not hasattr(self, "marks"):
            self.marks = []
        self.marks.append((label or _sys._getframe(1).f_code.co_name, {e: len(self.ops[e]) for e in ENGS}))
        lasts = []
        for e in ENGS:
            for o in reversed(self.ops[e]):
                if not o.isdma:
                    lasts.append(o)
                    break
        dmas = list(self.lastd.values())
        for e in ENGS:
            self.pending[e] = self.pending[e] + lasts + dmas

    def emit(self, stack):
        nc = self.nc
        self.barrier()
        fin = {}
        for e in ENGS:
            op = Op(e, len(self.ops[e]) + 1, None, False)
            for t in self.pending[e]:
                self._need(op, t)
            self.ops[e].append(op)
        for e in ENGS:
            cnt = 0
            sems = []
            for op in self.ops[e]:
                if op.isdma or not op.signal:
                    continue
                ep = cnt // EPOCH
                while len(sems) <= ep:
                    sems.append(stack.enter_context(nc.semaphore(f"c_{e}_{len(sems)}")))
                cnt += 1
                op.sem = sems[ep]
                op.val = cnt - ep * EPOCH
        dpool = {}
        duse = {}
        for e in ENGS:
            k = 0
            for op in self.ops[e]:
                if not op.isdma:
                    continue
                if e not in dpool:
                    dpool[e] = [stack.enter_context(nc.semaphore(f"d_{e}_{i}")) for i in range(NDMASEM)]
                    duse[e] = [0] * NDMASEM
                i = op.dslot
                k += 1
                op.dsem = dpool[e][i]
                op.dprev = 16 * duse[e][i]
                duse[e][i] += 1
                op.dval = 16 * duse[e][i]
        block = stack.enter_context(nc.Block())

        def run(eng, e):
            for op in self.ops[e]:
                for d in op.deps:
                    if d.isdma:
                        eng.wait_ge(d.dsem, d.dval)
                    else:
                        eng.wait_ge(d.sem, d.val)
                if op.fn is None:
                    continue
                if op.isdma:
                    if op.dprev:
                        eng.wait_ge(op.dsem, op.dprev)
                    op.fn(eng).then_inc(op.dsem, 16)
                else:
                    ins = op.fn(eng)
                    if op.signal:
                        ins.then_inc(op.sem, 1)

        block.tensor(lambda eng: run(eng, "pe"))
        block.scalar(lambda eng: run(eng, "act"))
        block.vector(lambda eng: run(eng, "dve"))
        block.gpsimd(lambda eng: run(eng, "pool"))
        block.sync(lambda eng: run(eng, "sp"))

    def stats(self):
        return {e: len(self.ops[e]) for e in ENGS}

import math

class Cfg:
    def __init__(self, D=2048, N=8192, L=256, DFF=5632, NL=2):
        self.D, self.N, self.L, self.DFF, self.NL = D, N, L, DFF, NL
        self.T = N + L
        self.KC = D // 128
        self.NT = self.T // 128
        self.IN_W = 1024 + 256 + 256 + 1024 + 1024 + 1024 + 3072 + 1024 + 16 + 16 + 3 * D

GRID_W = 64
ALPHA = None
LN_EPS = 1e-6

O_QA, O_KA, O_VA, O_QD, O_KD, O_VD, O_QKV, O_Z, O_A, O_B, O_G = (
    0, 1024, 1280, 1536, 2560, 3584, 4608, 7680, 8704, 8720, 8736)

M_ID, M_TRILS, M_TRIUI, M_TRIUS, M_TRILI, M_ONES, M_BD16, M_OFF32, M_OFF64, M_OFF128, M_NEG, M_PA, M_PD = range(13)
NMASK = 13


def rope_tables(n, dim):
    rows = n // GRID_W
    row = np.repeat(np.arange(rows, dtype=np.float32), GRID_W)
    col = np.tile(np.arange(GRID_W, dtype=np.float32), rows)
    axis_dim = dim // 2
    inv_freq = (10000.0 ** (-np.arange(0, axis_dim, 2, dtype=np.float32) / axis_dim)).astype(np.float32)
    ang_r = row[:, None] * inv_freq[None]
    ang_c = col[:, None] * inv_freq[None]
    ang = np.concatenate([ang_r, ang_r, ang_c, ang_c], axis=-1).astype(np.float32)
    q = dim // 4
    sgn = np.concatenate([-np.ones(q), np.ones(q), -np.ones(q), np.ones(q)]).astype(np.float32)
    return np.cos(ang).T.astype(np.float32), (np.sin(ang) * sgn[None]).T.astype(np.float32)


def make_consts(cfg):
    T, L = cfg.T, cfg.L
    def tab(dim, rep, scale):
        c, s = rope_tables(cfg.N, dim)
        out = np.zeros((4, 128, T), np.float32)
        c = np.tile(c, (rep, 1)); s = np.tile(s, (rep, 1))
        out[0, :, :L] = scale; out[0, :, L:] = c * scale
        out[1, :, L:] = s * scale
        out[2, :, :L] = 1.0; out[2, :, L:] = c
        out[3, :, L:] = s
        return out
    ropeA = tab(128, 1, 128 ** -0.5)
    ropeD = tab(64, 2, 64 ** -0.5)
    p = np.arange(128)[:, None]; f = np.arange(128)[None, :]
    m = np.zeros((NMASK, 128, 128), np.float32)
    m[M_ID] = (p == f); m[M_TRILS] = (p > f); m[M_TRIUI] = (p <= f); m[M_TRIUS] = (p < f); m[M_TRILI] = (p >= f)
    m[M_ONES] = 1.0
    m[M_NEG] = -1.0
    m[M_PA] = (p == (f ^ 32))
    m[M_PD] = (p == (f ^ 16))
    m[M_BD16] = (p // 16 == f // 16)
    m[M_OFF32] = (p // 32 == f // 32) & (p // 16 != f // 16)
    m[M_OFF64] = (p // 64 == f // 64) & (p // 32 != f // 32)
    m[M_OFF128] = (p // 64 != f // 64)
    return {"ropeA": ropeA, "ropeD": ropeD, "cmask": m}


def lockstep(gens, width):
    active = []
    it = iter(gens)
    done = False
    while True:
        while len(active) < width and not done:
            try:
                active.append(next(it))
            except StopIteration:
                done = True
        if not active:
            break
        for g in list(active):
            try:
                next(g)
            except StopIteration:
                active.remove(g)


class B:
    def __init__(self, cfg, dbg=()):
        self.cfg = cfg
        self.dbg = set(dbg)
        self.nc = bass.Bass("TRN2", target_bir_lowering=False)
        self.p = Prog(self.nc)
        self.dram = {}
        self.outs = []
        self.bank = 0

    def inp(self, name, shape, dt=F32):
        t = T(self.nc.dram_tensor(name, list(shape), dt, kind="ExternalInput").ap(), name)
        self.dram[name] = t
        return t

    def out(self, name, shape, dt=F32):
        t = T(self.nc.dram_tensor(name, list(shape), dt, kind="ExternalOutput").ap(), name)
        self.dram[name] = t
        self.outs.append(name)
        return t

    def scr(self, name, shape, dt):
        kind = "ExternalOutput" if name in self.dbg else "Internal"
        t = T(self.nc.dram_tensor(name, list(shape), dt, kind=kind).ap(), name)
        self.dram[name] = t
        if name in self.dbg:
            self.outs.append(name)
        return t

    def sb(self, st, name, shape, dt=F32):
        self.uid = getattr(self, "uid", 0) + 1
        name = f"{name}_{self.uid}"
        return T(st.enter_context(self.nc.sbuf_tensor(name, list(shape), dt)), name)

    def ps(self, st, name, shape, dt=F32):
        self.uid = getattr(self, "uid", 0) + 1
        name = f"{name}_{self.uid}"
        return T(st.enter_context(self.nc.psum_tensor(name, list(shape), dt)), name)

    def rsqrt(self, OUT, out_ap, IN, in_ap, eps, scale=1.0):
        p = self.p
        p.op("act", lambda e: e.activation(out=out_ap, in_=in_ap, func=AF.Sqrt, bias=self.epsc[eps][:out_ap.shape[0], :], scale=scale), reads=[IN, self.epst], writes=[OUT])
        p.op("dve", lambda e: e.reciprocal(out=out_ap, in_=out_ap), reads=[OUT], writes=[OUT])

    def declare(self):
        c = self.cfg
        D, T_, L, N, DFF, NL = c.D, c.T, c.L, c.N, c.DFF, c.NL
        i = self.inp
        i("x", [N, D]); i("c", [D]); i("ctx", [L, D]); i("c_ctx", [D])
        i("w_mod", [NL, D, 6 * D]); i("b_mod", [NL, 6 * D]); i("w_in", [NL, D, c.IN_W])
        i("wa_sink", [NL, 8])
        for n in ("df_lam_q1", "df_lam_k1", "df_lam_q2", "df_lam_k2"):
            i(n, [NL, 64])
        i("df_subln", [NL, 128]); i("dn_conv", [NL, 3, 3072]); i("dn_a_log", [NL, 16]); i("dn_dt_bias", [NL, 16])
        i("dn_norm", [NL, 128])
        i("w_branch_a", [NL, 1024, D]); i("w_branch_b", [NL, 1024, D]); i("w_branch_c", [NL, 1024, D])
        i("w_o", [NL, D, D]); i("ln1_g", [NL, D]); i("ln1_b", [NL, D])
        i("w_up", [NL, D, 2 * DFF]); i("ffn_conv_w", [NL, 3, 2 * DFF]); i("ffn_conv_b", [NL, 2 * DFF])
        i("w_down", [NL, DFF, D]); i("ln2_g", [NL, D]); i("ln2_b", [NL, D])
        i("ropeA", [4, 128, T_]); i("ropeD", [4, 128, T_]); i("cmask", [NMASK, 128, 128])
        self.out("y", [N, D])
        s = self.scr
        s("xres0", [T_, D], F32); s("xres1", [T_, D], F32); s("hT", [D, T_], BF16); s("modd", [NL, 2, 6 * D], F32)
        s("qaT", [1024, T_], BF16); s("kaT", [256, T_], BF16); s("va", [T_, 256], BF16)
        s("qdT", [1024, T_], BF16); s("kdT", [1024, T_], BF16); s("vd", [T_, 1024], BF16)
        s("gpre", [3072, T_], F32); s("zs", [T_, 1024], F32); s("ab", [T_, 32], F32); s("gtsT", [3 * D, T_], BF16)
        s("osub", [T_, D], F32)
        s("ya", [T_, 1024], BF16); s("yb", [T_, 1024], BF16); s("yc", [T_, 1024], BF16)
        s("qnT", [1024, T_], F32); s("knT", [1024, T_], F32); s("kn", [T_, 1024], F32); s("vn", [T_, 1024], F32)
        s("of", [T_, 1024], F32); s("mT", [D, T_], BF16); s("gT", [DFF, T_], BF16)
        s("yaT", [1024, T_], BF16); s("ybT", [1024, T_], BF16); s("ycT", [1024, T_], BF16)

    def consts(self, st):
        p = self.p
        cm = self.dram["cmask"]
        self.mask = self.sb(st, "mask", [128, NMASK, 128], F32)
        p.dma(self.mask[:], cm.t.rearrange("m p f -> p m f"), reads=[cm], writes=[self.mask])
        self.identb = self.sb(st, "identb", [128, 128], BF16)
        p.op("dve", lambda e: e.tensor_copy(out=self.identb[:], in_=self.mask[:, M_ID, :]), reads=[self.mask], writes=[self.identb])
        self.permb = self.sb(st, "permb", [128, 2, 128], BF16)
        p.op("dve", lambda e: e.tensor_copy(out=self.permb[:, 0, :], in_=self.mask[:, M_PA, :]), reads=[self.mask], writes=[self.permb])
        p.op("dve", lambda e: e.tensor_copy(out=self.permb[:, 1, :], in_=self.mask[:, M_PD, :]), reads=[self.mask], writes=[self.permb])
        self.epst = self.sb(st, "epst", [128, 4], F32)
        self.epsc = {}
        for i_, v_ in enumerate((1e-6, 0.0, 1.0)):
            p.op("pool", lambda e, i_=i_, v_=v_: e.memset(self.epst[:, i_:i_ + 1], v_), writes=[self.epst])
            self.epsc[v_] = self.epst[:, i_:i_ + 1]
        c = self.cfg
        self.modc = self.sb(st, "modc", [128, c.NL, 2, 6 * c.KC], F32)

    def stage_mod(self):
        c, p, nc = self.cfg, self.p, self.nc
        KC, D = c.KC, c.D
        NJ = 6 * KC
        wm, bm = self.dram["w_mod"], self.dram["b_mod"]
        with ExitStack() as st:
            craw = self.sb(st, "craw", [128, KC, 2], F32)
            sc = self.sb(st, "sc", [128, KC, 2], F32)
            bcol = self.sb(st, "bcol", [128, c.NL, NJ], F32)
            ps = [self.ps(st, f"mps{i}", [128, 4, 2], F32) for i in range(2)]
            pst = self.ps(st, "mpst", [128, 128], F32)
            wt = [self.sb(st, f"wmt{i}", [128, KC, 512], F32) for i in range(2)]
            tr = self.sb(st, "mtr", [NJ, 128], F32)
            cc, cx = self.dram["c"], self.dram["c_ctx"]
            p.dma(craw[:, :, 0], cc.t.rearrange("(kc p) -> p kc", p=128), reads=[cc], writes=[craw], allow_slow_non_contiguous=True)
            p.dma(craw[:, :, 1], cx.t.rearrange("(kc p) -> p kc", p=128), reads=[cx], writes=[craw], allow_slow_non_contiguous=True)
            for l in range(c.NL):
                p.dma(bcol[:, l, :], bm.t[l].rearrange("(j p) -> p j", p=128), reads=[bm], writes=[bcol], allow_slow_non_contiguous=True)
            p.op("act", lambda e: e.activation(out=sc[:], in_=craw[:], func=AF.Silu), reads=[craw], writes=[sc])
            k = 0
            for l in range(c.NL):
                wv = wm.t[l].rearrange("(kc p) n -> p kc n", p=128)
                for j0 in range(0, NJ, 4):
                    w = wt[k % 2]; pp = ps[k % 2]; k += 1
                    p.dma(w[:], wv[:, :, j0 * 128:(j0 + 4) * 128], reads=[wm], writes=[w])
                    for jj in range(4):
                        for kc in range(KC):
                            p.op("pe", lambda e, w=w, pp=pp, jj=jj, kc=kc: e.matmul(
                                pp[:, jj, :], w[:, kc, jj * 128:(jj + 1) * 128], sc[:, kc, :],
                                start=(kc == 0), stop=(kc == KC - 1)), reads=[w, sc], writes=[pp])
                    for s_ in range(2):
                        p.op("dve", lambda e, pp=pp, s_=s_, l=l, j0=j0: e.tensor_tensor(
                            out=self.modc[:, l, s_, j0:j0 + 4], in0=pp[:, :, s_], in1=bcol[:, l, j0:j0 + 4], op=ALU.add),
                            reads=[pp, bcol], writes=[self.modc])
                for s_ in range(2):
                    for v in (1, 4):
                        p.op("dve", lambda e, l=l, s_=s_, v=v: e.tensor_scalar_add(
                            out=self.modc[:, l, s_, v * KC:(v + 1) * KC], in0=self.modc[:, l, s_, v * KC:(v + 1) * KC], scalar1=1.0),
                            reads=[self.modc], writes=[self.modc])
                md = self.dram["modd"]
                for s_ in range(2):
                    p.op("pe", lambda e, l=l, s_=s_: e.matmul(pst[0:NJ, :], self.modc[:, l, s_, :], self.mask[:, M_ID, :], start=True, stop=True),
                         reads=[self.modc, self.mask], writes=[pst])
                    p.op("dve", lambda e: e.tensor_copy(out=tr[:], in_=pst[0:NJ, :]), reads=[pst], writes=[tr])
                    p.dma(md.t[l, s_].rearrange("(j p) -> j p", p=128), tr[:], reads=[tr], writes=[md])
        p.barrier()

    def stage_ln(self, l, comb, ada, final=False, src_inputs=False, skip_ctx=False):
        c, p = self.cfg, self.p
        D, KC, L = c.D, c.KC, c.L
        alpha = (2 * c.NL) ** 0.25
        xcur = getattr(self, "xcur", 0)
        xres, xdst = self.dram[f"xres{xcur}"], self.dram[f"xres{1 - xcur}"]
        osub, hT, md = self.dram["osub"], self.dram["hT"], self.dram["modd"]
        if comb and not final:
            self.xcur = 1 - xcur
        nch = (D + 511) // 512
        with ExitStack() as st:
            xt = [self.sb(st, f"lxt{i}", [128, D], F32) for i in range(3)]
            if comb:
                ot = [self.sb(st, f"lot{i}", [128, D], F32) for i in range(3)]
                gate = self.sb(st, "lgate", [128, 2, D], F32)
                gB = self.sb(st, "lgB", [128, D], F32)
                bB = self.sb(st, "lbB", [128, D], F32)
                gv = comb[0]
                for s_ in range(2):
                    p.dma(gate[:, s_, :], md.t[l, s_, gv * D:(gv + 1) * D].partition_broadcast(128), reads=[md], writes=[gate])
                gd, bd = self.dram[comb[1]], self.dram[comb[2]]
                p.dma(gB[:], gd.t[l].partition_broadcast(128), reads=[gd], writes=[gB])
                p.dma(bB[:], bd.t[l].partition_broadcast(128), reads=[bd], writes=[bB])
            stt = [self.sb(st, f"lst{i}", [128, nch, 6], F32) for i in range(3)]
            mv = [self.sb(st, f"lmv{i}", [128, 2], F32) for i in range(3)]
            rs = [self.sb(st, f"lrs{i}", [128, 1], F32) for i in range(3)]
            if ada:
                xb = [self.sb(st, f"lxb{i}", [128, D], BF16) for i in range(3)]
                ht = [self.sb(st, f"lht{i}", [128, KC, 128], BF16) for i in range(3)]
                pt = [self.ps(st, f"lpt{i}", [128, KC, 128], BF16) for i in range(3)]
            def tile(i):
                isctx = i * 128 < L
                s_ = 1 if isctx else 0
                b = i % 3
                X = xt[b]
                if src_inputs:
                    src = self.dram["ctx"] if isctx else self.dram["x"]
                    r0 = i * 128 if isctx else i * 128 - L
                else:
                    src = xres; r0 = i * 128
                p.dma(X[:], src.t[r0:r0 + 128, :], reads=[src], writes=[X])

                def lnstats(X, b):
                    S, MV, R = stt[b], mv[b], rs[b]
                    for ch in range(nch):
                        p.op("dve", lambda e, ch=ch: e.bn_stats(out=S[:, ch, :], in_=X[:, ch * 512:min(D, (ch + 1) * 512)]), reads=[X], writes=[S])
                    p.op("dve", lambda e: e.bn_aggr(out=MV[:], in_=S[:].rearrange("p a b -> p (a b)")), reads=[S], writes=[MV])
                    self.rsqrt(R, R[:], MV, MV[:, 1:2], LN_EPS)
                    return MV, R
                if comb:
                    O = ot[b]
                    p.dma(O[:], osub.t[i * 128:(i + 1) * 128, :], reads=[osub], writes=[O])
                    yield
                    p.op("pool", lambda e, O=O, s_=s_: e.tensor_tensor(out=O[:], in0=O[:], in1=gate[:, s_, :], op=ALU.mult), reads=[O, gate], writes=[O])
                    yield
                    p.op("dve", lambda e, O=O, X=X: e.scalar_tensor_tensor(out=X[:], in0=X[:], scalar=alpha, in1=O[:], op0=ALU.mult, op1=ALU.add), reads=[X, O], writes=[X])
                    MV, R = lnstats(X, b)
                    p.op("dve", lambda e, X=X, MV=MV, R=R: e.tensor_scalar(out=X[:], in0=X[:], scalar1=MV[:, 0:1], scalar2=R[:, 0:1], op0=ALU.subtract, op1=ALU.mult), reads=[X, MV, R], writes=[X])
                    yield
                    p.op("pool", lambda e, X=X: e.tensor_tensor(out=X[:], in0=X[:], in1=gB[:], op=ALU.mult), reads=[X, gB], writes=[X])
                    p.op("pool", lambda e, X=X: e.tensor_tensor(out=X[:], in0=X[:], in1=bB[:], op=ALU.add), reads=[X, bB], writes=[X])
                    if final:
                        if not isctx:
                            y = self.dram["y"]
                            p.dma(y.t[i * 128 - L:(i + 1) * 128 - L, :], X[:], reads=[X], writes=[y], is_out=True)
                    else:
                        p.dma(xdst.t[i * 128:(i + 1) * 128, :], X[:], reads=[X], writes=[xdst])
                if ada:
                    la, sv = ada
                    yield
                    MV, R = lnstats(X, b)
                    XB, HT, PT = xb[b], ht[b], pt[b]
                    p.op("dve", lambda e, X=X, XB=XB, MV=MV, R=R: e.tensor_scalar(out=XB[:], in0=X[:], scalar1=MV[:, 0:1], scalar2=R[:, 0:1], op0=ALU.subtract, op1=ALU.mult), reads=[X, MV, R], writes=[XB])
                    yield
                    for kc in range(KC):
                        p.op("pe", lambda e, kc=kc, XB=XB, PT=PT: e.transpose(PT[:, kc, :], XB[:, kc * 128:(kc + 1) * 128], self.identb[:]), reads=[XB, self.identb], writes=[PT])
                    yield
                    for kc in range(KC):
                        p.op("act", lambda e, kc=kc, HT=HT, PT=PT, s_=s_: e.activation(
                            out=HT[:, kc, :], in_=PT[:, kc, :], func=AF.Identity,
                            bias=self.modc[:, la, s_, sv * KC + kc:sv * KC + kc + 1],
                            scale=self.modc[:, la, s_, (sv + 1) * KC + kc:(sv + 1) * KC + kc + 1]),
                            reads=[PT, self.modc], writes=[HT])
                    p.dma(hT.t.rearrange("(kc p) t -> p kc t", p=128)[:, :, i * 128:(i + 1) * 128], HT[:], reads=[HT], writes=[hT])
            lockstep((tile(i) for i in range(c.NT) if not (skip_ctx and i * 128 < L)), 2)
        p.barrier()


def _proj(self, actT, K, W, wl, sections, tok0, tok1, NB=512, TG=512, tag="pj"):
    c, p = self.cfg, self.p
    KCs = K // 128
    wv = wl.rearrange("(kc p) n -> p kc n", p=128)
    av = actT.t.rearrange("(kc p) t -> p kc t", p=128)
    with ExitStack() as st:
        wbuf = [self.sb(st, f"{tag}w{i}", [128, KCs, NB], BF16) for i in range(2)]
        abuf = [self.sb(st, f"{tag}a{i}", [128, KCs, TG], BF16) for i in range(2)]
        nbank = 6
        banks = [self.ps(st, f"{tag}ps{i}", [128, 512], F32) for i in range(nbank)]
        self.pj_st = st
        for s in sections:
            if s.get("init"):
                s["init"](st)
        wi = 0; ai = 0; bi = 0
        bstate = {"bi": 0}

        def _alloc():
            b_ = banks[bstate["bi"] % nbank]; bstate["bi"] += 1
            return b_
        self.pj_alloc = _alloc
        blocks = []
        for s in sections:
            for nb0 in range(0, s["n"], NB):
                blocks.append((s, nb0, min(NB, s["n"] - nb0)))
        groups = [(g0, min(TG, tok1 - g0)) for g0 in range(tok0, tok1, TG)]
        nitems = len(blocks) * len(groups)

        def load_w(bx):
            s, nb0, nb = blocks[bx]
            wb = wbuf[bx % 2]
            for k0_ in range(0, KCs, 16):
                k1_ = min(KCs, k0_ + 16)
                p.dma(wb[:, k0_:k1_, :nb], wv[:, k0_:k1_, s["c0"] + nb0:s["c0"] + nb0 + nb], reads=[W], writes=[wb], eng="pool")
            return wb

        def load_a(kx):
            g0, ntok = groups[kx % len(groups)]
            ab = abuf[kx % 2]
            for k0_ in range(0, KCs, 16):
                k1_ = min(KCs, k0_ + 16)
                p.dma(ab[:, k0_:k1_, :ntok], av[:, k0_:k1_, g0:g0 + ntok], reads=[actT], writes=[ab])
            return ab
        wb_next = load_w(0); ab_next = load_a(0)
        kx = 0
        for bx, (s, nb0, nb) in enumerate(blocks):
            mode = s["mode"]
            wb = wb_next
            if bx + 1 < len(blocks):
                wb_next = load_w(bx + 1)
            for (g0, ntok) in groups:
                    ab = ab_next
                    if kx + 1 < nitems:
                        ab_next = load_a(kx + 1)
                    kx += 1
                    if s.get("pre"):
                        s["pre"](g0, ntok)
                    for sb0 in range(0, nb, 512):
                        sbn = min(512, nb - sb0)
                        if mode == "TM":
                            for tt in range(ntok // 128):
                                ps = _alloc()
                                for kc in range(KCs):
                                    p.op("pe", lambda e, ps=ps, ab=ab, wb=wb, kc=kc, tt=tt: e.matmul(
                                        ps[:, :sbn], ab[:, kc, tt * 128:(tt + 1) * 128], wb[:, kc, sb0:sb0 + sbn],
                                        start=(kc == 0), stop=(kc == KCs - 1)), reads=[ab, wb], writes=[ps])
                                s["epi"](ps, g0 + tt * 128, nb0 + sb0, sbn)
                        else:
                            for cc in range(sb0 // 128, (sb0 + sbn) // 128):
                                ps = _alloc()
                                for kc in range(KCs):
                                    p.op("pe", lambda e, ps=ps, ab=ab, wb=wb, kc=kc, cc=cc, ntok=ntok: e.matmul(
                                        ps[:, :ntok], wb[:, kc, cc * 128:(cc + 1) * 128], ab[:, kc, :ntok],
                                        start=(kc == 0), stop=(kc == KCs - 1)), reads=[ab, wb], writes=[ps])
                                s["epi"](ps, None, g0, ntok, nb0 + cc * 128)
            if s.get("flush") and (bx + 1 == len(blocks) or blocks[bx + 1][0] is not s):
                s["flush"]()
    p.barrier()
B.proj = _proj


def _stage_win(self, l, tok0=0):
    c, p = self.cfg, self.p
    D, T_ = c.D, c.T
    d = self.dram
    W = d["w_in"]
    state = {}

    def init(st):
        state["of"] = [self.sb(st, f"wiof{i}", [128, 512], F32) for i in range(3)]
        state["ob"] = [self.sb(st, f"wiob{i}", [128, 512], BF16) for i in range(3)]
        state["t1"] = [self.sb(st, f"wit1{i}", [128, 512], F32) for i in range(2)]
        state["t2"] = [self.sb(st, f"wit2{i}", [128, 512], F32) for i in range(2)]
        state["cos"] = [self.sb(st, f"wicos{i}", [128, 512], F32) for i in range(3)]
        state["sin"] = [self.sb(st, f"wisin{i}", [128, 512], F32) for i in range(3)]
        state["qb"] = [self.sb(st, f"wiqb{i}", [128, 512], BF16) for i in range(3)]
        state["k"] = 0; state["kc"] = 0; state["kq"] = 0; state["pend"] = None

    def tm_epi(dst, dcol0, func, bf):
        def epi(ps, tok, nb0, nb):
            k = state["k"]; state["k"] += 1
            o = (state["ob"] if bf else state["of"])[k % 3]
            p.op("act", lambda e: e.activation(out=o[:, :nb], in_=ps[:, :nb], func=func), reads=[ps], writes=[o])
            p.dma(dst.t[tok:tok + 128, dcol0 + nb0:dcol0 + nb0 + nb], o[:, :nb], reads=[o], writes=[dst])
        return epi

    def rope_pre(tabname, idx):
        tab = d[tabname]
        def pre(g0, ntok):
            k = state["kc"]; state["kc"] += 1
            cs, sn = state["cos"][k % 3], state["sin"][k % 3]
            p.dma(cs[:, :ntok], tab.t[idx, :, g0:g0 + ntok], reads=[tab], writes=[cs])
            p.dma(sn[:, :ntok], tab.t[idx + 1, :, g0:g0 + ntok], reads=[tab], writes=[sn])
            state["cs"] = (cs, sn)
        return pre

    def rope_finish():
        it = state["pend"]
        if it is None:
            return
        state["pend"] = None
        ps, qb, cs, sn, g0, ntok, c0, dst, which = it
        k = state["k"]; state["k"] += 1
        t1, t2, o = state["t1"][k % 2], state["t2"][k % 2], state["ob"][k % 3]
        ps2 = self.pj_alloc()
        p.op("pe", lambda e: e.matmul(ps2[:, :ntok], self.permb[:, which, :], qb[:, :ntok], start=True, stop=True), reads=[self.permb, qb], writes=[ps2])
        p.op("dve", lambda e: e.tensor_tensor(out=t1[:, :ntok], in0=ps[:, :ntok], in1=cs[:, :ntok], op=ALU.mult), reads=[ps, cs], writes=[t1])
        p.op("dve", lambda e: e.tensor_tensor(out=t2[:, :ntok], in0=ps2[:, :ntok], in1=sn[:, :ntok], op=ALU.mult), reads=[ps2, sn], writes=[t2])
        p.op("pool", lambda e: e.tensor_tensor(out=o[:, :ntok], in0=t1[:, :ntok], in1=t2[:, :ntok], op=ALU.add), reads=[t1, t2], writes=[o])
        p.dma(dst.t[c0:c0 + 128, g0:g0 + ntok], o[:, :ntok], reads=[o], writes=[dst])

    def rope_epi(dst, which):
        def epi(ps, ps2, g0, ntok, c0):
            rope_finish()
            kq = state["kq"]; state["kq"] += 1
            qb = state["qb"][kq % 3]
            cs, sn = state["cs"]
            p.op("dve", lambda e: e.tensor_copy(out=qb[:, :ntok], in_=ps[:, :ntok]), reads=[ps], writes=[qb])
            state["pend"] = (ps, qb, cs, sn, g0, ntok, c0, dst, which)
        return epi

    def fm_epi(dst, func=AF.Copy, bf=False):
        def epi(ps, ps2, g0, ntok, c0):
            k = state["k"]; state["k"] += 1
            o = (state["ob"] if bf else state["of"])[k % 3]
            p.op("act", lambda e: e.activation(out=o[:, :ntok], in_=ps[:, :ntok], func=func), reads=[ps], writes=[o])
            p.dma(dst.t[c0:c0 + 128, g0:g0 + ntok], o[:, :ntok], reads=[o], writes=[dst])
        return epi

    secs = [
        dict(c0=O_QA, n=1024, mode="FM", rope="A", pre=rope_pre("ropeA", 0), epi=rope_epi(d["qaT"], 0), flush=rope_finish, init=init),
        dict(c0=O_KA, n=256, mode="FM", rope="A", pre=rope_pre("ropeA", 2), epi=rope_epi(d["kaT"], 0), flush=rope_finish),
        dict(c0=O_QD, n=1024, mode="FM", rope="D", pre=rope_pre("ropeD", 0), epi=rope_epi(d["qdT"], 1), flush=rope_finish),
        dict(c0=O_KD, n=1024, mode="FM", rope="D", pre=rope_pre("ropeD", 2), epi=rope_epi(d["kdT"], 1), flush=rope_finish),
        dict(c0=O_QKV, n=3072, mode="FM", epi=fm_epi(d["gpre"])),
        dict(c0=O_VA, n=256, mode="TM", epi=tm_epi(d["va"], 0, AF.Copy, True)),
        dict(c0=O_VD, n=1024, mode="TM", epi=tm_epi(d["vd"], 0, AF.Copy, True)),
        dict(c0=O_Z, n=1024, mode="TM", epi=tm_epi(d["zs"], 0, AF.Silu, False)),
        dict(c0=O_A, n=32, mode="TM", epi=tm_epi(d["ab"], 0, AF.Copy, False)),
        dict(c0=O_G, n=3 * D, mode="FM", epi=fm_epi(d["gtsT"], AF.Sigmoid, True)),
    ]
    self.proj(d["hT"], D, W, W.t[l], secs, tok0, T_, NB=1024, tag="wi")
B.stage_win = _stage_win


def _stage_tm2fm(self, src, dstT, C, tok0=0):
    c, p = self.cfg, self.p
    nck = C // 128
    with ExitStack() as st:
        xt = [self.sb(st, f"tfx{i}", [128, C], BF16) for i in range(2)]
        ot = [self.sb(st, f"tfo{i}", [128, nck, 128], BF16) for i in range(2)]
        pt = [self.ps(st, f"tfp{i}", [128, nck, 128], BF16) for i in range(2)]
        for i in range(tok0 // 128, c.NT):
            X, O, P = xt[i % 2], ot[i % 2], pt[i % 2]
            p.dma(X[:], src.t[i * 128:(i + 1) * 128, :], reads=[src], writes=[X])
            for k in range(nck):
                p.op("pe", lambda e, k=k, X=X, P=P: e.transpose(P[:, k, :], X[:, k * 128:(k + 1) * 128], self.identb[:]), reads=[X, self.identb], writes=[P])
            p.op("act", lambda e, O=O, P=P: e.copy(out=O[:], in_=P[:]), reads=[P], writes=[O])
            p.dma(dstT.t.rearrange("(k p) t -> p k t", p=128)[:, :, i * 128:(i + 1) * 128], O[:], reads=[O], writes=[dstT])
    p.barrier()
B.stage_tm2fm = _stage_tm2fm


def _stage_diff(self, l, ctx_out):
    c, p = self.cfg, self.p
    T_, L, NT = c.T, c.L, c.NT
    d = self.dram
    qdT, kdT, vd, yb = d["qdT"], d["kdT"], d["vd"], d["yb"]
    lam_init = 0.8 - 0.6 * math.exp(-0.3 * l)
    with ExitStack() as st:
        lv = self.sb(st, "dflv", [128, 4, 64], F32)
        for i, n in enumerate(("df_lam_q1", "df_lam_k1", "df_lam_q2", "df_lam_k2")):
            p.dma(lv[:, i, :], d[n].t[l].partition_broadcast(128), reads=[d[n]], writes=[lv])
        lp = self.sb(st, "dflp", [128, 2, 64], F32)
        ls = self.sb(st, "dfls", [128, 2], F32)
        nlam = self.sb(st, "dfnl", [128, 1], F32)
        p.op("dve", lambda e: e.tensor_tensor(out=lp[:, 0, :], in0=lv[:, 0, :], in1=lv[:, 1, :], op=ALU.mult), reads=[lv], writes=[lp])
        p.op("dve", lambda e: e.tensor_tensor(out=lp[:, 1, :], in0=lv[:, 2, :], in1=lv[:, 3, :], op=ALU.mult), reads=[lv], writes=[lp])
        p.op("dve", lambda e: e.reduce_sum(out=ls[:], in_=lp[:], axis=AX.X), reads=[lp], writes=[ls])
        p.op("act", lambda e: e.activation(out=ls[:], in_=ls[:], func=AF.Exp), reads=[ls], writes=[ls])
        p.op("dve", lambda e: e.tensor_tensor(out=nlam[:], in0=ls[:, 1:2], in1=ls[:, 0:1], op=ALU.subtract), reads=[ls], writes=[nlam])
        p.op("dve", lambda e: e.tensor_scalar_add(out=nlam[:], in0=nlam[:], scalar1=-lam_init), reads=[nlam], writes=[nlam])
        sub = self.sb(st, "dfsub", [128, 128], F32)
        p.dma(sub[:], d["df_subln"].t[l].partition_broadcast(128), reads=[d["df_subln"]], writes=[sub])
        p.op("dve", lambda e: e.tensor_scalar_mul(out=sub[:], in0=sub[:], scalar1=1.0 - lam_init), reads=[sub], writes=[sub])

        kT = [self.sb(st, f"dfk{i}", [128, T_], BF16) for i in range(2)]
        vx = [self.sb(st, f"dfv{i}", [128, NT, 132], BF16) for i in range(2)]
        for i in range(2):
            p.op("pool", lambda e, i=i: e.memset(vx[i][:, :, 128:129], 1.0), writes=[vx[i]])
        qt = [self.sb(st, f"dfq{i}", [128, 512], BF16) for i in range(2)]
        pe_ = [[self.sb(st, f"dfp{j}{i}", [128, 512], BF16) for i in range(2)] for j in range(2)]
        sb_ = [[self.ps(st, f"dfs{j}{i}", [128, 512], F32) for i in range(2)] for j in range(2)]
        ob_ = [[self.ps(st, f"dfo{j}{i}", [128, 2, 256], F32) for i in range(2)] for j in range(2)]
        r12 = [self.sb(st, f"dfr{i}", [128, 2], F32) for i in range(2)]
        t1 = [self.sb(st, f"dft{i}", [128, 128], F32) for i in range(2)]
        o_ = [self.sb(st, f"dfoo{i}", [128, 128], F32) for i in range(2)]
        sq = [self.sb(st, f"dfsq{i}", [128, 128], F32) for i in range(2)]
        ss = [self.sb(st, f"dfss{i}", [128, 1], F32) for i in range(2)]
        yo = [self.sb(st, f"dfyo{i}", [128, 128], BF16) for i in range(2)]
        qi = 0; si = 0; ei = 0
        groups = []
        if ctx_out:
            groups.append((0, L, 0, L // 128))
        for g0 in range(L, T_, 512):
            groups.append((g0, min(512, T_ - g0), 0, NT))
        for h in range(8):
            K, V = kT[h % 2], vx[h % 2]
            p.dma(K[:], kdT.t[h * 128:(h + 1) * 128, :], reads=[kdT], writes=[K])
            vv_ = vd.t[:, h * 128:(h + 1) * 128].rearrange("(n p) c -> p n c", p=128)
            for n0_ in range(0, NT, 16):
                n1_ = min(NT, n0_ + 16)
                p.dma(V[:, n0_:n1_, 0:128], vv_[:, n0_:n1_, :], reads=[vd], writes=[V])
            for (g0, ntok, kt0, kt1) in groups:
                Q = qt[qi % 2]; qi += 1
                p.dma(Q[:, :ntok], qdT.t[h * 128:(h + 1) * 128, g0:g0 + ntok], reads=[qdT], writes=[Q])
                nq = ntok // 128

                def emit_S(kt):
                    for j in range(2):
                        S = sb_[j][kt % 2]
                        p.op("pe", lambda e, S=S, K=K, Q=Q, j=j, kt=kt, ntok=ntok: e.matmul(
                            S[:, :ntok], K[j * 64:(j + 1) * 64, kt * 128:(kt + 1) * 128], Q[j * 64:(j + 1) * 64, :ntok], start=True, stop=True),
                            reads=[K, Q], writes=[S])
                emit_S(kt0)
                for kt in range(kt0, kt1):
                    if kt + 1 < kt1:
                        emit_S(kt + 1)
                    for j in range(2):
                        S = sb_[j][kt % 2]; P = pe_[j][kt % 2]
                        p.op("act", lambda e, S=S, P=P, ntok=ntok: e.activation(out=P[:, :ntok], in_=S[:, :ntok], func=AF.Exp), reads=[S], writes=[P])
                        for qs in range(nq):
                            O = ob_[j][qs // 2]
                            p.op("pe", lambda e, O=O, P=P, V=V, qs=qs, kt=kt: e.matmul(
                                O[:, qs % 2, 0:129], P[:, qs * 128:(qs + 1) * 128], V[:, kt, 0:129], start=(kt == kt0 and qs % 2 == 0), stop=(kt == kt1 - 1)),
                                reads=[P, V], writes=[O])
                for qs in range(nq):
                    O1, O2 = ob_[0][qs // 2], ob_[1][qs // 2]
                    R_, T1, OO, SQ, SS, YO = r12[ei % 2], t1[ei % 2], o_[ei % 2], sq[ei % 2], ss[ei % 2], yo[ei % 2]; ei += 1
                    s2 = qs % 2
                    p.op("dve", lambda e, R_=R_, O1=O1, s2=s2: e.reciprocal(out=R_[:, 0:1], in_=O1[:, s2, 128:129]), reads=[O1], writes=[R_])
                    p.op("dve", lambda e, R_=R_, O2=O2, s2=s2: e.reciprocal(out=R_[:, 1:2], in_=O2[:, s2, 128:129]), reads=[O2], writes=[R_])
                    p.op("dve", lambda e, R_=R_: e.tensor_tensor(out=R_[:, 1:2], in0=R_[:, 1:2], in1=nlam[:], op=ALU.mult), reads=[R_, nlam], writes=[R_])
                    p.op("dve", lambda e, R_=R_, O1=O1, T1=T1, s2=s2: e.tensor_scalar_mul(out=T1[:], in0=O1[:, s2, 0:128], scalar1=R_[:, 0:1]), reads=[O1, R_], writes=[T1])
                    p.op("dve", lambda e, R_=R_, O2=O2, T1=T1, OO=OO, s2=s2: e.scalar_tensor_tensor(out=OO[:], in0=O2[:, s2, 0:128], scalar=R_[:, 1:2], in1=T1[:], op0=ALU.mult, op1=ALU.add), reads=[O2, R_, T1], writes=[OO])
                    p.op("act", lambda e, OO=OO, SQ=SQ, SS=SS: e.activation(out=SQ[:], in_=OO[:], func=AF.Square, accum_out=SS[:]), reads=[OO], writes=[SQ, SS])
                    self.rsqrt(SS, SS[:], SS, SS[:], 1e-6, scale=1.0 / 128)
                    p.op("dve", lambda e, OO=OO, SS=SS, YO=YO: e.scalar_tensor_tensor(out=YO[:], in0=OO[:], scalar=SS[:, 0:1], in1=sub[:], op0=ALU.mult, op1=ALU.mult), reads=[OO, SS, sub], writes=[YO])
                    t0 = g0 + qs * 128
                    p.dma(yb.t[t0:t0 + 128, h * 128:(h + 1) * 128], YO[:], reads=[YO], writes=[yb])
    p.barrier()
B.stage_diff = _stage_diff


def _stage_win_attn(self, l, ctx_out):
    c, p = self.cfg, self.p
    T_, L, NT = c.T, c.L, c.NT
    LT = L // 128
    d = self.dram
    qaT, kaT, va, ya = d["qaT"], d["kaT"], d["va"], d["ya"]
    with ExitStack() as st:
        es = self.sb(st, "waes", [128, 8], F32)
        p.dma(es[:], d["wa_sink"].t[l].partition_broadcast(128), reads=[d["wa_sink"]], writes=[es])
        p.op("act", lambda e: e.activation(out=es[:], in_=es[:], func=AF.Exp), reads=[es], writes=[es])
        mk = self.sb(st, "wamk", [128, 2, 4, 128], BF16)
        for h4 in range(4):
            p.op("dve", lambda e, h4=h4: e.tensor_copy(out=mk[:, 0, h4, :], in_=self.mask[:, M_TRILI, :]), reads=[self.mask], writes=[mk])
            p.op("dve", lambda e, h4=h4: e.tensor_copy(out=mk[:, 1, h4, :], in_=self.mask[:, M_TRIUI, :]), reads=[self.mask], writes=[mk])
        K = self.sb(st, "wak", [128, T_], BF16)
        V = self.sb(st, "wav", [128, NT, 132], BF16)
        p.op("pool", lambda e: e.memset(V[:, :, 128:129], 1.0), writes=[V])
        qt = [self.sb(st, f"waq{i}", [128, 4, 128], BF16) for i in range(2)]
        pp = [self.sb(st, f"wap{i}", [128, 512], BF16) for i in range(4)]
        sb_ = [self.ps(st, f"was{i}", [128, 512], F32) for i in range(4)]
        ob_ = [[self.ps(st, f"wao{i}{j}", [128, 2, 256], F32) for j in range(2)] for i in range(2)]
        rr = [self.sb(st, f"war{i}", [128, 1], F32) for i in range(4)]
        yo = [self.sb(st, f"wayo{i}", [128, 128], BF16) for i in range(4)]
        si = 0; ei = 0; bi = 0
        for g in range(2):
            p.dma(K[:], kaT.t[g * 128:(g + 1) * 128, :], reads=[kaT], writes=[K])
            vv_ = va.t[:, g * 128:(g + 1) * 128].rearrange("(n p) c -> p n c", p=128)
            for n0_ in range(0, NT, 16):
                n1_ = min(NT, n0_ + 16)
                p.dma(V[:, n0_:n1_, 0:128], vv_[:, n0_:n1_, :], reads=[va], writes=[V])
            def block(i, bi):
                Q = qt[bi % 2]; OB = ob_[bi % 2]
                SB = sb_[2 * (bi % 2):2 * (bi % 2) + 2]; PP = pp[2 * (bi % 2):2 * (bi % 2) + 2]
                for h4 in range(4):
                    h = g * 4 + h4
                    p.dma(Q[:, h4, :], qaT.t[h * 128:(h + 1) * 128, i * 128:(i + 1) * 128], reads=[qaT], writes=[Q])
                kts = [(kt, None) for kt in range(LT)]
                if i >= LT:
                    if i - 1 >= LT:
                        kts.append((i - 1, 0))
                    kts.append((i, None))
                    if i + 1 < NT:
                        kts.append((i + 1, 1))

                def emit_S(n_):
                    kt_ = kts[n_][0]
                    S_ = SB[n_ % 2]
                    p.op("pe", lambda e, S_=S_, Q=Q, kt_=kt_: e.matmul(S_[:], K[:, kt_ * 128:(kt_ + 1) * 128], Q[:].rearrange("p a b -> p (a b)"), start=True, stop=True),
                         reads=[K, Q], writes=[S_])
                yield
                emit_S(0)
                for n_, (kt, m) in enumerate(kts):
                    S = SB[n_ % 2]; P = PP[n_ % 2]
                    if n_ + 1 < len(kts):
                        emit_S(n_ + 1)
                    p.op("act", lambda e, S=S, P=P: e.activation(out=P[:], in_=S[:], func=AF.Exp), reads=[S], writes=[P])
                    if m is not None:
                        p.op("pool", lambda e, P=P, m=m: e.tensor_tensor(out=P[:], in0=P[:], in1=mk[:, m].rearrange("p a b -> p (a b)"), op=ALU.mult), reads=[P, mk], writes=[P])
                    yield
                    for h4 in range(4):
                        O = OB[h4 // 2]
                        p.op("pe", lambda e, O=O, P=P, h4=h4, kt=kt, n_=n_, nk=len(kts): e.matmul(
                            O[:, h4 % 2, 0:129], P[:, h4 * 128:(h4 + 1) * 128], V[:, kt, 0:129], start=(n_ == 0 and h4 % 2 == 0), stop=(n_ == nk - 1)),
                            reads=[P, V], writes=[O])
                yield
                for h4 in range(4):
                    h = g * 4 + h4
                    O = OB[h4 // 2]
                    R_, YO = rr[(bi * 4 + h4) % 4], yo[(bi * 4 + h4) % 4]
                    p.op("dve", lambda e, R_=R_, O=O, h4=h4, h=h: e.tensor_scalar(out=R_[:], in0=O[:, h4 % 2, 128:129], scalar1=es[:, h:h + 1], scalar2=None, op0=ALU.add), reads=[O, es], writes=[R_])
                    p.op("dve", lambda e, R_=R_: e.reciprocal(out=R_[:], in_=R_[:]), reads=[R_], writes=[R_])
                    p.op("dve", lambda e, R_=R_, O=O, YO=YO, h4=h4: e.tensor_scalar_mul(out=YO[:], in0=O[:, h4 % 2, 0:128], scalar1=R_[:, 0:1]), reads=[O, R_], writes=[YO])
                    p.dma(ya.t[i * 128:(i + 1) * 128, h * 128:(h + 1) * 128], YO[:], reads=[YO], writes=[ya])
            blocks = list(range(0 if ctx_out else LT, NT))
            lockstep((block(i, n) for n, i in enumerate(blocks)), 2)
    p.barrier()
B.stage_win_attn = _stage_win_attn


def _stage_gdn_conv(self, l):
    c, p = self.cfg, self.p
    T_, L, NT = c.T, c.L, c.NT
    d = self.dram
    gpre, qnT, knT, kn, vn = d["gpre"], d["qnT"], d["knT"], d["kn"], d["vn"]
    with ExitStack() as st:
        cw = self.sb(st, "gcw", [128, 24, 3], F32)
        for k_ in range(3):
            p.dma(cw[:, :, k_], d["dn_conv"].t[l, k_].rearrange("(ch p) -> p ch", p=128), reads=[d["dn_conv"]], writes=[cw], allow_slow_non_contiguous=True)
        G = [self.sb(st, f"gcG{i}", [128, 514], F32) for i in range(4)]
        Y = [self.sb(st, f"gcY{i}", [128, 512], F32) for i in range(4)]
        SQ = [self.sb(st, f"gcS{i}", [128, 512], F32) for i in range(4)]
        RI = [self.sb(st, f"gcR{i}", [128, 512], F32) for i in range(4)]
        YN = [self.sb(st, f"gcN{i}", [128, 512], F32) for i in range(4)]
        TT = [self.sb(st, f"gcT{i}", [128, 4, 128], F32) for i in range(4)]
        pss = [self.ps(st, f"gcps{i}", [128, 512], F32) for i in range(4)]
        ptt = [self.ps(st, f"gcpt{i}", [128, 4, 128], F32) for i in range(4)]
        ones = self.mask[:, M_ONES, :]
        ident = self.mask[:, M_ID, :]
        def iters():
            k = 0
            for ch in range(24):
                for (s0, s1) in ((0, L), (L, T_)):
                    for g0 in range(s0, s1, 512):
                        yield one(k, ch, s0, s1, g0)
                        k += 1

        def one(k, ch, s0, s1, g0):
                    kind = ch // 8
                    hh = ch % 8
                    ntok = min(512, s1 - g0)
                    g, y, sq, ri, yn, tt, ps1, pt1 = G[k % 4], Y[k % 4], SQ[k % 4], RI[k % 4], YN[k % 4], TT[k % 4], pss[k % 4], ptt[k % 4]
                    lo = max(s0, g0 - 1); hi = min(s1, g0 + ntok + 1)
                    if lo > g0 - 1:
                        p.op("pool", lambda e, g=g: e.memset(g[:, 0:1], 0.0), writes=[g])
                    if hi < g0 + ntok + 1:
                        p.op("pool", lambda e, g=g, ntok=ntok: e.memset(g[:, ntok + 1:ntok + 2], 0.0), writes=[g])
                    p.dma(g[:, lo - (g0 - 1):hi - (g0 - 1)], gpre.t[ch * 128:(ch + 1) * 128, lo:hi], reads=[gpre], writes=[g])
                    yield
                    p.op("dve", lambda e, g=g, y=y, ntok=ntok, ch=ch: e.tensor_scalar_mul(out=y[:, :ntok], in0=g[:, 0:ntok], scalar1=cw[:, ch, 0:1]), reads=[g, cw], writes=[y])
                    p.op("dve", lambda e, g=g, y=y, ntok=ntok, ch=ch: e.scalar_tensor_tensor(out=y[:, :ntok], in0=g[:, 1:ntok + 1], scalar=cw[:, ch, 1:2], in1=y[:, :ntok], op0=ALU.mult, op1=ALU.add), reads=[g, cw, y], writes=[y])
                    p.op("dve", lambda e, g=g, y=y, ntok=ntok, ch=ch: e.scalar_tensor_tensor(out=y[:, :ntok], in0=g[:, 2:ntok + 2], scalar=cw[:, ch, 2:3], in1=y[:, :ntok], op0=ALU.mult, op1=ALU.add), reads=[g, cw, y], writes=[y])
                    yield
                    p.op("act", lambda e, y=y, ntok=ntok: e.activation(out=y[:, :ntok], in_=y[:, :ntok], func=AF.Silu), reads=[y], writes=[y])
                    src = y
                    if kind < 2:
                        p.op("act", lambda e, y=y, sq=sq, ntok=ntok: e.activation(out=sq[:, :ntok], in_=y[:, :ntok], func=AF.Square), reads=[y], writes=[sq])
                        yield
                        p.op("pe", lambda e, ps1=ps1, sq=sq, ntok=ntok: e.matmul(ps1[:, :ntok], ones, sq[:, :ntok], start=True, stop=True), reads=[sq, self.mask], writes=[ps1])
                        yield
                        self.rsqrt(ri, ri[:, :ntok], ps1, ps1[:, :ntok], 1e-6)
                        sc_ = (128 ** -0.5) if kind == 0 else 1.0
                        p.op("dve", lambda e, y=y, ri=ri, yn=yn, ntok=ntok, sc_=sc_: e.scalar_tensor_tensor(out=yn[:, :ntok], in0=y[:, :ntok], scalar=sc_, in1=ri[:, :ntok], op0=ALU.mult, op1=ALU.mult), reads=[y, ri], writes=[yn])
                        dst = qnT if kind == 0 else knT
                        p.dma(dst.t[hh * 128:(hh + 1) * 128, g0:g0 + ntok], yn[:, :ntok], reads=[yn], writes=[dst])
                        src = yn
                    if kind >= 1:
                        nq = ntok // 128
                        for q_ in range(nq):
                            p.op("pe", lambda e, pt1=pt1, src=src, q_=q_: e.matmul(pt1[:, q_, :], src[:, q_ * 128:(q_ + 1) * 128], ident, start=True, stop=True), reads=[src, self.mask], writes=[pt1])
                        yield
                        p.op("act", lambda e, tt=tt, pt1=pt1, nq=nq: e.copy(out=tt[:, :nq, :], in_=pt1[:, :nq, :]), reads=[pt1], writes=[tt])
                        dst = kn if kind == 1 else vn
                        p.dma(dst.t[g0:g0 + ntok, hh * 128:(hh + 1) * 128].rearrange("(q p) c -> p q c", p=128), tt[:, :nq, :], reads=[tt], writes=[dst])
        lockstep(iters(), 3)
    p.barrier()
B.stage_gdn_conv = _stage_gdn_conv


def _stage_gdn_scan(self, l, ctx_out):
    c, p = self.cfg, self.p
    T_, L, NT = c.T, c.L, c.NT
    LT = L // 128
    d = self.dram
    qnT, knT, kn, vn, ab, of, zs, yc = d["qnT"], d["knT"], d["kn"], d["vn"], d["ab"], d["of"], d["zs"], d["yc"]
    M = lambda i: self.mask[:, i, :]
    with ExitStack() as st:
        S = self.sb(st, "gsS", [128, 8, 128], F32)
        S_h = [S.sub(f"S{h}") for h in range(8)]
        nal = self.sb(st, "gsnal", [128, 16], F32)
        dtb = self.sb(st, "gsdtb", [128, 16], F32)
        nw = self.sb(st, "gsnw", [128, 128], F32)
        p.dma(nal[:], d["dn_a_log"].t[l].partition_broadcast(128), reads=[d["dn_a_log"]], writes=[nal])
        p.dma(dtb[:], d["dn_dt_bias"].t[l].partition_broadcast(128), reads=[d["dn_dt_bias"]], writes=[dtb])
        p.dma(nw[:], d["dn_norm"].t[l].partition_broadcast(128), reads=[d["dn_norm"]], writes=[nw])
        p.op("act", lambda e: e.activation(out=nal[:], in_=nal[:], func=AF.Exp), reads=[nal], writes=[nal])
        p.op("dve", lambda e: e.tensor_scalar_mul(out=nal[:], in0=nal[:], scalar1=-1.0), reads=[nal], writes=[nal])
        banks = [self.ps(st, f"gsb{i}", [128, 4, 128], F32) for i in range(8)]
        slots = [(bnk, 0, bnk) for bnk in banks]
        sc = {"ps": 0}
        rings = {}

        def tmp(name, shape=(128, 128), n=6, dt=F32):
            if name == "X":
                n = 12
            if name not in rings:
                rings[name] = [[self.sb(st, f"gs_{name}{i}", list(shape), dt) for i in range(n)], 0]
            r = rings[name]
            t = r[0][r[1] % n]; r[1] += 1
            return t

        def mm(lhsT, lT, rhs, rT, ncols=128, acc=None):
            trk, j, bnk = slots[sc["ps"] % len(slots)]; sc["ps"] += 1
            ap = bnk[:, j, 0:ncols]
            p.op("pe", lambda e: e.matmul(ap, lhsT, rhs, start=True, stop=(acc is None)), reads=lT + rT, writes=[trk])
            if acc is not None:
                l2, l2T, r2, r2T = acc
                p.op("pe", lambda e: e.matmul(ap, l2, r2, start=False, stop=True), reads=l2T + r2T, writes=[trk])
            return trk, ap

        evi = {"k": 0}

        def evac(out_t, out_ap, trk, ap):
            k = evi["k"]; evi["k"] += 1
            if k % 2 == 0:
                p.op("act", lambda e: e.copy(out=out_ap, in_=ap), reads=[trk], writes=[out_t])
            else:
                p.op("dve", lambda e: e.tensor_copy(out=out_ap, in_=ap), reads=[trk], writes=[out_t])

        for dirn in range(2):
            cum = M_TRIUI if dirn == 0 else M_TRILI
            mL = M_TRILS if dirn == 0 else M_TRIUS
            mA = M_TRIUI if dirn == 0 else M_TRILI
            p.op("pool", lambda e: e.memset(S[:], 0.0), writes=[S] + S_h)
            if dirn == 0:
                order = list(range(NT))
            else:
                order = list(range(LT - 1, -1, -1)) + list(range(NT - 1, LT - 1, -1))
            for i in order:
                want = (i >= LT) or ctx_out
                tk = slice(i * 128, (i + 1) * 128)
                abt = tmp("abt", (128, 32), 2)
                p.dma(abt[:], ab.t[tk, :], reads=[ab], writes=[abt])
                knt = tmp("knt", (128, 1024), 2); vnt = tmp("vnt", (128, 1024), 2)
                kTt = tmp("kTt", (128, 8, 128), 2); qTt = tmp("qTt", (128, 8, 128), 2)
                p.dma(knt[:], kn.t[tk, :], reads=[kn], writes=[knt])
                p.dma(vnt[:], vn.t[tk, :], reads=[vn], writes=[vnt])
                p.dma(kTt[:], knT.t[:, tk].rearrange("(h p) t -> p h t", p=128), reads=[knT], writes=[kTt])
                p.dma(qTt[:], qnT.t[:, tk].rearrange("(h p) t -> p h t", p=128), reads=[qnT], writes=[qTt])
                gx = tmp("gx", (128, 8), 2); gax = tmp("gax", (128, 8), 2); g = tmp("g", (128, 8), 2); beta = tmp("beta", (128, 8), 2)
                ds = slice(dirn * 8, dirn * 8 + 8)
                p.op("dve", lambda e: e.tensor_tensor(out=gx[:], in0=abt[:, ds], in1=dtb[:, ds], op=ALU.add), reads=[abt, dtb], writes=[gx])
                p.op("act", lambda e: e.activation(out=gax[:], in_=gx[:], func=AF.Abs), reads=[gx], writes=[gax])
                p.op("act", lambda e: e.activation(out=gax[:], in_=gax[:], func=AF.Exp, scale=-1.0), reads=[gax], writes=[gax])
                p.op("act", lambda e: e.activation(out=gax[:], in_=gax[:], func=AF.Ln, bias=self.epsc[1.0], scale=1.0), reads=[gax, self.epst], writes=[gax])
                p.op("dve", lambda e: e.tensor_scalar_max(out=gx[:], in0=gx[:], scalar1=0.0), reads=[gx], writes=[gx])
                p.op("dve", lambda e: e.tensor_tensor(out=gx[:], in0=gx[:], in1=gax[:], op=ALU.add), reads=[gx, gax], writes=[gx])
                p.op("dve", lambda e: e.tensor_tensor(out=g[:], in0=gx[:], in1=nal[:, ds], op=ALU.mult), reads=[gx, nal], writes=[g])
                p.op("act", lambda e: e.activation(out=beta[:], in_=abt[:, 16 + dirn * 8:24 + dirn * 8], func=AF.Sigmoid), reads=[abt], writes=[beta])
                t1, a1 = mm(M(cum), [self.mask], g[:], [g], ncols=8)
                gcum = tmp("gcum", (128, 8), 2)
                evac(gcum, gcum[:], t1, a1)
                t2, a2 = mm(M(M_ONES), [self.mask], g[:], [g], ncols=8)
                gtot = tmp("gtot", (128, 8), 2)
                evac(gtot, gtot[:], t2, a2)
                eg = tmp("eg", (128, 8), 2); ekd = tmp("ekd", (128, 8), 2); egl = tmp("egl", (128, 8), 2); bk = tmp("bk", (128, 8), 2)
                p.op("act", lambda e: e.activation(out=eg[:], in_=gcum[:], func=AF.Exp), reads=[gcum], writes=[eg])
                p.op("dve", lambda e: e.tensor_tensor(out=ekd[:], in0=gtot[:], in1=gcum[:], op=ALU.subtract), reads=[gtot, gcum], writes=[ekd])
                p.op("act", lambda e: e.activation(out=ekd[:], in_=ekd[:], func=AF.Exp), reads=[ekd], writes=[ekd])
                p.op("act", lambda e: e.activation(out=egl[:], in_=gtot[:], func=AF.Exp), reads=[gtot], writes=[egl])
                p.op("dve", lambda e: e.tensor_tensor(out=bk[:], in0=beta[:], in1=eg[:], op=ALU.mult), reads=[beta, eg], writes=[bk])
                if want:
                    ot = tmp("ot", (128, 1024), 2)
                    if dirn == 1:
                        oft = tmp("oft", (128, 1024), 2); zt = tmp("zt", (128, 1024), 2)
                        p.dma(oft[:], of.t[tk, :], reads=[of], writes=[oft])
                        p.dma(zt[:], zs.t[tk, :], reads=[zs], writes=[zt])
                def unit(h):
                    hs = slice(h * 128, (h + 1) * 128)
                    hc = slice(h, h + 1)
                    kT = kTt[:, h, :]; qT = qTt[:, h, :]
                    Ug = tmp("Ug")
                    p.op("pool", lambda e: e.tensor_scalar(out=Ug[:], in0=M(cum), scalar1=g[:, hc], scalar2=None, op0=ALU.mult), reads=[self.mask, g], writes=[Ug])
                    yield
                    tD, aD = mm(M(M_ONES), [self.mask], Ug[:], [Ug])
                    DL = tmp("DL"); DU = tmp("DU")
                    p.op("dve", lambda e: e.tensor_scalar(out=DL[:], in0=aD, scalar1=gcum[:, hc], scalar2=0.0, op0=ALU.subtract, op1=ALU.max), reads=[tD, gcum], writes=[DL])
                    p.op("dve", lambda e: e.tensor_scalar(out=DU[:], in0=aD, scalar1=gcum[:, hc], scalar2=0.0, op0=ALU.subtract, op1=ALU.min), reads=[tD, gcum], writes=[DU])
                    p.op("act", lambda e: e.activation(out=DL[:], in_=DL[:], func=AF.Exp, scale=-1.0), reads=[DL], writes=[DL])
                    p.op("act", lambda e: e.activation(out=DU[:], in_=DU[:], func=AF.Exp), reads=[DU], writes=[DU])
                    p.op("pool", lambda e: e.tensor_tensor(out=DL[:], in0=DL[:], in1=M(mL), op=ALU.mult), reads=[DL, self.mask], writes=[DL])
                    p.op("pool", lambda e: e.tensor_tensor(out=DU[:], in0=DU[:], in1=M(mA), op=ALU.mult), reads=[DU, self.mask], writes=[DU])
                    yield
                    tG, aG = mm(kT, [kTt], kT, [kTt])
                    Lm = tmp("Lm")
                    p.op("dve", lambda e: e.scalar_tensor_tensor(out=Lm[:], in0=aG, scalar=beta[:, hc], in1=DL[:], op0=ALU.mult, op1=ALU.mult), reads=[tG, beta, DL], writes=[Lm])
                    if want:
                        tA, aA = mm(kT, [kTt], qT, [qTt])
                        AT = tmp("AT")
                        p.op("dve", lambda e: e.tensor_tensor(out=AT[:], in0=aA, in1=DU[:], op=ALU.mult), reads=[tA, DU], writes=[AT])
                    yield

                    def trn(src):
                        trk, j, bnk = slots[sc["ps"] % len(slots)]; sc["ps"] += 1
                        ap = bnk[:, j, 0:128]
                        p.op("pe", lambda e: e.transpose(ap, src[:], M(M_ID)), reads=[src, self.mask], writes=[trk])
                        return trk, ap
                    tN, aN = trn(Lm)
                    Nm = tmp("Nm")
                    evac(Nm, Nm[:], tN, aN)
                    L16 = tmp("L16"); N16 = tmp("N16")
                    p.op("pool", lambda e: e.tensor_tensor(out=L16[:], in0=Lm[:], in1=M(M_BD16), op=ALU.mult), reads=[Lm, self.mask], writes=[L16])
                    p.op("pool", lambda e: e.tensor_tensor(out=N16[:], in0=Nm[:], in1=M(M_BD16), op=ALU.mult), reads=[Nm, self.mask], writes=[N16])

                    def mmev(name, lh, lhT, rh, rhT):
                        t_, a_ = mm(lh[:], [lh], rh[:], [rh])
                        o_ = tmp(name)
                        evac(o_, o_[:], t_, a_)
                        return o_
                    yield
                    L2 = mmev("L2", N16, None, L16, None); N2 = mmev("N2", L16, None, N16, None)
                    yield
                    L4 = mmev("L4", N2, None, L2, None); N4 = mmev("N4", L2, None, N2, None)
                    yield
                    L8 = mmev("L8", N4, None, L4, None)
                    Q1 = tmp("P1")
                    p.op("pool", lambda e: e.tensor_tensor(out=Q1[:], in0=M(M_ID), in1=N16[:], op=ALU.subtract), reads=[N16, self.mask], writes=[Q1])

                    def mmadd(name, base, lh, rh):
                        t_, a_ = mm(lh[:], [lh], rh[:], [rh])
                        o_ = tmp(name)
                        p.op("dve", lambda e: e.tensor_tensor(out=o_[:], in0=base[:], in1=a_, op=ALU.add), reads=[base, t_], writes=[o_])
                        return o_
                    yield
                    Q2 = mmadd("P2", Q1, L2, Q1)
                    yield
                    Q3 = mmadd("P3", Q2, L4, Q2)
                    yield
                    Y = mmadd("X", Q3, L8, Q3)
                    for lvl in (M_OFF32, M_OFF64, M_OFF128):
                        Loff = tmp("Noff")
                        p.op("pool", lambda e, Loff=Loff, lvl=lvl: e.tensor_tensor(out=Loff[:], in0=Lm[:], in1=M(lvl), op=ALU.mult), reads=[Lm, self.mask], writes=[Loff])
                        yield
                        Xt_, Xa_ = trn(Y)
                        Xs = tmp("Y"); evac(Xs, Xs[:], Xt_, Xa_)
                        T2t, T2a = mm(Loff[:], [Loff], Y[:], [Y])
                        T2 = tmp("T2"); evac(T2, T2[:], T2t, T2a)
                        yield
                        Zt, Za = mm(Xs[:], [Xs], T2[:], [T2])
                        Yn = tmp("X")
                        p.op("dve", lambda e, Yn=Yn, Y=Y, Za=Za: e.tensor_tensor(out=Yn[:], in0=Y[:], in1=Za, op=ALU.subtract), reads=[Y, Zt], writes=[Yn])
                        Y = Yn
                    yield
                    RU = tmp("RU"); RW = tmp("RW"); KD = tmp("KD")
                    p.op("pool", lambda e: e.tensor_scalar(out=RU[:], in0=vnt[:, hs], scalar1=beta[:, hc], scalar2=None, op0=ALU.mult), reads=[vnt, beta], writes=[RU])
                    p.op("pool", lambda e: e.tensor_scalar(out=RW[:], in0=knt[:, hs], scalar1=bk[:, hc], scalar2=None, op0=ALU.mult), reads=[knt, bk], writes=[RW])
                    p.op("pool", lambda e: e.tensor_scalar(out=KD[:], in0=knt[:, hs], scalar1=ekd[:, hc], scalar2=None, op0=ALU.mult), reads=[knt, ekd], writes=[KD])
                    yield
                    ut, ua = mm(Y[:], [Y], RU[:], [RU])
                    U_ = tmp("U"); evac(U_, U_[:], ut, ua)
                    wt_, wa_ = mm(RW[:], [RW], Y[:], [Y])
                    WT = tmp("WT"); evac(WT, WT[:], wt_, wa_)
                    Sh = S_h[h]
                    yield
                    wst, wsa = mm(WT[:], [WT], S[:, h, :], [Sh])
                    VN = tmp("VN")
                    p.op("dve", lambda e: e.tensor_tensor(out=VN[:], in0=U_[:], in1=wsa, op=ALU.subtract), reads=[U_, wst], writes=[VN])
                    if want:
                        qst, qsa = mm(qT, [qTt], S[:, h, :], [Sh])
                        avt, ava = mm(AT[:], [AT], VN[:], [VN])
                        QS = tmp("QS")
                        p.op("dve", lambda e: e.tensor_scalar(out=QS[:], in0=qsa, scalar1=eg[:, hc], scalar2=None, op0=ALU.mult), reads=[qst, eg], writes=[QS])
                        if dirn == 0:
                            p.op("dve", lambda e: e.tensor_tensor(out=ot[:, hs], in0=QS[:], in1=ava, op=ALU.add), reads=[QS, avt], writes=[ot])
                        else:
                            p.op("dve", lambda e: e.tensor_tensor(out=QS[:], in0=QS[:], in1=ava, op=ALU.add), reads=[QS, avt], writes=[QS])
                            p.op("pool", lambda e: e.tensor_tensor(out=ot[:, hs], in0=QS[:], in1=oft[:, hs], op=ALU.add), reads=[QS, oft], writes=[ot])
                    yield
                    kvt, kva = mm(KD[:], [KD], VN[:], [VN])
                    p.op("dve", lambda e: e.scalar_tensor_tensor(out=S[:, h, :], in0=S[:, h, :], scalar=egl[:, hc], in1=kva, op0=ALU.mult, op1=ALU.add), reads=[Sh, egl, kvt], writes=[Sh])
                for hg in ((0, 1, 2, 3), (4, 5, 6, 7)):
                    gens = [unit(h) for h in hg]
                    while gens:
                        for g_ in list(gens):
                            try:
                                next(g_)
                            except StopIteration:
                                gens.remove(g_)
                if want:
                    if dirn == 0:
                        p.dma(of.t[tk, :], ot[:], reads=[ot], writes=[of])
                    else:
                        sq = tmp("osq", (128, 128), 2); ssq = tmp("ossq", (128, 8), 2)
                        yct = tmp("yct", (128, 1024), 2, BF16)
                        for h in range(8):
                            hs = slice(h * 128, (h + 1) * 128)
                            p.op("act", lambda e, hs=hs, h=h: e.activation(out=sq[:], in_=ot[:, hs], func=AF.Square, accum_out=ssq[:, h:h + 1]), reads=[ot], writes=[sq, ssq])
                        self.rsqrt(ssq, ssq[:], ssq, ssq[:], 1e-6, scale=1.0 / 128)
                        for h in range(8):
                            hs = slice(h * 128, (h + 1) * 128)
                            p.op("dve", lambda e, hs=hs, h=h: e.scalar_tensor_tensor(out=ot[:, hs], in0=ot[:, hs], scalar=ssq[:, h:h + 1], in1=nw[:], op0=ALU.mult, op1=ALU.mult), reads=[ot, ssq, nw], writes=[ot])
                        p.op("pool", lambda e: e.tensor_tensor(out=yct[:], in0=ot[:], in1=zt[:], op=ALU.mult), reads=[ot, zt], writes=[yct])
                        p.dma(yc.t[tk, :], yct[:], reads=[yct], writes=[yc])
    p.barrier()
B.stage_gdn_scan = _stage_gdn_scan


def _stage_merge(self, l):
    c, p = self.cfg, self.p
    D, T_ = c.D, c.T
    d = self.dram
    ys = [d["yaT"], d["ybT"], d["ycT"]]
    ws = [d["w_branch_a"], d["w_branch_b"], d["w_branch_c"]]
    gT, mT = d["gtsT"], d["mT"]
    NB = D
    with ExitStack() as st:
        wb = [[self.sb(st, f"mgw{b_}{i}", [128, 8, NB], BF16) for i in range(1)] for b_ in range(3)]
        ab = [[self.sb(st, f"mga{b_}{i}", [128, 8, 512], BF16) for i in range(2)] for b_ in range(3)]
        gt = [self.sb(st, f"mgg{i}", [128, 3, 512], BF16) for i in range(3)]
        t1 = [self.sb(st, f"mgt1{i}", [128, 512], F32) for i in range(2)]
        t2 = [self.sb(st, f"mgt2{i}", [128, 512], F32) for i in range(2)]
        mo = [self.sb(st, f"mgo{i}", [128, 512], BF16) for i in range(2)]
        banks = [self.ps(st, f"mgps{i}", [128, 512], F32) for i in range(6)]
        wi = ai = bi = k = 0
        for nb0 in range(0, D, NB):
            for b_ in range(3):
                p.dma(wb[b_][0][:], ws[b_].t[l].rearrange("(kc p) n -> p kc n", p=128)[:, :, nb0:nb0 + NB], reads=[ws[b_]], writes=[wb[b_][0]], eng="pool")
            W3 = [wb[b_][0] for b_ in range(3)]; wi += 1
            for g0 in range(0, T_, 512):
                ntok = min(512, T_ - g0)
                A3 = [ab[b_][ai % 2] for b_ in range(3)]; ai += 1
                for b_ in range(3):
                    p.dma(A3[b_][:, :, :ntok], ys[b_].t.rearrange("(kc p) t -> p kc t", p=128)[:, :, g0:g0 + ntok], reads=[ys[b_]], writes=[A3[b_]])
                for cc in range(NB // 128):
                    r0 = nb0 + cc * 128
                    G_ = gt[k % 3]; T1 = t1[k % 2]; T2 = t2[k % 2]; MO = mo[k % 2]; k += 1
                    for b_ in range(3):
                        p.dma(G_[:, b_, :ntok], gT.t[b_ * D + r0:b_ * D + r0 + 128, g0:g0 + ntok], reads=[gT], writes=[G_])
                    P3 = []
                    for b_ in range(3):
                        ps = banks[bi % 6]; bi += 1
                        for kc in range(8):
                            p.op("pe", lambda e, ps=ps, b_=b_, kc=kc: e.matmul(ps[:, :ntok], W3[b_][:, kc, cc * 128:(cc + 1) * 128], A3[b_][:, kc, :ntok],
                                                                              start=(kc == 0), stop=(kc == 7)), reads=[W3[b_], A3[b_]], writes=[ps])
                        P3.append(ps)
                    p.op("dve", lambda e: e.tensor_tensor(out=T1[:, :ntok], in0=P3[0][:, :ntok], in1=G_[:, 0, :ntok], op=ALU.mult), reads=[P3[0], G_], writes=[T1])
                    p.op("dve", lambda e: e.tensor_tensor(out=T2[:, :ntok], in0=P3[1][:, :ntok], in1=G_[:, 1, :ntok], op=ALU.mult), reads=[P3[1], G_], writes=[T2])
                    p.op("pool", lambda e: e.tensor_tensor(out=T1[:, :ntok], in0=T1[:, :ntok], in1=T2[:, :ntok], op=ALU.add), reads=[T1, T2], writes=[T1])
                    p.op("dve", lambda e: e.tensor_tensor(out=T2[:, :ntok], in0=P3[2][:, :ntok], in1=G_[:, 2, :ntok], op=ALU.mult), reads=[P3[2], G_], writes=[T2])
                    p.op("pool", lambda e: e.tensor_tensor(out=MO[:, :ntok], in0=T1[:, :ntok], in1=T2[:, :ntok], op=ALU.add), reads=[T1, T2], writes=[MO])
                    p.dma(mT.t[r0:r0 + 128, g0:g0 + ntok], MO[:, :ntok], reads=[MO], writes=[mT])
    p.barrier()
B.stage_merge = _stage_merge


def _stage_tm_proj(self, l, actname, K, wname, NB=512, TG=512):
    p = self.p
    d = self.dram
    osub = d["osub"]
    state = {"k": 0}

    def init(st):
        state["o"] = [self.sb(st, f"tpo{i}", [128, 512], F32) for i in range(3)]

    def epi(ps, tok, nb0, nb):
        o = state["o"][state["k"] % 3]; state["k"] += 1
        p.op("act", lambda e: e.copy(out=o[:, :nb], in_=ps[:, :nb]), reads=[ps], writes=[o])
        p.dma(osub.t[tok:tok + 128, nb0:nb0 + nb], o[:, :nb], reads=[o], writes=[osub])
    W = d[wname]
    secs = [dict(c0=0, n=self.cfg.D, mode="TM", epi=epi, init=init)]
    self.proj(d[actname], K, W, W.t[l], secs, 0, self.cfg.T, NB=NB, TG=TG, tag="tp")
B.stage_tm_proj = _stage_tm_proj


def _stage_ffn_up(self, l):
    c, p = self.cfg, self.p
    D, T_, L, DFF, KC = c.D, c.T, c.L, c.DFF, c.KC
    d = self.dram
    hT, gT, W = d["hT"], d["gT"], d["w_up"]
    NCH = 2 * DFF // 128
    HC = DFF // 128
    CB = 4 if HC % 4 == 0 else (2 if HC % 2 == 0 else 1)
    TG = 510
    with ExitStack() as st:
        cw = self.sb(st, "fucw", [128, NCH, 3], F32)
        cb = self.sb(st, "fucb", [128, NCH], F32)
        for k_ in range(3):
            p.dma(cw[:, :, k_], d["ffn_conv_w"].t[l, k_].rearrange("(ch p) -> p ch", p=128), reads=[d["ffn_conv_w"]], writes=[cw], allow_slow_non_contiguous=True)
        p.dma(cb[:], d["ffn_conv_b"].t[l].rearrange("(ch p) -> p ch", p=128), reads=[d["ffn_conv_b"]], writes=[cb], allow_slow_non_contiguous=True)
        wA = [self.sb(st, f"fuwa{i}", [128, KC, CB * 128], BF16) for i in range(2)]
        wB = [self.sb(st, f"fuwb{i}", [128, KC, CB * 128], BF16) for i in range(2)]
        ab = [self.sb(st, f"fua{i}", [128, KC, 512], BF16) for i in range(2)]
        ua = [self.sb(st, f"fuua{i}", [128, 512], F32) for i in range(2)]
        ub = [self.sb(st, f"fuub{i}", [128, 512], F32) for i in range(2)]
        go = [self.sb(st, f"fugo{i}", [128, 512], BF16) for i in range(2)]
        banks = [self.ps(st, f"fups{i}", [128, 512], F32) for i in range(6)]
        wv = W.t[l].rearrange("(kc p) n -> p kc n", p=128)
        av = hT.t.rearrange("(kc p) t -> p kc t", p=128)
        bi = k = 0
        cbs = list(range(0, HC, CB))
        items = [(s0, s1, g0) for (s0, s1) in ((0, L), (L, T_)) for g0 in range(s0, s1, TG)]
        ntot = len(cbs) * len(items)

        def load_w(bx):
            cb0 = cbs[bx]
            WA, WB = wA[bx % 2], wB[bx % 2]
            p.dma(WA[:], wv[:, :, cb0 * 128:(cb0 + CB) * 128], reads=[W], writes=[WA], eng="pool")
            p.dma(WB[:], wv[:, :, DFF + cb0 * 128:DFF + (cb0 + CB) * 128], reads=[W], writes=[WB], eng="pool")
            return WA, WB

        def load_a(kx):
            s0, s1, g0 = items[kx % len(items)]
            n = min(TG, s1 - g0)
            A = ab[kx % 2]
            lo = max(s0, g0 - 1); hi = min(s1, g0 + n + 1)
            if lo > g0 - 1:
                p.op("pool", lambda e: e.memset(A[:, :, 0:1], 0.0), writes=[A])
            if hi < g0 + n + 1:
                p.op("pool", lambda e: e.memset(A[:, :, n + 1:n + 2], 0.0), writes=[A])
            p.dma(A[:, :, lo - (g0 - 1):hi - (g0 - 1)], av[:, :, lo:hi], reads=[hT], writes=[A])
            return A
        w_next = load_w(0); a_next = load_a(0)
        kx = 0
        for bx, cb0 in enumerate(cbs):
            WA, WB = w_next
            if bx + 1 < len(cbs):
                w_next = load_w(bx + 1)
            for (s0, s1, g0) in items:
                n = min(TG, s1 - g0)
                A = a_next
                if kx + 1 < ntot:
                    a_next = load_a(kx + 1)
                kx += 1
                for cc in range(CB):
                    cha = cb0 + cc; chb = HC + cb0 + cc
                    pa = banks[bi % 6]; bi += 1
                    pb = banks[bi % 6]; bi += 1
                    for kc in range(KC):
                        p.op("pe", lambda e, kc=kc: e.matmul(pa[:, :n + 2], WA[:, kc, cc * 128:(cc + 1) * 128], A[:, kc, :n + 2], start=(kc == 0), stop=(kc == KC - 1)), reads=[WA, A], writes=[pa])
                    for kc in range(KC):
                        p.op("pe", lambda e, kc=kc: e.matmul(pb[:, :n + 2], WB[:, kc, cc * 128:(cc + 1) * 128], A[:, kc, :n + 2], start=(kc == 0), stop=(kc == KC - 1)), reads=[WB, A], writes=[pb])
                    UA, UB, GO = ua[k % 2], ub[k % 2], go[k % 2]; k += 1
                    for (U, ps, ch) in ((UA, pa, cha), (UB, pb, chb)):
                        p.op("dve", lambda e, U=U, ps=ps, ch=ch: e.tensor_scalar(out=U[:, :n], in0=ps[:, 0:n], scalar1=cw[:, ch, 0:1], scalar2=cb[:, ch:ch + 1], op0=ALU.mult, op1=ALU.add), reads=[ps, cw, cb], writes=[U])
                        p.op("dve", lambda e, U=U, ps=ps, ch=ch: e.scalar_tensor_tensor(out=U[:, :n], in0=ps[:, 1:n + 1], scalar=cw[:, ch, 1:2], in1=U[:, :n], op0=ALU.mult, op1=ALU.add), reads=[ps, cw, U], writes=[U])
                        p.op("dve", lambda e, U=U, ps=ps, ch=ch: e.scalar_tensor_tensor(out=U[:, :n], in0=ps[:, 2:n + 2], scalar=cw[:, ch, 2:3], in1=U[:, :n], op0=ALU.mult, op1=ALU.add), reads=[ps, cw, U], writes=[U])
                    p.op("act", lambda e: e.activation(out=UA[:, :n], in_=UA[:, :n], func=AF.Silu), reads=[UA], writes=[UA])
                    p.op("pool", lambda e: e.tensor_tensor(out=GO[:, :n], in0=UA[:, :n], in1=UB[:, :n], op=ALU.mult), reads=[UA, UB], writes=[GO])
                    p.dma(gT.t[cha * 128:(cha + 1) * 128, g0:g0 + n], GO[:, :n], reads=[GO], writes=[gT])
    p.barrier()
B.stage_ffn_up = _stage_ffn_up


def _build_all(self, st, upto=None):
    c = self.cfg
    self.declare()
    self.consts(st)
    self.stage_mod()
    self.stage_ln(0, None, (0, 0), src_inputs=True)
    for l in range(c.NL):
        last = (l == c.NL - 1)
        ctx_out = not last
        self.stage_win(l)
        self.stage_win_attn(l, ctx_out)
        self.stage_diff(l, ctx_out)
        self.stage_gdn_conv(l)
        self.stage_gdn_scan(l, ctx_out)
        d = self.dram
        self.stage_tm2fm(d["ya"], d["yaT"], 1024)
        self.stage_tm2fm(d["yb"], d["ybT"], 1024)
        self.stage_tm2fm(d["yc"], d["ycT"], 1024)
        self.stage_merge(l)
        self.stage_tm_proj(l, "mT", c.D, "w_o", NB=min(1024, c.D))
        if upto == "mix" and l == 0:
            break
        self.stage_ln(l, (2, "ln1_g", "ln1_b"), (l, 3), src_inputs=(l == 0))
        self.stage_ffn_up(l)
        kdown = c.DFF
        big = (kdown // 128) > 16
        self.stage_tm_proj(l, "gT", kdown, "w_down", NB=512, TG=256 if big else 512)
        if last:
            self.stage_ln(l, (5, "ln2_g", "ln2_b"), None, final=True)
        else:
            self.stage_ln(l, (5, "ln2_g", "ln2_b"), (l + 1, 0))
    self.p.emit(st)
B.build_all = _build_all


_CACHE = {}


def _get_nc(cfg_key):
    if cfg_key not in _CACHE:
        cfg = Cfg(*cfg_key)
        b = B(cfg)
        st = ExitStack()
        b.build_all(st)
        _CACHE[cfg_key] = (b, st, cfg)
    return _CACHE[cfg_key]


def kernel(**inputs):
    x = np.asarray(inputs["x"])
    bsz, N, D = x.shape
    L = inputs["ctx"].shape[1]
    DFF = inputs["w_down"].shape[1]
    NL = inputs["w_mod"].shape[0]
    b, st, cfg = _get_nc((D, N, L, DFF, NL))
    consts = make_consts(cfg)
    shared = {}
    for k, v in inputs.items():
        if k in ("x", "c", "ctx", "c_ctx"):
            continue
        shared[k] = np.ascontiguousarray(np.asarray(v), dtype=np.float32)
    shared["dn_a_log"] = shared["dn_a_log"].reshape(NL, 16)
    shared["dn_dt_bias"] = shared["dn_dt_bias"].reshape(NL, 16)
    shared["c_ctx"] = np.ascontiguousarray(np.asarray(inputs["c_ctx"]), dtype=np.float32)
    shared.update(consts)
    n_cores = 8 if bsz <= 4 else bsz
    hot = [0, 1, 4, 5][:bsz] if bsz <= 4 else list(range(bsz))
    zx = np.zeros_like(np.ascontiguousarray(x[0], dtype=np.float32))
    zc = np.zeros((D,), np.float32)
    zctx = np.zeros((L, D), np.float32)
    in_maps = []
    for core in range(n_cores):
        m = dict(shared)
        if core in hot:
            i = hot.index(core)
            m["x"] = np.ascontiguousarray(x[i], dtype=np.float32)
            m["c"] = np.ascontiguousarray(np.asarray(inputs["c"])[i], dtype=np.float32)
            m["ctx"] = np.ascontiguousarray(np.asarray(inputs["ctx"])[i], dtype=np.float32)
        else:
            m["x"], m["c"], m["ctx"] = zx, zc, zctx
        in_maps.append(m)
    res = run_bass_kernel_spmd(b.nc, in_maps, core_ids=list(range(n_cores)))
    return np.stack([np.asarray(res.results[core]["y"], dtype=np.float32) for core in hot], axis=0)
```
